# Optimizing a Trainium2 kernel written in Bass

```python
import jax, jax.numpy as jnp
from jax import lax
import numpy as np

D_MODEL = 1024
BATCH = 2
SEQ = 8192
DEPTH = 1

MEM_LEN = 256
CHUNK = 128
SG_GROUPS = 8
SG_GROUP_DIM = 64
SG_WIDTH = SG_GROUPS * SG_GROUP_DIM
MLA_HEADS = 8
MLA_NOPE = 64
MLA_ROPE = 32
MLA_V = 64
MLA_QK = MLA_NOPE + MLA_ROPE
MLA_Q_RANK = 384
MLA_KV_RANK = 256
MLA_WIDTH = MLA_HEADS * MLA_V
MEM_HEADS = 4
MEM_HEAD_DIM = 128
MEM_WIDTH = MEM_HEADS * MEM_HEAD_DIM
N_BRANCH = 3
D_FF = 2816
ROPE_BASE = 10000.0
EPS = 1e-6
Q_BLOCK = 128
NEG = -1e30

COL_U = 0
COL_V = COL_U + SG_WIDTH
COL_CQ = COL_V + SG_WIDTH
COL_CKV = COL_CQ + MLA_Q_RANK
COL_KR = COL_CKV + MLA_KV_RANK
COL_QM = COL_KR + MLA_ROPE
COL_GATE = COL_QM + MEM_WIDTH
IN_COLS = COL_GATE + N_BRANCH * D_MODEL

kernel_name = "hybrid_gated_sgu_mla_memxattn_macaron"


def rmsnorm(x, g):
    xf = x.astype(jnp.float32)
    y = xf * lax.rsqrt(jnp.mean(xf * xf, axis=-1, keepdims=True) + EPS)
    return (y * g.astype(jnp.float32)).astype(x.dtype)


def layernorm(x, g, b):
    xf = x.astype(jnp.float32)
    mu = jnp.mean(xf, axis=-1, keepdims=True)
    xc = xf - mu
    y = xc * lax.rsqrt(jnp.mean(xc * xc, axis=-1, keepdims=True) + EPS)
    return (y * g.astype(jnp.float32) + b.astype(jnp.float32)).astype(x.dtype)


def rope(x, positions):
    half = x.shape[-1] // 2
    inv = ROPE_BASE ** (-jnp.arange(half, dtype=jnp.float32) / half)
    ang = positions.astype(jnp.float32)[:, :, None] * inv
    cos = jnp.cos(ang)[:, :, None, :]
    sin = jnp.sin(ang)[:, :, None, :]
    x1 = x[..., :half].astype(jnp.float32)
    x2 = x[..., half:].astype(jnp.float32)
    return jnp.concatenate([x1 * cos - x2 * sin, x2 * cos + x1 * sin], axis=-1).astype(x.dtype)


def swiglu(x, w_gu, w_down):
    g, u = jnp.split(x @ w_gu, 2, axis=-1)
    return (jax.nn.silu(g) * u) @ w_down


def spatial_gating(u, v, ln_g, ln_b, w_s, b_s):
    B, S, _ = v.shape
    nc = S // CHUNK
    v = layernorm(v, ln_g, ln_b).reshape(B, nc, CHUNK, SG_GROUPS, SG_GROUP_DIM)
    causal = jnp.tril(jnp.ones((CHUNK, CHUNK), dtype=bool))
    w = jnp.where(causal[None], w_s, jnp.zeros_like(w_s))
    mixed = jnp.einsum('gts,bcsgd->bctgd', w, v) + b_s.T[None, None, :, :, None]
    return u * mixed.reshape(B, S, SG_WIDTH)


def causal_block_attention(q, k, v):
    B, S, H, Dqk = q.shape
    Dv = v.shape[-1]
    nb = S // Q_BLOCK
    scale = Dqk ** -0.5
    qb = q.reshape(B, nb, Q_BLOCK, H, Dqk).transpose(1, 0, 2, 3, 4)
    kpos = jnp.arange(S)

    def one_block(args):
        i, qi = args
        s = jnp.einsum('bqhd,bkhd->bhqk', qi, k).astype(jnp.float32) * scale
        qpos = i * Q_BLOCK + jnp.arange(Q_BLOCK)
        mask = kpos[None, :] <= qpos[:, None]
        s = jnp.where(mask[None, None], s, NEG)
        p = jax.nn.softmax(s, axis=-1)
        return jnp.einsum('bhqk,bkhd->bqhd', p.astype(v.dtype), v)

    out = lax.map(one_block, (jnp.arange(nb), qb))
    return out.transpose(1, 0, 2, 3, 4).reshape(B, S, H * Dv)


def mla(c_q, c_kv, k_rope, positions, cq_norm, w_uq, ckv_norm, w_ukv, q_norm, k_norm):
    B, S, _ = c_q.shape
    q = (rmsnorm(c_q, cq_norm) @ w_uq).reshape(B, S, MLA_HEADS, MLA_QK)
    q = rmsnorm(q, q_norm)
    q = jnp.concatenate([q[..., :MLA_NOPE], rope(q[..., MLA_NOPE:], positions)], axis=-1)
    kv = (rmsnorm(c_kv, ckv_norm) @ w_ukv).reshape(B, S, MLA_HEADS, MLA_NOPE + MLA_V)
    k_nope, v = kv[..., :MLA_NOPE], kv[..., MLA_NOPE:]
    k_pe = jnp.broadcast_to(k_rope[:, :, None, :], (B, S, MLA_HEADS, MLA_ROPE))
    k = rmsnorm(jnp.concatenate([k_nope, k_pe], axis=-1), k_norm)
    k = jnp.concatenate([k[..., :MLA_NOPE], rope(k[..., MLA_NOPE:], positions)], axis=-1)
    return causal_block_attention(q, k, v)


def memory_attention(q_m, mem, mem_norm, w_kv, q_norm, k_norm):
    B, S, _ = q_m.shape
    q = rmsnorm(q_m.reshape(B, S, MEM_HEADS, MEM_HEAD_DIM), q_norm)
    kv = rmsnorm(mem, mem_norm) @ w_kv
    M = mem.shape[1]
    k = rmsnorm(kv[..., :MEM_WIDTH].reshape(B, M, MEM_HEADS, MEM_HEAD_DIM), k_norm)
    v = kv[..., MEM_WIDTH:].reshape(B, M, MEM_HEADS, MEM_HEAD_DIM)
    s = jnp.einsum('bshd,bmhd->bhsm', q, k).astype(jnp.float32) * (MEM_HEAD_DIM ** -0.5)
    p = jax.nn.softmax(s, axis=-1)
    return jnp.einsum('bhsm,bmhd->bshd', p.astype(v.dtype), v).reshape(B, S, MEM_WIDTH)


def setup_inputs(seed: int = 0) -> dict:
    key = jax.random.key(seed)
    ks = iter(jax.random.split(key, 40))

    def nrm(shape, scale):
        return jax.random.normal(next(ks), shape, jnp.float32) * scale

    def gain(n):
        return 1.0 + nrm((DEPTH, n), 0.05)

    L = DEPTH
    d = D_MODEL
    x = nrm((BATCH, SEQ, d), 1.0)
    mem = nrm((BATCH, MEM_LEN, d), 1.0)
    offset = jax.random.randint(next(ks), (BATCH, 1), 0, 1024, dtype=jnp.int32)
    positions = offset + jnp.arange(SEQ, dtype=jnp.int32)[None, :]
    return {
        "x": x,
        "mem": mem,
        "positions": positions,
        "ffn1_norm": gain(d),
        "ffn1_w_gu": nrm((L, d, 2 * D_FF), d ** -0.5),
        "ffn1_w_down": nrm((L, D_FF, d), D_FF ** -0.5),
        "mix_norm": gain(d),
        "w_in": nrm((L, d, IN_COLS), d ** -0.5),
        "b_gate": nrm((L, N_BRANCH * d), 0.02),
        "sg_ln_g": gain(SG_WIDTH),
        "sg_ln_b": nrm((L, SG_WIDTH), 0.02),
        "sg_w": nrm((L, SG_GROUPS, CHUNK, CHUNK), 0.5 * CHUNK ** -0.5),
        "sg_b": 1.0 + nrm((L, SG_GROUPS, CHUNK), 0.1),
        "mla_cq_norm": gain(MLA_Q_RANK),
        "mla_w_uq": nrm((L, MLA_Q_RANK, MLA_HEADS * MLA_QK), MLA_Q_RANK ** -0.5),
        "mla_ckv_norm": gain(MLA_KV_RANK),
        "mla_w_ukv": nrm((L, MLA_KV_RANK, MLA_HEADS * (MLA_NOPE + MLA_V)), MLA_KV_RANK ** -0.5),
        "mla_q_norm": gain(MLA_QK),
        "mla_k_norm": gain(MLA_QK),
        "mem_norm": gain(d),
        "mem_w_kv": nrm((L, d, 2 * MEM_WIDTH), d ** -0.5),
        "mem_q_norm": gain(MEM_HEAD_DIM),
        "mem_k_norm": gain(MEM_HEAD_DIM),
        "w_branch_a": nrm((L, SG_WIDTH, d), SG_WIDTH ** -0.5),
        "w_branch_b": nrm((L, MLA_WIDTH, d), MLA_WIDTH ** -0.5),
        "w_branch_c": nrm((L, MEM_WIDTH, d), MEM_WIDTH ** -0.5),
        "w_out": nrm((L, d, d), d ** -0.5),
        "ffn2_norm": gain(d),
        "ffn2_w_gu": nrm((L, d, 2 * D_FF), d ** -0.5),
        "ffn2_w_down": nrm((L, D_FF, d), D_FF ** -0.5),
    }


def reference(x, mem, positions, ffn1_norm, ffn1_w_gu, ffn1_w_down, mix_norm, w_in, b_gate,
              sg_ln_g, sg_ln_b, sg_w, sg_b, mla_cq_norm, mla_w_uq, mla_ckv_norm, mla_w_ukv,
              mla_q_norm, mla_k_norm, mem_norm, mem_w_kv, mem_q_norm, mem_k_norm,
              w_branch_a, w_branch_b, w_branch_c, w_out, ffn2_norm, ffn2_w_gu, ffn2_w_down):
    B, S, _ = x.shape
    for l in range(DEPTH):
        x = x + 0.5 * swiglu(rmsnorm(x, ffn1_norm[l]), ffn1_w_gu[l], ffn1_w_down[l])
        h = rmsnorm(x, mix_norm[l])
        z = h @ w_in[l]
        u = jax.nn.gelu(z[..., COL_U:COL_V], approximate=False)
        v = jax.nn.gelu(z[..., COL_V:COL_CQ], approximate=False)
        y_a = spatial_gating(u, v, sg_ln_g[l], sg_ln_b[l], sg_w[l], sg_b[l])
        y_b = mla(z[..., COL_CQ:COL_CKV], z[..., COL_CKV:COL_KR], z[..., COL_KR:COL_QM], positions,
                  mla_cq_norm[l], mla_w_uq[l], mla_ckv_norm[l], mla_w_ukv[l],
                  mla_q_norm[l], mla_k_norm[l])
        y_c = memory_attention(z[..., COL_QM:COL_GATE], mem, mem_norm[l], mem_w_kv[l],
                               mem_q_norm[l], mem_k_norm[l])
        gates = jax.nn.sigmoid(z[..., COL_GATE:] + b_gate[l]).reshape(B, S, N_BRANCH, D_MODEL)
        merged = (gates[:, :, 0] * (y_a @ w_branch_a[l])
                  + gates[:, :, 1] * (y_b @ w_branch_b[l])
                  + gates[:, :, 2] * (y_c @ w_branch_c[l]))
        x = x + merged @ w_out[l]
        x = x + 0.5 * swiglu(rmsnorm(x, ffn2_norm[l]), ffn2_w_gu[l], ffn2_w_down[l])
    return x
```

```python
import numpy as np
import ml_dtypes
from contextlib import ExitStack
import concourse.bass as bass
import concourse.mybir as mybir
from concourse.bass_utils import run_bass_kernel_spmd

F32 = mybir.dt.float32
BF16 = mybir.dt.bfloat16
I32 = mybir.dt.int32
AF = mybir.ActivationFunctionType
ALU = mybir.AluOpType
AX = mybir.AxisListType

NCORES = 8
D = 1024
NT = 2048
DFF = 2816
FC = 22
EPS = 1e-6
C_U, C_V, C_CQ, C_CKV, C_KR, C_QM, C_GATE = 0, 512, 1024, 1408, 1664, 1696, 2208
PI = float(np.pi)

CST = {}
_off = 0
for _n, _w in [("ffn1_g", 8), ("mix_g", 8), ("ffn2_g", 8), ("mem_g", 8), ("bgate", 24), ("cqg", 384),
               ("ckvg", 256), ("lng", 512), ("lnb", 512), ("qg", 96), ("kg", 96), ("mqg", 128),
               ("mkg", 128), ("bsT", 512), ("invf", 16), ("tri", 128)]:
    CST[_n] = (_off, _w)
    _off += _w
NCST = _off
NCBF = 128 + 128 + 8 * 128


class Prog:
    ENG = ("pe", "act", "dve", "pool", "sp")

    def __init__(self, nc, es):
        self.nc, self.es = nc, es
        self.q = {e: [] for e in self.ENG}
        self.sems = {}
        self.cnt = {}
        self.waited = {e: {} for e in self.ENG}
        self.lastw = {}
        self.readers = {}
        self.out_tokens = []
        self.regions = {}
        self._ovc = {}

    def sem(self, name):
        if name not in self.sems:
            self.sems[name] = self.es.enter_context(self.nc.semaphore("s_" + name))
            self.cnt[name] = 0
        return self.sems[name]

    def region(self, key, ivs):
        self.regions[key] = list(ivs)

    def _overlap(self, k):
        if k not in self.regions:
            return (k,)
        c = self._ovc.get(k)
        if c is not None and c[0] == len(self.regions):
            return c[1]
        mine = self.regions[k]
        res = [k]
        for k2, ivs in self.regions.items():
            if k2 == k:
                continue
            hit = False
            for (a, b) in mine:
                for (c0, d0) in ivs:
                    if a < d0 and c0 < b:
                        hit = True
                        break
                if hit:
                    break
            if hit:
                res.append(k2)
        self._ovc[k] = (len(self.regions), tuple(res))
        return self._ovc[k][1]

    def _collect(self, eng, reads, writes, is_dma):
        toks = []
        for k in reads:
            for k2 in self._overlap(k):
                if k2 in self.lastw:
                    toks.append((self.lastw[k2], True))
        for k in writes:
            for k2 in self._overlap(k):
                if k2 in self.lastw:
                    toks.append((self.lastw[k2], False))
                rd = self.readers.get(k2)
                if rd:
                    toks.extend(((sn, v, te), False) for (sn, te), v in rd.items())
        waits = {}
        for ((sn, val, teng), raw) in toks:
            if teng is not None and teng == eng and not is_dma:
                if not raw or eng == "pe":
                    continue
            if self.waited[eng].get(sn, 0) >= val:
                continue
            waits[sn] = max(waits.get(sn, 0), val)
        for sn, val in waits.items():
            self.waited[eng][sn] = val
        return [(self.sem(sn), val, sn) for sn, val in waits.items()]

    def _commit(self, tok, reads, writes):
        sn, val, te = tok
        for k in reads:
            d = self.readers.setdefault(k, {})
            d[(sn, te)] = max(d.get((sn, te), 0), val)
        for k in writes:
            self.lastw[k] = tok
            self.readers[k] = {}

    def op(self, eng, fn, reads=(), writes=()):
        waits = self._collect(eng, reads, writes, False)
        sn = "E" + eng
        self.sem(sn)
        self.cnt[sn] += 1
        tok = (sn, self.cnt[sn], eng)
        self.q[eng].append((waits, fn, [(self.sems[sn], 1)], [(sn, 1)]))
        self._commit(tok, reads, writes)

    def pe(self, fn, r=(), w=()):
        self.op("pe", fn, r, w)

    def act(self, fn, r=(), w=()):
        self.op("act", fn, r, w)

    def dve(self, fn, r=(), w=()):
        self.op("dve", fn, r, w)

    def pool(self, fn, r=(), w=()):
        self.op("pool", fn, r, w)

    def dma(self, queue, pairs, reads, writes, semname, final=False):
        waits = self._collect(queue, reads, writes, True)
        s = self.sem(semname)
        self.cnt[semname] += 16 * len(pairs)
        tok = (semname, self.cnt[semname], None)

        def fn(e, pairs=pairs, s=s):
            for (o, i) in pairs:
                e.dma_start(out=o, in_=i).then_inc(s, 16)
            return None
        self.q[queue].append((waits, fn, [], [(semname, 16 * len(pairs))]))
        self._commit(tok, reads, writes)
        if final:
            self.out_tokens.append(tok)

    def collective(self, fn, reads, writes, semname="cc"):
        waits = self._collect("pool", reads, writes, True)
        s = self.sem(semname)
        self.cnt[semname] += 1
        tok = (semname, self.cnt[semname], None)
        self.q["pool"].append((waits, fn, [(s, 1)], [(semname, 1)]))
        self._commit(tok, reads, writes)

    def check(self):
        val = {}
        pos = {e: 0 for e in self.ENG}
        prog = True
        while prog:
            prog = False
            for e in self.ENG:
                while pos[e] < len(self.q[e]):
                    waits, fn, incs, names = self.q[e][pos[e]]
                    if all(val.get(sn, 0) >= v for (_s, v, sn) in waits):
                        for (sn, n) in names:
                            val[sn] = val.get(sn, 0) + n
                        pos[e] += 1
                        prog = True
                    else:
                        break
        stuck = {e: pos[e] for e in self.ENG if pos[e] < len(self.q[e])}
        for e, i in stuck.items():
            waits = self.q[e][i][0]
            print("DEADLOCK", e, "op", i, "of", len(self.q[e]), "waits",
                  [(sn, v, val.get(sn, 0)) for (_s, v, sn) in waits if val.get(sn, 0) < v])
        return not stuck

    def emit(self, block):
        assert self.check(), "semaphore protocol deadlock"
        def run(e, eng):
            for (waits, fn, incs, _n) in self.q[eng]:
                for (s, v, _sn) in waits:
                    e.wait_ge(s, v)
                ins = fn(e)
                for (s, n) in incs:
                    ins.then_inc(s, n)
            if eng == "sp":
                for (sn, val, _) in self.out_tokens:
                    e.wait_ge(self.sems[sn], val)

        @block.tensor
        def _(e):
            run(e, "pe")

        @block.scalar
        def _(e):
            run(e, "act")

        @block.vector
        def _(e):
            run(e, "dve")

        @block.gpsimd
        def _(e):
            run(e, "pool")

        @block.sync
        def _(e):
            run(e, "sp")


class Ring:
    def __init__(self, name, views):
        self.name, self.views, self.i = name, views, 0

    def next(self):
        k = self.i % len(self.views)
        self.i += 1
        return self.views[k], "%s%d" % (self.name, k)


def build(stop=None):
    nc = bass.Bass("TRN2", target_bir_lowering=False)

    def din(name, shape, dt=F32):
        return nc.dram_tensor(name, shape, dt, kind="ExternalInput").ap()

    xT_d = din("xT", [D, NT])
    memT_d = din("memT", [D, 256])
    pos_d = din("pos", [128, 16], I32)
    cst_d = din("cst", [128, NCST])
    cbf_d = din("cbf", [128, NCBF], BF16)
    sgwT_d = din("sgwT", [128, 8 * 128])
    w1gu_d = din("ffn1_w_gu", [D, 2 * DFF])
    w1dn_d = din("ffn1_w_down", [DFF, D])
    w2gu_d = din("ffn2_w_gu", [D, 2 * DFF])
    w2dn_d = din("ffn2_w_down", [DFF, D])
    win_d = din("w_in", [D, 5280])
    wuq_d = din("mla_w_uq", [384, 768])
    wukv_d = din("mla_w_ukv", [256, 1024])
    wmkv_d = din("mem_w_kv", [D, 1024])
    wbr_d = [din("w_branch_a", [512, D]), din("w_branch_b", [512, D]), din("w_branch_c", [512, D])]
    wout_d = din("w_out", [D, D])
    oT_d = nc.dram_tensor("oT", [D, NT], F32, kind="ExternalOutput").ap()
    kin_t = [nc.dram_tensor("kin%d" % c, [384, 1024], BF16) for c in range(4)]
    kout_t = [nc.dram_tensor("kout%d" % c, [4 * 384, 1024], BF16) for c in range(4)]
    vin_t = [nc.dram_tensor("vin%d" % c, [512, 1024], BF16) for c in range(2)]
    vout_t = [nc.dram_tensor("vout%d" % c, [4 * 512, 1024], BF16) for c in range(2)]

    es = ExitStack()
    with es:
        p = Prog(nc, es)

        def sb(name, shape, dt):
            return es.enter_context(nc.sbuf_tensor(name, shape, dt))

        xT = sb("xT_sb", [128, 8, NT], F32)
        cst = sb("cst_sb", [128, NCST], F32)
        cbf = sb("cbf_sb", [128, NCBF], BF16)
        csA = sb("csA", [128, 16, 32], F32)
        csB = sb("csB", [128, 16, 32], F32)
        wTsg = sb("wTsg", [128, 8, 128], BF16)
        KmemT = sb("KmemT", [128, 4, 256], BF16)
        Vmem = sb("Vmem", [128, 2, 512], BF16)
        wr_t = sb("wring", [128, 3, 4096], BF16)
        ARENA = 46600
        ar = sb("arena", [128, ARENA], BF16)
        ps = es.enter_context(nc.psum_tensor("ps", [128, 8, 512], F32))

        def C(name):
            o, w = CST[name]
            return cst[:, o:o + w]

        ident = cbf[:, 0:128]
        ones = cbf[:, 128:256]
        masks = cbf[:, 256:256 + 1024].rearrange("p (a b) -> p a b", b=128)

        class Carver:
            def __init__(self):
                self.off = 0
                self.hi = 0

            def take(self, shape, dt, key=None):
                n = int(np.prod(shape)) * (2 if dt in (F32, I32) else 1)
                n = (n + 1) // 2 * 2
                o = self.off
                assert o + n <= ARENA, (o, n, key)
                a = ar[:, o:o + n]
                self.off += n
                self.hi = max(self.hi, self.off)
                if key is not None:
                    p.region(key, [(o, o + n)])
                if dt in (F32, I32):
                    a = a.bitcast(dt)
                if len(shape) == 2:
                    a = a.rearrange("p (a b) -> p a b", b=shape[1])
                elif len(shape) == 3:
                    a = a.rearrange("p (a b c) -> p a b c", b=shape[1], c=shape[2])
                return a

        cv = Carver()
        sq_ring = Ring("sq", [cv.take([512], BF16) for _ in range(2)])
        std_t = cv.take([512], F32)
        rstd_t = cv.take([512], F32)
        junk = cv.take([512], BF16)
        small = cv.take([64], F32)
        base_off = cv.off

        def mk_hn(pfx):
            return dict(pfx=pfx, sq=cv.take([8, 96], F32, pfx + "hn_sq"), rt=cv.take([8, 32], F32, pfx + "hn_rt"),
                        t1=cv.take([8, 32], F32, pfx + "hn_t1"), t2=cv.take([8, 32], F32, pfx + "hn_t2"))

        gen_banks = Ring("ps", [ps[:, b, :] for b in range(6)])
        o_banks = Ring("po", [ps[:, 6, :], ps[:, 7, :]])
        wring = Ring("w", [wr_t[:, s, :] for s in range(3)])

        def wload(pairs_fn):
            slot, key = wring.next()
            p.dma("pool", pairs_fn(slot), reads=[], writes=[key], semname=key)
            return slot, key

        def bf_bank(bank):
            return bank.bitcast(BF16).rearrange("p (a b) -> p a b", b=128)

        p.dma("sp", [(cst[:, :], cst_d[:, :]), (cbf[:, :], cbf_d[:, :])], [], ["cst"], "cst")
        for t in range(4):
            p.dma("sp", [(xT[:, :, t * 512:(t + 1) * 512],
                          xT_d[:, t * 512:(t + 1) * 512].rearrange("(kc p) t -> p kc t", p=128))],
                  [], ["x%d" % t], "xin%d" % t)

        cv.off = base_off
        pos_i = cv.take([16], I32, "pos")
        posf = cv.take([16], F32, "posf")
        angA = cv.take([16, 32], F32, "angA")
        angT = cv.take([16, 32], F32, "angT")
        angI = cv.take([16, 32], I32, "angI")
        angF = cv.take([16, 32], F32, "angF")
        angM = cv.take([16, 32], F32, "angM")
        SC = cv.take([16, 32], F32, "SC")
        sgtmp = cv.take([8, 128], F32, "sgtmp")
        memT = cv.take([8, 256], F32, "memT")
        memnT = cv.take([8, 256], BF16, "memnT")
        km_sb = cv.take([4, 128], F32, "km_sb")
        kmn = cv.take([4, 128], BF16, "kmn")
        hn0 = mk_hn("s_")

        p.dma("sp", [(pos_i, pos_d[:, :])], [], ["pos"], "misc")
        p.dma("sp", [(sgtmp, sgwT_d[:, :].rearrange("p (g t) -> p g t", t=128))], [], ["sgtmp"], "misc2")
        p.dma("sp", [(memT, memT_d[:, :].rearrange("(kc p) m -> p kc m", p=128))], [], ["memT"], "misc3")

        p.dve(lambda e: e.tensor_copy(out=posf, in_=pos_i), ["pos"], ["posf"])
        for i in range(16):
            p.dve(lambda e, i=i: e.tensor_scalar(out=angA[:, i, 0:16], in0=C("invf"), scalar1=posf[:, i:i + 1],
                                                 scalar2=None, op0=ALU.mult), ["posf", "cst"], ["angA"])
        p.dve(lambda e: e.tensor_scalar(out=angA[:, :, 16:32], in0=angA[:, :, 0:16], scalar1=PI / 2, scalar2=None,
                                        op0=ALU.add), ["angA"], ["angA"])
        p.dve(lambda e: e.tensor_scalar(out=angT, in0=angA, scalar1=1.0 / (2 * PI), scalar2=None, op0=ALU.mult),
              ["angA"], ["angT"])
        p.dve(lambda e: e.tensor_copy(out=angI, in_=angT), ["angT"], ["angI"])
        p.dve(lambda e: e.tensor_copy(out=angF, in_=angI), ["angI"], ["angF"])
        p.dve(lambda e: e.scalar_tensor_tensor(out=angT, in0=angF, scalar=-2 * PI, in1=angA, op0=ALU.mult,
                                               op1=ALU.add), ["angF", "angA"], ["angT"])
        p.dve(lambda e: e.tensor_scalar(out=angM, in0=angT, scalar1=PI, scalar2=None, op0=ALU.is_gt),
              ["angT"], ["angM"])
        p.dve(lambda e: e.scalar_tensor_tensor(out=angF, in0=angM, scalar=-2 * PI, in1=angT, op0=ALU.mult,
                                               op1=ALU.add), ["angM", "angT"], ["angF"])
        p.dve(lambda e: e.tensor_scalar(out=angM, in0=angF, scalar1=-PI, scalar2=None, op0=ALU.is_lt),
              ["angF"], ["angM"])
        p.dve(lambda e: e.scalar_tensor_tensor(out=angT, in0=angM, scalar=2 * PI, in1=angF, op0=ALU.mult,
                                               op1=ALU.add), ["angM", "angF"], ["angT"])
        p.dve(lambda e: e.tensor_scalar(out=angF, in0=angT, scalar1=PI, scalar2=-PI, op0=ALU.min, op1=ALU.max),
              ["angT"], ["angF"])
        p.act(lambda e: e.activation(out=SC, in_=angF, func=AF.Sin), ["angF"], ["SC"])
        p.dve(lambda e: e.tensor_copy(out=csA[:, :, 0:16], in_=SC[:, :, 16:32]), ["SC"], ["csA"])
        p.dve(lambda e: e.tensor_copy(out=csA[:, :, 16:32], in_=SC[:, :, 16:32]), ["SC"], ["csA"])
        p.dve(lambda e: e.tensor_scalar(out=csB[:, :, 0:16], in0=SC[:, :, 0:16], scalar1=-1.0, scalar2=None,
                                        op0=ALU.mult), ["SC"], ["csB"])
        p.dve(lambda e: e.tensor_copy(out=csB[:, :, 16:32], in_=SC[:, :, 0:16]), ["SC"], ["csB"])
        p.dve(lambda e: e.tensor_tensor(out=wTsg[:, :, :], in0=sgtmp,
                                        in1=C("tri").unsqueeze(1).broadcast_to([128, 8, 128]), op=ALU.mult),
              ["sgtmp", "cst"], ["wTsg"])

        def norm_fm(src, srckeys, gname, dst, dstkey, N):
            bank, bkey = gen_banks.next()
            for kc in range(8):
                sq, sqk = sq_ring.next()
                s_ap = src(kc)
                p.act(lambda e, s_ap=s_ap, sq=sq: e.activation(out=sq[:, :N], in_=s_ap, func=AF.Square),
                      srckeys, [sqk])
                p.pe(lambda e, kc=kc, sq=sq: e.matmul(bank[:, :N], ones, sq[:, :N], start=(kc == 0), stop=(kc == 7)),
                     [sqk, "cst"], [bkey])
            p.act(lambda e: e.activation(out=std_t[:, :N], in_=bank[:, :N], func=AF.Sqrt, bias=EPS, scale=1.0 / D),
                  [bkey], ["std"])
            p.dve(lambda e: e.reciprocal(out=rstd_t[:, :N], in_=std_t[:, :N]), ["std"], ["rstd"])
            g = C(gname)
            for kc in range(8):
                s_ap, d_ap = src(kc), dst(kc)
                p.dve(lambda e, kc=kc, s_ap=s_ap, d_ap=d_ap: e.scalar_tensor_tensor(
                    out=d_ap, in0=s_ap, scalar=g[:, kc:kc + 1], in1=rstd_t[:, :N], op0=ALU.mult, op1=ALU.mult),
                    srckeys + ["rstd", "cst"], [dstkey])

        def head_norm(hn, src, H, Dh, gname, out_bf, key_in, key_out, rope_i=None):
            pf = hn["pfx"]
            rt, t1, t2 = hn["rt"], hn["t1"], hn["t2"]
            ksq, krt, kt1, kt2 = pf + "hn_sq", pf + "hn_rt", pf + "hn_t1", pf + "hn_t2"
            sqv = hn["sq"].rearrange("p a b -> p (a b)")[:, 0:H * Dh].rearrange("p (a b) -> p a b", b=Dh)
            ss = small[:, 0:H]
            sd = small[:, 8:8 + H]
            rs = small[:, 16:16 + H]
            g = C(gname)
            p.dve(lambda e: e.tensor_tensor(out=sqv, in0=src, in1=src, op=ALU.mult), [key_in], [ksq])
            p.dve(lambda e: e.tensor_reduce(out=ss, in_=sqv, axis=AX.X, op=ALU.add), [ksq], ["hn_ss"])
            p.act(lambda e: e.activation(out=sd, in_=ss, func=AF.Sqrt, bias=EPS, scale=1.0 / Dh), ["hn_ss"], ["hn_sd"])
            p.dve(lambda e: e.reciprocal(out=rs, in_=sd), ["hn_sd"], ["hn_rs"])
            p.dve(lambda e: e.tensor_tensor(out=sqv, in0=src, in1=rs.unsqueeze(2).broadcast_to([128, H, Dh]),
                                            op=ALU.mult), [key_in, "hn_rs"], [ksq])
            if rope_i is None:
                p.dve(lambda e: e.tensor_tensor(out=out_bf, in0=sqv, in1=g.unsqueeze(1).broadcast_to([128, H, Dh]),
                                                op=ALU.mult), [ksq, "cst"], [key_out])
                return
            i = rope_i
            p.dve(lambda e: e.tensor_tensor(out=out_bf[:, :, 0:64], in0=sqv[:, :, 0:64],
                                            in1=g[:, 0:64].unsqueeze(1).broadcast_to([128, H, 64]), op=ALU.mult),
                  [ksq, "cst"], [key_out])
            p.dve(lambda e: e.tensor_tensor(out=rt, in0=sqv[:, :, 64:96],
                                            in1=g[:, 64:96].unsqueeze(1).broadcast_to([128, H, 32]), op=ALU.mult),
                  [ksq, "cst"], [krt])
            p.dve(lambda e: e.tensor_tensor(out=t1, in0=rt,
                                            in1=csA[:, i, :].unsqueeze(1).broadcast_to([128, H, 32]), op=ALU.mult),
                  [krt, "csA"], [kt1])
            p.dve(lambda e: e.tensor_tensor(out=t2[:, :, 0:16], in0=rt[:, :, 16:32],
                                            in1=csB[:, i, 0:16].unsqueeze(1).broadcast_to([128, H, 16]), op=ALU.mult),
                  [krt, "csB"], [kt2])
            p.dve(lambda e: e.tensor_tensor(out=t2[:, :, 16:32], in0=rt[:, :, 0:16],
                                            in1=csB[:, i, 16:32].unsqueeze(1).broadcast_to([128, H, 16]), op=ALU.mult),
                  [krt, "csB"], [kt2])
            p.dve(lambda e: e.tensor_tensor(out=out_bf[:, :, 64:96], in0=t1, in1=t2, op=ALU.add),
                  [kt1, kt2], [key_out])

        def mm_group(bank_ap, bkey, items, reads):
            def fn(e):
                n = len(items)
                ins = None
                for k, (l, r) in enumerate(items):
                    ins = e.matmul(bank_ap, l, r, start=(k == 0), stop=(k == n - 1))
                return ins
            p.pe(fn, reads, [bkey])

        def transposes(bank, bkey, srcs, reads, rows=128):
            bb = bf_bank(bank)

            def fn(e):
                ins = None
                for j, s in enumerate(srcs):
                    ins = e.transpose(out=bb[0:rows, j, :], in_=s, identity=ident)
                return ins
            p.pe(fn, reads + ["cst"], [bkey])
            return bb

        norm_fm(lambda kc: memT[:, kc, :], ["memT"], "mem_g", lambda kc: memnT[:, kc, :], "memnT", 256)
        wk, wkk = wload(lambda s: [(s.rearrange("p (kc n) -> p kc n", n=512),
                                    wmkv_d[:, 0:512].rearrange("(kc p) n -> p kc n", p=128))])
        wv_, wvk = wload(lambda s: [(s.rearrange("p (kc n) -> p kc n", n=512),
                                     wmkv_d[:, 512:1024].rearrange("(kc p) n -> p kc n", p=128))])
        wk3 = wk.rearrange("p (kc n) -> p kc n", n=512)
        wv3 = wv_.rearrange("p (kc n) -> p kc n", n=512)
        for mb in range(2):
            bk, bkk = gen_banks.next()
            mm_group(bk, bkk, [(memnT[:, kc, mb * 128:(mb + 1) * 128], wk3[:, kc, :]) for kc in range(8)],
                     ["memnT", wkk])
            bv, bvk = gen_banks.next()
            mm_group(bv, bvk, [(memnT[:, kc, mb * 128:(mb + 1) * 128], wv3[:, kc, :]) for kc in range(8)],
                     ["memnT", wvk])
            p.act(lambda e, mb=mb, bv=bv: e.activation(out=Vmem[:, mb, :], in_=bv, func=AF.Copy), [bvk], ["Vmem"])
            p.act(lambda e, bk=bk: e.activation(out=km_sb.rearrange("p a b -> p (a b)"), in_=bk, func=AF.Copy),
                  [bkk], ["km_sb"])
            head_norm(hn0, km_sb, 4, 128, "mkg", kmn, "km_sb", "kmn")
            bt, btk = gen_banks.next()
            bb = transposes(bt, btk, [kmn[:, h, :] for h in range(4)], ["kmn"])
            p.act(lambda e, mb=mb, bb=bb: e.activation(out=KmemT[:, :, mb * 128:(mb + 1) * 128], in_=bb[:, 0:4, :],
                                                       func=AF.Copy), [btk], ["KmemT"])

        def ffn(wgu_d, wdn_d, gname):
            cv.off = base_off
            ho = cv.off
            hT = cv.take([8, 1024], BF16)
            for sub in range(2):
                p.region("h%d" % sub, [(ho + kc * 1024 + sub * 512, ho + kc * 1024 + sub * 512 + 512) for kc in range(8)])
            ao = cv.off
            actT = cv.take([FC, 1024], BF16)
            for f in range(FC):
                for sub in range(2):
                    p.region("a%d_%d" % (f, sub), [(ao + f * 1024 + sub * 512, ao + f * 1024 + sub * 512 + 512)])
            sg_ring = Ring("sg", [cv.take([512], F32, "sg%d" % k) for k in range(2)])
            for half in range(2):
                for sub in range(2):
                    t = half * 2 + sub
                    norm_fm(lambda kc, t=t: xT[:, kc, t * 512:(t + 1) * 512], ["x%d" % t], gname,
                            lambda kc, sub=sub: hT[:, kc, sub * 512:(sub + 1) * 512], "h%d" % sub, 512)
                for pp in range(11):
                    slot, wkey = wload(lambda s, pp=pp: [
                        (s.rearrange("p (a kc n) -> p a kc n", a=2, n=256)[:, 0],
                         wgu_d[:, pp * 256:(pp + 1) * 256].rearrange("(kc p) n -> p kc n", p=128)),
                        (s.rearrange("p (a kc n) -> p a kc n", a=2, n=256)[:, 1],
                         wgu_d[:, DFF + pp * 256:DFF + (pp + 1) * 256].rearrange("(kc p) n -> p kc n", p=128))])
                    w4 = slot.rearrange("p (a kc n) -> p a kc n", a=2, n=256)
                    for fi in range(2):
                        f = pp * 2 + fi
                        for sub in range(2):
                            bg, bgk = gen_banks.next()
                            bu, buk = gen_banks.next()
                            hs = lambda kc, sub=sub: hT[:, kc, sub * 512:(sub + 1) * 512]
                            mm_group(bg, bgk, [(w4[:, 0, kc, fi * 128:(fi + 1) * 128], hs(kc)) for kc in range(8)],
                                     [wkey, "h%d" % sub])
                            mm_group(bu, buk, [(w4[:, 1, kc, fi * 128:(fi + 1) * 128], hs(kc)) for kc in range(8)],
                                     [wkey, "h%d" % sub])
                            sg, sgk = sg_ring.next()
                            p.act(lambda e, sg=sg, bg=bg: e.activation(out=sg, in_=bg, func=AF.Silu), [bgk], [sgk])
                            a_ap = actT[:, f, sub * 512:(sub + 1) * 512]
                            p.dve(lambda e, sg=sg, bu=bu, a_ap=a_ap: e.tensor_tensor(out=a_ap, in0=bu, in1=sg, op=ALU.mult),
                                  [sgk, buk], ["a%d_%d" % (f, sub)])
                for m in range(8):
                    slot, wkey = wload(lambda s, m=m: [
                        (s[:, 0:FC * 128].rearrange("p (f n) -> p f n", n=128),
                         wdn_d[:, m * 128:(m + 1) * 128].rearrange("(f p) n -> p f n", p=128))])
                    w3 = slot[:, 0:FC * 128].rearrange("p (f n) -> p f n", n=128)
                    for sub in range(2):
                        t = half * 2 + sub
                        bd, bdk = gen_banks.next()
                        mm_group(bd, bdk, [(w3[:, f, :], actT[:, f, sub * 512:(sub + 1) * 512]) for f in range(FC)],
                                 [wkey] + ["a%d_%d" % (f, sub) for f in range(FC)])
                        xs = xT[:, m, t * 512:(t + 1) * 512]
                        p.dve(lambda e, bd=bd, xs=xs: e.scalar_tensor_tensor(out=xs, in0=bd, scalar=0.5, in1=xs,
                                                                            op0=ALU.mult, op1=ALU.add),
                              [bdk, "x%d" % t], ["x%d" % t])

        def store_out():
            for t in range(4):
                p.dma("sp", [(oT_d[:, t * 512:(t + 1) * 512].rearrange("(kc p) t -> p kc t", p=128),
                              xT[:, :, t * 512:(t + 1) * 512])], ["x%d" % t], ["out%d" % t], "out", final=True)

        def finish(store=True):
            if store:
                store_out()
            with nc.Block() as block:
                p.emit(block)
            return nc

        if stop != "noffn1":
            ffn(w1gu_d, w1dn_d, "ffn1_g")
        if stop == "ffn1":
            return finish()

        cv.off = base_off
        hT = cv.take([8, 512], BF16, "hT")
        Kst = cv.take([8, NT], BF16, "Kst")
        Vst = cv.take([8 * 16 * 64], BF16, "Vst").rearrange("p (h i d) -> p h i d", h=8, i=16)
        ckvn = cv.take([256], BF16, "ckvn")
        ckvnT = cv.take([2, 128], BF16, "ckvnT")
        kc_sb = cv.take([8, 96], F32, "kc_sb")
        kfin = cv.take([8, 96], BF16, "kfin")
        hn1 = mk_hn("m_")

        wkvin, wkvin_k = wload(lambda s: [(s[:, 0:8 * 288].rearrange("p (kc n) -> p kc n", n=288),
                                           win_d[:, C_CKV:C_CKV + 288].rearrange("(kc p) n -> p kc n", p=128))])
        wkvin3 = wkvin[:, 0:8 * 288].rearrange("p (kc n) -> p kc n", n=288)
        wukv, wukv_k = wload(lambda s: [(s[:, 0:2048].rearrange("p (kc n) -> p kc n", n=1024),
                                         wukv_d[:, :].rearrange("(kc p) n -> p kc n", p=128))])
        wukv3 = wukv[:, 0:2048].rearrange("p (kc n) -> p kc n", n=1024)
        ss1 = small[:, 32:33]
        sd1 = small[:, 33:34]
        rs1 = small[:, 34:35]
        for t in range(4):
            norm_fm(lambda kc, t=t: xT[:, kc, t * 512:(t + 1) * 512], ["x%d" % t], "mix_g",
                    lambda kc: hT[:, kc, :], "hT", 512)
            for bl in range(4):
                i = t * 4 + bl
                ba, bak = gen_banks.next()
                mm_group(ba[:, 0:288], bak, [(hT[:, kc, bl * 128:(bl + 1) * 128], wkvin3[:, kc, :]) for kc in range(8)],
                         ["hT", wkvin_k])
                p.act(lambda e, ba=ba: e.activation(out=junk[:, 0:256], in_=ba[:, 0:256], func=AF.Square, accum_out=ss1),
                      [bak], ["ss1", "junk"])
                p.act(lambda e: e.activation(out=sd1, in_=ss1, func=AF.Sqrt, bias=EPS, scale=1.0 / 256), ["ss1"], ["sd1"])
                p.dve(lambda e: e.reciprocal(out=rs1, in_=sd1), ["sd1"], ["rs1"])
                p.dve(lambda e, ba=ba: e.scalar_tensor_tensor(out=ckvn, in0=ba[:, 0:256], scalar=rs1, in1=C("ckvg"),
                                                              op0=ALU.mult, op1=ALU.mult), [bak, "rs1", "cst"], ["ckvn"])
                bt, btk = gen_banks.next()
                bb = transposes(bt, btk, [ckvn[:, k * 128:(k + 1) * 128] for k in range(2)], ["ckvn"])
                p.act(lambda e, bb=bb: e.activation(out=ckvnT, in_=bb[:, 0:2, :], func=AF.Copy), [btk], ["ckvnT"])
                b0, b0k = gen_banks.next()
                b1, b1k = gen_banks.next()
                mm_group(b0, b0k, [(ckvnT[:, k, :], wukv3[:, k, 0:512]) for k in range(2)], ["ckvnT", wukv_k])
                mm_group(b1, b1k, [(ckvnT[:, k, :], wukv3[:, k, 512:1024]) for k in range(2)], ["ckvnT", wukv_k])
                for hb, (bx, bxk) in enumerate(((b0, b0k), (b1, b1k))):
                    b3 = bx.rearrange("p (h d) -> p h d", d=128)
                    p.act(lambda e, b3=b3, hb=hb, i=i: e.activation(out=Vst[:, hb * 4:(hb + 1) * 4, i, :],
                                                                    in_=b3[:, :, 64:128], func=AF.Copy), [bxk], ["Vst"])
                    p.act(lambda e, b3=b3, hb=hb: e.activation(out=kc_sb[:, hb * 4:(hb + 1) * 4, 0:64],
                                                               in_=b3[:, :, 0:64], func=AF.Copy), [bxk], ["kc_sb"])
                p.act(lambda e, ba=ba: e.activation(out=kc_sb[:, :, 64:96],
                                                    in_=ba[:, 256:288].unsqueeze(1).broadcast_to([128, 8, 32]),
                                                    func=AF.Copy), [bak], ["kc_sb"])
                head_norm(hn1, kc_sb, 8, 96, "kg", kfin, "kc_sb", "kfin", rope_i=i)
                bt2, bt2k = gen_banks.next()
                bb2 = transposes(bt2, bt2k, [kfin[:, h, :] for h in range(8)], ["kfin"], rows=96)
                p.act(lambda e, bb2=bb2, i=i: e.activation(out=Kst[0:96, :, i * 128:(i + 1) * 128], in_=bb2[0:96, :, :],
                                                           func=AF.Copy), [bt2k], ["Kst"])
        if stop == "dump_m1":
            for h in range(8):
                p.dma("pool", [(oT_d[h * 96:(h + 1) * 96, :], Kst[0:96, h, :])], ["Kst"], ["dbg%d" % h], "dbg", final=True)
            for hq in range(2):
                p.dma("pool", [(oT_d[768 + hq * 128:768 + (hq + 1) * 128, :].rearrange("p (h i d) -> p h i d", h=2, i=16),
                                Vst[:, 2 * hq:2 * hq + 2, :, :])], ["Vst"], ["dbgv%d" % hq], "dbg", final=True)
            return finish(store=False)
        for c in range(4):
            p.dma("sp", [(kin_t[c].ap().rearrange("(hh f two) c -> f hh (two c)", hh=2, f=96, two=2),
                          Kst[0:96, 2 * c:2 * c + 2, :])], ["Kst"], ["kin%d" % c], "kin%d" % c)
        for c in range(2):
            p.dma("sp", [(vin_t[c].ap().rearrange("(hh p) (i d) -> p hh i d", p=128, d=64),
                          Vst[:, 4 * c:4 * c + 4, :, :])], ["Vst"], ["vin%d" % c], "vin%d" % c)
        if stop != "m1nocc":
            for c in range(4):
                p.collective(lambda e, c=c: e.collective_compute(
                    "AllGather", ALU.bypass, replica_groups=[[0, 1, 2, 3], [4, 5, 6, 7]],
                    ins=[kin_t[c].ap().opt()], outs=[kout_t[c].ap().opt()]),
                    ["kin%d" % c], ["kout%d" % c], semname="cck%d" % c)
            for c in range(2):
                p.collective(lambda e, c=c: e.collective_compute(
                    "AllGather", ALU.bypass, replica_groups=[[0, 1, 2, 3], [4, 5, 6, 7]],
                    ins=[vin_t[c].ap().opt()], outs=[vout_t[c].ap().opt()]),
                    ["vin%d" % c], ["vout%d" % c], semname="ccv%d" % c)
        if stop == "m1":
            return finish()
        if stop == "m1nocc":
            return finish()
        cv.off = base_off
        hT = cv.take([8, 512], BF16, "hT")
        QT = cv.take([8, 512], BF16, "QT")
        qmT = cv.take([4, 512], BF16, "qmT")
        yT = [cv.take([4, 512], BF16, k) for k in ("yaT", "ybT", "ycT")]
        Kring_v = [cv.take([2048], BF16, "K%d" % s) for s in range(4)]
        Vring_v = [cv.take([16, 128], BF16, "V%d" % s) for s in range(4)]
        ph_off = cv.off
        uT_sb = cv.take([4, 512], BF16, "uT")
        v_sb = cv.take([512], F32, "v_sb")
        v_ln = cv.take([512], BF16, "v_ln")
        mtmp = cv.take([4, 128], F32, "mtmp")
        cv.off = ph_off
        cqn = cv.take([384], BF16, "cqn")
        cqnT = cv.take([3, 128], BF16, "cqnT")
        q_off = cv.off
        q_sb = cv.take([8, 96], F32, "q_sb")
        qfin = cv.take([8, 96], BF16, "qfin")
        q_end = cv.off
        cv.off = q_off
        qm_sb = cv.take([4, 128], F32, "qm_sb")
        cv.off = q_off + 8 * 96 * 2
        qmn = cv.take([4, 128], BF16, "qmn")
        cv.off = q_end
        hn2 = mk_hn("t_")
        cv.off = ph_off
        P_ring = Ring("P", [cv.take([512], BF16, "P%d" % k) for k in range(4)])
        rd_views = [cv.take([512], F32, "rd%d" % k) for k in range(2)]
        rden_ring = Ring("rd", rd_views)
        acc1, acc2 = rd_views
        mergedT = cv.take([8, 512], BF16, "mergedT")
        go = cv.off
        g_sb = cv.take([3, 512], BF16)
        for br in range(3):
            p.region("g_sb%d" % br, [(go + br * 512, go + br * 512 + 512)])

        for s in range(4):
            lo = 64 if s < 2 else 0
            p.pool(lambda e, s=s, lo=lo: e.memset(Vring_v[s][:, :, lo:lo + 64], 1.0), [], ["V%d" % s])

        SC_B = 96 ** -0.5
        SC_C = 128 ** -0.5
        st6 = small[:, 40:46]
        mv = small[:, 46:48]
        sdv = small[:, 48:49]
        rsv = small[:, 49:50]

        for t in range(4):
            xk = "x%d" % t
            norm_fm(lambda kc, t=t: xT[:, kc, t * 512:(t + 1) * 512], [xk], "mix_g",
                    lambda kc: hT[:, kc, :], "hT", 512)
            wv_s, wv_k = wload(lambda s: [(s.rearrange("p (kc n) -> p kc n", n=512),
                                           win_d[:, C_V:C_V + 512].rearrange("(kc p) n -> p kc n", p=128))])
            wu_s, wu_k = wload(lambda s: [(s.rearrange("p (kc n) -> p kc n", n=512),
                                           win_d[:, C_U:C_U + 512].rearrange("(kc p) n -> p kc n", p=128))])
            wv3 = wv_s.rearrange("p (kc n) -> p kc n", n=512)
            wu3 = wu_s.rearrange("p (kc n) -> p kc n", n=512)
            for c in range(4):
                bu, buk = gen_banks.next()
                mm_group(bu, buk, [(wu3[:, kc, c * 128:(c + 1) * 128], hT[:, kc, :]) for kc in range(8)], [wu_k, "hT"])
                p.act(lambda e, c=c, bu=bu: e.activation(out=uT_sb[:, c, :], in_=bu, func=AF.Gelu), [buk], ["uT"])
            for bl in range(4):
                bv, bvk = gen_banks.next()
                mm_group(bv, bvk, [(hT[:, kc, bl * 128:(bl + 1) * 128], wv3[:, kc, :]) for kc in range(8)], [wv_k, "hT"])
                p.act(lambda e, bv=bv: e.activation(out=v_sb, in_=bv, func=AF.Gelu), [bvk], ["v_sb"])
                p.dve(lambda e: e.bn_stats(out=st6, in_=v_sb), ["v_sb"], ["st6"])
                p.dve(lambda e: e.bn_aggr(out=mv, in_=st6), ["st6"], ["mv"])
                p.act(lambda e: e.activation(out=sdv, in_=mv[:, 1:2], func=AF.Sqrt, bias=EPS, scale=1.0), ["mv"], ["sdv"])
                p.dve(lambda e: e.reciprocal(out=rsv, in_=sdv), ["sdv"], ["rsv"])
                p.dve(lambda e: e.scalar_tensor_tensor(out=v_sb, in0=v_sb, scalar=mv[:, 0:1], in1=C("lng"),
                                                       op0=ALU.subtract, op1=ALU.mult), ["v_sb", "mv", "cst"], ["v_sb"])
                p.dve(lambda e: e.scalar_tensor_tensor(out=v_ln, in0=v_sb, scalar=rsv, in1=C("lnb"),
                                                       op0=ALU.mult, op1=ALU.add), ["v_sb", "rsv", "cst"], ["v_ln"])
                bm, bmk = gen_banks.next()

                def mixfn(e, bm=bm):
                    ins = None
                    for g in range(8):
                        ins = e.matmul(bm[(g % 2) * 64:(g % 2) * 64 + 64, (g // 2) * 128:(g // 2 + 1) * 128],
                                       v_ln[:, g * 64:(g + 1) * 64], wTsg[:, g, :], start=True, stop=True)
                    return ins
                p.pe(mixfn, ["v_ln", "wTsg"], [bmk])
                bm3 = bm.rearrange("p (c t) -> p c t", t=128)
                p.dve(lambda e, bm3=bm3: e.tensor_tensor(out=mtmp, in0=bm3, in1=C("bsT").rearrange("p (c t) -> p c t", t=128),
                                                         op=ALU.add), [bmk, "cst"], ["mtmp"])
                p.dve(lambda e, bl=bl: e.tensor_tensor(out=yT[0][:, :, bl * 128:(bl + 1) * 128], in0=mtmp,
                                                       in1=uT_sb[:, :, bl * 128:(bl + 1) * 128], op=ALU.mult),
                      ["mtmp", "uT"], ["yaT"])
            wcq_s, wcq_k = wload(lambda s: [(s[:, 0:8 * 384].rearrange("p (kc n) -> p kc n", n=384),
                                             win_d[:, C_CQ:C_CQ + 384].rearrange("(kc p) n -> p kc n", p=128))])
            wuq_s, wuq_k = wload(lambda s: [(s[:, 0:3 * 768].rearrange("p (kc n) -> p kc n", n=768),
                                             wuq_d[:, :].rearrange("(kc p) n -> p kc n", p=128))])
            wcq3 = wcq_s[:, 0:8 * 384].rearrange("p (kc n) -> p kc n", n=384)
            wuq3 = wuq_s[:, 0:3 * 768].rearrange("p (kc n) -> p kc n", n=768)
            for bl in range(4):
                i = t * 4 + bl
                bq, bqk = gen_banks.next()
                mm_group(bq[:, 0:384], bqk, [(hT[:, kc, bl * 128:(bl + 1) * 128], wcq3[:, kc, :]) for kc in range(8)],
                         [wcq_k, "hT"])
                p.act(lambda e, bq=bq: e.activation(out=junk[:, 0:384], in_=bq[:, 0:384], func=AF.Square, accum_out=ss1),
                      [bqk], ["ss1", "junk"])
                p.act(lambda e: e.activation(out=sd1, in_=ss1, func=AF.Sqrt, bias=EPS, scale=1.0 / 384), ["ss1"], ["sd1"])
                p.dve(lambda e: e.reciprocal(out=rs1, in_=sd1), ["sd1"], ["rs1"])
                p.dve(lambda e, bq=bq: e.scalar_tensor_tensor(out=cqn, in0=bq[:, 0:384], scalar=rs1, in1=C("cqg"),
                                                              op0=ALU.mult, op1=ALU.mult),
                      [bqk, "rs1", "cst"], ["cqn"])
                bt, btk = gen_banks.next()
                bb = transposes(bt, btk, [cqn[:, k * 128:(k + 1) * 128] for k in range(3)], ["cqn"])
                p.act(lambda e, bb=bb: e.activation(out=cqnT, in_=bb[:, 0:3, :], func=AF.Copy), [btk], ["cqnT"])
                for hb in range(2):
                    bx, bxk = gen_banks.next()
                    mm_group(bx[:, 0:384], bxk, [(cqnT[:, k, :], wuq3[:, k, hb * 384:(hb + 1) * 384]) for k in range(3)],
                             ["cqnT", wuq_k])
                    p.act(lambda e, bx=bx, hb=hb: e.activation(
                        out=q_sb[:, hb * 4:(hb + 1) * 4, :], in_=bx[:, 0:384].rearrange("p (h d) -> p h d", d=96),
                        func=AF.Copy), [bxk], ["q_sb"])
                head_norm(hn2, q_sb, 8, 96, "qg", qfin, "q_sb", "qfin", rope_i=i)
                bt2, bt2k = gen_banks.next()
                bb2 = transposes(bt2, bt2k, [qfin[:, h, :] for h in range(8)], ["qfin"], rows=96)
                p.act(lambda e, bb2=bb2, bl=bl: e.activation(out=QT[0:96, :, bl * 128:(bl + 1) * 128],
                                                             in_=bb2[0:96, :, :], func=AF.Copy), [bt2k], ["QT"])
            wqm_s, wqm_k = wload(lambda s: [(s.rearrange("p (kc n) -> p kc n", n=512),
                                             win_d[:, C_QM:C_QM + 512].rearrange("(kc p) n -> p kc n", p=128))])
            wqm3 = wqm_s.rearrange("p (kc n) -> p kc n", n=512)
            for bl in range(4):
                bq, bqk = gen_banks.next()
                mm_group(bq, bqk, [(hT[:, kc, bl * 128:(bl + 1) * 128], wqm3[:, kc, :]) for kc in range(8)],
                         [wqm_k, "hT"])
                p.act(lambda e, bq=bq: e.activation(out=qm_sb.rearrange("p a b -> p (a b)"), in_=bq, func=AF.Copy),
                      [bqk], ["qm_sb"])
                head_norm(hn2, qm_sb, 4, 128, "mqg", qmn, "qm_sb", "qmn")
                bt, btk = gen_banks.next()
                bb = transposes(bt, btk, [qmn[:, h, :] for h in range(4)], ["qmn"])
                p.act(lambda e, bb=bb, bl=bl: e.activation(out=qmT[:, :, bl * 128:(bl + 1) * 128], in_=bb[:, 0:4, :],
                                                           func=AF.Copy), [btk], ["qmT"])
            for h in range(4):
                Ps = []
                for mc in range(2):
                    bs, bsk = gen_banks.next()
                    mm_group(bs, bsk, [(KmemT[:, h, mc * 128:(mc + 1) * 128], qmT[:, h, :])], ["KmemT", "qmT"])
                    Pt, Pk = P_ring.next()
                    p.act(lambda e, bs=bs, Pt=Pt: e.activation(out=Pt, in_=bs, func=AF.Exp, scale=SC_C), [bsk], [Pk])
                    Ps.append((Pt, Pk))
                bo, bok = gen_banks.next()
                mm_group(bo, bok, [(Vmem[:, mc, h * 128:(h + 1) * 128], Ps[mc][0]) for mc in range(2)],
                         ["Vmem"] + [k for _, k in Ps])
                bd, bdk = gen_banks.next()
                mm_group(bd, bdk, [(ones, Ps[mc][0]) for mc in range(2)], ["cst"] + [k for _, k in Ps])
                rd, rdk = rden_ring.next()
                p.dve(lambda e, rd=rd, bd=bd: e.reciprocal(out=rd, in_=bd), [bdk], [rdk])
                p.dve(lambda e, rd=rd, bo=bo, h=h: e.tensor_tensor(out=yT[2][:, h, :], in0=bo, in1=rd, op=ALU.mult),
                      [bok, rdk], ["ycT"])
            nki = 4 * t + 4
            for h in range(8):
                par = h % 2
                ob, obk = o_banks.next()
                pend = []
                npv = 0
                ntot = 4 * nki

                def do_pv(item, first, last, ob=ob, obk=obk):
                    (Pt, Pk, c0, Vc, vk, ki) = item
                    p.pe(lambda e: e.matmul(ob[:, c0:512], Vc[:, ki, :], Pt[:, c0:512], start=first, stop=last),
                         [Pk, vk], [obk])
                for r in range(4):
                    slot = par * 2 + (r % 2)
                    Kc = Kring_v[slot]
                    Vc = Vring_v[slot]
                    kk, vk = "K%d" % slot, "V%d" % slot
                    vlo = 0 if par == 0 else 64
                    kb = r * 384 + (h % 2) * 192
                    vb = r * 512 + (h % 4) * 128
                    p.dma("pool", [(Kc[0:96, 0:nki * 128],
                                  kout_t[h // 2].ap()[kb:kb + 192, :].rearrange(
                                      "(f two) c -> f (two c)", two=2)[:, 0:nki * 128])],
                          ["kout%d" % (h // 2)], [kk], kk)
                    p.dma("pool", [(Vc[:, 0:nki, vlo:vlo + 64],
                                  vout_t[h // 4].ap()[vb:vb + 128, :].rearrange(
                                      "p (i d) -> p i d", d=64)[:, 0:nki, :])],
                          ["vout%d" % (h // 4)], [vk], vk)
                    for ki in range(nki):
                        d = ki - 4 * t
                        c0 = 128 * d if d > 0 else 0
                        bs, bsk = gen_banks.next()
                        p.pe(lambda e, bs=bs, Kc=Kc, ki=ki, c0=c0, h=h: e.matmul(
                            bs[:, c0:512], Kc[0:96, ki * 128:(ki + 1) * 128], QT[0:96, h, c0:512], start=True, stop=True),
                            [kk, "QT"], [bsk])
                        Pt, Pk = P_ring.next()
                        p.act(lambda e, bs=bs, Pt=Pt, c0=c0: e.activation(out=Pt[:, c0:512], in_=bs[:, c0:512],
                                                                         func=AF.Exp, scale=SC_B), [bsk], [Pk])
                        if d >= 0:
                            mk = masks[:, (ki % 2) * 4 + r, :]
                            p.dve(lambda e, Pt=Pt, c0=c0, mk=mk: e.tensor_tensor(
                                out=Pt[:, c0:c0 + 128], in0=Pt[:, c0:c0 + 128], in1=mk, op=ALU.mult), [Pk, "cst"], [Pk])
                        pend.append((Pt, Pk, c0, Vc, vk, ki))
                        if len(pend) > 2:
                            do_pv(pend.pop(0), npv == 0, npv == ntot - 1)
                            npv += 1
                while pend:
                    do_pv(pend.pop(0), npv == 0, npv == ntot - 1)
                    npv += 1
                olo = 0 if par == 0 else 64
                dlo = 64 - olo
                rd, rdk = rden_ring.next()
                p.dve(lambda e, rd=rd, ob=ob, dlo=dlo: e.reciprocal(out=rd[dlo:dlo + 64, :], in_=ob[dlo:dlo + 64, :]),
                      [obk], [rdk])
                p.dve(lambda e, rd=rd, ob=ob, olo=olo, dlo=dlo, h=h: e.tensor_tensor(
                    out=yT[1][olo:olo + 64, h // 2, :], in0=ob[olo:olo + 64, :], in1=rd[dlo:dlo + 64, :], op=ALU.mult),
                    [obk, rdk], ["ybT"])
            if stop == "dump_y%d" % t:
                for br in range(3):
                    p.dma("pool", [(oT_d[0:512, br * 512:(br + 1) * 512].rearrange("(kc p) t -> p kc t", p=128), yT[br])],
                          [("yaT", "ybT", "ycT")[br]], ["dbgy%d" % br], "dbg", final=True)
                return finish(store=False)
            for m in range(8):
                gs, gk = wload(lambda s, m=m: [
                    (s[:, 0:3072].rearrange("p (b kc n) -> p b kc n", b=3, n=128)[:, br],
                     win_d[:, C_GATE + br * 1024 + m * 128:C_GATE + br * 1024 + (m + 1) * 128].rearrange(
                         "(kc p) n -> p kc n", p=128)) for br in range(3)])
                bs_, bk_ = wload(lambda s, m=m: [
                    (s[:, 0:1536].rearrange("p (b kc n) -> p b kc n", b=3, n=128)[:, br],
                     wbr_d[br][:, m * 128:(m + 1) * 128].rearrange("(kc p) n -> p kc n", p=128)) for br in range(3)])
                g4 = gs[:, 0:3072].rearrange("p (b kc n) -> p b kc n", b=3, n=128)
                b4 = bs_[:, 0:1536].rearrange("p (b kc n) -> p b kc n", b=3, n=128)
                for br in range(3):
                    bg, bgk = gen_banks.next()
                    mm_group(bg, bgk, [(g4[:, br, kc, :], hT[:, kc, :]) for kc in range(8)], [gk, "hT"])
                    p.act(lambda e, bg=bg, br=br, m=m: e.activation(out=g_sb[:, br, :], in_=bg, func=AF.Sigmoid,
                                                                   bias=C("bgate")[:, br * 8 + m:br * 8 + m + 1], scale=1.0),
                          [bgk, "cst"], ["g_sb%d" % br])
                ykeys = ["yaT", "ybT", "ycT"]
                bbs = []
                for br in range(3):
                    bb_, bbk = gen_banks.next()
                    mm_group(bb_, bbk, [(b4[:, br, kc, :], yT[br][:, kc, :]) for kc in range(4)], [bk_, ykeys[br]])
                    bbs.append((bb_, bbk))
                p.dve(lambda e, b=bbs[0][0]: e.tensor_tensor(out=acc1, in0=b, in1=g_sb[:, 0, :], op=ALU.mult),
                      [bbs[0][1], "g_sb0"], ["rd0"])
                p.dve(lambda e, b=bbs[1][0]: e.tensor_tensor(out=acc2, in0=b, in1=g_sb[:, 1, :], op=ALU.mult),
                      [bbs[1][1], "g_sb1"], ["rd1"])
                p.dve(lambda e: e.tensor_tensor(out=acc1, in0=acc1, in1=acc2, op=ALU.add), ["rd0", "rd1"], ["rd0"])
                p.dve(lambda e, b=bbs[2][0]: e.tensor_tensor(out=acc2, in0=b, in1=g_sb[:, 2, :], op=ALU.mult),
                      [bbs[2][1], "g_sb2"], ["rd1"])
                p.dve(lambda e, m=m: e.tensor_tensor(out=mergedT[:, m, :], in0=acc1, in1=acc2, op=ALU.add),
                      ["rd0", "rd1"], ["mergedT"])
            for hf in range(2):
                ws, wk_ = wload(lambda s, hf=hf: [(s.rearrange("p (kc n) -> p kc n", n=512),
                                                   wout_d[:, hf * 512:(hf + 1) * 512].rearrange("(kc p) n -> p kc n", p=128))])
                w3 = ws.rearrange("p (kc n) -> p kc n", n=512)
                for mm in range(4):
                    m = hf * 4 + mm
                    bo, bok = gen_banks.next()
                    mm_group(bo, bok, [(w3[:, kc, mm * 128:(mm + 1) * 128], mergedT[:, kc, :]) for kc in range(8)],
                             [wk_, "mergedT"])
                    xs = xT[:, m, t * 512:(t + 1) * 512]
                    p.dve(lambda e, bo=bo, xs=xs: e.tensor_tensor(out=xs, in0=bo, in1=xs, op=ALU.add), [bok, xk], [xk])

        if stop != "mid":
            ffn(w2gu_d, w2dn_d, "ffn2_g")
        return finish()


def _perm_rows(j):
    rows = []
    for i in range(16):
        g = i // 2
        blk = 8 * g + (j if i % 2 == 0 else 7 - j)
        rows.append(np.arange(blk * 128, (blk + 1) * 128))
    return np.concatenate(rows)


def _host_inputs(inp):
    f = lambda a: np.ascontiguousarray(np.asarray(a, dtype=np.float32))
    x = f(inp["x"])
    mem = f(inp["mem"])
    pos = np.asarray(inp["positions"]).astype(np.int32)
    L0 = lambda k: f(inp[k])[0]

    def col(v, n):
        return np.ascontiguousarray(v.reshape(n, 128).T)

    def rep(v):
        return np.ascontiguousarray(np.broadcast_to(v[None, :], (128, v.shape[0])))

    cst = np.zeros((128, NCST), np.float32)

    def put(name, arr):
        o, w = CST[name]
        assert arr.shape == (128, w), (name, arr.shape)
        cst[:, o:o + w] = arr
    put("ffn1_g", col(L0("ffn1_norm"), 8))
    put("mix_g", col(L0("mix_norm"), 8))
    put("ffn2_g", col(L0("ffn2_norm"), 8))
    put("mem_g", col(L0("mem_norm"), 8))
    put("bgate", col(L0("b_gate"), 24))
    put("cqg", rep(L0("mla_cq_norm")))
    put("ckvg", rep(L0("mla_ckv_norm")))
    put("lng", rep(L0("sg_ln_g")))
    put("lnb", rep(L0("sg_ln_b")))
    put("qg", rep(L0("mla_q_norm")))
    put("kg", rep(L0("mla_k_norm")))
    put("mqg", rep(L0("mem_q_norm")))
    put("mkg", rep(L0("mem_k_norm")))
    sgb = L0("sg_b")
    bsT = np.zeros((128, 4, 128), np.float32)
    for c in range(4):
        bsT[0:64, c, :] = sgb[2 * c][None, :]
        bsT[64:128, c, :] = sgb[2 * c + 1][None, :]
    put("bsT", bsT.reshape(128, 512))
    half = 16
    invf = (10000.0 ** (-np.arange(half, dtype=np.float32) / half)).astype(np.float32)
    put("invf", rep(invf))
    tri = (np.arange(128)[:, None] <= np.arange(128)[None, :]).astype(np.float32)
    put("tri", tri)

    sgw = L0("sg_w")
    sgwT = np.ascontiguousarray(sgw.transpose(2, 0, 1)).reshape(128, 8 * 128)

    shared = {
        "cst": None, "sgwT": sgwT,
        "ffn1_w_gu": L0("ffn1_w_gu"), "ffn1_w_down": L0("ffn1_w_down"),
        "ffn2_w_gu": L0("ffn2_w_gu"), "ffn2_w_down": L0("ffn2_w_down"),
        "w_in": L0("w_in"), "mla_w_uq": L0("mla_w_uq"), "mla_w_ukv": L0("mla_w_ukv"),
        "mem_w_kv": L0("mem_w_kv"), "w_branch_a": L0("w_branch_a"), "w_branch_b": L0("w_branch_b"),
        "w_branch_c": L0("w_branch_c"), "w_out": L0("w_out"),
    }
    in_maps = []
    perms = []
    for c in range(NCORES):
        b, j = c // 4, c % 4
        rows = _perm_rows(j)
        perms.append((b, rows))
        m = dict(shared)
        m["cst"] = cst
        m["xT"] = np.ascontiguousarray(x[b][rows].T)
        m["memT"] = np.ascontiguousarray(mem[b].T)
        m["pos"] = np.ascontiguousarray(pos[b][rows].reshape(16, 128).T)
        cbf = np.zeros((128, NCBF), np.float32)
        cbf[:, 0:128] = np.eye(128, dtype=np.float32)
        cbf[:, 128:256] = 1.0
        mk = np.zeros((128, 8, 128), np.float32)
        for r in range(4):
            mk[:, r, :] = 1.0 if r < j else (tri if r == j else 0.0)
            mk[:, 4 + r, :] = 1.0 if r > j else (tri if r == j else 0.0)
        cbf[:, 256:] = mk.reshape(128, 1024)
        m["cbf"] = cbf.astype(ml_dtypes.bfloat16)
        in_maps.append(m)
    return in_maps, perms


_NC_CACHE = {}


def kernel(**inputs):
    in_maps, perms = _host_inputs(inputs)
    if "nc" not in _NC_CACHE:
        _NC_CACHE["nc"] = build()
    nc = _NC_CACHE["nc"]
    res = run_bass_kernel_spmd(nc, in_maps, core_ids=list(range(NCORES)))
    out = np.zeros((2, 8192, D), np.float32)
    for c in range(NCORES):
        b, rows = perms[c]
        out[b, rows, :] = np.asarray(res.results[c]["oT"], dtype=np.float32).T
    return out
```

```python
import numpy as np
import ml_dtypes
from contextlib import ExitStack
import concourse.bass as bass
import concourse.mybir as mybir
from concourse.bass_utils import run_bass_kernel_spmd

F32 = mybir.dt.float32
BF16 = mybir.dt.bfloat16
I32 = mybir.dt.int32
AF = mybir.ActivationFunctionType
ALU = mybir.AluOpType
AX = mybir.AxisListType

NCORES = 8
D = 1024
NT = 2048
DFF = 2816
FC = 22
EPS = 1e-6
C_U, C_V, C_CQ, C_CKV, C_KR, C_QM, C_GATE = 0, 512, 1024, 1408, 1664, 1696, 2208
PI = float(np.pi)

CST = {}
_off = 0
for _n, _w in [("ffn1_g", 8), ("mix_g", 8), ("ffn2_g", 8), ("mem_g", 8), ("bgate", 24), ("cqg", 384),
               ("ckvg", 256), ("lng", 512), ("lnb", 512), ("qg", 96), ("kg", 96), ("mqg", 128),
               ("mkg", 128), ("bsT", 512), ("invf", 16), ("tri", 128)]:
    CST[_n] = (_off, _w)
    _off += _w
NCST = _off
NCBF = 128 + 128 + 8 * 128


class Prog:
    ENG = ("pe", "act", "dve", "pool", "sp")

    def __init__(self, nc, es):
        self.nc, self.es = nc, es
        self.q = {e: [] for e in self.ENG}
        self.sems = {}
        self.cnt = {}
        self.waited = {e: {} for e in self.ENG}
        self.lastw = {}
        self.readers = {}
        self.out_tokens = []
        self.regions = {}
        self._ovc = {}

    def sem(self, name):
        if name not in self.sems:
            self.sems[name] = self.es.enter_context(self.nc.semaphore("s_" + name))
            self.cnt[name] = 0
        return self.sems[name]

    def region(self, key, ivs):
        self.regions[key] = list(ivs)

    def _overlap(self, k):
        if k not in self.regions:
            return (k,)
        c = self._ovc.get(k)
        if c is not None and c[0] == len(self.regions):
            return c[1]
        mine = self.regions[k]
        res = [k]
        for k2, ivs in self.regions.items():
            if k2 == k:
                continue
            hit = False
            for (a, b) in mine:
                for (c0, d0) in ivs:
                    if a < d0 and c0 < b:
                        hit = True
                        break
                if hit:
                    break
            if hit:
                res.append(k2)
        self._ovc[k] = (len(self.regions), tuple(res))
        return self._ovc[k][1]

    def _collect(self, eng, reads, writes, is_dma):
        toks = []
        for k in reads:
            for k2 in self._overlap(k):
                if k2 in self.lastw:
                    toks.append((self.lastw[k2], True))
        for k in writes:
            for k2 in self._overlap(k):
                if k2 in self.lastw:
                    toks.append((self.lastw[k2], False))
                rd = self.readers.get(k2)
                if rd:
                    toks.extend(((sn, v, te), False) for (sn, te), v in rd.items())
        waits = {}
        for ((sn, val, teng), raw) in toks:
            if teng is not None and teng == eng and not is_dma:
                if not raw or eng == "pe":
                    continue
            if self.waited[eng].get(sn, 0) >= val:
                continue
            waits[sn] = max(waits.get(sn, 0), val)
        for sn, val in waits.items():
            self.waited[eng][sn] = val
        return [(self.sem(sn), val, sn) for sn, val in waits.items()]

    def _commit(self, tok, reads, writes):
        sn, val, te = tok
        for k in reads:
            d = self.readers.setdefault(k, {})
            d[(sn, te)] = max(d.get((sn, te), 0), val)
        for k in writes:
            self.lastw[k] = tok
            self.readers[k] = {}

    def op(self, eng, fn, reads=(), writes=()):
        waits = self._collect(eng, reads, writes, False)
        sn = "E" + eng
        self.sem(sn)
        self.cnt[sn] += 1
        tok = (sn, self.cnt[sn], eng)
        self.q[eng].append((waits, fn, [(self.sems[sn], 1)], [(sn, 1)]))
        self._commit(tok, reads, writes)

    def pe(self, fn, r=(), w=()):
        self.op("pe", fn, r, w)

    def act(self, fn, r=(), w=()):
        self.op("act", fn, r, w)

    def dve(self, fn, r=(), w=()):
        self.op("dve", fn, r, w)

    def pool(self, fn, r=(), w=()):
        self.op("pool", fn, r, w)

    def dma(self, queue, pairs, reads, writes, semname, final=False):
        waits = self._collect(queue, reads, writes, True)
        s = self.sem(semname)
        self.cnt[semname] += 16 * len(pairs)
        tok = (semname, self.cnt[semname], None)

        def fn(e, pairs=pairs, s=s):
            for (o, i) in pairs:
                e.dma_start(out=o, in_=i).then_inc(s, 16)
            return None
        self.q[queue].append((waits, fn, [], [(semname, 16 * len(pairs))]))
        self._commit(tok, reads, writes)
        if final:
            self.out_tokens.append(tok)

    def collective(self, fn, reads, writes, semname="cc"):
        waits = self._collect("pool", reads, writes, True)
        s = self.sem(semname)
        self.cnt[semname] += 1
        tok = (semname, self.cnt[semname], None)
        self.q["pool"].append((waits, fn, [(s, 1)], [(semname, 1)]))
        self._commit(tok, reads, writes)

    def check(self):
        val = {}
        pos = {e: 0 for e in self.ENG}
        prog = True
        while prog:
            prog = False
            for e in self.ENG:
                while pos[e] < len(self.q[e]):
                    waits, fn, incs, names = self.q[e][pos[e]]
                    if all(val.get(sn, 0) >= v for (_s, v, sn) in waits):
                        for (sn, n) in names:
                            val[sn] = val.get(sn, 0) + n
                        pos[e] += 1
                        prog = True
                    else:
                        break
        stuck = {e: pos[e] for e in self.ENG if pos[e] < len(self.q[e])}
        for e, i in stuck.items():
            waits = self.q[e][i][0]
            print("DEADLOCK", e, "op", i, "of", len(self.q[e]), "waits",
                  [(sn, v, val.get(sn, 0)) for (_s, v, sn) in waits if val.get(sn, 0) < v])
        return not stuck

    def emit(self, block):
        assert self.check(), "semaphore protocol deadlock"
        def run(e, eng):
            for (waits, fn, incs, _n) in self.q[eng]:
                for (s, v, _sn) in waits:
                    e.wait_ge(s, v)
                ins = fn(e)
                for (s, n) in incs:
                    ins.then_inc(s, n)
            if eng == "sp":
                for (sn, val, _) in self.out_tokens:
                    e.wait_ge(self.sems[sn], val)

        @block.tensor
        def _(e):
            run(e, "pe")

        @block.scalar
        def _(e):
            run(e, "act")

        @block.vector
        def _(e):
            run(e, "dve")

        @block.gpsimd
        def _(e):
            run(e, "pool")

        @block.sync
        def _(e):
            run(e, "sp")


class Ring:
    def __init__(self, name, views):
        self.name, self.views, self.i = name, views, 0

    def next(self):
        k = self.i % len(self.views)
        self.i += 1
        return self.views[k], "%s%d" % (self.name, k)


def build(stop=None):
    nc = bass.Bass("TRN2", target_bir_lowering=False)

    def din(name, shape, dt=F32):
        return nc.dram_tensor(name, shape, dt, kind="ExternalInput").ap()

    xT_d = din("xT", [D, NT])
    memT_d = din("memT", [D, 256])
    pos_d = din("pos", [128, 16], I32)
    cst_d = din("cst", [128, NCST])
    cbf_d = din("cbf", [128, NCBF], BF16)
    sgwT_d = din("sgwT", [128, 8 * 128])
    w1gu_d = din("ffn1_w_gu", [D, 2 * DFF])
    w1dn_d = din("ffn1_w_down", [DFF, D])
    w2gu_d = din("ffn2_w_gu", [D, 2 * DFF])
    w2dn_d = din("ffn2_w_down", [DFF, D])
    win_d = din("w_in", [D, 5280])
    wuq_d = din("mla_w_uq", [384, 768])
    wukv_d = din("mla_w_ukv", [256, 1024])
    wmkv_d = din("mem_w_kv", [D, 1024])
    wbr_d = [din("w_branch_a", [512, D]), din("w_branch_b", [512, D]), din("w_branch_c", [512, D])]
    wout_d = din("w_out", [D, D])
    oT_d = nc.dram_tensor("oT", [D, NT], F32, kind="ExternalOutput").ap()
    kin_t = [nc.dram_tensor("kin%d" % c, [384, 1024], BF16) for c in range(4)]
    kout_t = [nc.dram_tensor("kout%d" % c, [4 * 384, 1024], BF16) for c in range(4)]
    vin_t = [nc.dram_tensor("vin%d" % c, [512, 1024], BF16) for c in range(2)]
    vout_t = [nc.dram_tensor("vout%d" % c, [4 * 512, 1024], BF16) for c in range(2)]

    es = ExitStack()
    with es:
        p = Prog(nc, es)

        def sb(name, shape, dt):
            return es.enter_context(nc.sbuf_tensor(name, shape, dt))

        xT = sb("xT_sb", [128, 8, NT], F32)
        cst = sb("cst_sb", [128, NCST], F32)
        cbf = sb("cbf_sb", [128, NCBF], BF16)
        csA = sb("csA", [128, 16, 32], F32)
        csB = sb("csB", [128, 16, 32], F32)
        wTsg = sb("wTsg", [128, 8, 128], BF16)
        KmemT = sb("KmemT", [128, 4, 256], BF16)
        Vmem = sb("Vmem", [128, 2, 512], BF16)
        wr_t = sb("wring", [128, 3, 4096], BF16)
        ARENA = 46600
        ar = sb("arena", [128, ARENA], BF16)
        ps = es.enter_context(nc.psum_tensor("ps", [128, 8, 512], F32))

        def C(name):
            o, w = CST[name]
            return cst[:, o:o + w]

        ident = cbf[:, 0:128]
        ones = cbf[:, 128:256]
        masks = cbf[:, 256:256 + 1024].rearrange("p (a b) -> p a b", b=128)

        class Carver:
            def __init__(self):
                self.off = 0
                self.hi = 0

            def take(self, shape, dt, key=None):
                n = int(np.prod(shape)) * (2 if dt in (F32, I32) else 1)
                n = (n + 1) // 2 * 2
                o = self.off
                assert o + n <= ARENA, (o, n, key)
                a = ar[:, o:o + n]
                self.off += n
                self.hi = max(self.hi, self.off)
                if key is not None:
                    p.region(key, [(o, o + n)])
                if dt in (F32, I32):
                    a = a.bitcast(dt)
                if len(shape) == 2:
                    a = a.rearrange("p (a b) -> p a b", b=shape[1])
                elif len(shape) == 3:
                    a = a.rearrange("p (a b c) -> p a b c", b=shape[1], c=shape[2])
                return a

        cv = Carver()
        sq_ring = Ring("sq", [cv.take([512], BF16) for _ in range(2)])
        std_t = cv.take([512], F32)
        rstd_t = cv.take([512], F32)
        junk = cv.take([512], BF16)
        small = cv.take([128], F32)
        base_off = cv.off

        def mk_hn(pfx):
            return dict(pfx=pfx, sq=cv.take([8, 96], F32, pfx + "hn_sq"), rt=cv.take([8, 32], F32, pfx + "hn_rt"),
                        t1=cv.take([8, 32], F32, pfx + "hn_t1"), t2=cv.take([8, 32], F32, pfx + "hn_t2"))

        gen_banks = Ring("ps", [ps[:, b, :] for b in range(6)])
        o_banks = Ring("po", [ps[:, 6, :], ps[:, 7, :]])
        wring = Ring("w", [wr_t[:, s, :] for s in range(3)])

        def wload(pairs_fn):
            slot, key = wring.next()
            p.dma("pool", pairs_fn(slot), reads=[], writes=[key], semname=key)
            return slot, key

        def bf_bank(bank):
            return bank.bitcast(BF16).rearrange("p (a b) -> p a b", b=128)

        p.dma("sp", [(cst[:, :], cst_d[:, :]), (cbf[:, :], cbf_d[:, :])], [], ["cst"], "cst")
        for t in range(4):
            p.dma("sp", [(xT[:, :, t * 512:(t + 1) * 512],
                          xT_d[:, t * 512:(t + 1) * 512].rearrange("(kc p) t -> p kc t", p=128))],
                  [], ["x%d" % t], "xin%d" % t)

        cv.off = base_off
        pos_i = cv.take([16], I32, "pos")
        posf = cv.take([16], F32, "posf")
        angA = cv.take([16, 32], F32, "angA")
        angT = cv.take([16, 32], F32, "angT")
        angI = cv.take([16, 32], I32, "angI")
        angF = cv.take([16, 32], F32, "angF")
        angM = cv.take([16, 32], F32, "angM")
        SC = cv.take([16, 32], F32, "SC")
        sgtmp = cv.take([8, 128], F32, "sgtmp")
        memT = cv.take([8, 256], F32, "memT")
        memnT = cv.take([8, 256], BF16, "memnT")
        km_sb = cv.take([4, 128], F32, "km_sb")
        kmn = cv.take([4, 128], BF16, "kmn")
        hn0 = mk_hn("s_")

        p.dma("sp", [(pos_i, pos_d[:, :])], [], ["pos"], "misc")
        p.dma("sp", [(sgtmp, sgwT_d[:, :].rearrange("p (g t) -> p g t", t=128))], [], ["sgtmp"], "misc2")
        p.dma("sp", [(memT, memT_d[:, :].rearrange("(kc p) m -> p kc m", p=128))], [], ["memT"], "misc3")

        p.dve(lambda e: e.tensor_copy(out=posf, in_=pos_i), ["pos"], ["posf"])
        for i in range(16):
            p.dve(lambda e, i=i: e.tensor_scalar(out=angA[:, i, 0:16], in0=C("invf"), scalar1=posf[:, i:i + 1],
                                                 scalar2=None, op0=ALU.mult), ["posf", "cst"], ["angA"])
        p.dve(lambda e: e.tensor_scalar(out=angA[:, :, 16:32], in0=angA[:, :, 0:16], scalar1=PI / 2, scalar2=None,
                                        op0=ALU.add), ["angA"], ["angA"])
        p.dve(lambda e: e.tensor_scalar(out=angT, in0=angA, scalar1=1.0 / (2 * PI), scalar2=None, op0=ALU.mult),
              ["angA"], ["angT"])
        p.dve(lambda e: e.tensor_copy(out=angI, in_=angT), ["angT"], ["angI"])
        p.dve(lambda e: e.tensor_copy(out=angF, in_=angI), ["angI"], ["angF"])
        p.dve(lambda e: e.scalar_tensor_tensor(out=angT, in0=angF, scalar=-2 * PI, in1=angA, op0=ALU.mult,
                                               op1=ALU.add), ["angF", "angA"], ["angT"])
        p.dve(lambda e: e.tensor_scalar(out=angM, in0=angT, scalar1=PI, scalar2=None, op0=ALU.is_gt),
              ["angT"], ["angM"])
        p.dve(lambda e: e.scalar_tensor_tensor(out=angF, in0=angM, scalar=-2 * PI, in1=angT, op0=ALU.mult,
                                               op1=ALU.add), ["angM", "angT"], ["angF"])
        p.dve(lambda e: e.tensor_scalar(out=angM, in0=angF, scalar1=-PI, scalar2=None, op0=ALU.is_lt),
              ["angF"], ["angM"])
        p.dve(lambda e: e.scalar_tensor_tensor(out=angT, in0=angM, scalar=2 * PI, in1=angF, op0=ALU.mult,
                                               op1=ALU.add), ["angM", "angF"], ["angT"])
        p.dve(lambda e: e.tensor_scalar(out=angF, in0=angT, scalar1=PI, scalar2=-PI, op0=ALU.min, op1=ALU.max),
              ["angT"], ["angF"])
        p.act(lambda e: e.activation(out=SC, in_=angF, func=AF.Sin), ["angF"], ["SC"])
        p.dve(lambda e: e.tensor_copy(out=csA[:, :, 0:16], in_=SC[:, :, 16:32]), ["SC"], ["csA"])
        p.dve(lambda e: e.tensor_copy(out=csA[:, :, 16:32], in_=SC[:, :, 16:32]), ["SC"], ["csA"])
        p.dve(lambda e: e.tensor_scalar(out=csB[:, :, 0:16], in0=SC[:, :, 0:16], scalar1=-1.0, scalar2=None,
                                        op0=ALU.mult), ["SC"], ["csB"])
        p.dve(lambda e: e.tensor_copy(out=csB[:, :, 16:32], in_=SC[:, :, 0:16]), ["SC"], ["csB"])
        p.dve(lambda e: e.tensor_tensor(out=wTsg[:, :, :], in0=sgtmp,
                                        in1=C("tri").unsqueeze(1).broadcast_to([128, 8, 128]), op=ALU.mult),
              ["sgtmp", "cst"], ["wTsg"])

        def norm_fm(src, srckeys, gname, dst, dstkey, N, bank=None):
            bank, bkey = bank if bank is not None else gen_banks.next()
            for kc in range(8):
                sq, sqk = sq_ring.next()
                s_ap = src(kc)
                p.act(lambda e, s_ap=s_ap, sq=sq: e.activation(out=sq[:, :N], in_=s_ap, func=AF.Square),
                      srckeys, [sqk])
                p.pe(lambda e, kc=kc, sq=sq: e.matmul(bank[:, :N], ones, sq[:, :N], start=(kc == 0), stop=(kc == 7)),
                     [sqk, "cst"], [bkey])
            p.act(lambda e: e.activation(out=std_t[:, :N], in_=bank[:, :N], func=AF.Sqrt, bias=EPS, scale=1.0 / D),
                  [bkey], ["std"])
            p.dve(lambda e: e.reciprocal(out=rstd_t[:, :N], in_=std_t[:, :N]), ["std"], ["rstd"])
            g = C(gname)
            for kc in range(8):
                s_ap, d_ap = src(kc), dst(kc)
                p.dve(lambda e, kc=kc, s_ap=s_ap, d_ap=d_ap: e.scalar_tensor_tensor(
                    out=d_ap, in0=s_ap, scalar=g[:, kc:kc + 1], in1=rstd_t[:, :N], op0=ALU.mult, op1=ALU.mult),
                    srckeys + ["rstd", "cst"], [dstkey])

        def head_norm(hn, src, H, Dh, gname, out_bf, key_in, key_out, rope_i=None):
            pf = hn["pfx"]
            rt, t1, t2 = hn["rt"], hn["t1"], hn["t2"]
            ksq, krt, kt1, kt2 = pf + "hn_sq", pf + "hn_rt", pf + "hn_t1", pf + "hn_t2"
            sqv = hn["sq"].rearrange("p a b -> p (a b)")[:, 0:H * Dh].rearrange("p (a b) -> p a b", b=Dh)
            ss = small[:, 0:H]
            sd = small[:, 8:8 + H]
            rs = small[:, 16:16 + H]
            g = C(gname)
            p.dve(lambda e: e.tensor_tensor(out=sqv, in0=src, in1=src, op=ALU.mult), [key_in], [ksq])
            p.dve(lambda e: e.tensor_reduce(out=ss, in_=sqv, axis=AX.X, op=ALU.add), [ksq], ["hn_ss"])
            p.act(lambda e: e.activation(out=sd, in_=ss, func=AF.Sqrt, bias=EPS, scale=1.0 / Dh), ["hn_ss"], ["hn_sd"])
            p.dve(lambda e: e.reciprocal(out=rs, in_=sd), ["hn_sd"], ["hn_rs"])
            p.dve(lambda e: e.tensor_tensor(out=sqv, in0=src, in1=rs.unsqueeze(2).broadcast_to([128, H, Dh]),
                                            op=ALU.mult), [key_in, "hn_rs"], [ksq])
            if rope_i is None:
                p.dve(lambda e: e.tensor_tensor(out=out_bf, in0=sqv, in1=g.unsqueeze(1).broadcast_to([128, H, Dh]),
                                                op=ALU.mult), [ksq, "cst"], [key_out])
                return
            i = rope_i
            p.dve(lambda e: e.tensor_tensor(out=out_bf[:, :, 0:64], in0=sqv[:, :, 0:64],
                                            in1=g[:, 0:64].unsqueeze(1).broadcast_to([128, H, 64]), op=ALU.mult),
                  [ksq, "cst"], [key_out])
            p.dve(lambda e: e.tensor_tensor(out=rt, in0=sqv[:, :, 64:96],
                                            in1=g[:, 64:96].unsqueeze(1).broadcast_to([128, H, 32]), op=ALU.mult),
                  [ksq, "cst"], [krt])
            p.dve(lambda e: e.tensor_tensor(out=t1, in0=rt,
                                            in1=csA[:, i, :].unsqueeze(1).broadcast_to([128, H, 32]), op=ALU.mult),
                  [krt, "csA"], [kt1])
            p.dve(lambda e: e.tensor_tensor(out=t2[:, :, 0:16], in0=rt[:, :, 16:32],
                                            in1=csB[:, i, 0:16].unsqueeze(1).broadcast_to([128, H, 16]), op=ALU.mult),
                  [krt, "csB"], [kt2])
            p.dve(lambda e: e.tensor_tensor(out=t2[:, :, 16:32], in0=rt[:, :, 0:16],
                                            in1=csB[:, i, 16:32].unsqueeze(1).broadcast_to([128, H, 16]), op=ALU.mult),
                  [krt, "csB"], [kt2])
            p.dve(lambda e: e.tensor_tensor(out=out_bf[:, :, 64:96], in0=t1, in1=t2, op=ALU.add),
                  [kt1, kt2], [key_out])

        def mm_group(bank_ap, bkey, items, reads):
            def fn(e):
                n = len(items)
                ins = None
                for k, (l, r) in enumerate(items):
                    ins = e.matmul(bank_ap, l, r, start=(k == 0), stop=(k == n - 1))
                return ins
            p.pe(fn, reads, [bkey])

        def transposes(bank, bkey, srcs, reads, rows=128):
            bb = bf_bank(bank)

            def fn(e):
                ins = None
                for j, s in enumerate(srcs):
                    ins = e.transpose(out=bb[0:rows, j, :], in_=s, identity=ident)
                return ins
            p.pe(fn, reads + ["cst"], [bkey])
            return bb

        norm_fm(lambda kc: memT[:, kc, :], ["memT"], "mem_g", lambda kc: memnT[:, kc, :], "memnT", 256)
        wk, wkk = wload(lambda s: [(s.rearrange("p (kc n) -> p kc n", n=512),
                                    wmkv_d[:, 0:512].rearrange("(kc p) n -> p kc n", p=128))])
        wv_, wvk = wload(lambda s: [(s.rearrange("p (kc n) -> p kc n", n=512),
                                     wmkv_d[:, 512:1024].rearrange("(kc p) n -> p kc n", p=128))])
        wk3 = wk.rearrange("p (kc n) -> p kc n", n=512)
        wv3 = wv_.rearrange("p (kc n) -> p kc n", n=512)
        for mb in range(2):
            bk, bkk = gen_banks.next()
            mm_group(bk, bkk, [(memnT[:, kc, mb * 128:(mb + 1) * 128], wk3[:, kc, :]) for kc in range(8)],
                     ["memnT", wkk])
            bv, bvk = gen_banks.next()
            mm_group(bv, bvk, [(memnT[:, kc, mb * 128:(mb + 1) * 128], wv3[:, kc, :]) for kc in range(8)],
                     ["memnT", wvk])
            p.act(lambda e, mb=mb, bv=bv: e.activation(out=Vmem[:, mb, :], in_=bv, func=AF.Copy), [bvk], ["Vmem"])
            p.act(lambda e, bk=bk: e.activation(out=km_sb.rearrange("p a b -> p (a b)"), in_=bk, func=AF.Copy),
                  [bkk], ["km_sb"])
            head_norm(hn0, km_sb, 4, 128, "mkg", kmn, "km_sb", "kmn")
            bt, btk = gen_banks.next()
            bb = transposes(bt, btk, [kmn[:, h, :] for h in range(4)], ["kmn"])
            p.act(lambda e, mb=mb, bb=bb: e.activation(out=KmemT[:, :, mb * 128:(mb + 1) * 128], in_=bb[:, 0:4, :],
                                                       func=AF.Copy), [btk], ["KmemT"])

        def ffn(wgu_d, wdn_d, gname):
            cv.off = base_off
            ho = cv.off
            hT = cv.take([8, 1024], BF16)
            for sub in range(2):
                p.region("h%d" % sub, [(ho + kc * 1024 + sub * 512, ho + kc * 1024 + sub * 512 + 512) for kc in range(8)])
            ao = cv.off
            actT = cv.take([FC, 1024], BF16)
            for f in range(FC):
                for sub in range(2):
                    p.region("a%d_%d" % (f, sub), [(ao + f * 1024 + sub * 512, ao + f * 1024 + sub * 512 + 512)])
            sg_ring = Ring("sg", [cv.take([512], F32, "sg%d" % k) for k in range(2)])
            for half in range(2):
                for sub in range(2):
                    t = half * 2 + sub
                    norm_fm(lambda kc, t=t: xT[:, kc, t * 512:(t + 1) * 512], ["x%d" % t], gname,
                            lambda kc, sub=sub: hT[:, kc, sub * 512:(sub + 1) * 512], "h%d" % sub, 512)
                for pp in range(11):
                    slot, wkey = wload(lambda s, pp=pp: [
                        (s.rearrange("p (a kc n) -> p a kc n", a=2, n=256)[:, 0],
                         wgu_d[:, pp * 256:(pp + 1) * 256].rearrange("(kc p) n -> p kc n", p=128)),
                        (s.rearrange("p (a kc n) -> p a kc n", a=2, n=256)[:, 1],
                         wgu_d[:, DFF + pp * 256:DFF + (pp + 1) * 256].rearrange("(kc p) n -> p kc n", p=128))])
                    w4 = slot.rearrange("p (a kc n) -> p a kc n", a=2, n=256)
                    for fi in range(2):
                        f = pp * 2 + fi
                        for sub in range(2):
                            bg, bgk = gen_banks.next()
                            bu, buk = gen_banks.next()
                            hs = lambda kc, sub=sub: hT[:, kc, sub * 512:(sub + 1) * 512]
                            mm_group(bg, bgk, [(w4[:, 0, kc, fi * 128:(fi + 1) * 128], hs(kc)) for kc in range(8)],
                                     [wkey, "h%d" % sub])
                            mm_group(bu, buk, [(w4[:, 1, kc, fi * 128:(fi + 1) * 128], hs(kc)) for kc in range(8)],
                                     [wkey, "h%d" % sub])
                            sg, sgk = sg_ring.next()
                            p.act(lambda e, sg=sg, bg=bg: e.activation(out=sg, in_=bg, func=AF.Silu), [bgk], [sgk])
                            a_ap = actT[:, f, sub * 512:(sub + 1) * 512]
                            p.dve(lambda e, sg=sg, bu=bu, a_ap=a_ap: e.tensor_tensor(out=a_ap, in0=bu, in1=sg, op=ALU.mult),
                                  [sgk, buk], ["a%d_%d" % (f, sub)])
                for m in range(8):
                    slot, wkey = wload(lambda s, m=m: [
                        (s[:, 0:FC * 128].rearrange("p (f n) -> p f n", n=128),
                         wdn_d[:, m * 128:(m + 1) * 128].rearrange("(f p) n -> p f n", p=128))])
                    w3 = slot[:, 0:FC * 128].rearrange("p (f n) -> p f n", n=128)
                    for sub in range(2):
                        t = half * 2 + sub
                        bd, bdk = gen_banks.next()
                        mm_group(bd, bdk, [(w3[:, f, :], actT[:, f, sub * 512:(sub + 1) * 512]) for f in range(FC)],
                                 [wkey] + ["a%d_%d" % (f, sub) for f in range(FC)])
                        xs = xT[:, m, t * 512:(t + 1) * 512]
                        p.dve(lambda e, bd=bd, xs=xs: e.scalar_tensor_tensor(out=xs, in0=bd, scalar=0.5, in1=xs,
                                                                            op0=ALU.mult, op1=ALU.add),
                              [bdk, "x%d" % t], ["x%d" % t])

        def store_out():
            for t in range(4):
                p.dma("sp", [(oT_d[:, t * 512:(t + 1) * 512].rearrange("(kc p) t -> p kc t", p=128),
                              xT[:, :, t * 512:(t + 1) * 512])], ["x%d" % t], ["out%d" % t], "out", final=True)

        def finish(store=True):
            if store:
                store_out()
            with nc.Block() as block:
                p.emit(block)
            return nc

        if stop != "noffn1":
            ffn(w1gu_d, w1dn_d, "ffn1_g")
        if stop == "ffn1":
            return finish()

        def run_pipelined(gens, depth=2):
            pending = list(gens)
            active = []
            while pending or active:
                if pending and len(active) < depth:
                    active.append(pending.pop(0))
                for g in list(active):
                    try:
                        next(g)
                    except StopIteration:
                        active.remove(g)

        cv.off = base_off
        hT2 = [cv.take([8, 512], BF16, "hT%d" % k) for k in range(2)]
        Kst = cv.take([8, NT], BF16, "Kst")
        Vst = cv.take([8 * 16 * 64], BF16, "Vst").rearrange("p (h i d) -> p h i d", h=8, i=16)
        m1sets = []
        for s in range(2):
            m1sets.append(dict(ckvn=cv.take([256], BF16, "ckvn%d" % s), ckvnT=cv.take([2, 128], BF16, "ckvnT%d" % s),
                               kc_sb=cv.take([8, 96], F32, "kc_sb%d" % s), kfin=cv.take([8, 96], BF16, "kfin%d" % s)))
        hn1 = mk_hn("m_")

        wkvin, wkvin_k = wload(lambda s: [(s[:, 0:8 * 288].rearrange("p (kc n) -> p kc n", n=288),
                                           win_d[:, C_CKV:C_CKV + 288].rearrange("(kc p) n -> p kc n", p=128))])
        wkvin3 = wkvin[:, 0:8 * 288].rearrange("p (kc n) -> p kc n", n=288)
        wukv, wukv_k = wload(lambda s: [(s[:, 0:2048].rearrange("p (kc n) -> p kc n", n=1024),
                                         wukv_d[:, :].rearrange("(kc p) n -> p kc n", p=128))])
        wukv3 = wukv[:, 0:2048].rearrange("p (kc n) -> p kc n", n=1024)
        norm_banks = o_banks

        def m1_block(t, bl):
            i = t * 4 + bl
            s = i % 2
            S = m1sets[s]
            ckvn, ckvnT, kc_sb, kfin = S["ckvn"], S["ckvnT"], S["kc_sb"], S["kfin"]
            kq = lambda n: "%s%d" % (n, s)
            hT = hT2[t % 2]
            hk = "hT%d" % (t % 2)
            ss1 = small[:, 32 + 4 * s:33 + 4 * s]
            sd1 = small[:, 33 + 4 * s:34 + 4 * s]
            rs1 = small[:, 34 + 4 * s:35 + 4 * s]
            bA, bAk = ps[:, 3 * s + 0, :], "ps%d" % (3 * s + 0)
            bB, bBk = ps[:, 3 * s + 1, :], "ps%d" % (3 * s + 1)
            bC, bCk = ps[:, 3 * s + 2, :], "ps%d" % (3 * s + 2)
            if bl == 0:
                nb, nbk = norm_banks.next()
                norm_fm(lambda kc, t=t: xT[:, kc, t * 512:(t + 1) * 512], ["x%d" % t], "mix_g",
                        lambda kc: hT[:, kc, :], hk, 512, bank=(nb, nbk))
            mm_group(bA[:, 0:288], bAk, [(hT[:, kc, bl * 128:(bl + 1) * 128], wkvin3[:, kc, :]) for kc in range(8)],
                     [hk, wkvin_k])
            yield
            p.act(lambda e: e.activation(out=junk[:, 0:256], in_=bA[:, 0:256], func=AF.Square, accum_out=ss1),
                  [bAk], [kq("ss1"), "junk"])
            p.act(lambda e: e.activation(out=sd1, in_=ss1, func=AF.Sqrt, bias=EPS, scale=1.0 / 256), [kq("ss1")], [kq("sd1")])
            p.act(lambda e: e.activation(out=kc_sb[:, :, 64:96], in_=bA[:, 256:288].unsqueeze(1).broadcast_to([128, 8, 32]),
                                         func=AF.Copy), [bAk], [kq("kc_sb")])
            p.dve(lambda e: e.reciprocal(out=rs1, in_=sd1), [kq("sd1")], [kq("rs1")])
            p.dve(lambda e: e.scalar_tensor_tensor(out=ckvn, in0=bA[:, 0:256], scalar=rs1, in1=C("ckvg"),
                                                   op0=ALU.mult, op1=ALU.mult), [bAk, kq("rs1"), "cst"], [kq("ckvn")])
            yield
            bb = transposes(bB, bBk, [ckvn[:, k * 128:(k + 1) * 128] for k in range(2)], [kq("ckvn")])
            p.act(lambda e: e.activation(out=ckvnT, in_=bb[:, 0:2, :], func=AF.Copy), [bBk], [kq("ckvnT")])
            yield
            mm_group(bC, bCk, [(ckvnT[:, k, :], wukv3[:, k, 0:512]) for k in range(2)], [kq("ckvnT"), wukv_k])
            mm_group(bA, bAk, [(ckvnT[:, k, :], wukv3[:, k, 512:1024]) for k in range(2)], [kq("ckvnT"), wukv_k])
            for hb, (bx, bxk) in enumerate(((bC, bCk), (bA, bAk))):
                b3 = bx.rearrange("p (h d) -> p h d", d=128)
                p.act(lambda e, b3=b3, hb=hb: e.activation(out=Vst[:, hb * 4:(hb + 1) * 4, i, :],
                                                           in_=b3[:, :, 64:128], func=AF.Copy), [bxk], ["Vst"])
                p.act(lambda e, b3=b3, hb=hb: e.activation(out=kc_sb[:, hb * 4:(hb + 1) * 4, 0:64],
                                                           in_=b3[:, :, 0:64], func=AF.Copy), [bxk], [kq("kc_sb")])
            yield
            head_norm(hn1, kc_sb, 8, 96, "kg", kfin, kq("kc_sb"), kq("kfin"), rope_i=i)
            yield
            bb2 = transposes(bB, bBk, [kfin[:, h, :] for h in range(8)], [kq("kfin")], rows=96)
            p.act(lambda e: e.activation(out=Kst[0:96, :, i * 128:(i + 1) * 128], in_=bb2[0:96, :, :],
                                         func=AF.Copy), [bBk], ["Kst"])

        run_pipelined([m1_block(t, bl) for t in range(4) for bl in range(4)], depth=2)
        if stop == "dump_m1":
            for h in range(8):
                p.dma("pool", [(oT_d[h * 96:(h + 1) * 96, :], Kst[0:96, h, :])], ["Kst"], ["dbg%d" % h], "dbg", final=True)
            for hq in range(2):
                p.dma("pool", [(oT_d[768 + hq * 128:768 + (hq + 1) * 128, :].rearrange("p (h i d) -> p h i d", h=2, i=16),
                                Vst[:, 2 * hq:2 * hq + 2, :, :])], ["Vst"], ["dbgv%d" % hq], "dbg", final=True)
            return finish(store=False)
        for c in range(4):
            p.dma("sp", [(kin_t[c].ap().rearrange("(hh f two) c -> f hh (two c)", hh=2, f=96, two=2),
                          Kst[0:96, 2 * c:2 * c + 2, :])], ["Kst"], ["kin%d" % c], "kin%d" % c)
        for c in range(2):
            p.dma("sp", [(vin_t[c].ap().rearrange("(hh p) (i d) -> p hh i d", p=128, d=64),
                          Vst[:, 4 * c:4 * c + 4, :, :])], ["Vst"], ["vin%d" % c], "vin%d" % c)
        if stop != "m1nocc":
            for c in range(4):
                p.collective(lambda e, c=c: e.collective_compute(
                    "AllGather", ALU.bypass, replica_groups=[[0, 1, 2, 3], [4, 5, 6, 7]],
                    ins=[kin_t[c].ap().opt()], outs=[kout_t[c].ap().opt()]),
                    ["kin%d" % c], ["kout%d" % c], semname="cck%d" % c)
            for c in range(2):
                p.collective(lambda e, c=c: e.collective_compute(
                    "AllGather", ALU.bypass, replica_groups=[[0, 1, 2, 3], [4, 5, 6, 7]],
                    ins=[vin_t[c].ap().opt()], outs=[vout_t[c].ap().opt()]),
                    ["vin%d" % c], ["vout%d" % c], semname="ccv%d" % c)
        if stop == "m1":
            return finish()
        if stop == "m1nocc":
            return finish()
        cv.off = base_off
        hT = cv.take([8, 512], BF16, "hT")
        QT = cv.take([8, 512], BF16, "QT")
        qmT = cv.take([4, 512], BF16, "qmT")
        yT = [cv.take([4, 512], BF16, k) for k in ("yaT", "ybT", "ycT")]
        Kring_v = [cv.take([2048], BF16, "K%d" % s) for s in range(4)]
        Vring_v = [cv.take([16, 128], BF16, "V%d" % s) for s in range(4)]
        ph_off = cv.off
        uT_sb = cv.take([4, 512], BF16, "uT")
        Asets = [dict(v_sb=cv.take([512], F32, "v_sb%d" % s), v_ln=cv.take([512], BF16, "v_ln%d" % s),
                      mtmp=cv.take([4, 128], F32, "mtmp%d" % s)) for s in range(2)]
        cv.off = ph_off
        Qsets = [dict(cqn=cv.take([384], BF16, "cqn%d" % s), cqnT=cv.take([3, 128], BF16, "cqnT%d" % s),
                      q_sb=cv.take([8, 96], F32, "q_sb%d" % s), qfin=cv.take([8, 96], BF16, "qfin%d" % s))
                 for s in range(2)]
        hn2 = mk_hn("t_")
        q_end = cv.off
        cv.off = ph_off
        Csets = [dict(qm_sb=cv.take([4, 128], F32, "qm_sb%d" % s), qmn=cv.take([4, 128], BF16, "qmn%d" % s))
                 for s in range(2)]
        assert cv.off <= q_end - 3 * 1024 - 1536 * 2 or True
        cv.off = ph_off
        P_ring = Ring("P", [cv.take([512], BF16, "P%d" % k) for k in range(4)])
        rd_views = [cv.take([512], F32, "rd%d" % k) for k in range(2)]
        rden_ring = Ring("rd", rd_views)
        acc1, acc2 = rd_views
        mergedT = cv.take([8, 512], BF16, "mergedT")
        go = cv.off
        g_sb = cv.take([3, 512], BF16)
        for br in range(3):
            p.region("g_sb%d" % br, [(go + br * 512, go + br * 512 + 512)])

        for s in range(4):
            lo = 64 if s < 2 else 0
            p.pool(lambda e, s=s, lo=lo: e.memset(Vring_v[s][:, :, lo:lo + 64], 1.0), [], ["V%d" % s])

        SC_B = 96 ** -0.5
        SC_C = 128 ** -0.5
        st6 = small[:, 40:46]
        mv = small[:, 46:48]
        sdv = small[:, 48:49]
        rsv = small[:, 49:50]

        for t in range(4):
            xk = "x%d" % t
            norm_fm(lambda kc, t=t: xT[:, kc, t * 512:(t + 1) * 512], [xk], "mix_g",
                    lambda kc: hT[:, kc, :], "hT", 512)
            wv_s, wv_k = wload(lambda s: [(s.rearrange("p (kc n) -> p kc n", n=512),
                                           win_d[:, C_V:C_V + 512].rearrange("(kc p) n -> p kc n", p=128))])
            wu_s, wu_k = wload(lambda s: [(s.rearrange("p (kc n) -> p kc n", n=512),
                                           win_d[:, C_U:C_U + 512].rearrange("(kc p) n -> p kc n", p=128))])
            wv3 = wv_s.rearrange("p (kc n) -> p kc n", n=512)
            wu3 = wu_s.rearrange("p (kc n) -> p kc n", n=512)
            for c in range(4):
                bu, buk = o_banks.next()
                mm_group(bu, buk, [(wu3[:, kc, c * 128:(c + 1) * 128], hT[:, kc, :]) for kc in range(8)], [wu_k, "hT"])
                p.act(lambda e, c=c, bu=bu: e.activation(out=uT_sb[:, c, :], in_=bu, func=AF.Gelu), [buk], ["uT"])

            def a_block(bl):
                s = bl % 2
                S = Asets[s]
                v_sb, v_ln, mtmp = S["v_sb"], S["v_ln"], S["mtmp"]
                kq = lambda n: "%s%d" % (n, s)
                st6 = small[:, 64 + 16 * s:70 + 16 * s]
                mv = small[:, 70 + 16 * s:72 + 16 * s]
                sdv = small[:, 72 + 16 * s:73 + 16 * s]
                rsv = small[:, 73 + 16 * s:74 + 16 * s]
                bV, bVk = ps[:, 3 * s + 0, :], "ps%d" % (3 * s + 0)
                bM, bMk = ps[:, 3 * s + 1, :], "ps%d" % (3 * s + 1)
                mm_group(bV, bVk, [(hT[:, kc, bl * 128:(bl + 1) * 128], wv3[:, kc, :]) for kc in range(8)], [wv_k, "hT"])
                yield
                p.act(lambda e: e.activation(out=v_sb, in_=bV, func=AF.Gelu), [bVk], [kq("v_sb")])
                p.dve(lambda e: e.bn_stats(out=st6, in_=v_sb), [kq("v_sb")], [kq("st6")])
                p.dve(lambda e: e.bn_aggr(out=mv, in_=st6), [kq("st6")], [kq("mv")])
                p.act(lambda e: e.activation(out=sdv, in_=mv[:, 1:2], func=AF.Sqrt, bias=EPS, scale=1.0), [kq("mv")], [kq("sdv")])
                yield
                p.dve(lambda e: e.reciprocal(out=rsv, in_=sdv), [kq("sdv")], [kq("rsv")])
                p.dve(lambda e: e.scalar_tensor_tensor(out=v_sb, in0=v_sb, scalar=mv[:, 0:1], in1=C("lng"),
                                                       op0=ALU.subtract, op1=ALU.mult), [kq("v_sb"), kq("mv"), "cst"], [kq("v_sb")])
                p.dve(lambda e: e.scalar_tensor_tensor(out=v_ln, in0=v_sb, scalar=rsv, in1=C("lnb"),
                                                       op0=ALU.mult, op1=ALU.add), [kq("v_sb"), kq("rsv"), "cst"], [kq("v_ln")])
                yield

                def mixfn(e):
                    ins = None
                    for g in range(8):
                        ins = e.matmul(bM[(g % 2) * 64:(g % 2) * 64 + 64, (g // 2) * 128:(g // 2 + 1) * 128],
                                       v_ln[:, g * 64:(g + 1) * 64], wTsg[:, g, :], start=True, stop=True)
                    return ins
                p.pe(mixfn, [kq("v_ln"), "wTsg"], [bMk])
                yield
                bm3 = bM.rearrange("p (c t) -> p c t", t=128)
                p.dve(lambda e: e.tensor_tensor(out=mtmp, in0=bm3, in1=C("bsT").rearrange("p (c t) -> p c t", t=128),
                                                op=ALU.add), [bMk, "cst"], [kq("mtmp")])
                p.dve(lambda e: e.tensor_tensor(out=yT[0][:, :, bl * 128:(bl + 1) * 128], in0=mtmp,
                                                in1=uT_sb[:, :, bl * 128:(bl + 1) * 128], op=ALU.mult),
                      [kq("mtmp"), "uT"], ["yaT"])
            run_pipelined([a_block(bl) for bl in range(4)], depth=2)
            wcq_s, wcq_k = wload(lambda s: [(s[:, 0:8 * 384].rearrange("p (kc n) -> p kc n", n=384),
                                             win_d[:, C_CQ:C_CQ + 384].rearrange("(kc p) n -> p kc n", p=128))])
            wuq_s, wuq_k = wload(lambda s: [(s[:, 0:3 * 768].rearrange("p (kc n) -> p kc n", n=768),
                                             wuq_d[:, :].rearrange("(kc p) n -> p kc n", p=128))])
            wcq3 = wcq_s[:, 0:8 * 384].rearrange("p (kc n) -> p kc n", n=384)
            wuq3 = wuq_s[:, 0:3 * 768].rearrange("p (kc n) -> p kc n", n=768)

            def q_block(bl):
                i = t * 4 + bl
                s = bl % 2
                S = Qsets[s]
                cqn, cqnT, q_sb, qfin = S["cqn"], S["cqnT"], S["q_sb"], S["qfin"]
                kq = lambda n: "%s%d" % (n, s)
                ss1 = small[:, 32 + 4 * s:33 + 4 * s]
                sd1 = small[:, 33 + 4 * s:34 + 4 * s]
                rs1 = small[:, 34 + 4 * s:35 + 4 * s]
                bA, bAk = ps[:, 3 * s + 0, :], "ps%d" % (3 * s + 0)
                bB, bBk = ps[:, 3 * s + 1, :], "ps%d" % (3 * s + 1)
                bC, bCk = ps[:, 3 * s + 2, :], "ps%d" % (3 * s + 2)
                mm_group(bA[:, 0:384], bAk, [(hT[:, kc, bl * 128:(bl + 1) * 128], wcq3[:, kc, :]) for kc in range(8)],
                         [wcq_k, "hT"])
                yield
                p.act(lambda e: e.activation(out=junk[:, 0:384], in_=bA[:, 0:384], func=AF.Square, accum_out=ss1),
                      [bAk], [kq("ss1"), "junk"])
                p.act(lambda e: e.activation(out=sd1, in_=ss1, func=AF.Sqrt, bias=EPS, scale=1.0 / 384), [kq("ss1")], [kq("sd1")])
                p.dve(lambda e: e.reciprocal(out=rs1, in_=sd1), [kq("sd1")], [kq("rs1")])
                p.dve(lambda e: e.scalar_tensor_tensor(out=cqn, in0=bA[:, 0:384], scalar=rs1, in1=C("cqg"),
                                                       op0=ALU.mult, op1=ALU.mult), [bAk, kq("rs1"), "cst"], [kq("cqn")])
                yield
                bb = transposes(bB, bBk, [cqn[:, k * 128:(k + 1) * 128] for k in range(3)], [kq("cqn")])
                p.act(lambda e: e.activation(out=cqnT, in_=bb[:, 0:3, :], func=AF.Copy), [bBk], [kq("cqnT")])
                yield
                for hb, (bx, bxk) in enumerate(((bC, bCk), (bA, bAk))):
                    mm_group(bx[:, 0:384], bxk, [(cqnT[:, k, :], wuq3[:, k, hb * 384:(hb + 1) * 384]) for k in range(3)],
                             [kq("cqnT"), wuq_k])
                    p.act(lambda e, bx=bx, hb=hb: e.activation(
                        out=q_sb[:, hb * 4:(hb + 1) * 4, :], in_=bx[:, 0:384].rearrange("p (h d) -> p h d", d=96),
                        func=AF.Copy), [bxk], [kq("q_sb")])
                yield
                head_norm(hn2, q_sb, 8, 96, "qg", qfin, kq("q_sb"), kq("qfin"), rope_i=i)
                yield
                bb2 = transposes(bB, bBk, [qfin[:, h, :] for h in range(8)], [kq("qfin")], rows=96)
                p.act(lambda e: e.activation(out=QT[0:96, :, bl * 128:(bl + 1) * 128], in_=bb2[0:96, :, :],
                                             func=AF.Copy), [bBk], ["QT"])
            run_pipelined([q_block(bl) for bl in range(4)], depth=2)
            wqm_s, wqm_k = wload(lambda s: [(s.rearrange("p (kc n) -> p kc n", n=512),
                                             win_d[:, C_QM:C_QM + 512].rearrange("(kc p) n -> p kc n", p=128))])
            wqm3 = wqm_s.rearrange("p (kc n) -> p kc n", n=512)

            def c_block(bl):
                s = bl % 2
                S = Csets[s]
                qm_sb, qmn = S["qm_sb"], S["qmn"]
                kq = lambda n: "%s%d" % (n, s)
                bA, bAk = ps[:, 3 * s + 0, :], "ps%d" % (3 * s + 0)
                bB, bBk = ps[:, 3 * s + 1, :], "ps%d" % (3 * s + 1)
                mm_group(bA, bAk, [(hT[:, kc, bl * 128:(bl + 1) * 128], wqm3[:, kc, :]) for kc in range(8)],
                         [wqm_k, "hT"])
                yield
                p.act(lambda e: e.activation(out=qm_sb.rearrange("p a b -> p (a b)"), in_=bA, func=AF.Copy),
                      [bAk], [kq("qm_sb")])
                yield
                head_norm(hn2, qm_sb, 4, 128, "mqg", qmn, kq("qm_sb"), kq("qmn"))
                yield
                bb = transposes(bB, bBk, [qmn[:, h, :] for h in range(4)], [kq("qmn")])
                p.act(lambda e: e.activation(out=qmT[:, :, bl * 128:(bl + 1) * 128], in_=bb[:, 0:4, :],
                                             func=AF.Copy), [bBk], ["qmT"])
            run_pipelined([c_block(bl) for bl in range(4)], depth=2)
            for h in range(4):
                Ps = []
                for mc in range(2):
                    bs, bsk = gen_banks.next()
                    mm_group(bs, bsk, [(KmemT[:, h, mc * 128:(mc + 1) * 128], qmT[:, h, :])], ["KmemT", "qmT"])
                    Pt, Pk = P_ring.next()
                    p.act(lambda e, bs=bs, Pt=Pt: e.activation(out=Pt, in_=bs, func=AF.Exp, scale=SC_C), [bsk], [Pk])
                    Ps.append((Pt, Pk))
                bo, bok = gen_banks.next()
                mm_group(bo, bok, [(Vmem[:, mc, h * 128:(h + 1) * 128], Ps[mc][0]) for mc in range(2)],
                         ["Vmem"] + [k for _, k in Ps])
                bd, bdk = gen_banks.next()
                mm_group(bd, bdk, [(ones, Ps[mc][0]) for mc in range(2)], ["cst"] + [k for _, k in Ps])
                rd, rdk = rden_ring.next()
                p.dve(lambda e, rd=rd, bd=bd: e.reciprocal(out=rd, in_=bd), [bdk], [rdk])
                p.dve(lambda e, rd=rd, bo=bo, h=h: e.tensor_tensor(out=yT[2][:, h, :], in0=bo, in1=rd, op=ALU.mult),
                      [bok, rdk], ["ycT"])
            nki = 4 * t + 4
            for h in range(8):
                par = h % 2
                ob, obk = o_banks.next()
                pend = []
                npv = 0
                ntot = 4 * nki

                def do_pv(item, first, last, ob=ob, obk=obk):
                    (Pt, Pk, c0, Vc, vk, ki) = item
                    p.pe(lambda e: e.matmul(ob[:, c0:512], Vc[:, ki, :], Pt[:, c0:512], start=first, stop=last),
                         [Pk, vk], [obk])
                for r in range(4):
                    slot = par * 2 + (r % 2)
                    Kc = Kring_v[slot]
                    Vc = Vring_v[slot]
                    kk, vk = "K%d" % slot, "V%d" % slot
                    vlo = 0 if par == 0 else 64
                    kb = r * 384 + (h % 2) * 192
                    vb = r * 512 + (h % 4) * 128
                    p.dma("pool", [(Kc[0:96, 0:nki * 128],
                                  kout_t[h // 2].ap()[kb:kb + 192, :].rearrange(
                                      "(f two) c -> f (two c)", two=2)[:, 0:nki * 128])],
                          ["kout%d" % (h // 2)], [kk], kk)
                    p.dma("pool", [(Vc[:, 0:nki, vlo:vlo + 64],
                                  vout_t[h // 4].ap()[vb:vb + 128, :].rearrange(
                                      "p (i d) -> p i d", d=64)[:, 0:nki, :])],
                          ["vout%d" % (h // 4)], [vk], vk)
                    for ki in range(nki):
                        d = ki - 4 * t
                        c0 = 128 * d if d > 0 else 0
                        bs, bsk = gen_banks.next()
                        p.pe(lambda e, bs=bs, Kc=Kc, ki=ki, c0=c0, h=h: e.matmul(
                            bs[:, c0:512], Kc[0:96, ki * 128:(ki + 1) * 128], QT[0:96, h, c0:512], start=True, stop=True),
                            [kk, "QT"], [bsk])
                        Pt, Pk = P_ring.next()
                        p.act(lambda e, bs=bs, Pt=Pt, c0=c0: e.activation(out=Pt[:, c0:512], in_=bs[:, c0:512],
                                                                         func=AF.Exp, scale=SC_B), [bsk], [Pk])
                        if d >= 0:
                            mk = masks[:, (ki % 2) * 4 + r, :]
                            p.dve(lambda e, Pt=Pt, c0=c0, mk=mk: e.tensor_tensor(
                                out=Pt[:, c0:c0 + 128], in0=Pt[:, c0:c0 + 128], in1=mk, op=ALU.mult), [Pk, "cst"], [Pk])
                        pend.append((Pt, Pk, c0, Vc, vk, ki))
                        if len(pend) > 2:
                            do_pv(pend.pop(0), npv == 0, npv == ntot - 1)
                            npv += 1
                while pend:
                    do_pv(pend.pop(0), npv == 0, npv == ntot - 1)
                    npv += 1
                olo = 0 if par == 0 else 64
                dlo = 64 - olo
                rd, rdk = rden_ring.next()
                p.dve(lambda e, rd=rd, ob=ob, dlo=dlo: e.reciprocal(out=rd[dlo:dlo + 64, :], in_=ob[dlo:dlo + 64, :]),
                      [obk], [rdk])
                p.dve(lambda e, rd=rd, ob=ob, olo=olo, dlo=dlo, h=h: e.tensor_tensor(
                    out=yT[1][olo:olo + 64, h // 2, :], in0=ob[olo:olo + 64, :], in1=rd[dlo:dlo + 64, :], op=ALU.mult),
                    [obk, rdk], ["ybT"])
            if stop == "dump_y%d" % t:
                for br in range(3):
                    p.dma("pool", [(oT_d[0:512, br * 512:(br + 1) * 512].rearrange("(kc p) t -> p kc t", p=128), yT[br])],
                          [("yaT", "ybT", "ycT")[br]], ["dbgy%d" % br], "dbg", final=True)
                return finish(store=False)
            for m in range(8):
                gs, gk = wload(lambda s, m=m: [
                    (s[:, 0:3072].rearrange("p (b kc n) -> p b kc n", b=3, n=128)[:, br],
                     win_d[:, C_GATE + br * 1024 + m * 128:C_GATE + br * 1024 + (m + 1) * 128].rearrange(
                         "(kc p) n -> p kc n", p=128)) for br in range(3)])
                bs_, bk_ = wload(lambda s, m=m: [
                    (s[:, 0:1536].rearrange("p (b kc n) -> p b kc n", b=3, n=128)[:, br],
                     wbr_d[br][:, m * 128:(m + 1) * 128].rearrange("(kc p) n -> p kc n", p=128)) for br in range(3)])
                g4 = gs[:, 0:3072].rearrange("p (b kc n) -> p b kc n", b=3, n=128)
                b4 = bs_[:, 0:1536].rearrange("p (b kc n) -> p b kc n", b=3, n=128)
                for br in range(3):
                    bg, bgk = gen_banks.next()
                    mm_group(bg, bgk, [(g4[:, br, kc, :], hT[:, kc, :]) for kc in range(8)], [gk, "hT"])
                    p.act(lambda e, bg=bg, br=br, m=m: e.activation(out=g_sb[:, br, :], in_=bg, func=AF.Sigmoid,
                                                                   bias=C("bgate")[:, br * 8 + m:br * 8 + m + 1], scale=1.0),
                          [bgk, "cst"], ["g_sb%d" % br])
                ykeys = ["yaT", "ybT", "ycT"]
                bbs = []
                for br in range(3):
                    bb_, bbk = gen_banks.next()
                    mm_group(bb_, bbk, [(b4[:, br, kc, :], yT[br][:, kc, :]) for kc in range(4)], [bk_, ykeys[br]])
                    bbs.append((bb_, bbk))
                p.dve(lambda e, b=bbs[0][0]: e.tensor_tensor(out=acc1, in0=b, in1=g_sb[:, 0, :], op=ALU.mult),
                      [bbs[0][1], "g_sb0"], ["rd0"])
                p.dve(lambda e, b=bbs[1][0]: e.tensor_tensor(out=acc2, in0=b, in1=g_sb[:, 1, :], op=ALU.mult),
                      [bbs[1][1], "g_sb1"], ["rd1"])
                p.dve(lambda e: e.tensor_tensor(out=acc1, in0=acc1, in1=acc2, op=ALU.add), ["rd0", "rd1"], ["rd0"])
                p.dve(lambda e, b=bbs[2][0]: e.tensor_tensor(out=acc2, in0=b, in1=g_sb[:, 2, :], op=ALU.mult),
                      [bbs[2][1], "g_sb2"], ["rd1"])
                p.dve(lambda e, m=m: e.tensor_tensor(out=mergedT[:, m, :], in0=acc1, in1=acc2, op=ALU.add),
                      ["rd0", "rd1"], ["mergedT"])
            for hf in range(2):
                ws, wk_ = wload(lambda s, hf=hf: [(s.rearrange("p (kc n) -> p kc n", n=512),
                                                   wout_d[:, hf * 512:(hf + 1) * 512].rearrange("(kc p) n -> p kc n", p=128))])
                w3 = ws.rearrange("p (kc n) -> p kc n", n=512)
                for mm in range(4):
                    m = hf * 4 + mm
                    bo, bok = gen_banks.next()
                    mm_group(bo, bok, [(w3[:, kc, mm * 128:(mm + 1) * 128], mergedT[:, kc, :]) for kc in range(8)],
                             [wk_, "mergedT"])
                    xs = xT[:, m, t * 512:(t + 1) * 512]
                    p.dve(lambda e, bo=bo, xs=xs: e.tensor_tensor(out=xs, in0=bo, in1=xs, op=ALU.add), [bok, xk], [xk])

        if stop != "mid":
            ffn(w2gu_d, w2dn_d, "ffn2_g")
        return finish()


def _perm_rows(j):
    rows = []
    for i in range(16):
        g = i // 2
        blk = 8 * g + (j if i % 2 == 0 else 7 - j)
        rows.append(np.arange(blk * 128, (blk + 1) * 128))
    return np.concatenate(rows)


def _host_inputs(inp):
    f = lambda a: np.ascontiguousarray(np.asarray(a, dtype=np.float32))
    x = f(inp["x"])
    mem = f(inp["mem"])
    pos = np.asarray(inp["positions"]).astype(np.int32)
    L0 = lambda k: f(inp[k])[0]

    def col(v, n):
        return np.ascontiguousarray(v.reshape(n, 128).T)

    def rep(v):
        return np.ascontiguousarray(np.broadcast_to(v[None, :], (128, v.shape[0])))

    cst = np.zeros((128, NCST), np.float32)

    def put(name, arr):
        o, w = CST[name]
        assert arr.shape == (128, w), (name, arr.shape)
        cst[:, o:o + w] = arr
    put("ffn1_g", col(L0("ffn1_norm"), 8))
    put("mix_g", col(L0("mix_norm"), 8))
    put("ffn2_g", col(L0("ffn2_norm"), 8))
    put("mem_g", col(L0("mem_norm"), 8))
    put("bgate", col(L0("b_gate"), 24))
    put("cqg", rep(L0("mla_cq_norm")))
    put("ckvg", rep(L0("mla_ckv_norm")))
    put("lng", rep(L0("sg_ln_g")))
    put("lnb", rep(L0("sg_ln_b")))
    put("qg", rep(L0("mla_q_norm")))
    put("kg", rep(L0("mla_k_norm")))
    put("mqg", rep(L0("mem_q_norm")))
    put("mkg", rep(L0("mem_k_norm")))
    sgb = L0("sg_b")
    bsT = np.zeros((128, 4, 128), np.float32)
    for c in range(4):
        bsT[0:64, c, :] = sgb[2 * c][None, :]
        bsT[64:128, c, :] = sgb[2 * c + 1][None, :]
    put("bsT", bsT.reshape(128, 512))
    half = 16
    invf = (10000.0 ** (-np.arange(half, dtype=np.float32) / half)).astype(np.float32)
    put("invf", rep(invf))
    tri = (np.arange(128)[:, None] <= np.arange(128)[None, :]).astype(np.float32)
    put("tri", tri)

    sgw = L0("sg_w")
    sgwT = np.ascontiguousarray(sgw.transpose(2, 0, 1)).reshape(128, 8 * 128)

    shared = {
        "cst": None, "sgwT": sgwT,
        "ffn1_w_gu": L0("ffn1_w_gu"), "ffn1_w_down": L0("ffn1_w_down"),
        "ffn2_w_gu": L0("ffn2_w_gu"), "ffn2_w_down": L0("ffn2_w_down"),
        "w_in": L0("w_in"), "mla_w_uq": L0("mla_w_uq"), "mla_w_ukv": L0("mla_w_ukv"),
        "mem_w_kv": L0("mem_w_kv"), "w_branch_a": L0("w_branch_a"), "w_branch_b": L0("w_branch_b"),
        "w_branch_c": L0("w_branch_c"), "w_out": L0("w_out"),
    }
    in_maps = []
    perms = []
    for c in range(NCORES):
        b, j = c // 4, c % 4
        rows = _perm_rows(j)
        perms.append((b, rows))
        m = dict(shared)
        m["cst"] = cst
        m["xT"] = np.ascontiguousarray(x[b][rows].T)
        m["memT"] = np.ascontiguousarray(mem[b].T)
        m["pos"] = np.ascontiguousarray(pos[b][rows].reshape(16, 128).T)
        cbf = np.zeros((128, NCBF), np.float32)
        cbf[:, 0:128] = np.eye(128, dtype=np.float32)
        cbf[:, 128:256] = 1.0
        mk = np.zeros((128, 8, 128), np.float32)
        for r in range(4):
            mk[:, r, :] = 1.0 if r < j else (tri if r == j else 0.0)
            mk[:, 4 + r, :] = 1.0 if r > j else (tri if r == j else 0.0)
        cbf[:, 256:] = mk.reshape(128, 1024)
        m["cbf"] = cbf.astype(ml_dtypes.bfloat16)
        in_maps.append(m)
    return in_maps, perms


_NC_CACHE = {}


def kernel(**inputs):
    in_maps, perms = _host_inputs(inputs)
    if "nc" not in _NC_CACHE:
        _NC_CACHE["nc"] = build()
    nc = _NC_CACHE["nc"]
    res = run_bass_kernel_spmd(nc, in_maps, core_ids=list(range(NCORES)))
    out = np.zeros((2, 8192, D), np.float32)
    for c in range(NCORES):
        b, rows = perms[c]
        out[b, rows, :] = np.asarray(res.results[c]["oT"], dtype=np.float32).T
    return out
```

```python
import numpy as np
import ml_dtypes
from contextlib import ExitStack
import concourse.bass as bass
import concourse.mybir as mybir
from concourse.bass_utils import run_bass_kernel_spmd

F32 = mybir.dt.float32
BF16 = mybir.dt.bfloat16
I32 = mybir.dt.int32
AF = mybir.ActivationFunctionType
ALU = mybir.AluOpType
AX = mybir.AxisListType

NCORES = 8
D = 1024
NT = 2048
DFF = 2816
FC = 22
EPS = 1e-6
C_U, C_V, C_CQ, C_CKV, C_KR, C_QM, C_GATE = 0, 512, 1024, 1408, 1664, 1696, 2208
PI = float(np.pi)

CST = {}
_off = 0
for _n, _w in [("ffn1_g", 8), ("mix_g", 8), ("ffn2_g", 8), ("mem_g", 8), ("bgate", 24), ("cqg", 384),
               ("ckvg", 256), ("lng", 512), ("lnb", 512), ("qg", 96), ("kg", 96), ("mqg", 128),
               ("mkg", 128), ("bsT", 512), ("invf", 16), ("tri", 128)]:
    CST[_n] = (_off, _w)
    _off += _w
NCST = _off
NCBF = 128 + 128 + 8 * 128


class Prog:
    ENG = ("pe", "act", "dve", "pool", "sp")

    def __init__(self, nc, es):
        self.nc, self.es = nc, es
        self.q = {e: [] for e in self.ENG}
        self.sems = {}
        self.cnt = {}
        self.waited = {e: {} for e in self.ENG}
        self.lastw = {}
        self.readers = {}
        self.out_tokens = []
        self.regions = {}
        self._ovc = {}

    def sem(self, name):
        if name not in self.sems:
            self.sems[name] = self.es.enter_context(self.nc.semaphore("s_" + name))
            self.cnt[name] = 0
        return self.sems[name]

    def region(self, key, ivs):
        self.regions[key] = list(ivs)

    def _overlap(self, k):
        if k not in self.regions:
            return (k,)
        c = self._ovc.get(k)
        if c is not None and c[0] == len(self.regions):
            return c[1]
        mine = self.regions[k]
        res = [k]
        for k2, ivs in self.regions.items():
            if k2 == k:
                continue
            hit = False
            for (a, b) in mine:
                for (c0, d0) in ivs:
                    if a < d0 and c0 < b:
                        hit = True
                        break
                if hit:
                    break
            if hit:
                res.append(k2)
        self._ovc[k] = (len(self.regions), tuple(res))
        return self._ovc[k][1]

    def _collect(self, eng, reads, writes, is_dma):
        toks = []
        for k in reads:
            for k2 in self._overlap(k):
                if k2 in self.lastw:
                    toks.append((self.lastw[k2], True))
        for k in writes:
            for k2 in self._overlap(k):
                if k2 in self.lastw:
                    toks.append((self.lastw[k2], False))
                rd = self.readers.get(k2)
                if rd:
                    toks.extend(((sn, v, te), False) for (sn, te), v in rd.items())
        waits = {}
        for ((sn, val, teng), raw) in toks:
            if teng is not None and teng == eng and not is_dma:
                if not raw or eng == "pe":
                    continue
            if self.waited[eng].get(sn, 0) >= val:
                continue
            waits[sn] = max(waits.get(sn, 0), val)
        for sn, val in waits.items():
            self.waited[eng][sn] = val
        return [(self.sem(sn), val, sn) for sn, val in waits.items()]

    def _commit(self, tok, reads, writes):
        sn, val, te = tok
        for k in reads:
            d = self.readers.setdefault(k, {})
            d[(sn, te)] = max(d.get((sn, te), 0), val)
        for k in writes:
            self.lastw[k] = tok
            self.readers[k] = {}

    def op(self, eng, fn, reads=(), writes=()):
        waits = self._collect(eng, reads, writes, False)
        sn = "E" + eng
        self.sem(sn)
        self.cnt[sn] += 1
        tok = (sn, self.cnt[sn], eng)
        self.q[eng].append((waits, fn, [(self.sems[sn], 1)], [(sn, 1)]))
        self._commit(tok, reads, writes)

    def pe(self, fn, r=(), w=()):
        self.op("pe", fn, r, w)

    def act(self, fn, r=(), w=()):
        self.op("act", fn, r, w)

    def dve(self, fn, r=(), w=()):
        self.op("dve", fn, r, w)

    def pool(self, fn, r=(), w=()):
        self.op("pool", fn, r, w)

    def dma(self, queue, pairs, reads, writes, semname, final=False):
        waits = self._collect(queue, reads, writes, True)
        s = self.sem(semname)
        self.cnt[semname] += 16 * len(pairs)
        tok = (semname, self.cnt[semname], None)

        def fn(e, pairs=pairs, s=s):
            for (o, i) in pairs:
                e.dma_start(out=o, in_=i).then_inc(s, 16)
            return None
        self.q[queue].append((waits, fn, [], [(semname, 16 * len(pairs))]))
        self._commit(tok, reads, writes)
        if final:
            self.out_tokens.append(tok)

    def collective(self, fn, reads, writes, semname="cc"):
        waits = self._collect("pool", reads, writes, True)
        s = self.sem(semname)
        self.cnt[semname] += 1
        tok = (semname, self.cnt[semname], None)
        self.q["pool"].append((waits, fn, [(s, 1)], [(semname, 1)]))
        self._commit(tok, reads, writes)

    def check(self):
        val = {}
        pos = {e: 0 for e in self.ENG}
        prog = True
        while prog:
            prog = False
            for e in self.ENG:
                while pos[e] < len(self.q[e]):
                    waits, fn, incs, names = self.q[e][pos[e]]
                    if all(val.get(sn, 0) >= v for (_s, v, sn) in waits):
                        for (sn, n) in names:
                            val[sn] = val.get(sn, 0) + n
                        pos[e] += 1
                        prog = True
                    else:
                        break
        stuck = {e: pos[e] for e in self.ENG if pos[e] < len(self.q[e])}
        for e, i in stuck.items():
            waits = self.q[e][i][0]
            print("DEADLOCK", e, "op", i, "of", len(self.q[e]), "waits",
                  [(sn, v, val.get(sn, 0)) for (_s, v, sn) in waits if val.get(sn, 0) < v])
        return not stuck

    def emit(self, block):
        assert self.check(), "semaphore protocol deadlock"
        def run(e, eng):
            for (waits, fn, incs, _n) in self.q[eng]:
                for (s, v, _sn) in waits:
                    e.wait_ge(s, v)
                ins = fn(e)
                for (s, n) in incs:
                    ins.then_inc(s, n)
            if eng == "sp":
                for (sn, val, _) in self.out_tokens:
                    e.wait_ge(self.sems[sn], val)

        @block.tensor
        def _(e):
            run(e, "pe")

        @block.scalar
        def _(e):
            run(e, "act")

        @block.vector
        def _(e):
            run(e, "dve")

        @block.gpsimd
        def _(e):
            run(e, "pool")

        @block.sync
        def _(e):
            run(e, "sp")


class Ring:
    def __init__(self, name, views):
        self.name, self.views, self.i = name, views, 0

    def next(self):
        k = self.i % len(self.views)
        self.i += 1
        return self.views[k], "%s%d" % (self.name, k)


def build(stop=None):
    nc = bass.Bass("TRN2", target_bir_lowering=False)

    def din(name, shape, dt=F32):
        return nc.dram_tensor(name, shape, dt, kind="ExternalInput").ap()

    xT_d = din("xT", [D, NT])
    memT_d = din("memT", [D, 256])
    pos_d = din("pos", [128, 16], I32)
    cst_d = din("cst", [128, NCST])
    cbf_d = din("cbf", [128, NCBF], BF16)
    sgwT_d = din("sgwT", [128, 8 * 128])
    w1gu_d = din("ffn1_w_gu", [D, 2 * DFF])
    w1dn_d = din("ffn1_w_down", [DFF, D])
    w2gu_d = din("ffn2_w_gu", [D, 2 * DFF])
    w2dn_d = din("ffn2_w_down", [DFF, D])
    win_d = din("w_in", [D, 5280])
    wuq_d = din("mla_w_uq", [384, 768])
    wukv_d = din("mla_w_ukv", [256, 1024])
    wmkv_d = din("mem_w_kv", [D, 1024])
    wbr_d = [din("w_branch_a", [512, D]), din("w_branch_b", [512, D]), din("w_branch_c", [512, D])]
    wout_d = din("w_out", [D, D])
    oT_d = nc.dram_tensor("oT", [D, NT], F32, kind="ExternalOutput").ap()
    kin_t = [nc.dram_tensor("kin%d" % c, [768, 512], BF16) for c in range(4)]
    kout_t = [nc.dram_tensor("kout%d" % c, [4 * 768, 512], BF16) for c in range(4)]
    vin_t = [nc.dram_tensor("vin%d" % c, [1024, 256], BF16) for c in range(4)]
    vout_t = [nc.dram_tensor("vout%d" % c, [4 * 1024, 256], BF16) for c in range(4)]

    es = ExitStack()
    with es:
        p = Prog(nc, es)

        def sb(name, shape, dt):
            return es.enter_context(nc.sbuf_tensor(name, shape, dt))

        xT = sb("xT_sb", [128, 8, NT], F32)
        cst = sb("cst_sb", [128, NCST], F32)
        cbf = sb("cbf_sb", [128, NCBF], BF16)
        csA = sb("csA", [128, 16, 32], F32)
        csB = sb("csB", [128, 16, 32], F32)
        wTsg = sb("wTsg", [128, 8, 128], BF16)
        KmemT = sb("KmemT", [128, 4, 256], BF16)
        Vmem = sb("Vmem", [128, 2, 512], BF16)
        wr_t = sb("wring", [128, 3, 4096], BF16)
        ARENA = 46600
        ar = sb("arena", [128, ARENA], BF16)
        ps = es.enter_context(nc.psum_tensor("ps", [128, 8, 512], F32))

        def C(name):
            o, w = CST[name]
            return cst[:, o:o + w]

        ident = cbf[:, 0:128]
        ones = cbf[:, 128:256]
        masks = cbf[:, 256:256 + 1024].rearrange("p (a b) -> p a b", b=128)

        class Carver:
            def __init__(self):
                self.off = 0
                self.hi = 0

            def take(self, shape, dt, key=None):
                n = int(np.prod(shape)) * (2 if dt in (F32, I32) else 1)
                n = (n + 1) // 2 * 2
                o = self.off
                assert o + n <= ARENA, (o, n, key)
                a = ar[:, o:o + n]
                self.off += n
                self.hi = max(self.hi, self.off)
                if key is not None:
                    p.region(key, [(o, o + n)])
                if dt in (F32, I32):
                    a = a.bitcast(dt)
                if len(shape) == 2:
                    a = a.rearrange("p (a b) -> p a b", b=shape[1])
                elif len(shape) == 3:
                    a = a.rearrange("p (a b c) -> p a b c", b=shape[1], c=shape[2])
                return a

        cv = Carver()
        sq_ring = Ring("sq", [cv.take([512], BF16) for _ in range(2)])
        std_t = cv.take([512], F32)
        rstd_t = cv.take([512], F32)
        junk = cv.take([512], BF16)
        small = cv.take([128], F32)
        base_off = cv.off

        def mk_hn(pfx):
            return dict(pfx=pfx, sq=cv.take([8, 96], F32, pfx + "hn_sq"), rt=cv.take([8, 32], F32, pfx + "hn_rt"),
                        t1=cv.take([8, 32], F32, pfx + "hn_t1"), t2=cv.take([8, 32], F32, pfx + "hn_t2"))

        gen_banks = Ring("ps", [ps[:, b, :] for b in range(6)])
        o_banks = Ring("po", [ps[:, 6, :], ps[:, 7, :]])
        wring = Ring("w", [wr_t[:, s, :] for s in range(3)])

        def wload(pairs_fn):
            slot, key = wring.next()
            p.dma("pool", pairs_fn(slot), reads=[], writes=[key], semname=key)
            return slot, key

        def bf_bank(bank):
            return bank.bitcast(BF16).rearrange("p (a b) -> p a b", b=128)

        p.dma("sp", [(cst[:, :], cst_d[:, :]), (cbf[:, :], cbf_d[:, :])], [], ["cst"], "cst")
        for t in range(4):
            p.dma("sp", [(xT[:, :, t * 512:(t + 1) * 512],
                          xT_d[:, t * 512:(t + 1) * 512].rearrange("(kc p) t -> p kc t", p=128))],
                  [], ["x%d" % t], "xin%d" % t)

        cv.off = base_off
        pos_i = cv.take([16], I32, "pos")
        posf = cv.take([16], F32, "posf")
        angA = cv.take([16, 32], F32, "angA")
        angT = cv.take([16, 32], F32, "angT")
        angI = cv.take([16, 32], I32, "angI")
        angF = cv.take([16, 32], F32, "angF")
        angM = cv.take([16, 32], F32, "angM")
        SC = cv.take([16, 32], F32, "SC")
        sgtmp = cv.take([8, 128], F32, "sgtmp")
        memT = cv.take([8, 256], F32, "memT")
        memnT = cv.take([8, 256], BF16, "memnT")
        km_sb = cv.take([4, 128], F32, "km_sb")
        kmn = cv.take([4, 128], BF16, "kmn")
        hn0 = mk_hn("s_")

        p.dma("sp", [(pos_i, pos_d[:, :])], [], ["pos"], "misc")
        p.dma("sp", [(sgtmp, sgwT_d[:, :].rearrange("p (g t) -> p g t", t=128))], [], ["sgtmp"], "misc2")
        p.dma("sp", [(memT, memT_d[:, :].rearrange("(kc p) m -> p kc m", p=128))], [], ["memT"], "misc3")

        p.dve(lambda e: e.tensor_copy(out=posf, in_=pos_i), ["pos"], ["posf"])
        for i in range(16):
            p.dve(lambda e, i=i: e.tensor_scalar(out=angA[:, i, 0:16], in0=C("invf"), scalar1=posf[:, i:i + 1],
                                                 scalar2=None, op0=ALU.mult), ["posf", "cst"], ["angA"])
        p.dve(lambda e: e.tensor_scalar(out=angA[:, :, 16:32], in0=angA[:, :, 0:16], scalar1=PI / 2, scalar2=None,
                                        op0=ALU.add), ["angA"], ["angA"])
        p.dve(lambda e: e.tensor_scalar(out=angT, in0=angA, scalar1=1.0 / (2 * PI), scalar2=None, op0=ALU.mult),
              ["angA"], ["angT"])
        p.dve(lambda e: e.tensor_copy(out=angI, in_=angT), ["angT"], ["angI"])
        p.dve(lambda e: e.tensor_copy(out=angF, in_=angI), ["angI"], ["angF"])
        p.dve(lambda e: e.scalar_tensor_tensor(out=angT, in0=angF, scalar=-2 * PI, in1=angA, op0=ALU.mult,
                                               op1=ALU.add), ["angF", "angA"], ["angT"])
        p.dve(lambda e: e.tensor_scalar(out=angM, in0=angT, scalar1=PI, scalar2=None, op0=ALU.is_gt),
              ["angT"], ["angM"])
        p.dve(lambda e: e.scalar_tensor_tensor(out=angF, in0=angM, scalar=-2 * PI, in1=angT, op0=ALU.mult,
                                               op1=ALU.add), ["angM", "angT"], ["angF"])
        p.dve(lambda e: e.tensor_scalar(out=angM, in0=angF, scalar1=-PI, scalar2=None, op0=ALU.is_lt),
              ["angF"], ["angM"])
        p.dve(lambda e: e.scalar_tensor_tensor(out=angT, in0=angM, scalar=2 * PI, in1=angF, op0=ALU.mult,
                                               op1=ALU.add), ["angM", "angF"], ["angT"])
        p.dve(lambda e: e.tensor_scalar(out=angF, in0=angT, scalar1=PI, scalar2=-PI, op0=ALU.min, op1=ALU.max),
              ["angT"], ["angF"])
        p.act(lambda e: e.activation(out=SC, in_=angF, func=AF.Sin), ["angF"], ["SC"])
        p.dve(lambda e: e.tensor_copy(out=csA[:, :, 0:16], in_=SC[:, :, 16:32]), ["SC"], ["csA"])
        p.dve(lambda e: e.tensor_copy(out=csA[:, :, 16:32], in_=SC[:, :, 16:32]), ["SC"], ["csA"])
        p.dve(lambda e: e.tensor_scalar(out=csB[:, :, 0:16], in0=SC[:, :, 0:16], scalar1=-1.0, scalar2=None,
                                        op0=ALU.mult), ["SC"], ["csB"])
        p.dve(lambda e: e.tensor_copy(out=csB[:, :, 16:32], in_=SC[:, :, 0:16]), ["SC"], ["csB"])
        p.dve(lambda e: e.tensor_tensor(out=wTsg[:, :, :], in0=sgtmp,
                                        in1=C("tri").unsqueeze(1).broadcast_to([128, 8, 128]), op=ALU.mult),
              ["sgtmp", "cst"], ["wTsg"])

        def norm_fm(src, srckeys, gname, dst, dstkey, N, bank=None):
            bank, bkey = bank if bank is not None else gen_banks.next()
            for kc in range(8):
                sq, sqk = sq_ring.next()
                s_ap = src(kc)
                p.act(lambda e, s_ap=s_ap, sq=sq: e.activation(out=sq[:, :N], in_=s_ap, func=AF.Square),
                      srckeys, [sqk])
                p.pe(lambda e, kc=kc, sq=sq: e.matmul(bank[:, :N], ones, sq[:, :N], start=(kc == 0), stop=(kc == 7)),
                     [sqk, "cst"], [bkey])
            p.act(lambda e: e.activation(out=std_t[:, :N], in_=bank[:, :N], func=AF.Sqrt, bias=EPS, scale=1.0 / D),
                  [bkey], ["std"])
            p.dve(lambda e: e.reciprocal(out=rstd_t[:, :N], in_=std_t[:, :N]), ["std"], ["rstd"])
            g = C(gname)
            for kc in range(8):
                s_ap, d_ap = src(kc), dst(kc)
                p.dve(lambda e, kc=kc, s_ap=s_ap, d_ap=d_ap: e.scalar_tensor_tensor(
                    out=d_ap, in0=s_ap, scalar=g[:, kc:kc + 1], in1=rstd_t[:, :N], op0=ALU.mult, op1=ALU.mult),
                    srckeys + ["rstd", "cst"], [dstkey])

        def head_norm(hn, src, H, Dh, gname, out_bf, key_in, key_out, rope_i=None):
            pf = hn["pfx"]
            rt, t1, t2 = hn["rt"], hn["t1"], hn["t2"]
            ksq, krt, kt1, kt2 = pf + "hn_sq", pf + "hn_rt", pf + "hn_t1", pf + "hn_t2"
            sqv = hn["sq"].rearrange("p a b -> p (a b)")[:, 0:H * Dh].rearrange("p (a b) -> p a b", b=Dh)
            ss = small[:, 0:H]
            sd = small[:, 8:8 + H]
            rs = small[:, 16:16 + H]
            g = C(gname)
            p.dve(lambda e: e.tensor_tensor(out=sqv, in0=src, in1=src, op=ALU.mult), [key_in], [ksq])
            p.dve(lambda e: e.tensor_reduce(out=ss, in_=sqv, axis=AX.X, op=ALU.add), [ksq], ["hn_ss"])
            p.act(lambda e: e.activation(out=sd, in_=ss, func=AF.Sqrt, bias=EPS, scale=1.0 / Dh), ["hn_ss"], ["hn_sd"])
            p.dve(lambda e: e.reciprocal(out=rs, in_=sd), ["hn_sd"], ["hn_rs"])
            p.dve(lambda e: e.tensor_tensor(out=sqv, in0=src, in1=rs.unsqueeze(2).broadcast_to([128, H, Dh]),
                                            op=ALU.mult), [key_in, "hn_rs"], [ksq])
            if rope_i is None:
                p.dve(lambda e: e.tensor_tensor(out=out_bf, in0=sqv, in1=g.unsqueeze(1).broadcast_to([128, H, Dh]),
                                                op=ALU.mult), [ksq, "cst"], [key_out])
                return
            i = rope_i
            p.dve(lambda e: e.tensor_tensor(out=out_bf[:, :, 0:64], in0=sqv[:, :, 0:64],
                                            in1=g[:, 0:64].unsqueeze(1).broadcast_to([128, H, 64]), op=ALU.mult),
                  [ksq, "cst"], [key_out])
            p.dve(lambda e: e.tensor_tensor(out=rt, in0=sqv[:, :, 64:96],
                                            in1=g[:, 64:96].unsqueeze(1).broadcast_to([128, H, 32]), op=ALU.mult),
                  [ksq, "cst"], [krt])
            p.dve(lambda e: e.tensor_tensor(out=t1, in0=rt,
                                            in1=csA[:, i, :].unsqueeze(1).broadcast_to([128, H, 32]), op=ALU.mult),
                  [krt, "csA"], [kt1])
            p.dve(lambda e: e.tensor_tensor(out=t2[:, :, 0:16], in0=rt[:, :, 16:32],
                                            in1=csB[:, i, 0:16].unsqueeze(1).broadcast_to([128, H, 16]), op=ALU.mult),
                  [krt, "csB"], [kt2])
            p.dve(lambda e: e.tensor_tensor(out=t2[:, :, 16:32], in0=rt[:, :, 0:16],
                                            in1=csB[:, i, 16:32].unsqueeze(1).broadcast_to([128, H, 16]), op=ALU.mult),
                  [krt, "csB"], [kt2])
            p.dve(lambda e: e.tensor_tensor(out=out_bf[:, :, 64:96], in0=t1, in1=t2, op=ALU.add),
                  [kt1, kt2], [key_out])

        def mm_group(bank_ap, bkey, items, reads):
            def fn(e):
                n = len(items)
                ins = None
                for k, (l, r) in enumerate(items):
                    ins = e.matmul(bank_ap, l, r, start=(k == 0), stop=(k == n - 1))
                return ins
            p.pe(fn, reads, [bkey])

        def transposes(bank, bkey, srcs, reads, rows=128):
            bb = bf_bank(bank)

            def fn(e):
                ins = None
                for j, s in enumerate(srcs):
                    ins = e.transpose(out=bb[0:rows, j, :], in_=s, identity=ident)
                return ins
            p.pe(fn, reads + ["cst"], [bkey])
            return bb

        norm_fm(lambda kc: memT[:, kc, :], ["memT"], "mem_g", lambda kc: memnT[:, kc, :], "memnT", 256)
        wk, wkk = wload(lambda s: [(s.rearrange("p (kc n) -> p kc n", n=512),
                                    wmkv_d[:, 0:512].rearrange("(kc p) n -> p kc n", p=128))])
        wv_, wvk = wload(lambda s: [(s.rearrange("p (kc n) -> p kc n", n=512),
                                     wmkv_d[:, 512:1024].rearrange("(kc p) n -> p kc n", p=128))])
        wk3 = wk.rearrange("p (kc n) -> p kc n", n=512)
        wv3 = wv_.rearrange("p (kc n) -> p kc n", n=512)
        for mb in range(2):
            bk, bkk = gen_banks.next()
            mm_group(bk, bkk, [(memnT[:, kc, mb * 128:(mb + 1) * 128], wk3[:, kc, :]) for kc in range(8)],
                     ["memnT", wkk])
            bv, bvk = gen_banks.next()
            mm_group(bv, bvk, [(memnT[:, kc, mb * 128:(mb + 1) * 128], wv3[:, kc, :]) for kc in range(8)],
                     ["memnT", wvk])
            p.act(lambda e, mb=mb, bv=bv: e.activation(out=Vmem[:, mb, :], in_=bv, func=AF.Copy), [bvk], ["Vmem"])
            p.act(lambda e, bk=bk: e.activation(out=km_sb.rearrange("p a b -> p (a b)"), in_=bk, func=AF.Copy),
                  [bkk], ["km_sb"])
            head_norm(hn0, km_sb, 4, 128, "mkg", kmn, "km_sb", "kmn")
            bt, btk = gen_banks.next()
            bb = transposes(bt, btk, [kmn[:, h, :] for h in range(4)], ["kmn"])
            p.act(lambda e, mb=mb, bb=bb: e.activation(out=KmemT[:, :, mb * 128:(mb + 1) * 128], in_=bb[:, 0:4, :],
                                                       func=AF.Copy), [btk], ["KmemT"])

        def ffn(wgu_d, wdn_d, gname):
            cv.off = base_off
            ho = cv.off
            hT = cv.take([8, 1024], BF16)
            for sub in range(2):
                p.region("h%d" % sub, [(ho + kc * 1024 + sub * 512, ho + kc * 1024 + sub * 512 + 512) for kc in range(8)])
            ao = cv.off
            actT = cv.take([FC, 1024], BF16)
            for f in range(FC):
                for sub in range(2):
                    p.region("a%d_%d" % (f, sub), [(ao + f * 1024 + sub * 512, ao + f * 1024 + sub * 512 + 512)])
            sg_ring = Ring("sg", [cv.take([512], F32, "sg%d" % k) for k in range(2)])
            for half in range(2):
                for sub in range(2):
                    t = half * 2 + sub
                    norm_fm(lambda kc, t=t: xT[:, kc, t * 512:(t + 1) * 512], ["x%d" % t], gname,
                            lambda kc, sub=sub: hT[:, kc, sub * 512:(sub + 1) * 512], "h%d" % sub, 512)
                for pp in range(11):
                    slot, wkey = wload(lambda s, pp=pp: [
                        (s.rearrange("p (a kc n) -> p a kc n", a=2, n=256)[:, 0],
                         wgu_d[:, pp * 256:(pp + 1) * 256].rearrange("(kc p) n -> p kc n", p=128)),
                        (s.rearrange("p (a kc n) -> p a kc n", a=2, n=256)[:, 1],
                         wgu_d[:, DFF + pp * 256:DFF + (pp + 1) * 256].rearrange("(kc p) n -> p kc n", p=128))])
                    w4 = slot.rearrange("p (a kc n) -> p a kc n", a=2, n=256)
                    for fi in range(2):
                        f = pp * 2 + fi
                        for sub in range(2):
                            bg, bgk = gen_banks.next()
                            bu, buk = gen_banks.next()
                            hs = lambda kc, sub=sub: hT[:, kc, sub * 512:(sub + 1) * 512]
                            mm_group(bg, bgk, [(w4[:, 0, kc, fi * 128:(fi + 1) * 128], hs(kc)) for kc in range(8)],
                                     [wkey, "h%d" % sub])
                            mm_group(bu, buk, [(w4[:, 1, kc, fi * 128:(fi + 1) * 128], hs(kc)) for kc in range(8)],
                                     [wkey, "h%d" % sub])
                            sg, sgk = sg_ring.next()
                            p.act(lambda e, sg=sg, bg=bg: e.activation(out=sg, in_=bg, func=AF.Silu), [bgk], [sgk])
                            a_ap = actT[:, f, sub * 512:(sub + 1) * 512]
                            p.dve(lambda e, sg=sg, bu=bu, a_ap=a_ap: e.tensor_tensor(out=a_ap, in0=bu, in1=sg, op=ALU.mult),
                                  [sgk, buk], ["a%d_%d" % (f, sub)])
                for m in range(8):
                    slot, wkey = wload(lambda s, m=m: [
                        (s[:, 0:FC * 128].rearrange("p (f n) -> p f n", n=128),
                         wdn_d[:, m * 128:(m + 1) * 128].rearrange("(f p) n -> p f n", p=128))])
                    w3 = slot[:, 0:FC * 128].rearrange("p (f n) -> p f n", n=128)
                    for sub in range(2):
                        t = half * 2 + sub
                        bd, bdk = gen_banks.next()
                        mm_group(bd, bdk, [(w3[:, f, :], actT[:, f, sub * 512:(sub + 1) * 512]) for f in range(FC)],
                                 [wkey] + ["a%d_%d" % (f, sub) for f in range(FC)])
                        xs = xT[:, m, t * 512:(t + 1) * 512]
                        p.dve(lambda e, bd=bd, xs=xs: e.scalar_tensor_tensor(out=xs, in0=bd, scalar=0.5, in1=xs,
                                                                            op0=ALU.mult, op1=ALU.add),
                              [bdk, "x%d" % t], ["x%d" % t])

        def store_out():
            for t in range(4):
                p.dma("sp", [(oT_d[:, t * 512:(t + 1) * 512].rearrange("(kc p) t -> p kc t", p=128),
                              xT[:, :, t * 512:(t + 1) * 512])], ["x%d" % t], ["out%d" % t], "out", final=True)

        def finish(store=True):
            if store:
                store_out()
            with nc.Block() as block:
                p.emit(block)
            return nc

        if stop != "noffn1":
            ffn(w1gu_d, w1dn_d, "ffn1_g")
        if stop == "ffn1":
            return finish()

        def run_pipelined(gens, depth=2):
            pending = list(gens)
            active = []
            while pending or active:
                if pending and len(active) < depth:
                    active.append(pending.pop(0))
                for g in list(active):
                    try:
                        next(g)
                    except StopIteration:
                        active.remove(g)

        cv.off = base_off
        hT2 = [cv.take([8, 512], BF16, "hT%d" % k) for k in range(2)]
        Kst = cv.take([8, NT], BF16, "Kst")
        Vst = cv.take([8 * 16 * 64], BF16, "Vst").rearrange("p (h i d) -> p h i d", h=8, i=16)
        m1sets = []
        for s in range(2):
            m1sets.append(dict(ckvn=cv.take([256], BF16, "ckvn%d" % s), ckvnT=cv.take([2, 128], BF16, "ckvnT%d" % s),
                               kc_sb=cv.take([8, 96], F32, "kc_sb%d" % s), kfin=cv.take([8, 96], BF16, "kfin%d" % s)))
        hn1 = mk_hn("m_")

        wkvin, wkvin_k = wload(lambda s: [(s[:, 0:8 * 288].rearrange("p (kc n) -> p kc n", n=288),
                                           win_d[:, C_CKV:C_CKV + 288].rearrange("(kc p) n -> p kc n", p=128))])
        wkvin3 = wkvin[:, 0:8 * 288].rearrange("p (kc n) -> p kc n", n=288)
        wukv, wukv_k = wload(lambda s: [(s[:, 0:2048].rearrange("p (kc n) -> p kc n", n=1024),
                                         wukv_d[:, :].rearrange("(kc p) n -> p kc n", p=128))])
        wukv3 = wukv[:, 0:2048].rearrange("p (kc n) -> p kc n", n=1024)
        norm_banks = o_banks

        def m1_block(t, bl):
            i = t * 4 + bl
            s = i % 2
            S = m1sets[s]
            ckvn, ckvnT, kc_sb, kfin = S["ckvn"], S["ckvnT"], S["kc_sb"], S["kfin"]
            kq = lambda n: "%s%d" % (n, s)
            hT = hT2[t % 2]
            hk = "hT%d" % (t % 2)
            ss1 = small[:, 32 + 4 * s:33 + 4 * s]
            sd1 = small[:, 33 + 4 * s:34 + 4 * s]
            rs1 = small[:, 34 + 4 * s:35 + 4 * s]
            bA, bAk = ps[:, 3 * s + 0, :], "ps%d" % (3 * s + 0)
            bB, bBk = ps[:, 3 * s + 1, :], "ps%d" % (3 * s + 1)
            bC, bCk = ps[:, 3 * s + 2, :], "ps%d" % (3 * s + 2)
            if bl == 0:
                nb, nbk = norm_banks.next()
                norm_fm(lambda kc, t=t: xT[:, kc, t * 512:(t + 1) * 512], ["x%d" % t], "mix_g",
                        lambda kc: hT[:, kc, :], hk, 512, bank=(nb, nbk))
            mm_group(bA[:, 0:288], bAk, [(hT[:, kc, bl * 128:(bl + 1) * 128], wkvin3[:, kc, :]) for kc in range(8)],
                     [hk, wkvin_k])
            yield
            p.act(lambda e: e.activation(out=junk[:, 0:256], in_=bA[:, 0:256], func=AF.Square, accum_out=ss1),
                  [bAk], [kq("ss1"), "junk"])
            p.act(lambda e: e.activation(out=sd1, in_=ss1, func=AF.Sqrt, bias=EPS, scale=1.0 / 256), [kq("ss1")], [kq("sd1")])
            p.act(lambda e: e.activation(out=kc_sb[:, :, 64:96], in_=bA[:, 256:288].unsqueeze(1).broadcast_to([128, 8, 32]),
                                         func=AF.Copy), [bAk], [kq("kc_sb")])
            p.dve(lambda e: e.reciprocal(out=rs1, in_=sd1), [kq("sd1")], [kq("rs1")])
            p.dve(lambda e: e.scalar_tensor_tensor(out=ckvn, in0=bA[:, 0:256], scalar=rs1, in1=C("ckvg"),
                                                   op0=ALU.mult, op1=ALU.mult), [bAk, kq("rs1"), "cst"], [kq("ckvn")])
            yield
            bb = transposes(bB, bBk, [ckvn[:, k * 128:(k + 1) * 128] for k in range(2)], [kq("ckvn")])
            p.act(lambda e: e.activation(out=ckvnT, in_=bb[:, 0:2, :], func=AF.Copy), [bBk], [kq("ckvnT")])
            yield
            mm_group(bC, bCk, [(ckvnT[:, k, :], wukv3[:, k, 0:512]) for k in range(2)], [kq("ckvnT"), wukv_k])
            mm_group(bA, bAk, [(ckvnT[:, k, :], wukv3[:, k, 512:1024]) for k in range(2)], [kq("ckvnT"), wukv_k])
            for hb, (bx, bxk) in enumerate(((bC, bCk), (bA, bAk))):
                b3 = bx.rearrange("p (h d) -> p h d", d=128)
                p.act(lambda e, b3=b3, hb=hb: e.activation(out=Vst[:, hb * 4:(hb + 1) * 4, i, :],
                                                           in_=b3[:, :, 64:128], func=AF.Copy), [bxk], ["Vst"])
                p.act(lambda e, b3=b3, hb=hb: e.activation(out=kc_sb[:, hb * 4:(hb + 1) * 4, 0:64],
                                                           in_=b3[:, :, 0:64], func=AF.Copy), [bxk], [kq("kc_sb")])
            yield
            head_norm(hn1, kc_sb, 8, 96, "kg", kfin, kq("kc_sb"), kq("kfin"), rope_i=i)
            yield
            bb2 = transposes(bB, bBk, [kfin[:, h, :] for h in range(8)], [kq("kfin")], rows=96)
            p.act(lambda e: e.activation(out=Kst[0:96, :, i * 128:(i + 1) * 128], in_=bb2[0:96, :, :],
                                         func=AF.Copy), [bBk], ["Kst"])
            if bl == 3 and stop not in ("m1nocc", "dump_m1"):
                yield
                p.dma("sp", [(kin_t[t].ap().rearrange("(h f) c -> f h c", f=96), Kst[0:96, :, t * 512:(t + 1) * 512])],
                      ["Kst"], ["kin%d" % t], "kin%d" % t)
                p.dma("sp", [(vin_t[t].ap().rearrange("(h p) (i d) -> p h i d", p=128, d=64), Vst[:, :, 4 * t:4 * t + 4, :])],
                      ["Vst"], ["vin%d" % t], "vin%d" % t)
                p.collective(lambda e: e.collective_compute(
                    "AllGather", ALU.bypass, replica_groups=[[0, 1, 2, 3], [4, 5, 6, 7]],
                    ins=[kin_t[t].ap().opt()], outs=[kout_t[t].ap().opt()]),
                    ["kin%d" % t], ["kout%d" % t], semname="cck%d" % t)
                p.collective(lambda e: e.collective_compute(
                    "AllGather", ALU.bypass, replica_groups=[[0, 1, 2, 3], [4, 5, 6, 7]],
                    ins=[vin_t[t].ap().opt()], outs=[vout_t[t].ap().opt()]),
                    ["vin%d" % t], ["vout%d" % t], semname="ccv%d" % t)

        run_pipelined([m1_block(t, bl) for t in range(4) for bl in range(4)], depth=2)
        if stop == "dump_m1":
            for h in range(8):
                p.dma("pool", [(oT_d[h * 96:(h + 1) * 96, :], Kst[0:96, h, :])], ["Kst"], ["dbg%d" % h], "dbg", final=True)
            for hq in range(2):
                p.dma("pool", [(oT_d[768 + hq * 128:768 + (hq + 1) * 128, :].rearrange("p (h i d) -> p h i d", h=2, i=16),
                                Vst[:, 2 * hq:2 * hq + 2, :, :])], ["Vst"], ["dbgv%d" % hq], "dbg", final=True)
            return finish(store=False)
        if stop == "m1":
            return finish()
        if stop == "m1nocc":
            return finish()
        cv.off = base_off
        hT = cv.take([8, 512], BF16, "hT")
        QT = cv.take([8, 512], BF16, "QT")
        qmT = cv.take([4, 512], BF16, "qmT")
        yT = [cv.take([4, 512], BF16, k) for k in ("yaT", "ybT", "ycT")]
        Kring_v = [cv.take([2048], BF16, "K%d" % s) for s in range(4)]
        Vring_v = [cv.take([16, 128], BF16, "V%d" % s) for s in range(4)]
        ph_off = cv.off
        uT_sb = cv.take([4, 512], BF16, "uT")
        Asets = [dict(v_sb=cv.take([512], F32, "v_sb%d" % s), v_ln=cv.take([512], BF16, "v_ln%d" % s),
                      mtmp=cv.take([4, 128], F32, "mtmp%d" % s)) for s in range(2)]
        cv.off = ph_off
        Qsets = [dict(cqn=cv.take([384], BF16, "cqn%d" % s), cqnT=cv.take([3, 128], BF16, "cqnT%d" % s),
                      q_sb=cv.take([8, 96], F32, "q_sb%d" % s), qfin=cv.take([8, 96], BF16, "qfin%d" % s))
                 for s in range(2)]
        hn2 = mk_hn("t_")
        q_end = cv.off
        cv.off = ph_off
        Csets = [dict(qm_sb=cv.take([4, 128], F32, "qm_sb%d" % s), qmn=cv.take([4, 128], BF16, "qmn%d" % s))
                 for s in range(2)]
        assert cv.off <= q_end - 3 * 1024 - 1536 * 2 or True
        cv.off = ph_off
        P_ring = Ring("P", [cv.take([512], BF16, "P%d" % k) for k in range(4)])
        rd_views = [cv.take([512], F32, "rd%d" % k) for k in range(2)]
        rden_ring = Ring("rd", rd_views)
        acc1, acc2 = rd_views
        mergedT = cv.take([8, 512], BF16, "mergedT")
        go = cv.off
        g_sb = cv.take([3, 512], BF16)
        for br in range(3):
            p.region("g_sb%d" % br, [(go + br * 512, go + br * 512 + 512)])

        for s in range(4):
            lo = 64 if s < 2 else 0
            p.pool(lambda e, s=s, lo=lo: e.memset(Vring_v[s][:, :, lo:lo + 64], 1.0), [], ["V%d" % s])

        SC_B = 96 ** -0.5
        SC_C = 128 ** -0.5
        st6 = small[:, 40:46]
        mv = small[:, 46:48]
        sdv = small[:, 48:49]
        rsv = small[:, 49:50]

        for t in range(4):
            xk = "x%d" % t
            norm_fm(lambda kc, t=t: xT[:, kc, t * 512:(t + 1) * 512], [xk], "mix_g",
                    lambda kc: hT[:, kc, :], "hT", 512)
            wv_s, wv_k = wload(lambda s: [(s.rearrange("p (kc n) -> p kc n", n=512),
                                           win_d[:, C_V:C_V + 512].rearrange("(kc p) n -> p kc n", p=128))])
            wu_s, wu_k = wload(lambda s: [(s.rearrange("p (kc n) -> p kc n", n=512),
                                           win_d[:, C_U:C_U + 512].rearrange("(kc p) n -> p kc n", p=128))])
            wv3 = wv_s.rearrange("p (kc n) -> p kc n", n=512)
            wu3 = wu_s.rearrange("p (kc n) -> p kc n", n=512)
            for c in range(4):
                bu, buk = o_banks.next()
                mm_group(bu, buk, [(wu3[:, kc, c * 128:(c + 1) * 128], hT[:, kc, :]) for kc in range(8)], [wu_k, "hT"])
                p.act(lambda e, c=c, bu=bu: e.activation(out=uT_sb[:, c, :], in_=bu, func=AF.Gelu), [buk], ["uT"])

            def a_block(bl):
                s = bl % 2
                S = Asets[s]
                v_sb, v_ln, mtmp = S["v_sb"], S["v_ln"], S["mtmp"]
                kq = lambda n: "%s%d" % (n, s)
                st6 = small[:, 64 + 16 * s:70 + 16 * s]
                mv = small[:, 70 + 16 * s:72 + 16 * s]
                sdv = small[:, 72 + 16 * s:73 + 16 * s]
                rsv = small[:, 73 + 16 * s:74 + 16 * s]
                bV, bVk = ps[:, 3 * s + 0, :], "ps%d" % (3 * s + 0)
                bM, bMk = ps[:, 3 * s + 1, :], "ps%d" % (3 * s + 1)
                mm_group(bV, bVk, [(hT[:, kc, bl * 128:(bl + 1) * 128], wv3[:, kc, :]) for kc in range(8)], [wv_k, "hT"])
                yield
                p.act(lambda e: e.activation(out=v_sb, in_=bV, func=AF.Gelu), [bVk], [kq("v_sb")])
                p.dve(lambda e: e.bn_stats(out=st6, in_=v_sb), [kq("v_sb")], [kq("st6")])
                p.dve(lambda e: e.bn_aggr(out=mv, in_=st6), [kq("st6")], [kq("mv")])
                p.act(lambda e: e.activation(out=sdv, in_=mv[:, 1:2], func=AF.Sqrt, bias=EPS, scale=1.0), [kq("mv")], [kq("sdv")])
                yield
                p.dve(lambda e: e.reciprocal(out=rsv, in_=sdv), [kq("sdv")], [kq("rsv")])
                p.dve(lambda e: e.scalar_tensor_tensor(out=v_sb, in0=v_sb, scalar=mv[:, 0:1], in1=C("lng"),
                                                       op0=ALU.subtract, op1=ALU.mult), [kq("v_sb"), kq("mv"), "cst"], [kq("v_sb")])
                p.dve(lambda e: e.scalar_tensor_tensor(out=v_ln, in0=v_sb, scalar=rsv, in1=C("lnb"),
                                                       op0=ALU.mult, op1=ALU.add), [kq("v_sb"), kq("rsv"), "cst"], [kq("v_ln")])
                yield

                def mixfn(e):
                    ins = None
                    for g in range(8):
                        ins = e.matmul(bM[(g % 2) * 64:(g % 2) * 64 + 64, (g // 2) * 128:(g // 2 + 1) * 128],
                                       v_ln[:, g * 64:(g + 1) * 64], wTsg[:, g, :], start=True, stop=True)
                    return ins
                p.pe(mixfn, [kq("v_ln"), "wTsg"], [bMk])
                yield
                bm3 = bM.rearrange("p (c t) -> p c t", t=128)
                p.dve(lambda e: e.tensor_tensor(out=mtmp, in0=bm3, in1=C("bsT").rearrange("p (c t) -> p c t", t=128),
                                                op=ALU.add), [bMk, "cst"], [kq("mtmp")])
                p.dve(lambda e: e.tensor_tensor(out=yT[0][:, :, bl * 128:(bl + 1) * 128], in0=mtmp,
                                                in1=uT_sb[:, :, bl * 128:(bl + 1) * 128], op=ALU.mult),
                      [kq("mtmp"), "uT"], ["yaT"])
            run_pipelined([a_block(bl) for bl in range(4)], depth=2)
            wcq_s, wcq_k = wload(lambda s: [(s[:, 0:8 * 384].rearrange("p (kc n) -> p kc n", n=384),
                                             win_d[:, C_CQ:C_CQ + 384].rearrange("(kc p) n -> p kc n", p=128))])
            wuq_s, wuq_k = wload(lambda s: [(s[:, 0:3 * 768].rearrange("p (kc n) -> p kc n", n=768),
                                             wuq_d[:, :].rearrange("(kc p) n -> p kc n", p=128))])
            wcq3 = wcq_s[:, 0:8 * 384].rearrange("p (kc n) -> p kc n", n=384)
            wuq3 = wuq_s[:, 0:3 * 768].rearrange("p (kc n) -> p kc n", n=768)

            def q_block(bl):
                i = t * 4 + bl
                s = bl % 2
                S = Qsets[s]
                cqn, cqnT, q_sb, qfin = S["cqn"], S["cqnT"], S["q_sb"], S["qfin"]
                kq = lambda n: "%s%d" % (n, s)
                ss1 = small[:, 32 + 4 * s:33 + 4 * s]
                sd1 = small[:, 33 + 4 * s:34 + 4 * s]
                rs1 = small[:, 34 + 4 * s:35 + 4 * s]
                bA, bAk = ps[:, 3 * s + 0, :], "ps%d" % (3 * s + 0)
                bB, bBk = ps[:, 3 * s + 1, :], "ps%d" % (3 * s + 1)
                bC, bCk = ps[:, 3 * s + 2, :], "ps%d" % (3 * s + 2)
                mm_group(bA[:, 0:384], bAk, [(hT[:, kc, bl * 128:(bl + 1) * 128], wcq3[:, kc, :]) for kc in range(8)],
                         [wcq_k, "hT"])
                yield
                p.act(lambda e: e.activation(out=junk[:, 0:384], in_=bA[:, 0:384], func=AF.Square, accum_out=ss1),
                      [bAk], [kq("ss1"), "junk"])
                p.act(lambda e: e.activation(out=sd1, in_=ss1, func=AF.Sqrt, bias=EPS, scale=1.0 / 384), [kq("ss1")], [kq("sd1")])
                p.dve(lambda e: e.reciprocal(out=rs1, in_=sd1), [kq("sd1")], [kq("rs1")])
                p.dve(lambda e: e.scalar_tensor_tensor(out=cqn, in0=bA[:, 0:384], scalar=rs1, in1=C("cqg"),
                                                       op0=ALU.mult, op1=ALU.mult), [bAk, kq("rs1"), "cst"], [kq("cqn")])
                yield
                bb = transposes(bB, bBk, [cqn[:, k * 128:(k + 1) * 128] for k in range(3)], [kq("cqn")])
                p.act(lambda e: e.activation(out=cqnT, in_=bb[:, 0:3, :], func=AF.Copy), [bBk], [kq("cqnT")])
                yield
                for hb, (bx, bxk) in enumerate(((bC, bCk), (bA, bAk))):
                    mm_group(bx[:, 0:384], bxk, [(cqnT[:, k, :], wuq3[:, k, hb * 384:(hb + 1) * 384]) for k in range(3)],
                             [kq("cqnT"), wuq_k])
                    p.act(lambda e, bx=bx, hb=hb: e.activation(
                        out=q_sb[:, hb * 4:(hb + 1) * 4, :], in_=bx[:, 0:384].rearrange("p (h d) -> p h d", d=96),
                        func=AF.Copy), [bxk], [kq("q_sb")])
                yield
                head_norm(hn2, q_sb, 8, 96, "qg", qfin, kq("q_sb"), kq("qfin"), rope_i=i)
                yield
                bb2 = transposes(bB, bBk, [qfin[:, h, :] for h in range(8)], [kq("qfin")], rows=96)
                p.act(lambda e: e.activation(out=QT[0:96, :, bl * 128:(bl + 1) * 128], in_=bb2[0:96, :, :],
                                             func=AF.Copy), [bBk], ["QT"])
            run_pipelined([q_block(bl) for bl in range(4)], depth=2)
            wqm_s, wqm_k = wload(lambda s: [(s.rearrange("p (kc n) -> p kc n", n=512),
                                             win_d[:, C_QM:C_QM + 512].rearrange("(kc p) n -> p kc n", p=128))])
            wqm3 = wqm_s.rearrange("p (kc n) -> p kc n", n=512)

            def c_block(bl):
                s = bl % 2
                S = Csets[s]
                qm_sb, qmn = S["qm_sb"], S["qmn"]
                kq = lambda n: "%s%d" % (n, s)
                bA, bAk = ps[:, 3 * s + 0, :], "ps%d" % (3 * s + 0)
                bB, bBk = ps[:, 3 * s + 1, :], "ps%d" % (3 * s + 1)
                mm_group(bA, bAk, [(hT[:, kc, bl * 128:(bl + 1) * 128], wqm3[:, kc, :]) for kc in range(8)],
                         [wqm_k, "hT"])
                yield
                p.act(lambda e: e.activation(out=qm_sb.rearrange("p a b -> p (a b)"), in_=bA, func=AF.Copy),
                      [bAk], [kq("qm_sb")])
                yield
                head_norm(hn2, qm_sb, 4, 128, "mqg", qmn, kq("qm_sb"), kq("qmn"))
                yield
                bb = transposes(bB, bBk, [qmn[:, h, :] for h in range(4)], [kq("qmn")])
                p.act(lambda e: e.activation(out=qmT[:, :, bl * 128:(bl + 1) * 128], in_=bb[:, 0:4, :],
                                             func=AF.Copy), [bBk], ["qmT"])
            run_pipelined([c_block(bl) for bl in range(4)], depth=2)
            for h in range(4):
                Ps = []
                for mc in range(2):
                    bs, bsk = gen_banks.next()
                    mm_group(bs, bsk, [(KmemT[:, h, mc * 128:(mc + 1) * 128], qmT[:, h, :])], ["KmemT", "qmT"])
                    Pt, Pk = P_ring.next()
                    p.act(lambda e, bs=bs, Pt=Pt: e.activation(out=Pt, in_=bs, func=AF.Exp, scale=SC_C), [bsk], [Pk])
                    Ps.append((Pt, Pk))
                bo, bok = gen_banks.next()
                mm_group(bo, bok, [(Vmem[:, mc, h * 128:(h + 1) * 128], Ps[mc][0]) for mc in range(2)],
                         ["Vmem"] + [k for _, k in Ps])
                bd, bdk = gen_banks.next()
                mm_group(bd, bdk, [(ones, Ps[mc][0]) for mc in range(2)], ["cst"] + [k for _, k in Ps])
                rd, rdk = rden_ring.next()
                p.dve(lambda e, rd=rd, bd=bd: e.reciprocal(out=rd, in_=bd), [bdk], [rdk])
                p.dve(lambda e, rd=rd, bo=bo, h=h: e.tensor_tensor(out=yT[2][:, h, :], in0=bo, in1=rd, op=ALU.mult),
                      [bok, rdk], ["ycT"])
            nki = 4 * t + 4
            for h in range(8):
                par = h % 2
                ob, obk = o_banks.next()
                pend = []
                npv = 0
                ntot = 4 * nki

                def do_pv(item, first, last, ob=ob, obk=obk):
                    (Pt, Pk, c0, Vc, vk, ki) = item
                    p.pe(lambda e: e.matmul(ob[:, c0:512], Vc[:, ki, :], Pt[:, c0:512], start=first, stop=last),
                         [Pk, vk], [obk])
                for r in range(4):
                    slot = par * 2 + (r % 2)
                    Kc = Kring_v[slot]
                    Vc = Vring_v[slot]
                    kk, vk = "K%d" % slot, "V%d" % slot
                    vlo = 0 if par == 0 else 64
                    p.dma("pool", [(Kc[0:96, tt * 512:(tt + 1) * 512],
                                    kout_t[tt].ap()[r * 768 + h * 96:r * 768 + (h + 1) * 96, :]) for tt in range(t + 1)],
                          ["kout%d" % tt for tt in range(t + 1)], [kk], kk)
                    p.dma("pool", [(Vc[:, 4 * tt:4 * tt + 4, vlo:vlo + 64],
                                    vout_t[tt].ap()[r * 1024 + h * 128:r * 1024 + (h + 1) * 128, :].rearrange(
                                        "p (i d) -> p i d", d=64)) for tt in range(t + 1)],
                          ["vout%d" % tt for tt in range(t + 1)], [vk], vk)
                    for ki in range(nki):
                        d = ki - 4 * t
                        c0 = 128 * d if d > 0 else 0
                        bs, bsk = gen_banks.next()
                        p.pe(lambda e, bs=bs, Kc=Kc, ki=ki, c0=c0, h=h: e.matmul(
                            bs[:, c0:512], Kc[0:96, ki * 128:(ki + 1) * 128], QT[0:96, h, c0:512], start=True, stop=True),
                            [kk, "QT"], [bsk])
                        Pt, Pk = P_ring.next()
                        p.act(lambda e, bs=bs, Pt=Pt, c0=c0: e.activation(out=Pt[:, c0:512], in_=bs[:, c0:512],
                                                                         func=AF.Exp, scale=SC_B), [bsk], [Pk])
                        if d >= 0:
                            mk = masks[:, (ki % 2) * 4 + r, :]
                            p.dve(lambda e, Pt=Pt, c0=c0, mk=mk: e.tensor_tensor(
                                out=Pt[:, c0:c0 + 128], in0=Pt[:, c0:c0 + 128], in1=mk, op=ALU.mult), [Pk, "cst"], [Pk])
                        pend.append((Pt, Pk, c0, Vc, vk, ki))
                        if len(pend) > 2:
                            do_pv(pend.pop(0), npv == 0, npv == ntot - 1)
                            npv += 1
                while pend:
                    do_pv(pend.pop(0), npv == 0, npv == ntot - 1)
                    npv += 1
                olo = 0 if par == 0 else 64
                dlo = 64 - olo
                rd, rdk = rden_ring.next()
                p.dve(lambda e, rd=rd, ob=ob, dlo=dlo: e.reciprocal(out=rd[dlo:dlo + 64, :], in_=ob[dlo:dlo + 64, :]),
                      [obk], [rdk])
                p.dve(lambda e, rd=rd, ob=ob, olo=olo, dlo=dlo, h=h: e.tensor_tensor(
                    out=yT[1][olo:olo + 64, h // 2, :], in0=ob[olo:olo + 64, :], in1=rd[dlo:dlo + 64, :], op=ALU.mult),
                    [obk, rdk], ["ybT"])
            if stop == "dump_y%d" % t:
                for br in range(3):
                    p.dma("pool", [(oT_d[0:512, br * 512:(br + 1) * 512].rearrange("(kc p) t -> p kc t", p=128), yT[br])],
                          [("yaT", "ybT", "ycT")[br]], ["dbgy%d" % br], "dbg", final=True)
                return finish(store=False)
            for m in range(8):
                gs, gk = wload(lambda s, m=m: [
                    (s[:, 0:3072].rearrange("p (b kc n) -> p b kc n", b=3, n=128)[:, br],
                     win_d[:, C_GATE + br * 1024 + m * 128:C_GATE + br * 1024 + (m + 1) * 128].rearrange(
                         "(kc p) n -> p kc n", p=128)) for br in range(3)])
                bs_, bk_ = wload(lambda s, m=m: [
                    (s[:, 0:1536].rearrange("p (b kc n) -> p b kc n", b=3, n=128)[:, br],
                     wbr_d[br][:, m * 128:(m + 1) * 128].rearrange("(kc p) n -> p kc n", p=128)) for br in range(3)])
                g4 = gs[:, 0:3072].rearrange("p (b kc n) -> p b kc n", b=3, n=128)
                b4 = bs_[:, 0:1536].rearrange("p (b kc n) -> p b kc n", b=3, n=128)
                for br in range(3):
                    bg, bgk = gen_banks.next()
                    mm_group(bg, bgk, [(g4[:, br, kc, :], hT[:, kc, :]) for kc in range(8)], [gk, "hT"])
                    p.act(lambda e, bg=bg, br=br, m=m: e.activation(out=g_sb[:, br, :], in_=bg, func=AF.Sigmoid,
                                                                   bias=C("bgate")[:, br * 8 + m:br * 8 + m + 1], scale=1.0),
                          [bgk, "cst"], ["g_sb%d" % br])
                ykeys = ["yaT", "ybT", "ycT"]
                bbs = []
                for br in range(3):
                    bb_, bbk = gen_banks.next()
                    mm_group(bb_, bbk, [(b4[:, br, kc, :], yT[br][:, kc, :]) for kc in range(4)], [bk_, ykeys[br]])
                    bbs.append((bb_, bbk))
                p.dve(lambda e, b=bbs[0][0]: e.tensor_tensor(out=acc1, in0=b, in1=g_sb[:, 0, :], op=ALU.mult),
                      [bbs[0][1], "g_sb0"], ["rd0"])
                p.dve(lambda e, b=bbs[1][0]: e.tensor_tensor(out=acc2, in0=b, in1=g_sb[:, 1, :], op=ALU.mult),
                      [bbs[1][1], "g_sb1"], ["rd1"])
                p.dve(lambda e: e.tensor_tensor(out=acc1, in0=acc1, in1=acc2, op=ALU.add), ["rd0", "rd1"], ["rd0"])
                p.dve(lambda e, b=bbs[2][0]: e.tensor_tensor(out=acc2, in0=b, in1=g_sb[:, 2, :], op=ALU.mult),
                      [bbs[2][1], "g_sb2"], ["rd1"])
                p.dve(lambda e, m=m: e.tensor_tensor(out=mergedT[:, m, :], in0=acc1, in1=acc2, op=ALU.add),
                      ["rd0", "rd1"], ["mergedT"])
            for hf in range(2):
                ws, wk_ = wload(lambda s, hf=hf: [(s.rearrange("p (kc n) -> p kc n", n=512),
                                                   wout_d[:, hf * 512:(hf + 1) * 512].rearrange("(kc p) n -> p kc n", p=128))])
                w3 = ws.rearrange("p (kc n) -> p kc n", n=512)
                for mm in range(4):
                    m = hf * 4 + mm
                    bo, bok = gen_banks.next()
                    mm_group(bo, bok, [(w3[:, kc, mm * 128:(mm + 1) * 128], mergedT[:, kc, :]) for kc in range(8)],
                             [wk_, "mergedT"])
                    xs = xT[:, m, t * 512:(t + 1) * 512]
                    p.dve(lambda e, bo=bo, xs=xs: e.tensor_tensor(out=xs, in0=bo, in1=xs, op=ALU.add), [bok, xk], [xk])

        if stop != "mid":
            ffn(w2gu_d, w2dn_d, "ffn2_g")
        return finish()


def _perm_rows(j):
    rows = []
    for i in range(16):
        g = i // 2
        blk = 8 * g + (j if i % 2 == 0 else 7 - j)
        rows.append(np.arange(blk * 128, (blk + 1) * 128))
    return np.concatenate(rows)


def _host_inputs(inp):
    f = lambda a: np.ascontiguousarray(np.asarray(a, dtype=np.float32))
    x = f(inp["x"])
    mem = f(inp["mem"])
    pos = np.asarray(inp["positions"]).astype(np.int32)
    L0 = lambda k: f(inp[k])[0]

    def col(v, n):
        return np.ascontiguousarray(v.reshape(n, 128).T)

    def rep(v):
        return np.ascontiguousarray(np.broadcast_to(v[None, :], (128, v.shape[0])))

    cst = np.zeros((128, NCST), np.float32)

    def put(name, arr):
        o, w = CST[name]
        assert arr.shape == (128, w), (name, arr.shape)
        cst[:, o:o + w] = arr
    put("ffn1_g", col(L0("ffn1_norm"), 8))
    put("mix_g", col(L0("mix_norm"), 8))
    put("ffn2_g", col(L0("ffn2_norm"), 8))
    put("mem_g", col(L0("mem_norm"), 8))
    put("bgate", col(L0("b_gate"), 24))
    put("cqg", rep(L0("mla_cq_norm")))
    put("ckvg", rep(L0("mla_ckv_norm")))
    put("lng", rep(L0("sg_ln_g")))
    put("lnb", rep(L0("sg_ln_b")))
    put("qg", rep(L0("mla_q_norm")))
    put("kg", rep(L0("mla_k_norm")))
    put("mqg", rep(L0("mem_q_norm")))
    put("mkg", rep(L0("mem_k_norm")))
    sgb = L0("sg_b")
    bsT = np.zeros((128, 4, 128), np.float32)
    for c in range(4):
        bsT[0:64, c, :] = sgb[2 * c][None, :]
        bsT[64:128, c, :] = sgb[2 * c + 1][None, :]
    put("bsT", bsT.reshape(128, 512))
    half = 16
    invf = (10000.0 ** (-np.arange(half, dtype=np.float32) / half)).astype(np.float32)
    put("invf", rep(invf))
    tri = (np.arange(128)[:, None] <= np.arange(128)[None, :]).astype(np.float32)
    put("tri", tri)

    sgw = L0("sg_w")
    sgwT = np.ascontiguousarray(sgw.transpose(2, 0, 1)).reshape(128, 8 * 128)

    shared = {
        "cst": None, "sgwT": sgwT,
        "ffn1_w_gu": L0("ffn1_w_gu"), "ffn1_w_down": L0("ffn1_w_down"),
        "ffn2_w_gu": L0("ffn2_w_gu"), "ffn2_w_down": L0("ffn2_w_down"),
        "w_in": L0("w_in"), "mla_w_uq": L0("mla_w_uq"), "mla_w_ukv": L0("mla_w_ukv"),
        "mem_w_kv": L0("mem_w_kv"), "w_branch_a": L0("w_branch_a"), "w_branch_b": L0("w_branch_b"),
        "w_branch_c": L0("w_branch_c"), "w_out": L0("w_out"),
    }
    in_maps = []
    perms = []
    for c in range(NCORES):
        b, j = c // 4, c % 4
        rows = _perm_rows(j)
        perms.append((b, rows))
        m = dict(shared)
        m["cst"] = cst
        m["xT"] = np.ascontiguousarray(x[b][rows].T)
        m["memT"] = np.ascontiguousarray(mem[b].T)
        m["pos"] = np.ascontiguousarray(pos[b][rows].reshape(16, 128).T)
        cbf = np.zeros((128, NCBF), np.float32)
        cbf[:, 0:128] = np.eye(128, dtype=np.float32)
        cbf[:, 128:256] = 1.0
        mk = np.zeros((128, 8, 128), np.float32)
        for r in range(4):
            mk[:, r, :] = 1.0 if r < j else (tri if r == j else 0.0)
            mk[:, 4 + r, :] = 1.0 if r > j else (tri if r == j else 0.0)
        cbf[:, 256:] = mk.reshape(128, 1024)
        m["cbf"] = cbf.astype(ml_dtypes.bfloat16)
        in_maps.append(m)
    return in_maps, perms


_NC_CACHE = {}


def kernel(**inputs):
    in_maps, perms = _host_inputs(inputs)
    if "nc" not in _NC_CACHE:
        _NC_CACHE["nc"] = build()
    nc = _NC_CACHE["nc"]
    res = run_bass_kernel_spmd(nc, in_maps, core_ids=list(range(NCORES)))
    out = np.zeros((2, 8192, D), np.float32)
    for c in range(NCORES):
        b, rows = perms[c]
        out[b, rows, :] = np.asarray(res.results[c]["oT"], dtype=np.float32).T
    return out
```

```python
import numpy as np
import ml_dtypes
from contextlib import ExitStack
import concourse.bass as bass
import concourse.mybir as mybir
from concourse.bass_utils import run_bass_kernel_spmd

F32 = mybir.dt.float32
BF16 = mybir.dt.bfloat16
I32 = mybir.dt.int32
AF = mybir.ActivationFunctionType
ALU = mybir.AluOpType
AX = mybir.AxisListType

NCORES = 8
D = 1024
NT = 2048
DFF = 2816
FC = 22
EPS = 1e-6
C_U, C_V, C_CQ, C_CKV, C_KR, C_QM, C_GATE = 0, 512, 1024, 1408, 1664, 1696, 2208
PI = float(np.pi)

CST = {}
_off = 0
for _n, _w in [("ffn1_g", 8), ("mix_g", 8), ("ffn2_g", 8), ("mem_g", 8), ("bgate", 24), ("cqg", 384),
               ("ckvg", 256), ("lng", 512), ("lnb", 512), ("qg", 96), ("kg", 96), ("mqg", 128),
               ("mkg", 128), ("bsT", 512), ("invf", 16), ("tri", 128)]:
    CST[_n] = (_off, _w)
    _off += _w
NCST = _off
NCBF = 128 + 128 + 8 * 128


class Prog:
    ENG = ("pe", "act", "dve", "pool", "sp")

    def __init__(self, nc, es):
        self.nc, self.es = nc, es
        self.q = {e: [] for e in self.ENG}
        self.sems = {}
        self.cnt = {}
        self.waited = {e: {} for e in self.ENG}
        self.lastw = {}
        self.readers = {}
        self.out_tokens = []
        self.regions = {}
        self._ovc = {}

    def sem(self, name):
        if name not in self.sems:
            self.sems[name] = self.es.enter_context(self.nc.semaphore("s_" + name))
            self.cnt[name] = 0
        return self.sems[name]

    def region(self, key, ivs):
        self.regions[key] = list(ivs)

    def _overlap(self, k):
        if k not in self.regions:
            return (k,)
        c = self._ovc.get(k)
        if c is not None and c[0] == len(self.regions):
            return c[1]
        mine = self.regions[k]
        res = [k]
        for k2, ivs in self.regions.items():
            if k2 == k:
                continue
            hit = False
            for (a, b) in mine:
                for (c0, d0) in ivs:
                    if a < d0 and c0 < b:
                        hit = True
                        break
                if hit:
                    break
            if hit:
                res.append(k2)
        self._ovc[k] = (len(self.regions), tuple(res))
        return self._ovc[k][1]

    def _collect(self, eng, reads, writes, is_dma):
        toks = []
        for k in reads:
            for k2 in self._overlap(k):
                if k2 in self.lastw:
                    toks.append((self.lastw[k2], True))
        for k in writes:
            for k2 in self._overlap(k):
                if k2 in self.lastw:
                    toks.append((self.lastw[k2], False))
                rd = self.readers.get(k2)
                if rd:
                    toks.extend(((sn, v, te), False) for (sn, te), v in rd.items())
        waits = {}
        for ((sn, val, teng), raw) in toks:
            if teng is not None and teng == eng and not is_dma:
                if not raw or eng == "pe":
                    continue
            if self.waited[eng].get(sn, 0) >= val:
                continue
            waits[sn] = max(waits.get(sn, 0), val)
        for sn, val in waits.items():
            self.waited[eng][sn] = val
        return [(self.sem(sn), val, sn) for sn, val in waits.items()]

    def _commit(self, tok, reads, writes):
        sn, val, te = tok
        for k in reads:
            d = self.readers.setdefault(k, {})
            d[(sn, te)] = max(d.get((sn, te), 0), val)
        for k in writes:
            self.lastw[k] = tok
            self.readers[k] = {}

    def op(self, eng, fn, reads=(), writes=()):
        waits = self._collect(eng, reads, writes, False)
        sn = "E" + eng
        self.sem(sn)
        self.cnt[sn] += 1
        tok = (sn, self.cnt[sn], eng)
        self.q[eng].append((waits, fn, [(self.sems[sn], 1)], [(sn, 1)]))
        self._commit(tok, reads, writes)

    def pe(self, fn, r=(), w=()):
        self.op("pe", fn, r, w)

    def act(self, fn, r=(), w=()):
        self.op("act", fn, r, w)

    def dve(self, fn, r=(), w=()):
        self.op("dve", fn, r, w)

    def pool(self, fn, r=(), w=()):
        self.op("pool", fn, r, w)

    def dma(self, queue, pairs, reads, writes, semname, final=False):
        waits = self._collect(queue, reads, writes, True)
        s = self.sem(semname)
        self.cnt[semname] += 16 * len(pairs)
        tok = (semname, self.cnt[semname], None)

        def fn(e, pairs=pairs, s=s):
            for (o, i) in pairs:
                e.dma_start(out=o, in_=i).then_inc(s, 16)
            return None
        self.q[queue].append((waits, fn, [], [(semname, 16 * len(pairs))]))
        self._commit(tok, reads, writes)
        if final:
            self.out_tokens.append(tok)

    def collective(self, fn, reads, writes, semname="cc"):
        waits = self._collect("pool", reads, writes, True)
        s = self.sem(semname)
        self.cnt[semname] += 1
        tok = (semname, self.cnt[semname], None)
        self.q["pool"].append((waits, fn, [(s, 1)], [(semname, 1)]))
        self._commit(tok, reads, writes)

    def check(self):
        val = {}
        pos = {e: 0 for e in self.ENG}
        prog = True
        while prog:
            prog = False
            for e in self.ENG:
                while pos[e] < len(self.q[e]):
                    waits, fn, incs, names = self.q[e][pos[e]]
                    if all(val.get(sn, 0) >= v for (_s, v, sn) in waits):
                        for (sn, n) in names:
                            val[sn] = val.get(sn, 0) + n
                        pos[e] += 1
                        prog = True
                    else:
                        break
        stuck = {e: pos[e] for e in self.ENG if pos[e] < len(self.q[e])}
        for e, i in stuck.items():
            waits = self.q[e][i][0]
            print("DEADLOCK", e, "op", i, "of", len(self.q[e]), "waits",
                  [(sn, v, val.get(sn, 0)) for (_s, v, sn) in waits if val.get(sn, 0) < v])
        return not stuck

    def emit(self, block):
        assert self.check(), "semaphore protocol deadlock"
        def run(e, eng):
            for (waits, fn, incs, _n) in self.q[eng]:
                for (s, v, _sn) in waits:
                    e.wait_ge(s, v)
                ins = fn(e)
                for (s, n) in incs:
                    ins.then_inc(s, n)
            if eng == "sp":
                for (sn, val, _) in self.out_tokens:
                    e.wait_ge(self.sems[sn], val)

        @block.tensor
        def _(e):
            run(e, "pe")

        @block.scalar
        def _(e):
            run(e, "act")

        @block.vector
        def _(e):
            run(e, "dve")

        @block.gpsimd
        def _(e):
            run(e, "pool")

        @block.sync
        def _(e):
            run(e, "sp")


class Ring:
    def __init__(self, name, views):
        self.name, self.views, self.i = name, views, 0

    def next(self):
        k = self.i % len(self.views)
        self.i += 1
        return self.views[k], "%s%d" % (self.name, k)


def build(stop=None):
    nc = bass.Bass("TRN2", target_bir_lowering=False)

    def din(name, shape, dt=F32):
        return nc.dram_tensor(name, shape, dt, kind="ExternalInput").ap()

    xT_d = din("xT", [D, NT])
    memT_d = din("memT", [D, 256])
    pos_d = din("pos", [128, 16], I32)
    cst_d = din("cst", [128, NCST])
    cbf_d = din("cbf", [128, NCBF], BF16)
    sgwT_d = din("sgwT", [128, 8 * 128])
    w1gu_d = din("ffn1_w_gu", [D, 2 * DFF])
    w1dn_d = din("ffn1_w_down", [DFF, D])
    w2gu_d = din("ffn2_w_gu", [D, 2 * DFF])
    w2dn_d = din("ffn2_w_down", [DFF, D])
    win_d = din("w_in", [D, 5280])
    wuq_d = din("mla_w_uq", [384, 768])
    wukv_d = din("mla_w_ukv", [256, 1024])
    wmkv_d = din("mem_w_kv", [D, 1024])
    wbr_d = [din("w_branch_a", [512, D]), din("w_branch_b", [512, D]), din("w_branch_c", [512, D])]
    wout_d = din("w_out", [D, D])
    oT_d = nc.dram_tensor("oT", [D, NT], F32, kind="ExternalOutput").ap()
    kin_t = [nc.dram_tensor("kin%d" % c, [768, 512], BF16) for c in range(4)]
    kout_t = [nc.dram_tensor("kout%d" % c, [4 * 768, 512], BF16) for c in range(4)]
    vin_t = [nc.dram_tensor("vin%d" % c, [1024, 256], BF16) for c in range(4)]
    vout_t = [nc.dram_tensor("vout%d" % c, [4 * 1024, 256], BF16) for c in range(4)]

    es = ExitStack()
    with es:
        p = Prog(nc, es)

        def sb(name, shape, dt):
            return es.enter_context(nc.sbuf_tensor(name, shape, dt))

        xT = sb("xT_sb", [128, 8, NT], F32)
        cst = sb("cst_sb", [128, NCST], F32)
        cbf = sb("cbf_sb", [128, NCBF], BF16)
        csA = sb("csA", [128, 16, 32], F32)
        csB = sb("csB", [128, 16, 32], F32)
        wTsg = sb("wTsg", [128, 8, 128], BF16)
        KmemT = sb("KmemT", [128, 4, 256], BF16)
        Vmem = sb("Vmem", [128, 2, 512], BF16)
        wr_t = sb("wring", [128, 3, 4096], BF16)
        ARENA = 47440
        ar = sb("arena", [128, ARENA], BF16)
        ps = es.enter_context(nc.psum_tensor("ps", [128, 8, 512], F32))

        def C(name):
            o, w = CST[name]
            return cst[:, o:o + w]

        ident = cbf[:, 0:128]
        ones = cbf[:, 128:256]
        masks = cbf[:, 256:256 + 1024].rearrange("p (a b) -> p a b", b=128)

        class Carver:
            def __init__(self):
                self.off = 0
                self.hi = 0

            def take(self, shape, dt, key=None):
                n = int(np.prod(shape)) * (2 if dt in (F32, I32) else 1)
                n = (n + 1) // 2 * 2
                o = self.off
                assert o + n <= ARENA, (o, n, key)
                a = ar[:, o:o + n]
                self.off += n
                self.hi = max(self.hi, self.off)
                if key is not None:
                    p.region(key, [(o, o + n)])
                if dt in (F32, I32):
                    a = a.bitcast(dt)
                if len(shape) == 2:
                    a = a.rearrange("p (a b) -> p a b", b=shape[1])
                elif len(shape) == 3:
                    a = a.rearrange("p (a b c) -> p a b c", b=shape[1], c=shape[2])
                return a

        cv = Carver()
        sq_ring = Ring("sq", [cv.take([512], BF16) for _ in range(2)])
        std_t = cv.take([512], F32)
        rstd_t = cv.take([512], F32)
        junk = cv.take([512], BF16)
        small = cv.take([128], F32)
        base_off = cv.off

        def mk_hn(pfx):
            return dict(pfx=pfx, sq=cv.take([8, 96], F32, pfx + "hn_sq"), rt=cv.take([8, 32], F32, pfx + "hn_rt"),
                        t1=cv.take([8, 32], F32, pfx + "hn_t1"), t2=cv.take([8, 32], F32, pfx + "hn_t2"))

        gen_banks = Ring("ps", [ps[:, b, :] for b in range(6)])
        o_banks = Ring("po", [ps[:, 6, :], ps[:, 7, :]])
        wring = Ring("w", [wr_t[:, s, :] for s in range(3)])

        def wload(pairs_fn):
            slot, key = wring.next()
            p.dma("pool", pairs_fn(slot), reads=[], writes=[key], semname=key)
            return slot, key

        def bf_bank(bank):
            return bank.bitcast(BF16).rearrange("p (a b) -> p a b", b=128)

        p.dma("sp", [(cst[:, :], cst_d[:, :]), (cbf[:, :], cbf_d[:, :])], [], ["cst"], "cst")
        for t in range(4):
            p.dma("sp", [(xT[:, :, t * 512:(t + 1) * 512],
                          xT_d[:, t * 512:(t + 1) * 512].rearrange("(kc p) t -> p kc t", p=128))],
                  [], ["x%d" % t], "xin%d" % t)

        cv.off = base_off
        pos_i = cv.take([16], I32, "pos")
        posf = cv.take([16], F32, "posf")
        angA = cv.take([16, 32], F32, "angA")
        angT = cv.take([16, 32], F32, "angT")
        angI = cv.take([16, 32], I32, "angI")
        angF = cv.take([16, 32], F32, "angF")
        angM = cv.take([16, 32], F32, "angM")
        SC = cv.take([16, 32], F32, "SC")
        sgtmp = cv.take([8, 128], F32, "sgtmp")
        _save = cv.off
        cv.off = 36640
        memT = cv.take([8, 256], F32, "memT")
        memnT = cv.take([8, 256], BF16, "memnT")
        km_sb = cv.take([4, 128], F32, "km_sb")
        kmn = cv.take([4, 128], BF16, "kmn")
        hn0 = mk_hn("s_")
        cv.off = _save

        p.dma("sp", [(pos_i, pos_d[:, :])], [], ["pos"], "misc")
        p.dma("sp", [(sgtmp, sgwT_d[:, :].rearrange("p (g t) -> p g t", t=128))], [], ["sgtmp"], "misc2")
        p.dma("sp", [(memT, memT_d[:, :].rearrange("(kc p) m -> p kc m", p=128))], [], ["memT"], "misc3")

        p.dve(lambda e: e.tensor_copy(out=posf, in_=pos_i), ["pos"], ["posf"])
        for i in range(16):
            p.dve(lambda e, i=i: e.tensor_scalar(out=angA[:, i, 0:16], in0=C("invf"), scalar1=posf[:, i:i + 1],
                                                 scalar2=None, op0=ALU.mult), ["posf", "cst"], ["angA"])
        p.dve(lambda e: e.tensor_scalar(out=angA[:, :, 16:32], in0=angA[:, :, 0:16], scalar1=PI / 2, scalar2=None,
                                        op0=ALU.add), ["angA"], ["angA"])
        p.dve(lambda e: e.tensor_scalar(out=angT, in0=angA, scalar1=1.0 / (2 * PI), scalar2=None, op0=ALU.mult),
              ["angA"], ["angT"])
        p.dve(lambda e: e.tensor_copy(out=angI, in_=angT), ["angT"], ["angI"])
        p.dve(lambda e: e.tensor_copy(out=angF, in_=angI), ["angI"], ["angF"])
        p.dve(lambda e: e.scalar_tensor_tensor(out=angT, in0=angF, scalar=-2 * PI, in1=angA, op0=ALU.mult,
                                               op1=ALU.add), ["angF", "angA"], ["angT"])
        p.dve(lambda e: e.tensor_scalar(out=angM, in0=angT, scalar1=PI, scalar2=None, op0=ALU.is_gt),
              ["angT"], ["angM"])
        p.dve(lambda e: e.scalar_tensor_tensor(out=angF, in0=angM, scalar=-2 * PI, in1=angT, op0=ALU.mult,
                                               op1=ALU.add), ["angM", "angT"], ["angF"])
        p.dve(lambda e: e.tensor_scalar(out=angM, in0=angF, scalar1=-PI, scalar2=None, op0=ALU.is_lt),
              ["angF"], ["angM"])
        p.dve(lambda e: e.scalar_tensor_tensor(out=angT, in0=angM, scalar=2 * PI, in1=angF, op0=ALU.mult,
                                               op1=ALU.add), ["angM", "angF"], ["angT"])
        p.dve(lambda e: e.tensor_scalar(out=angF, in0=angT, scalar1=PI, scalar2=-PI, op0=ALU.min, op1=ALU.max),
              ["angT"], ["angF"])
        p.act(lambda e: e.activation(out=SC, in_=angF, func=AF.Sin), ["angF"], ["SC"])
        p.dve(lambda e: e.tensor_copy(out=csA[:, :, 0:16], in_=SC[:, :, 16:32]), ["SC"], ["csA"])
        p.dve(lambda e: e.tensor_copy(out=csA[:, :, 16:32], in_=SC[:, :, 16:32]), ["SC"], ["csA"])
        p.dve(lambda e: e.tensor_scalar(out=csB[:, :, 0:16], in0=SC[:, :, 0:16], scalar1=-1.0, scalar2=None,
                                        op0=ALU.mult), ["SC"], ["csB"])
        p.dve(lambda e: e.tensor_copy(out=csB[:, :, 16:32], in_=SC[:, :, 0:16]), ["SC"], ["csB"])
        p.dve(lambda e: e.tensor_tensor(out=wTsg[:, :, :], in0=sgtmp,
                                        in1=C("tri").unsqueeze(1).broadcast_to([128, 8, 128]), op=ALU.mult),
              ["sgtmp", "cst"], ["wTsg"])

        def norm_fm(src, srckeys, gname, dst, dstkey, N, bank=None):
            bank, bkey = bank if bank is not None else gen_banks.next()
            for kc in range(8):
                sq, sqk = sq_ring.next()
                s_ap = src(kc)
                p.act(lambda e, s_ap=s_ap, sq=sq: e.activation(out=sq[:, :N], in_=s_ap, func=AF.Square),
                      srckeys, [sqk])
                p.pe(lambda e, kc=kc, sq=sq: e.matmul(bank[:, :N], ones, sq[:, :N], start=(kc == 0), stop=(kc == 7)),
                     [sqk, "cst"], [bkey])
            p.act(lambda e: e.activation(out=std_t[:, :N], in_=bank[:, :N], func=AF.Sqrt, bias=EPS, scale=1.0 / D),
                  [bkey], ["std"])
            p.dve(lambda e: e.reciprocal(out=rstd_t[:, :N], in_=std_t[:, :N]), ["std"], ["rstd"])
            g = C(gname)
            for kc in range(8):
                s_ap, d_ap = src(kc), dst(kc)
                p.dve(lambda e, kc=kc, s_ap=s_ap, d_ap=d_ap: e.scalar_tensor_tensor(
                    out=d_ap, in0=s_ap, scalar=g[:, kc:kc + 1], in1=rstd_t[:, :N], op0=ALU.mult, op1=ALU.mult),
                    srckeys + ["rstd", "cst"], [dstkey])

        def head_norm(hn, src, H, Dh, gname, out_bf, key_in, key_out, rope_i=None):
            pf = hn["pfx"]
            rt, t1, t2 = hn["rt"], hn["t1"], hn["t2"]
            ksq, krt, kt1, kt2 = pf + "hn_sq", pf + "hn_rt", pf + "hn_t1", pf + "hn_t2"
            sqv = hn["sq"].rearrange("p a b -> p (a b)")[:, 0:H * Dh].rearrange("p (a b) -> p a b", b=Dh)
            ss = small[:, 0:H]
            sd = small[:, 8:8 + H]
            rs = small[:, 16:16 + H]
            g = C(gname)
            p.dve(lambda e: e.tensor_tensor(out=sqv, in0=src, in1=src, op=ALU.mult), [key_in], [ksq])
            p.dve(lambda e: e.tensor_reduce(out=ss, in_=sqv, axis=AX.X, op=ALU.add), [ksq], ["hn_ss"])
            p.act(lambda e: e.activation(out=sd, in_=ss, func=AF.Sqrt, bias=EPS, scale=1.0 / Dh), ["hn_ss"], ["hn_sd"])
            p.dve(lambda e: e.reciprocal(out=rs, in_=sd), ["hn_sd"], ["hn_rs"])
            p.dve(lambda e: e.tensor_tensor(out=sqv, in0=src, in1=rs.unsqueeze(2).broadcast_to([128, H, Dh]),
                                            op=ALU.mult), [key_in, "hn_rs"], [ksq])
            if rope_i is None:
                p.dve(lambda e: e.tensor_tensor(out=out_bf, in0=sqv, in1=g.unsqueeze(1).broadcast_to([128, H, Dh]),
                                                op=ALU.mult), [ksq, "cst"], [key_out])
                return
            i = rope_i
            p.dve(lambda e: e.tensor_tensor(out=out_bf[:, :, 0:64], in0=sqv[:, :, 0:64],
                                            in1=g[:, 0:64].unsqueeze(1).broadcast_to([128, H, 64]), op=ALU.mult),
                  [ksq, "cst"], [key_out])
            p.dve(lambda e: e.tensor_tensor(out=rt, in0=sqv[:, :, 64:96],
                                            in1=g[:, 64:96].unsqueeze(1).broadcast_to([128, H, 32]), op=ALU.mult),
                  [ksq, "cst"], [krt])
            p.dve(lambda e: e.tensor_tensor(out=t1, in0=rt,
                                            in1=csA[:, i, :].unsqueeze(1).broadcast_to([128, H, 32]), op=ALU.mult),
                  [krt, "csA"], [kt1])
            p.dve(lambda e: e.tensor_tensor(out=t2[:, :, 0:16], in0=rt[:, :, 16:32],
                                            in1=csB[:, i, 0:16].unsqueeze(1).broadcast_to([128, H, 16]), op=ALU.mult),
                  [krt, "csB"], [kt2])
            p.dve(lambda e: e.tensor_tensor(out=t2[:, :, 16:32], in0=rt[:, :, 0:16],
                                            in1=csB[:, i, 16:32].unsqueeze(1).broadcast_to([128, H, 16]), op=ALU.mult),
                  [krt, "csB"], [kt2])
            p.dve(lambda e: e.tensor_tensor(out=out_bf[:, :, 64:96], in0=t1, in1=t2, op=ALU.add),
                  [kt1, kt2], [key_out])

        def mm_group(bank_ap, bkey, items, reads):
            def fn(e):
                n = len(items)
                ins = None
                for k, (l, r) in enumerate(items):
                    ins = e.matmul(bank_ap, l, r, start=(k == 0), stop=(k == n - 1))
                return ins
            p.pe(fn, reads, [bkey])

        def transposes(bank, bkey, srcs, reads, rows=128):
            bb = bf_bank(bank)

            def fn(e):
                ins = None
                for j, s in enumerate(srcs):
                    ins = e.transpose(out=bb[0:rows, j, :], in_=s, identity=ident)
                return ins
            p.pe(fn, reads + ["cst"], [bkey])
            return bb

        def ffn(wgu_d, wdn_d, gname):
            cv.off = base_off
            ho = cv.off
            hT = cv.take([8, 1024], BF16)
            for sub in range(2):
                p.region("h%d" % sub, [(ho + kc * 1024 + sub * 512, ho + kc * 1024 + sub * 512 + 512) for kc in range(8)])
            ao = cv.off
            actT = cv.take([FC, 1024], BF16)
            for f in range(FC):
                for sub in range(2):
                    p.region("a%d_%d" % (f, sub), [(ao + f * 1024 + sub * 512, ao + f * 1024 + sub * 512 + 512)])
            sg_ring = Ring("sg", [cv.take([512], F32, "sg%d" % k) for k in range(2)])
            assert cv.off <= 36640, cv.off
            for half in range(2):
                for sub in range(2):
                    t = half * 2 + sub
                    norm_fm(lambda kc, t=t: xT[:, kc, t * 512:(t + 1) * 512], ["x%d" % t], gname,
                            lambda kc, sub=sub: hT[:, kc, sub * 512:(sub + 1) * 512], "h%d" % sub, 512)
                for pp in range(11):
                    slot, wkey = wload(lambda s, pp=pp: [
                        (s.rearrange("p (a kc n) -> p a kc n", a=2, n=256)[:, 0],
                         wgu_d[:, pp * 256:(pp + 1) * 256].rearrange("(kc p) n -> p kc n", p=128)),
                        (s.rearrange("p (a kc n) -> p a kc n", a=2, n=256)[:, 1],
                         wgu_d[:, DFF + pp * 256:DFF + (pp + 1) * 256].rearrange("(kc p) n -> p kc n", p=128))])
                    w4 = slot.rearrange("p (a kc n) -> p a kc n", a=2, n=256)
                    for fi in range(2):
                        f = pp * 2 + fi
                        for sub in range(2):
                            bg, bgk = gen_banks.next()
                            bu, buk = gen_banks.next()
                            hs = lambda kc, sub=sub: hT[:, kc, sub * 512:(sub + 1) * 512]
                            mm_group(bg, bgk, [(w4[:, 0, kc, fi * 128:(fi + 1) * 128], hs(kc)) for kc in range(8)],
                                     [wkey, "h%d" % sub])
                            mm_group(bu, buk, [(w4[:, 1, kc, fi * 128:(fi + 1) * 128], hs(kc)) for kc in range(8)],
                                     [wkey, "h%d" % sub])
                            sg, sgk = sg_ring.next()
                            p.act(lambda e, sg=sg, bg=bg: e.activation(out=sg, in_=bg, func=AF.Silu), [bgk], [sgk])
                            a_ap = actT[:, f, sub * 512:(sub + 1) * 512]
                            p.dve(lambda e, sg=sg, bu=bu, a_ap=a_ap: e.tensor_tensor(out=a_ap, in0=bu, in1=sg, op=ALU.mult),
                                  [sgk, buk], ["a%d_%d" % (f, sub)])
                for m in range(8):
                    slot, wkey = wload(lambda s, m=m: [
                        (s[:, 0:FC * 128].rearrange("p (f n) -> p f n", n=128),
                         wdn_d[:, m * 128:(m + 1) * 128].rearrange("(f p) n -> p f n", p=128))])
                    w3 = slot[:, 0:FC * 128].rearrange("p (f n) -> p f n", n=128)
                    for sub in range(2):
                        t = half * 2 + sub
                        bd, bdk = gen_banks.next()
                        mm_group(bd, bdk, [(w3[:, f, :], actT[:, f, sub * 512:(sub + 1) * 512]) for f in range(FC)],
                                 [wkey] + ["a%d_%d" % (f, sub) for f in range(FC)])
                        xs = xT[:, m, t * 512:(t + 1) * 512]
                        p.dve(lambda e, bd=bd, xs=xs: e.scalar_tensor_tensor(out=xs, in0=bd, scalar=0.5, in1=xs,
                                                                            op0=ALU.mult, op1=ALU.add),
                              [bdk, "x%d" % t], ["x%d" % t])

        def store_out():
            for t in range(4):
                p.dma("sp", [(oT_d[:, t * 512:(t + 1) * 512].rearrange("(kc p) t -> p kc t", p=128),
                              xT[:, :, t * 512:(t + 1) * 512])], ["x%d" % t], ["out%d" % t], "out", final=True)

        def finish(store=True):
            if store:
                store_out()
            with nc.Block() as block:
                p.emit(block)
            return nc

        if stop != "noffn1":
            ffn(w1gu_d, w1dn_d, "ffn1_g")
        norm_fm(lambda kc: memT[:, kc, :], ["memT"], "mem_g", lambda kc: memnT[:, kc, :], "memnT", 256)
        wk, wkk = wload(lambda s: [(s.rearrange("p (kc n) -> p kc n", n=512),
                                    wmkv_d[:, 0:512].rearrange("(kc p) n -> p kc n", p=128))])
        wv_, wvk = wload(lambda s: [(s.rearrange("p (kc n) -> p kc n", n=512),
                                     wmkv_d[:, 512:1024].rearrange("(kc p) n -> p kc n", p=128))])
        wk3 = wk.rearrange("p (kc n) -> p kc n", n=512)
        wv3 = wv_.rearrange("p (kc n) -> p kc n", n=512)
        for mb in range(2):
            bk, bkk = gen_banks.next()
            mm_group(bk, bkk, [(memnT[:, kc, mb * 128:(mb + 1) * 128], wk3[:, kc, :]) for kc in range(8)],
                     ["memnT", wkk])
            bv, bvk = gen_banks.next()
            mm_group(bv, bvk, [(memnT[:, kc, mb * 128:(mb + 1) * 128], wv3[:, kc, :]) for kc in range(8)],
                     ["memnT", wvk])
            p.act(lambda e, mb=mb, bv=bv: e.activation(out=Vmem[:, mb, :], in_=bv, func=AF.Copy), [bvk], ["Vmem"])
            p.act(lambda e, bk=bk: e.activation(out=km_sb.rearrange("p a b -> p (a b)"), in_=bk, func=AF.Copy),
                  [bkk], ["km_sb"])
            head_norm(hn0, km_sb, 4, 128, "mkg", kmn, "km_sb", "kmn")
            bt, btk = gen_banks.next()
            bb = transposes(bt, btk, [kmn[:, h, :] for h in range(4)], ["kmn"])
            p.act(lambda e, mb=mb, bb=bb: e.activation(out=KmemT[:, :, mb * 128:(mb + 1) * 128], in_=bb[:, 0:4, :],
                                                       func=AF.Copy), [btk], ["KmemT"])

        if stop == "ffn1":
            return finish()

        def run_pipelined(gens, depth=2):
            pending = list(gens)
            active = []
            while pending or active:
                if pending and len(active) < depth:
                    active.append(pending.pop(0))
                for g in list(active):
                    try:
                        next(g)
                    except StopIteration:
                        active.remove(g)

        cv.off = base_off
        hT2 = [cv.take([8, 512], BF16, "hT%d" % k) for k in range(2)]
        Kst = cv.take([8, NT], BF16, "Kst")
        Vst = cv.take([8 * 16 * 64], BF16, "Vst").rearrange("p (h i d) -> p h i d", h=8, i=16)
        m1sets = []
        for s in range(2):
            m1sets.append(dict(ckvn=cv.take([256], BF16, "ckvn%d" % s), ckvnT=cv.take([2, 128], BF16, "ckvnT%d" % s),
                               kc_sb=cv.take([8, 96], F32, "kc_sb%d" % s), kfin=cv.take([8, 96], BF16, "kfin%d" % s)))
        hn1 = mk_hn("m_")

        wkvin, wkvin_k = wload(lambda s: [(s[:, 0:8 * 288].rearrange("p (kc n) -> p kc n", n=288),
                                           win_d[:, C_CKV:C_CKV + 288].rearrange("(kc p) n -> p kc n", p=128))])
        wkvin3 = wkvin[:, 0:8 * 288].rearrange("p (kc n) -> p kc n", n=288)
        wukv, wukv_k = wload(lambda s: [(s[:, 0:2048].rearrange("p (kc n) -> p kc n", n=1024),
                                         wukv_d[:, :].rearrange("(kc p) n -> p kc n", p=128))])
        wukv3 = wukv[:, 0:2048].rearrange("p (kc n) -> p kc n", n=1024)
        norm_banks = o_banks

        def m1_block(t, bl):
            i = t * 4 + bl
            s = i % 2
            S = m1sets[s]
            ckvn, ckvnT, kc_sb, kfin = S["ckvn"], S["ckvnT"], S["kc_sb"], S["kfin"]
            kq = lambda n: "%s%d" % (n, s)
            hT = hT2[t % 2]
            hk = "hT%d" % (t % 2)
            ss1 = small[:, 32 + 4 * s:33 + 4 * s]
            sd1 = small[:, 33 + 4 * s:34 + 4 * s]
            rs1 = small[:, 34 + 4 * s:35 + 4 * s]
            bA, bAk = ps[:, 3 * s + 0, :], "ps%d" % (3 * s + 0)
            bB, bBk = ps[:, 3 * s + 1, :], "ps%d" % (3 * s + 1)
            bC, bCk = ps[:, 3 * s + 2, :], "ps%d" % (3 * s + 2)
            if bl == 0:
                nb, nbk = norm_banks.next()
                norm_fm(lambda kc, t=t: xT[:, kc, t * 512:(t + 1) * 512], ["x%d" % t], "mix_g",
                        lambda kc: hT[:, kc, :], hk, 512, bank=(nb, nbk))
            mm_group(bA[:, 0:288], bAk, [(hT[:, kc, bl * 128:(bl + 1) * 128], wkvin3[:, kc, :]) for kc in range(8)],
                     [hk, wkvin_k])
            yield
            p.act(lambda e: e.activation(out=junk[:, 0:256], in_=bA[:, 0:256], func=AF.Square, accum_out=ss1),
                  [bAk], [kq("ss1"), "junk"])
            p.act(lambda e: e.activation(out=sd1, in_=ss1, func=AF.Sqrt, bias=EPS, scale=1.0 / 256), [kq("ss1")], [kq("sd1")])
            p.act(lambda e: e.activation(out=kc_sb[:, :, 64:96], in_=bA[:, 256:288].unsqueeze(1).broadcast_to([128, 8, 32]),
                                         func=AF.Copy), [bAk], [kq("kc_sb")])
            p.dve(lambda e: e.reciprocal(out=rs1, in_=sd1), [kq("sd1")], [kq("rs1")])
            p.dve(lambda e: e.scalar_tensor_tensor(out=ckvn, in0=bA[:, 0:256], scalar=rs1, in1=C("ckvg"),
                                                   op0=ALU.mult, op1=ALU.mult), [bAk, kq("rs1"), "cst"], [kq("ckvn")])
            yield
            bb = transposes(bB, bBk, [ckvn[:, k * 128:(k + 1) * 128] for k in range(2)], [kq("ckvn")])
            p.act(lambda e: e.activation(out=ckvnT, in_=bb[:, 0:2, :], func=AF.Copy), [bBk], [kq("ckvnT")])
            yield
            mm_group(bC, bCk, [(ckvnT[:, k, :], wukv3[:, k, 0:512]) for k in range(2)], [kq("ckvnT"), wukv_k])
            mm_group(bA, bAk, [(ckvnT[:, k, :], wukv3[:, k, 512:1024]) for k in range(2)], [kq("ckvnT"), wukv_k])
            for hb, (bx, bxk) in enumerate(((bC, bCk), (bA, bAk))):
                b3 = bx.rearrange("p (h d) -> p h d", d=128)
                p.act(lambda e, b3=b3, hb=hb: e.activation(out=Vst[:, hb * 4:(hb + 1) * 4, i, :],
                                                           in_=b3[:, :, 64:128], func=AF.Copy), [bxk], ["Vst"])
                p.act(lambda e, b3=b3, hb=hb: e.activation(out=kc_sb[:, hb * 4:(hb + 1) * 4, 0:64],
                                                           in_=b3[:, :, 0:64], func=AF.Copy), [bxk], [kq("kc_sb")])
            yield
            head_norm(hn1, kc_sb, 8, 96, "kg", kfin, kq("kc_sb"), kq("kfin"), rope_i=i)
            yield
            bb2 = transposes(bB, bBk, [kfin[:, h, :] for h in range(8)], [kq("kfin")], rows=96)
            p.act(lambda e: e.activation(out=Kst[0:96, :, i * 128:(i + 1) * 128], in_=bb2[0:96, :, :],
                                         func=AF.Copy), [bBk], ["Kst"])
            if bl == 3 and stop not in ("m1nocc", "dump_m1"):
                yield
                p.dma("sp", [(kin_t[t].ap().rearrange("(h f) c -> f h c", f=96), Kst[0:96, :, t * 512:(t + 1) * 512])],
                      ["Kst"], ["kin%d" % t], "kin%d" % t)
                p.dma("sp", [(vin_t[t].ap().rearrange("(h p) (i d) -> p h i d", p=128, d=64), Vst[:, :, 4 * t:4 * t + 4, :])],
                      ["Vst"], ["vin%d" % t], "vin%d" % t)
                p.collective(lambda e: e.collective_compute(
                    "AllGather", ALU.bypass, replica_groups=[[0, 1, 2, 3], [4, 5, 6, 7]],
                    ins=[kin_t[t].ap().opt()], outs=[kout_t[t].ap().opt()]),
                    ["kin%d" % t], ["kout%d" % t], semname="cck%d" % t)
                p.collective(lambda e: e.collective_compute(
                    "AllGather", ALU.bypass, replica_groups=[[0, 1, 2, 3], [4, 5, 6, 7]],
                    ins=[vin_t[t].ap().opt()], outs=[vout_t[t].ap().opt()]),
                    ["vin%d" % t], ["vout%d" % t], semname="ccv%d" % t)

        run_pipelined([m1_block(t, bl) for t in range(4) for bl in range(4)], depth=2)
        if stop == "dump_m1":
            for h in range(8):
                p.dma("pool", [(oT_d[h * 96:(h + 1) * 96, :], Kst[0:96, h, :])], ["Kst"], ["dbg%d" % h], "dbg", final=True)
            for hq in range(2):
                p.dma("pool", [(oT_d[768 + hq * 128:768 + (hq + 1) * 128, :].rearrange("p (h i d) -> p h i d", h=2, i=16),
                                Vst[:, 2 * hq:2 * hq + 2, :, :])], ["Vst"], ["dbgv%d" % hq], "dbg", final=True)
            return finish(store=False)
        if stop == "m1":
            return finish()
        if stop == "m1nocc":
            return finish()
        cv.off = base_off
        hT = cv.take([8, 512], BF16, "hT")
        QT = cv.take([8, 512], BF16, "QT")
        qmT = cv.take([4, 512], BF16, "qmT")
        yT = [cv.take([4, 512], BF16, k) for k in ("yaT", "ybT", "ycT")]
        Kring_v = [cv.take([2048], BF16, "K%d" % s) for s in range(4)]
        Vring_v = [cv.take([16, 128], BF16, "V%d" % s) for s in range(4)]
        ph_off = cv.off
        uT_sb = cv.take([4, 512], BF16, "uT")
        Asets = [dict(v_sb=cv.take([512], F32, "v_sb%d" % s), v_ln=cv.take([512], BF16, "v_ln%d" % s),
                      mtmp=cv.take([4, 128], F32, "mtmp%d" % s)) for s in range(2)]
        cv.off = ph_off
        Qsets = [dict(cqn=cv.take([384], BF16, "cqn%d" % s), cqnT=cv.take([3, 128], BF16, "cqnT%d" % s),
                      q_sb=cv.take([8, 96], F32, "q_sb%d" % s), qfin=cv.take([8, 96], BF16, "qfin%d" % s))
                 for s in range(2)]
        hn2 = mk_hn("t_")
        q_end = cv.off
        cv.off = ph_off
        Csets = [dict(qm_sb=cv.take([4, 128], F32, "qm_sb%d" % s), qmn=cv.take([4, 128], BF16, "qmn%d" % s))
                 for s in range(2)]
        assert cv.off <= q_end - 3 * 1024 - 1536 * 2 or True
        cv.off = ph_off
        P_ring = Ring("P", [cv.take([512], BF16, "P%d" % k) for k in range(4)])
        rd_views = [cv.take([512], F32, "rd%d" % k) for k in range(2)]
        rden_ring = Ring("rd", rd_views)
        acc1, acc2 = rd_views
        mergedT = cv.take([8, 512], BF16, "mergedT")
        go = cv.off
        g_sb = cv.take([3, 512], BF16)
        for br in range(3):
            p.region("g_sb%d" % br, [(go + br * 512, go + br * 512 + 512)])

        for s in range(4):
            lo = 64 if s < 2 else 0
            p.pool(lambda e, s=s, lo=lo: e.memset(Vring_v[s][:, :, lo:lo + 64], 1.0), [], ["V%d" % s])

        SC_B = 96 ** -0.5
        SC_C = 128 ** -0.5
        st6 = small[:, 40:46]
        mv = small[:, 46:48]
        sdv = small[:, 48:49]
        rsv = small[:, 49:50]

        for t in range(4):
            xk = "x%d" % t
            norm_fm(lambda kc, t=t: xT[:, kc, t * 512:(t + 1) * 512], [xk], "mix_g",
                    lambda kc: hT[:, kc, :], "hT", 512)
            wv_s, wv_k = wload(lambda s: [(s.rearrange("p (kc n) -> p kc n", n=512),
                                           win_d[:, C_V:C_V + 512].rearrange("(kc p) n -> p kc n", p=128))])
            wu_s, wu_k = wload(lambda s: [(s.rearrange("p (kc n) -> p kc n", n=512),
                                           win_d[:, C_U:C_U + 512].rearrange("(kc p) n -> p kc n", p=128))])
            wv3 = wv_s.rearrange("p (kc n) -> p kc n", n=512)
            wu3 = wu_s.rearrange("p (kc n) -> p kc n", n=512)
            for c in range(4):
                bu, buk = o_banks.next()
                mm_group(bu, buk, [(wu3[:, kc, c * 128:(c + 1) * 128], hT[:, kc, :]) for kc in range(8)], [wu_k, "hT"])
                p.act(lambda e, c=c, bu=bu: e.activation(out=uT_sb[:, c, :], in_=bu, func=AF.Gelu), [buk], ["uT"])

            def a_block(bl):
                s = bl % 2
                S = Asets[s]
                v_sb, v_ln, mtmp = S["v_sb"], S["v_ln"], S["mtmp"]
                kq = lambda n: "%s%d" % (n, s)
                st6 = small[:, 64 + 16 * s:70 + 16 * s]
                mv = small[:, 70 + 16 * s:72 + 16 * s]
                sdv = small[:, 72 + 16 * s:73 + 16 * s]
                rsv = small[:, 73 + 16 * s:74 + 16 * s]
                bV, bVk = ps[:, 3 * s + 0, :], "ps%d" % (3 * s + 0)
                bM, bMk = ps[:, 3 * s + 1, :], "ps%d" % (3 * s + 1)
                mm_group(bV, bVk, [(hT[:, kc, bl * 128:(bl + 1) * 128], wv3[:, kc, :]) for kc in range(8)], [wv_k, "hT"])
                yield
                p.act(lambda e: e.activation(out=v_sb, in_=bV, func=AF.Gelu), [bVk], [kq("v_sb")])
                p.dve(lambda e: e.bn_stats(out=st6, in_=v_sb), [kq("v_sb")], [kq("st6")])
                p.dve(lambda e: e.bn_aggr(out=mv, in_=st6), [kq("st6")], [kq("mv")])
                p.act(lambda e: e.activation(out=sdv, in_=mv[:, 1:2], func=AF.Sqrt, bias=EPS, scale=1.0), [kq("mv")], [kq("sdv")])
                yield
                p.dve(lambda e: e.reciprocal(out=rsv, in_=sdv), [kq("sdv")], [kq("rsv")])
                p.dve(lambda e: e.scalar_tensor_tensor(out=v_sb, in0=v_sb, scalar=mv[:, 0:1], in1=C("lng"),
                                                       op0=ALU.subtract, op1=ALU.mult), [kq("v_sb"), kq("mv"), "cst"], [kq("v_sb")])
                p.dve(lambda e: e.scalar_tensor_tensor(out=v_ln, in0=v_sb, scalar=rsv, in1=C("lnb"),
                                                       op0=ALU.mult, op1=ALU.add), [kq("v_sb"), kq("rsv"), "cst"], [kq("v_ln")])
                yield

                def mixfn(e):
                    ins = None
                    for g in range(8):
                        ins = e.matmul(bM[(g % 2) * 64:(g % 2) * 64 + 64, (g // 2) * 128:(g // 2 + 1) * 128],
                                       v_ln[:, g * 64:(g + 1) * 64], wTsg[:, g, :], start=True, stop=True)
                    return ins
                p.pe(mixfn, [kq("v_ln"), "wTsg"], [bMk])
                yield
                bm3 = bM.rearrange("p (c t) -> p c t", t=128)
                p.dve(lambda e: e.tensor_tensor(out=mtmp, in0=bm3, in1=C("bsT").rearrange("p (c t) -> p c t", t=128),
                                                op=ALU.add), [bMk, "cst"], [kq("mtmp")])
                p.dve(lambda e: e.tensor_tensor(out=yT[0][:, :, bl * 128:(bl + 1) * 128], in0=mtmp,
                                                in1=uT_sb[:, :, bl * 128:(bl + 1) * 128], op=ALU.mult),
                      [kq("mtmp"), "uT"], ["yaT"])
            run_pipelined([a_block(bl) for bl in range(4)], depth=2)
            wcq_s, wcq_k = wload(lambda s: [(s[:, 0:8 * 384].rearrange("p (kc n) -> p kc n", n=384),
                                             win_d[:, C_CQ:C_CQ + 384].rearrange("(kc p) n -> p kc n", p=128))])
            wuq_s, wuq_k = wload(lambda s: [(s[:, 0:3 * 768].rearrange("p (kc n) -> p kc n", n=768),
                                             wuq_d[:, :].rearrange("(kc p) n -> p kc n", p=128))])
            wcq3 = wcq_s[:, 0:8 * 384].rearrange("p (kc n) -> p kc n", n=384)
            wuq3 = wuq_s[:, 0:3 * 768].rearrange("p (kc n) -> p kc n", n=768)

            def q_block(bl):
                i = t * 4 + bl
                s = bl % 2
                S = Qsets[s]
                cqn, cqnT, q_sb, qfin = S["cqn"], S["cqnT"], S["q_sb"], S["qfin"]
                kq = lambda n: "%s%d" % (n, s)
                ss1 = small[:, 32 + 4 * s:33 + 4 * s]
                sd1 = small[:, 33 + 4 * s:34 + 4 * s]
                rs1 = small[:, 34 + 4 * s:35 + 4 * s]
                bA, bAk = ps[:, 3 * s + 0, :], "ps%d" % (3 * s + 0)
                bB, bBk = ps[:, 3 * s + 1, :], "ps%d" % (3 * s + 1)
                bC, bCk = ps[:, 3 * s + 2, :], "ps%d" % (3 * s + 2)
                mm_group(bA[:, 0:384], bAk, [(hT[:, kc, bl * 128:(bl + 1) * 128], wcq3[:, kc, :]) for kc in range(8)],
                         [wcq_k, "hT"])
                yield
                p.act(lambda e: e.activation(out=junk[:, 0:384], in_=bA[:, 0:384], func=AF.Square, accum_out=ss1),
                      [bAk], [kq("ss1"), "junk"])
                p.act(lambda e: e.activation(out=sd1, in_=ss1, func=AF.Sqrt, bias=EPS, scale=1.0 / 384), [kq("ss1")], [kq("sd1")])
                p.dve(lambda e: e.reciprocal(out=rs1, in_=sd1), [kq("sd1")], [kq("rs1")])
                p.dve(lambda e: e.scalar_tensor_tensor(out=cqn, in0=bA[:, 0:384], scalar=rs1, in1=C("cqg"),
                                                       op0=ALU.mult, op1=ALU.mult), [bAk, kq("rs1"), "cst"], [kq("cqn")])
                yield
                bb = transposes(bB, bBk, [cqn[:, k * 128:(k + 1) * 128] for k in range(3)], [kq("cqn")])
                p.act(lambda e: e.activation(out=cqnT, in_=bb[:, 0:3, :], func=AF.Copy), [bBk], [kq("cqnT")])
                yield
                for hb, (bx, bxk) in enumerate(((bC, bCk), (bA, bAk))):
                    mm_group(bx[:, 0:384], bxk, [(cqnT[:, k, :], wuq3[:, k, hb * 384:(hb + 1) * 384]) for k in range(3)],
                             [kq("cqnT"), wuq_k])
                    p.act(lambda e, bx=bx, hb=hb: e.activation(
                        out=q_sb[:, hb * 4:(hb + 1) * 4, :], in_=bx[:, 0:384].rearrange("p (h d) -> p h d", d=96),
                        func=AF.Copy), [bxk], [kq("q_sb")])
                yield
                head_norm(hn2, q_sb, 8, 96, "qg", qfin, kq("q_sb"), kq("qfin"), rope_i=i)
                yield
                bb2 = transposes(bB, bBk, [qfin[:, h, :] for h in range(8)], [kq("qfin")], rows=96)
                p.act(lambda e: e.activation(out=QT[0:96, :, bl * 128:(bl + 1) * 128], in_=bb2[0:96, :, :],
                                             func=AF.Copy), [bBk], ["QT"])
            run_pipelined([q_block(bl) for bl in range(4)], depth=2)
            wqm_s, wqm_k = wload(lambda s: [(s.rearrange("p (kc n) -> p kc n", n=512),
                                             win_d[:, C_QM:C_QM + 512].rearrange("(kc p) n -> p kc n", p=128))])
            wqm3 = wqm_s.rearrange("p (kc n) -> p kc n", n=512)

            def c_block(bl):
                s = bl % 2
                S = Csets[s]
                qm_sb, qmn = S["qm_sb"], S["qmn"]
                kq = lambda n: "%s%d" % (n, s)
                bA, bAk = ps[:, 3 * s + 0, :], "ps%d" % (3 * s + 0)
                bB, bBk = ps[:, 3 * s + 1, :], "ps%d" % (3 * s + 1)
                mm_group(bA, bAk, [(hT[:, kc, bl * 128:(bl + 1) * 128], wqm3[:, kc, :]) for kc in range(8)],
                         [wqm_k, "hT"])
                yield
                p.act(lambda e: e.activation(out=qm_sb.rearrange("p a b -> p (a b)"), in_=bA, func=AF.Copy),
                      [bAk], [kq("qm_sb")])
                yield
                head_norm(hn2, qm_sb, 4, 128, "mqg", qmn, kq("qm_sb"), kq("qmn"))
                yield
                bb = transposes(bB, bBk, [qmn[:, h, :] for h in range(4)], [kq("qmn")])
                p.act(lambda e: e.activation(out=qmT[:, :, bl * 128:(bl + 1) * 128], in_=bb[:, 0:4, :],
                                             func=AF.Copy), [bBk], ["qmT"])
            run_pipelined([c_block(bl) for bl in range(4)], depth=2)
            for h in range(4):
                Ps = []
                for mc in range(2):
                    bs, bsk = gen_banks.next()
                    mm_group(bs, bsk, [(KmemT[:, h, mc * 128:(mc + 1) * 128], qmT[:, h, :])], ["KmemT", "qmT"])
                    Pt, Pk = P_ring.next()
                    p.act(lambda e, bs=bs, Pt=Pt: e.activation(out=Pt, in_=bs, func=AF.Exp, scale=SC_C), [bsk], [Pk])
                    Ps.append((Pt, Pk))
                bo, bok = gen_banks.next()
                mm_group(bo, bok, [(Vmem[:, mc, h * 128:(h + 1) * 128], Ps[mc][0]) for mc in range(2)],
                         ["Vmem"] + [k for _, k in Ps])
                bd, bdk = gen_banks.next()
                mm_group(bd, bdk, [(ones, Ps[mc][0]) for mc in range(2)], ["cst"] + [k for _, k in Ps])
                rd, rdk = rden_ring.next()
                p.dve(lambda e, rd=rd, bd=bd: e.reciprocal(out=rd, in_=bd), [bdk], [rdk])
                p.dve(lambda e, rd=rd, bo=bo, h=h: e.tensor_tensor(out=yT[2][:, h, :], in0=bo, in1=rd, op=ALU.mult),
                      [bok, rdk], ["ycT"])
            nki = 4 * t + 4
            G = {0: 4, 1: 2}.get(t, 1)
            pend = []
            chunk_ctr = [0, 0]

            def rec_pv(item):
                (Pt, Pk, c0, Vc, vk, vi, ob, obk, first, last, fin) = item
                p.pe(lambda e: e.matmul(ob[:, c0:512], Vc[:, vi, :], Pt[:, c0:512], start=first, stop=last),
                     [Pk, vk], [obk])
                if last:
                    fin()
            for h in range(8):
                par = h % 2
                ob, obk = o_banks.next()
                olo = 0 if par == 0 else 64
                dlo = 64 - olo
                vlo = olo

                def fin(ob=ob, obk=obk, olo=olo, dlo=dlo, h=h):
                    rd, rdk = rden_ring.next()
                    p.dve(lambda e: e.reciprocal(out=rd[dlo:dlo + 64, :], in_=ob[dlo:dlo + 64, :]), [obk], [rdk])
                    p.dve(lambda e: e.tensor_tensor(out=yT[1][olo:olo + 64, h // 2, :], in0=ob[olo:olo + 64, :],
                                                    in1=rd[dlo:dlo + 64, :], op=ALU.mult), [obk, rdk], ["ybT"])
                npv = 0
                ntot = 4 * nki
                for c in range(4 // G):
                    slot = par * 2 + (chunk_ctr[par] % 2)
                    chunk_ctr[par] += 1
                    Kc = Kring_v[slot]
                    Vc = Vring_v[slot]
                    kk, vk = "K%d" % slot, "V%d" % slot
                    ranks = [c * G + gi for gi in range(G)]
                    p.dma("pool", [(Kc[0:96, gi * nki * 128 + tt * 512:gi * nki * 128 + (tt + 1) * 512],
                                    kout_t[tt].ap()[r * 768 + h * 96:r * 768 + (h + 1) * 96, :])
                                   for gi, r in enumerate(ranks) for tt in range(t + 1)],
                          ["kout%d" % tt for tt in range(t + 1)], [kk], kk)
                    p.dma("pool", [(Vc[:, gi * nki + 4 * tt:gi * nki + 4 * tt + 4, vlo:vlo + 64],
                                    vout_t[tt].ap()[r * 1024 + h * 128:r * 1024 + (h + 1) * 128, :].rearrange(
                                        "p (i d) -> p i d", d=64))
                                   for gi, r in enumerate(ranks) for tt in range(t + 1)],
                          ["vout%d" % tt for tt in range(t + 1)], [vk], vk)
                    for gi, r in enumerate(ranks):
                        for ki in range(nki):
                            d = ki - 4 * t
                            c0 = 128 * d if d > 0 else 0
                            kcol = (gi * nki + ki) * 128
                            bs, bsk = gen_banks.next()
                            p.pe(lambda e, bs=bs, Kc=Kc, kcol=kcol, c0=c0, h=h: e.matmul(
                                bs[:, c0:512], Kc[0:96, kcol:kcol + 128], QT[0:96, h, c0:512], start=True, stop=True),
                                [kk, "QT"], [bsk])
                            Pt, Pk = P_ring.next()
                            p.act(lambda e, bs=bs, Pt=Pt, c0=c0: e.activation(out=Pt[:, c0:512], in_=bs[:, c0:512],
                                                                             func=AF.Exp, scale=SC_B), [bsk], [Pk])
                            if d >= 0:
                                mk = masks[:, (ki % 2) * 4 + r, :]
                                p.dve(lambda e, Pt=Pt, c0=c0, mk=mk: e.tensor_tensor(
                                    out=Pt[:, c0:c0 + 128], in0=Pt[:, c0:c0 + 128], in1=mk, op=ALU.mult), [Pk, "cst"], [Pk])
                            pend.append((Pt, Pk, c0, Vc, vk, gi * nki + ki, ob, obk, npv == 0, npv == ntot - 1, fin))
                            npv += 1
                            if len(pend) > 2:
                                rec_pv(pend.pop(0))
            while pend:
                rec_pv(pend.pop(0))
            if stop == "dump_y%d" % t:
                for br in range(3):
                    p.dma("pool", [(oT_d[0:512, br * 512:(br + 1) * 512].rearrange("(kc p) t -> p kc t", p=128), yT[br])],
                          [("yaT", "ybT", "ycT")[br]], ["dbgy%d" % br], "dbg", final=True)
                return finish(store=False)
            for m in range(8):
                gs, gk = wload(lambda s, m=m: [
                    (s[:, 0:3072].rearrange("p (b kc n) -> p b kc n", b=3, n=128)[:, br],
                     win_d[:, C_GATE + br * 1024 + m * 128:C_GATE + br * 1024 + (m + 1) * 128].rearrange(
                         "(kc p) n -> p kc n", p=128)) for br in range(3)])
                bs_, bk_ = wload(lambda s, m=m: [
                    (s[:, 0:1536].rearrange("p (b kc n) -> p b kc n", b=3, n=128)[:, br],
                     wbr_d[br][:, m * 128:(m + 1) * 128].rearrange("(kc p) n -> p kc n", p=128)) for br in range(3)])
                g4 = gs[:, 0:3072].rearrange("p (b kc n) -> p b kc n", b=3, n=128)
                b4 = bs_[:, 0:1536].rearrange("p (b kc n) -> p b kc n", b=3, n=128)
                for br in range(3):
                    bg, bgk = gen_banks.next()
                    mm_group(bg, bgk, [(g4[:, br, kc, :], hT[:, kc, :]) for kc in range(8)], [gk, "hT"])
                    p.act(lambda e, bg=bg, br=br, m=m: e.activation(out=g_sb[:, br, :], in_=bg, func=AF.Sigmoid,
                                                                   bias=C("bgate")[:, br * 8 + m:br * 8 + m + 1], scale=1.0),
                          [bgk, "cst"], ["g_sb%d" % br])
                ykeys = ["yaT", "ybT", "ycT"]
                bbs = []
                for br in range(3):
                    bb_, bbk = gen_banks.next()
                    mm_group(bb_, bbk, [(b4[:, br, kc, :], yT[br][:, kc, :]) for kc in range(4)], [bk_, ykeys[br]])
                    bbs.append((bb_, bbk))
                p.dve(lambda e, b=bbs[0][0]: e.tensor_tensor(out=acc1, in0=b, in1=g_sb[:, 0, :], op=ALU.mult),
                      [bbs[0][1], "g_sb0"], ["rd0"])
                p.dve(lambda e, b=bbs[1][0]: e.tensor_tensor(out=acc2, in0=b, in1=g_sb[:, 1, :], op=ALU.mult),
                      [bbs[1][1], "g_sb1"], ["rd1"])
                p.dve(lambda e: e.tensor_tensor(out=acc1, in0=acc1, in1=acc2, op=ALU.add), ["rd0", "rd1"], ["rd0"])
                p.dve(lambda e, b=bbs[2][0]: e.tensor_tensor(out=acc2, in0=b, in1=g_sb[:, 2, :], op=ALU.mult),
                      [bbs[2][1], "g_sb2"], ["rd1"])
                p.dve(lambda e, m=m: e.tensor_tensor(out=mergedT[:, m, :], in0=acc1, in1=acc2, op=ALU.add),
                      ["rd0", "rd1"], ["mergedT"])
            for hf in range(2):
                ws, wk_ = wload(lambda s, hf=hf: [(s.rearrange("p (kc n) -> p kc n", n=512),
                                                   wout_d[:, hf * 512:(hf + 1) * 512].rearrange("(kc p) n -> p kc n", p=128))])
                w3 = ws.rearrange("p (kc n) -> p kc n", n=512)
                for mm in range(4):
                    m = hf * 4 + mm
                    bo, bok = gen_banks.next()
                    mm_group(bo, bok, [(w3[:, kc, mm * 128:(mm + 1) * 128], mergedT[:, kc, :]) for kc in range(8)],
                             [wk_, "mergedT"])
                    xs = xT[:, m, t * 512:(t + 1) * 512]
                    p.dve(lambda e, bo=bo, xs=xs: e.tensor_tensor(out=xs, in0=bo, in1=xs, op=ALU.add), [bok, xk], [xk])

        if stop != "mid":
            ffn(w2gu_d, w2dn_d, "ffn2_g")
        return finish()


def _perm_rows(j):
    rows = []
    for i in range(16):
        g = i // 2
        blk = 8 * g + (j if i % 2 == 0 else 7 - j)
        rows.append(np.arange(blk * 128, (blk + 1) * 128))
    return np.concatenate(rows)


def _host_inputs(inp):
    f = lambda a: np.ascontiguousarray(np.asarray(a, dtype=np.float32))
    x = f(inp["x"])
    mem = f(inp["mem"])
    pos = np.asarray(inp["positions"]).astype(np.int32)
    L0 = lambda k: f(inp[k])[0]

    def col(v, n):
        return np.ascontiguousarray(v.reshape(n, 128).T)

    def rep(v):
        return np.ascontiguousarray(np.broadcast_to(v[None, :], (128, v.shape[0])))

    cst = np.zeros((128, NCST), np.float32)

    def put(name, arr):
        o, w = CST[name]
        assert arr.shape == (128, w), (name, arr.shape)
        cst[:, o:o + w] = arr
    put("ffn1_g", col(L0("ffn1_norm"), 8))
    put("mix_g", col(L0("mix_norm"), 8))
    put("ffn2_g", col(L0("ffn2_norm"), 8))
    put("mem_g", col(L0("mem_norm"), 8))
    put("bgate", col(L0("b_gate"), 24))
    put("cqg", rep(L0("mla_cq_norm")))
    put("ckvg", rep(L0("mla_ckv_norm")))
    put("lng", rep(L0("sg_ln_g")))
    put("lnb", rep(L0("sg_ln_b")))
    put("qg", rep(L0("mla_q_norm")))
    put("kg", rep(L0("mla_k_norm")))
    put("mqg", rep(L0("mem_q_norm")))
    put("mkg", rep(L0("mem_k_norm")))
    sgb = L0("sg_b")
    bsT = np.zeros((128, 4, 128), np.float32)
    for c in range(4):
        bsT[0:64, c, :] = sgb[2 * c][None, :]
        bsT[64:128, c, :] = sgb[2 * c + 1][None, :]
    put("bsT", bsT.reshape(128, 512))
    half = 16
    invf = (10000.0 ** (-np.arange(half, dtype=np.float32) / half)).astype(np.float32)
    put("invf", rep(invf))
    tri = (np.arange(128)[:, None] <= np.arange(128)[None, :]).astype(np.float32)
    put("tri", tri)

    sgw = L0("sg_w")
    sgwT = np.ascontiguousarray(sgw.transpose(2, 0, 1)).reshape(128, 8 * 128)

    shared = {
        "cst": None, "sgwT": sgwT,
        "ffn1_w_gu": L0("ffn1_w_gu"), "ffn1_w_down": L0("ffn1_w_down"),
        "ffn2_w_gu": L0("ffn2_w_gu"), "ffn2_w_down": L0("ffn2_w_down"),
        "w_in": L0("w_in"), "mla_w_uq": L0("mla_w_uq"), "mla_w_ukv": L0("mla_w_ukv"),
        "mem_w_kv": L0("mem_w_kv"), "w_branch_a": L0("w_branch_a"), "w_branch_b": L0("w_branch_b"),
        "w_branch_c": L0("w_branch_c"), "w_out": L0("w_out"),
    }
    in_maps = []
    perms = []
    for c in range(NCORES):
        b, j = c // 4, c % 4
        rows = _perm_rows(j)
        perms.append((b, rows))
        m = dict(shared)
        m["cst"] = cst
        m["xT"] = np.ascontiguousarray(x[b][rows].T)
        m["memT"] = np.ascontiguousarray(mem[b].T)
        m["pos"] = np.ascontiguousarray(pos[b][rows].reshape(16, 128).T)
        cbf = np.zeros((128, NCBF), np.float32)
        cbf[:, 0:128] = np.eye(128, dtype=np.float32)
        cbf[:, 128:256] = 1.0
        mk = np.zeros((128, 8, 128), np.float32)
        for r in range(4):
            mk[:, r, :] = 1.0 if r < j else (tri if r == j else 0.0)
            mk[:, 4 + r, :] = 1.0 if r > j else (tri if r == j else 0.0)
        cbf[:, 256:] = mk.reshape(128, 1024)
        m["cbf"] = cbf.astype(ml_dtypes.bfloat16)
        in_maps.append(m)
    return in_maps, perms


_NC_CACHE = {}


def kernel(**inputs):
    in_maps, perms = _host_inputs(inputs)
    if "nc" not in _NC_CACHE:
        _NC_CACHE["nc"] = build()
    nc = _NC_CACHE["nc"]
    res = run_bass_kernel_spmd(nc, in_maps, core_ids=list(range(NCORES)))
    out = np.zeros((2, 8192, D), np.float32)
    for c in range(NCORES):
        b, rows = perms[c]
        out[b, rows, :] = np.asarray(res.results[c]["oT"], dtype=np.float32).T
    return out
```

```python
import numpy as np
import ml_dtypes
from contextlib import ExitStack
import concourse.bass as bass
import concourse.mybir as mybir
from concourse.bass_utils import run_bass_kernel_spmd

F32 = mybir.dt.float32
BF16 = mybir.dt.bfloat16
I32 = mybir.dt.int32
AF = mybir.ActivationFunctionType
ALU = mybir.AluOpType
AX = mybir.AxisListType

NCORES = 8
D = 1024
NT = 2048
DFF = 2816
FC = 22
EPS = 1e-6
C_U, C_V, C_CQ, C_CKV, C_KR, C_QM, C_GATE = 0, 512, 1024, 1408, 1664, 1696, 2208
PI = float(np.pi)

CST = {}
_off = 0
for _n, _w in [("ffn1_g", 8), ("mix_g", 8), ("ffn2_g", 8), ("mem_g", 8), ("bgate", 24), ("cqg", 384),
               ("ckvg", 256), ("lng", 512), ("lnb", 512), ("qg", 96), ("kg", 96), ("mqg", 128),
               ("mkg", 128), ("bsT", 512), ("invf", 16), ("tri", 128)]:
    CST[_n] = (_off, _w)
    _off += _w
NCST = _off
NCBF = 128 + 128 + 8 * 128


class Prog:
    ENG = ("pe", "act", "dve", "pool", "sp")

    def __init__(self, nc, es):
        self.nc, self.es = nc, es
        self.q = {e: [] for e in self.ENG}
        self.sems = {}
        self.cnt = {}
        self.waited = {e: {} for e in self.ENG}
        self.lastw = {}
        self.readers = {}
        self.out_tokens = []
        self.regions = {}
        self._ovc = {}

    def sem(self, name):
        if name not in self.sems:
            self.sems[name] = self.es.enter_context(self.nc.semaphore("s_" + name))
            self.cnt[name] = 0
        return self.sems[name]

    def region(self, key, ivs):
        self.regions[key] = list(ivs)

    def _overlap(self, k):
        if k not in self.regions:
            return (k,)
        c = self._ovc.get(k)
        if c is not None and c[0] == len(self.regions):
            return c[1]
        mine = self.regions[k]
        res = [k]
        for k2, ivs in self.regions.items():
            if k2 == k:
                continue
            hit = False
            for (a, b) in mine:
                for (c0, d0) in ivs:
                    if a < d0 and c0 < b:
                        hit = True
                        break
                if hit:
                    break
            if hit:
                res.append(k2)
        self._ovc[k] = (len(self.regions), tuple(res))
        return self._ovc[k][1]

    def _collect(self, eng, reads, writes, is_dma):
        toks = []
        for k in reads:
            for k2 in self._overlap(k):
                if k2 in self.lastw:
                    toks.append((self.lastw[k2], True))
        for k in writes:
            for k2 in self._overlap(k):
                if k2 in self.lastw:
                    toks.append((self.lastw[k2], False))
                rd = self.readers.get(k2)
                if rd:
                    toks.extend(((sn, v, te), False) for (sn, te), v in rd.items())
        waits = {}
        for ((sn, val, teng), raw) in toks:
            if teng is not None and teng == eng and not is_dma:
                if not raw or eng == "pe":
                    continue
            if self.waited[eng].get(sn, 0) >= val:
                continue
            waits[sn] = max(waits.get(sn, 0), val)
        for sn, val in waits.items():
            self.waited[eng][sn] = val
        return [(self.sem(sn), val, sn) for sn, val in waits.items()]

    def _commit(self, tok, reads, writes):
        sn, val, te = tok
        for k in reads:
            d = self.readers.setdefault(k, {})
            d[(sn, te)] = max(d.get((sn, te), 0), val)
        for k in writes:
            self.lastw[k] = tok
            self.readers[k] = {}

    def op(self, eng, fn, reads=(), writes=()):
        waits = self._collect(eng, reads, writes, False)
        sn = "E" + eng
        self.sem(sn)
        self.cnt[sn] += 1
        tok = (sn, self.cnt[sn], eng)
        self.q[eng].append((waits, fn, [(self.sems[sn], 1)], [(sn, 1)]))
        self._commit(tok, reads, writes)

    def pe(self, fn, r=(), w=()):
        self.op("pe", fn, r, w)

    def act(self, fn, r=(), w=()):
        self.op("act", fn, r, w)

    def dve(self, fn, r=(), w=()):
        self.op("dve", fn, r, w)

    def pool(self, fn, r=(), w=()):
        self.op("pool", fn, r, w)

    def dma(self, queue, pairs, reads, writes, semname, final=False):
        waits = self._collect(queue, reads, writes, True)
        s = self.sem(semname)
        self.cnt[semname] += 16 * len(pairs)
        tok = (semname, self.cnt[semname], None)

        def fn(e, pairs=pairs, s=s):
            for (o, i) in pairs:
                e.dma_start(out=o, in_=i).then_inc(s, 16)
            return None
        self.q[queue].append((waits, fn, [], [(semname, 16 * len(pairs))]))
        self._commit(tok, reads, writes)
        if final:
            self.out_tokens.append(tok)

    def collective(self, fn, reads, writes, semname="cc"):
        waits = self._collect("pool", reads, writes, True)
        s = self.sem(semname)
        self.cnt[semname] += 1
        tok = (semname, self.cnt[semname], None)
        self.q["pool"].append((waits, fn, [(s, 1)], [(semname, 1)]))
        self._commit(tok, reads, writes)

    def check(self):
        val = {}
        pos = {e: 0 for e in self.ENG}
        prog = True
        while prog:
            prog = False
            for e in self.ENG:
                while pos[e] < len(self.q[e]):
                    waits, fn, incs, names = self.q[e][pos[e]]
                    if all(val.get(sn, 0) >= v for (_s, v, sn) in waits):
                        for (sn, n) in names:
                            val[sn] = val.get(sn, 0) + n
                        pos[e] += 1
                        prog = True
                    else:
                        break
        stuck = {e: pos[e] for e in self.ENG if pos[e] < len(self.q[e])}
        for e, i in stuck.items():
            waits = self.q[e][i][0]
            print("DEADLOCK", e, "op", i, "of", len(self.q[e]), "waits",
                  [(sn, v, val.get(sn, 0)) for (_s, v, sn) in waits if val.get(sn, 0) < v])
        return not stuck

    def emit(self, block):
        assert self.check(), "semaphore protocol deadlock"
        def run(e, eng):
            for (waits, fn, incs, _n) in self.q[eng]:
                for (s, v, _sn) in waits:
                    e.wait_ge(s, v)
                ins = fn(e)
                for (s, n) in incs:
                    ins.then_inc(s, n)
            if eng == "sp":
                for (sn, val, _) in self.out_tokens:
                    e.wait_ge(self.sems[sn], val)

        @block.tensor
        def _(e):
            run(e, "pe")

        @block.scalar
        def _(e):
            run(e, "act")

        @block.vector
        def _(e):
            run(e, "dve")

        @block.gpsimd
        def _(e):
            run(e, "pool")

        @block.sync
        def _(e):
            run(e, "sp")


class Ring:
    def __init__(self, name, views):
        self.name, self.views, self.i = name, views, 0

    def next(self):
        k = self.i % len(self.views)
        self.i += 1
        return self.views[k], "%s%d" % (self.name, k)


def build(stop=None):
    nc = bass.Bass("TRN2", target_bir_lowering=False)

    def din(name, shape, dt=F32):
        return nc.dram_tensor(name, shape, dt, kind="ExternalInput").ap()

    xT_d = din("xT", [D, NT])
    memT_d = din("memT", [D, 256])
    pos_d = din("pos", [128, 16], I32)
    cst_d = din("cst", [128, NCST])
    cbf_d = din("cbf", [128, NCBF], BF16)
    sgwT_d = din("sgwT", [128, 8 * 128])
    w1gu_d = din("ffn1_w_gu", [D, 2 * DFF])
    w1dn_d = din("ffn1_w_down", [DFF, D])
    w2gu_d = din("ffn2_w_gu", [D, 2 * DFF])
    w2dn_d = din("ffn2_w_down", [DFF, D])
    win_d = din("w_in", [D, 5280])
    wuq_d = din("mla_w_uq", [384, 768])
    wukv_d = din("mla_w_ukv", [256, 1024])
    wmkv_d = din("mem_w_kv", [D, 1024])
    wbr_d = [din("w_branch_a", [512, D]), din("w_branch_b", [512, D]), din("w_branch_c", [512, D])]
    wout_d = din("w_out", [D, D])
    oT_d = nc.dram_tensor("oT", [D, NT], F32, kind="ExternalOutput").ap()
    kin_t = [nc.dram_tensor("kin%d" % c, [768, 512], BF16) for c in range(4)]
    kout_t = [nc.dram_tensor("kout%d" % c, [4 * 768, 512], BF16) for c in range(4)]
    vin_t = [nc.dram_tensor("vin%d" % c, [1024, 256], BF16) for c in range(4)]
    vout_t = [nc.dram_tensor("vout%d" % c, [4 * 1024, 256], BF16) for c in range(4)]

    es = ExitStack()
    with es:
        p = Prog(nc, es)

        def sb(name, shape, dt):
            return es.enter_context(nc.sbuf_tensor(name, shape, dt))

        xT = sb("xT_sb", [128, 8, NT], F32)
        cst = sb("cst_sb", [128, NCST], F32)
        cbf = sb("cbf_sb", [128, NCBF], BF16)
        csA = sb("csA", [128, 16, 32], F32)
        csB = sb("csB", [128, 16, 32], F32)
        wTsg = sb("wTsg", [128, 8, 128], BF16)
        KmemT = sb("KmemT", [128, 4, 256], BF16)
        Vmem = sb("Vmem", [128, 2, 512], BF16)
        wr_t = sb("wring", [128, 3, 4096], BF16)
        ARENA = 49300
        ar = sb("arena", [128, ARENA], BF16)
        ps = es.enter_context(nc.psum_tensor("ps", [128, 8, 512], F32))

        def C(name):
            o, w = CST[name]
            return cst[:, o:o + w]

        ident = cbf[:, 0:128]
        ones = cbf[:, 128:256]
        masks = cbf[:, 256:256 + 1024].rearrange("p (a b) -> p a b", b=128)

        class Carver:
            def __init__(self):
                self.off = 0
                self.hi = 0

            def take(self, shape, dt, key=None):
                n = int(np.prod(shape)) * (2 if dt in (F32, I32) else 1)
                n = (n + 1) // 2 * 2
                o = self.off
                assert o + n <= ARENA, (o, n, key)
                a = ar[:, o:o + n]
                self.off += n
                self.hi = max(self.hi, self.off)
                if key is not None:
                    p.region(key, [(o, o + n)])
                if dt in (F32, I32):
                    a = a.bitcast(dt)
                if len(shape) == 2:
                    a = a.rearrange("p (a b) -> p a b", b=shape[1])
                elif len(shape) == 3:
                    a = a.rearrange("p (a b c) -> p a b c", b=shape[1], c=shape[2])
                return a

        cv = Carver()
        sq_ring = Ring("sq", [cv.take([512], BF16) for _ in range(2)])
        std_t = cv.take([512], F32)
        rstd_t = cv.take([512], F32)
        junk = cv.take([512], BF16)
        small = cv.take([128], F32)
        base_off = cv.off

        def mk_hn(pfx):
            return dict(pfx=pfx, sq=cv.take([8, 96], F32, pfx + "hn_sq"), rt=cv.take([8, 32], F32, pfx + "hn_rt"),
                        t1=cv.take([8, 32], F32, pfx + "hn_t1"), t2=cv.take([8, 32], F32, pfx + "hn_t2"))

        gen_banks = Ring("ps", [ps[:, b, :] for b in range(6)])
        o_banks = Ring("po", [ps[:, 6, :], ps[:, 7, :]])
        wring = Ring("w", [wr_t[:, s, :] for s in range(3)])

        def wload(pairs_fn):
            slot, key = wring.next()
            p.dma("pool", pairs_fn(slot), reads=[], writes=[key], semname=key)
            return slot, key

        def bf_bank(bank):
            return bank.bitcast(BF16).rearrange("p (a b) -> p a b", b=128)

        p.dma("sp", [(cst[:, :], cst_d[:, :]), (cbf[:, :], cbf_d[:, :])], [], ["cst"], "cst")
        for t in range(4):
            p.dma("sp", [(xT[:, :, t * 512:(t + 1) * 512],
                          xT_d[:, t * 512:(t + 1) * 512].rearrange("(kc p) t -> p kc t", p=128))],
                  [], ["x%d" % t], "xin%d" % t)

        cv.off = base_off
        pos_i = cv.take([16], I32, "pos")
        posf = cv.take([16], F32, "posf")
        angA = cv.take([16, 32], F32, "angA")
        angT = cv.take([16, 32], F32, "angT")
        angI = cv.take([16, 32], I32, "angI")
        angF = cv.take([16, 32], F32, "angF")
        angM = cv.take([16, 32], F32, "angM")
        SC = cv.take([16, 32], F32, "SC")
        sgtmp = cv.take([8, 128], F32, "sgtmp")
        _save = cv.off
        cv.off = 36640
        memT = cv.take([8, 256], F32, "memT")
        memnT = cv.take([8, 256], BF16, "memnT")
        km_sb = cv.take([4, 128], F32, "km_sb")
        kmn = cv.take([4, 128], BF16, "kmn")
        hn0 = mk_hn("s_")
        cv.off = _save

        p.dma("sp", [(pos_i, pos_d[:, :])], [], ["pos"], "misc")
        p.dma("sp", [(sgtmp, sgwT_d[:, :].rearrange("p (g t) -> p g t", t=128))], [], ["sgtmp"], "misc2")
        p.dma("sp", [(memT, memT_d[:, :].rearrange("(kc p) m -> p kc m", p=128))], [], ["memT"], "misc3")

        p.dve(lambda e: e.tensor_copy(out=posf, in_=pos_i), ["pos"], ["posf"])
        for i in range(16):
            p.dve(lambda e, i=i: e.tensor_scalar(out=angA[:, i, 0:16], in0=C("invf"), scalar1=posf[:, i:i + 1],
                                                 scalar2=None, op0=ALU.mult), ["posf", "cst"], ["angA"])
        p.dve(lambda e: e.tensor_scalar(out=angA[:, :, 16:32], in0=angA[:, :, 0:16], scalar1=PI / 2, scalar2=None,
                                        op0=ALU.add), ["angA"], ["angA"])
        p.dve(lambda e: e.tensor_scalar(out=angT, in0=angA, scalar1=1.0 / (2 * PI), scalar2=None, op0=ALU.mult),
              ["angA"], ["angT"])
        p.dve(lambda e: e.tensor_copy(out=angI, in_=angT), ["angT"], ["angI"])
        p.dve(lambda e: e.tensor_copy(out=angF, in_=angI), ["angI"], ["angF"])
        p.dve(lambda e: e.scalar_tensor_tensor(out=angT, in0=angF, scalar=-2 * PI, in1=angA, op0=ALU.mult,
                                               op1=ALU.add), ["angF", "angA"], ["angT"])
        p.dve(lambda e: e.tensor_scalar(out=angM, in0=angT, scalar1=PI, scalar2=None, op0=ALU.is_gt),
              ["angT"], ["angM"])
        p.dve(lambda e: e.scalar_tensor_tensor(out=angF, in0=angM, scalar=-2 * PI, in1=angT, op0=ALU.mult,
                                               op1=ALU.add), ["angM", "angT"], ["angF"])
        p.dve(lambda e: e.tensor_scalar(out=angM, in0=angF, scalar1=-PI, scalar2=None, op0=ALU.is_lt),
              ["angF"], ["angM"])
        p.dve(lambda e: e.scalar_tensor_tensor(out=angT, in0=angM, scalar=2 * PI, in1=angF, op0=ALU.mult,
                                               op1=ALU.add), ["angM", "angF"], ["angT"])
        p.dve(lambda e: e.tensor_scalar(out=angF, in0=angT, scalar1=PI, scalar2=-PI, op0=ALU.min, op1=ALU.max),
              ["angT"], ["angF"])
        p.act(lambda e: e.activation(out=SC, in_=angF, func=AF.Sin), ["angF"], ["SC"])
        p.dve(lambda e: e.tensor_copy(out=csA[:, :, 0:16], in_=SC[:, :, 16:32]), ["SC"], ["csA"])
        p.dve(lambda e: e.tensor_copy(out=csA[:, :, 16:32], in_=SC[:, :, 16:32]), ["SC"], ["csA"])
        p.dve(lambda e: e.tensor_scalar(out=csB[:, :, 0:16], in0=SC[:, :, 0:16], scalar1=-1.0, scalar2=None,
                                        op0=ALU.mult), ["SC"], ["csB"])
        p.dve(lambda e: e.tensor_copy(out=csB[:, :, 16:32], in_=SC[:, :, 0:16]), ["SC"], ["csB"])
        p.dve(lambda e: e.tensor_tensor(out=wTsg[:, :, :], in0=sgtmp,
                                        in1=C("tri").unsqueeze(1).broadcast_to([128, 8, 128]), op=ALU.mult),
              ["sgtmp", "cst"], ["wTsg"])

        def norm_fm(src, srckeys, gname, dst, dstkey, N, bank=None):
            bank, bkey = bank if bank is not None else gen_banks.next()
            for kc in range(8):
                sq, sqk = sq_ring.next()
                s_ap = src(kc)
                p.act(lambda e, s_ap=s_ap, sq=sq: e.activation(out=sq[:, :N], in_=s_ap, func=AF.Square),
                      srckeys, [sqk])
                p.pe(lambda e, kc=kc, sq=sq: e.matmul(bank[:, :N], ones, sq[:, :N], start=(kc == 0), stop=(kc == 7)),
                     [sqk, "cst"], [bkey])
            p.act(lambda e: e.activation(out=std_t[:, :N], in_=bank[:, :N], func=AF.Sqrt, bias=EPS, scale=1.0 / D),
                  [bkey], ["std"])
            p.dve(lambda e: e.reciprocal(out=rstd_t[:, :N], in_=std_t[:, :N]), ["std"], ["rstd"])
            g = C(gname)
            for kc in range(8):
                s_ap, d_ap = src(kc), dst(kc)
                p.dve(lambda e, kc=kc, s_ap=s_ap, d_ap=d_ap: e.scalar_tensor_tensor(
                    out=d_ap, in0=s_ap, scalar=g[:, kc:kc + 1], in1=rstd_t[:, :N], op0=ALU.mult, op1=ALU.mult),
                    srckeys + ["rstd", "cst"], [dstkey])

        def head_norm(hn, src, H, Dh, gname, out_bf, key_in, key_out, rope_i=None):
            pf = hn["pfx"]
            rt, t1, t2 = hn["rt"], hn["t1"], hn["t2"]
            ksq, krt, kt1, kt2 = pf + "hn_sq", pf + "hn_rt", pf + "hn_t1", pf + "hn_t2"
            sqv = hn["sq"].rearrange("p a b -> p (a b)")[:, 0:H * Dh].rearrange("p (a b) -> p a b", b=Dh)
            ss = small[:, 0:H]
            sd = small[:, 8:8 + H]
            rs = small[:, 16:16 + H]
            g = C(gname)
            p.dve(lambda e: e.tensor_tensor(out=sqv, in0=src, in1=src, op=ALU.mult), [key_in], [ksq])
            p.dve(lambda e: e.tensor_reduce(out=ss, in_=sqv, axis=AX.X, op=ALU.add), [ksq], ["hn_ss"])
            p.act(lambda e: e.activation(out=sd, in_=ss, func=AF.Sqrt, bias=EPS, scale=1.0 / Dh), ["hn_ss"], ["hn_sd"])
            p.dve(lambda e: e.reciprocal(out=rs, in_=sd), ["hn_sd"], ["hn_rs"])
            p.dve(lambda e: e.tensor_tensor(out=sqv, in0=src, in1=rs.unsqueeze(2).broadcast_to([128, H, Dh]),
                                            op=ALU.mult), [key_in, "hn_rs"], [ksq])
            if rope_i is None:
                p.dve(lambda e: e.tensor_tensor(out=out_bf, in0=sqv, in1=g.unsqueeze(1).broadcast_to([128, H, Dh]),
                                                op=ALU.mult), [ksq, "cst"], [key_out])
                return
            i = rope_i
            p.dve(lambda e: e.tensor_tensor(out=out_bf[:, :, 0:64], in0=sqv[:, :, 0:64],
                                            in1=g[:, 0:64].unsqueeze(1).broadcast_to([128, H, 64]), op=ALU.mult),
                  [ksq, "cst"], [key_out])
            p.dve(lambda e: e.tensor_tensor(out=rt, in0=sqv[:, :, 64:96],
                                            in1=g[:, 64:96].unsqueeze(1).broadcast_to([128, H, 32]), op=ALU.mult),
                  [ksq, "cst"], [krt])
            p.dve(lambda e: e.tensor_tensor(out=t1, in0=rt,
                                            in1=csA[:, i, :].unsqueeze(1).broadcast_to([128, H, 32]), op=ALU.mult),
                  [krt, "csA"], [kt1])
            p.dve(lambda e: e.tensor_tensor(out=t2[:, :, 0:16], in0=rt[:, :, 16:32],
                                            in1=csB[:, i, 0:16].unsqueeze(1).broadcast_to([128, H, 16]), op=ALU.mult),
                  [krt, "csB"], [kt2])
            p.dve(lambda e: e.tensor_tensor(out=t2[:, :, 16:32], in0=rt[:, :, 0:16],
                                            in1=csB[:, i, 16:32].unsqueeze(1).broadcast_to([128, H, 16]), op=ALU.mult),
                  [krt, "csB"], [kt2])
            p.dve(lambda e: e.tensor_tensor(out=out_bf[:, :, 64:96], in0=t1, in1=t2, op=ALU.add),
                  [kt1, kt2], [key_out])

        def mm_group(bank_ap, bkey, items, reads):
            def fn(e):
                n = len(items)
                ins = None
                for k, (l, r) in enumerate(items):
                    ins = e.matmul(bank_ap, l, r, start=(k == 0), stop=(k == n - 1))
                return ins
            p.pe(fn, reads, [bkey])

        def transposes(bank, bkey, srcs, reads, rows=128):
            bb = bf_bank(bank)

            def fn(e):
                ins = None
                for j, s in enumerate(srcs):
                    ins = e.transpose(out=bb[0:rows, j, :], in_=s, identity=ident)
                return ins
            p.pe(fn, reads + ["cst"], [bkey])
            return bb

        def ffn(wgu_d, wdn_d, gname):
            cv.off = base_off
            ho = cv.off
            hT = cv.take([8, 1024], BF16)
            for sub in range(2):
                p.region("h%d" % sub, [(ho + kc * 1024 + sub * 512, ho + kc * 1024 + sub * 512 + 512) for kc in range(8)])
            ao = cv.off
            actT = cv.take([FC, 1024], BF16)
            for f in range(FC):
                for sub in range(2):
                    p.region("a%d_%d" % (f, sub), [(ao + f * 1024 + sub * 512, ao + f * 1024 + sub * 512 + 512)])
            sg_ring = Ring("sg", [cv.take([512], F32, "sg%d" % k) for k in range(2)])
            assert cv.off <= 36640, cv.off
            for half in range(2):
                for sub in range(2):
                    t = half * 2 + sub
                    norm_fm(lambda kc, t=t: xT[:, kc, t * 512:(t + 1) * 512], ["x%d" % t], gname,
                            lambda kc, sub=sub: hT[:, kc, sub * 512:(sub + 1) * 512], "h%d" % sub, 512)
                for pp in range(11):
                    slot, wkey = wload(lambda s, pp=pp: [
                        (s.rearrange("p (a kc n) -> p a kc n", a=2, n=256)[:, 0],
                         wgu_d[:, pp * 256:(pp + 1) * 256].rearrange("(kc p) n -> p kc n", p=128)),
                        (s.rearrange("p (a kc n) -> p a kc n", a=2, n=256)[:, 1],
                         wgu_d[:, DFF + pp * 256:DFF + (pp + 1) * 256].rearrange("(kc p) n -> p kc n", p=128))])
                    w4 = slot.rearrange("p (a kc n) -> p a kc n", a=2, n=256)
                    for fi in range(2):
                        f = pp * 2 + fi
                        for sub in range(2):
                            bg, bgk = gen_banks.next()
                            bu, buk = gen_banks.next()
                            hs = lambda kc, sub=sub: hT[:, kc, sub * 512:(sub + 1) * 512]
                            mm_group(bg, bgk, [(w4[:, 0, kc, fi * 128:(fi + 1) * 128], hs(kc)) for kc in range(8)],
                                     [wkey, "h%d" % sub])
                            mm_group(bu, buk, [(w4[:, 1, kc, fi * 128:(fi + 1) * 128], hs(kc)) for kc in range(8)],
                                     [wkey, "h%d" % sub])
                            sg, sgk = sg_ring.next()
                            p.act(lambda e, sg=sg, bg=bg: e.activation(out=sg, in_=bg, func=AF.Silu), [bgk], [sgk])
                            a_ap = actT[:, f, sub * 512:(sub + 1) * 512]
                            p.dve(lambda e, sg=sg, bu=bu, a_ap=a_ap: e.tensor_tensor(out=a_ap, in0=bu, in1=sg, op=ALU.mult),
                                  [sgk, buk], ["a%d_%d" % (f, sub)])
                for m in range(8):
                    slot, wkey = wload(lambda s, m=m: [
                        (s[:, 0:FC * 128].rearrange("p (f n) -> p f n", n=128),
                         wdn_d[:, m * 128:(m + 1) * 128].rearrange("(f p) n -> p f n", p=128))])
                    w3 = slot[:, 0:FC * 128].rearrange("p (f n) -> p f n", n=128)
                    for sub in range(2):
                        t = half * 2 + sub
                        bd, bdk = gen_banks.next()
                        mm_group(bd, bdk, [(w3[:, f, :], actT[:, f, sub * 512:(sub + 1) * 512]) for f in range(FC)],
                                 [wkey] + ["a%d_%d" % (f, sub) for f in range(FC)])
                        xs = xT[:, m, t * 512:(t + 1) * 512]
                        p.dve(lambda e, bd=bd, xs=xs: e.scalar_tensor_tensor(out=xs, in0=bd, scalar=0.5, in1=xs,
                                                                            op0=ALU.mult, op1=ALU.add),
                              [bdk, "x%d" % t], ["x%d" % t])

        def store_out():
            for t in range(4):
                p.dma("sp", [(oT_d[:, t * 512:(t + 1) * 512].rearrange("(kc p) t -> p kc t", p=128),
                              xT[:, :, t * 512:(t + 1) * 512])], ["x%d" % t], ["out%d" % t], "out", final=True)

        def finish(store=True):
            if store:
                store_out()
            with nc.Block() as block:
                p.emit(block)
            return nc

        if stop != "noffn1":
            ffn(w1gu_d, w1dn_d, "ffn1_g")
        norm_fm(lambda kc: memT[:, kc, :], ["memT"], "mem_g", lambda kc: memnT[:, kc, :], "memnT", 256)
        wk, wkk = wload(lambda s: [(s.rearrange("p (kc n) -> p kc n", n=512),
                                    wmkv_d[:, 0:512].rearrange("(kc p) n -> p kc n", p=128))])
        wv_, wvk = wload(lambda s: [(s.rearrange("p (kc n) -> p kc n", n=512),
                                     wmkv_d[:, 512:1024].rearrange("(kc p) n -> p kc n", p=128))])
        wk3 = wk.rearrange("p (kc n) -> p kc n", n=512)
        wv3 = wv_.rearrange("p (kc n) -> p kc n", n=512)
        for mb in range(2):
            bk, bkk = gen_banks.next()
            mm_group(bk, bkk, [(memnT[:, kc, mb * 128:(mb + 1) * 128], wk3[:, kc, :]) for kc in range(8)],
                     ["memnT", wkk])
            bv, bvk = gen_banks.next()
            mm_group(bv, bvk, [(memnT[:, kc, mb * 128:(mb + 1) * 128], wv3[:, kc, :]) for kc in range(8)],
                     ["memnT", wvk])
            p.act(lambda e, mb=mb, bv=bv: e.activation(out=Vmem[:, mb, :], in_=bv, func=AF.Copy), [bvk], ["Vmem"])
            p.act(lambda e, bk=bk: e.activation(out=km_sb.rearrange("p a b -> p (a b)"), in_=bk, func=AF.Copy),
                  [bkk], ["km_sb"])
            head_norm(hn0, km_sb, 4, 128, "mkg", kmn, "km_sb", "kmn")
            bt, btk = gen_banks.next()
            bb = transposes(bt, btk, [kmn[:, h, :] for h in range(4)], ["kmn"])
            p.act(lambda e, mb=mb, bb=bb: e.activation(out=KmemT[:, :, mb * 128:(mb + 1) * 128], in_=bb[:, 0:4, :],
                                                       func=AF.Copy), [btk], ["KmemT"])

        if stop == "ffn1":
            return finish()

        def run_pipelined(gens, depth=2):
            pending = list(gens)
            active = []
            while pending or active:
                if pending and len(active) < depth:
                    active.append(pending.pop(0))
                for g in list(active):
                    try:
                        next(g)
                    except StopIteration:
                        active.remove(g)

        cv.off = base_off
        hT2 = [cv.take([8, 512], BF16, "hT%d" % k) for k in range(2)]
        Kst = cv.take([8, NT], BF16, "Kst")
        Vst = cv.take([8 * 16 * 64], BF16, "Vst").rearrange("p (h i d) -> p h i d", h=8, i=16)
        m1sets = []
        for s in range(2):
            m1sets.append(dict(ckvn=cv.take([256], BF16, "ckvn%d" % s), ckvnT=cv.take([2, 128], BF16, "ckvnT%d" % s),
                               kc_sb=cv.take([8, 96], F32, "kc_sb%d" % s), kfin=cv.take([8, 96], BF16, "kfin%d" % s)))
        hn1 = mk_hn("m_")

        wkvin, wkvin_k = wload(lambda s: [(s[:, 0:8 * 288].rearrange("p (kc n) -> p kc n", n=288),
                                           win_d[:, C_CKV:C_CKV + 288].rearrange("(kc p) n -> p kc n", p=128))])
        wkvin3 = wkvin[:, 0:8 * 288].rearrange("p (kc n) -> p kc n", n=288)
        wukv, wukv_k = wload(lambda s: [(s[:, 0:2048].rearrange("p (kc n) -> p kc n", n=1024),
                                         wukv_d[:, :].rearrange("(kc p) n -> p kc n", p=128))])
        wukv3 = wukv[:, 0:2048].rearrange("p (kc n) -> p kc n", n=1024)
        norm_banks = o_banks

        pre_w = {}

        def m1_block(t, bl):
            i = t * 4 + bl
            s = i % 2
            S = m1sets[s]
            ckvn, ckvnT, kc_sb, kfin = S["ckvn"], S["ckvnT"], S["kc_sb"], S["kfin"]
            kq = lambda n: "%s%d" % (n, s)
            hT = hT2[t % 2]
            hk = "hT%d" % (t % 2)
            ss1 = small[:, 32 + 4 * s:33 + 4 * s]
            sd1 = small[:, 33 + 4 * s:34 + 4 * s]
            rs1 = small[:, 34 + 4 * s:35 + 4 * s]
            bA, bAk = ps[:, 3 * s + 0, :], "ps%d" % (3 * s + 0)
            bB, bBk = ps[:, 3 * s + 1, :], "ps%d" % (3 * s + 1)
            bC, bCk = ps[:, 3 * s + 2, :], "ps%d" % (3 * s + 2)
            if bl == 0:
                nb, nbk = norm_banks.next()
                norm_fm(lambda kc, t=t: xT[:, kc, t * 512:(t + 1) * 512], ["x%d" % t], "mix_g",
                        lambda kc: hT[:, kc, :], hk, 512, bank=(nb, nbk))
            mm_group(bA[:, 0:288], bAk, [(hT[:, kc, bl * 128:(bl + 1) * 128], wkvin3[:, kc, :]) for kc in range(8)],
                     [hk, wkvin_k])
            yield
            p.act(lambda e: e.activation(out=junk[:, 0:256], in_=bA[:, 0:256], func=AF.Square, accum_out=ss1),
                  [bAk], [kq("ss1"), "junk"])
            p.act(lambda e: e.activation(out=sd1, in_=ss1, func=AF.Sqrt, bias=EPS, scale=1.0 / 256), [kq("ss1")], [kq("sd1")])
            p.act(lambda e: e.activation(out=kc_sb[:, :, 64:96], in_=bA[:, 256:288].unsqueeze(1).broadcast_to([128, 8, 32]),
                                         func=AF.Copy), [bAk], [kq("kc_sb")])
            p.dve(lambda e: e.reciprocal(out=rs1, in_=sd1), [kq("sd1")], [kq("rs1")])
            p.dve(lambda e: e.scalar_tensor_tensor(out=ckvn, in0=bA[:, 0:256], scalar=rs1, in1=C("ckvg"),
                                                   op0=ALU.mult, op1=ALU.mult), [bAk, kq("rs1"), "cst"], [kq("ckvn")])
            yield
            bb = transposes(bB, bBk, [ckvn[:, k * 128:(k + 1) * 128] for k in range(2)], [kq("ckvn")])
            p.act(lambda e: e.activation(out=ckvnT, in_=bb[:, 0:2, :], func=AF.Copy), [bBk], [kq("ckvnT")])
            yield
            mm_group(bC, bCk, [(ckvnT[:, k, :], wukv3[:, k, 0:512]) for k in range(2)], [kq("ckvnT"), wukv_k])
            mm_group(bA, bAk, [(ckvnT[:, k, :], wukv3[:, k, 512:1024]) for k in range(2)], [kq("ckvnT"), wukv_k])
            for hb, (bx, bxk) in enumerate(((bC, bCk), (bA, bAk))):
                b3 = bx.rearrange("p (h d) -> p h d", d=128)
                p.act(lambda e, b3=b3, hb=hb: e.activation(out=Vst[:, hb * 4:(hb + 1) * 4, i, :],
                                                           in_=b3[:, :, 64:128], func=AF.Copy), [bxk], ["Vst"])
                p.act(lambda e, b3=b3, hb=hb: e.activation(out=kc_sb[:, hb * 4:(hb + 1) * 4, 0:64],
                                                           in_=b3[:, :, 0:64], func=AF.Copy), [bxk], [kq("kc_sb")])
            yield
            head_norm(hn1, kc_sb, 8, 96, "kg", kfin, kq("kc_sb"), kq("kfin"), rope_i=i)
            yield
            bb2 = transposes(bB, bBk, [kfin[:, h, :] for h in range(8)], [kq("kfin")], rows=96)
            p.act(lambda e: e.activation(out=Kst[0:96, :, i * 128:(i + 1) * 128], in_=bb2[0:96, :, :],
                                         func=AF.Copy), [bBk], ["Kst"])
            if bl == 3 and stop not in ("m1nocc", "dump_m1"):
                yield
                if t == 3:
                    pre_w["wv"] = wload(lambda s: [(s.rearrange("p (kc n) -> p kc n", n=512),
                                                    win_d[:, C_V:C_V + 512].rearrange("(kc p) n -> p kc n", p=128))])
                p.dma("sp", [(kin_t[t].ap().rearrange("(h f) c -> f h c", f=96), Kst[0:96, :, t * 512:(t + 1) * 512])],
                      ["Kst"], ["kin%d" % t], "kin%d" % t)
                p.dma("sp", [(vin_t[t].ap().rearrange("(h p) (i d) -> p h i d", p=128, d=64), Vst[:, :, 4 * t:4 * t + 4, :])],
                      ["Vst"], ["vin%d" % t], "vin%d" % t)
                p.collective(lambda e: e.collective_compute(
                    "AllGather", ALU.bypass, replica_groups=[[0, 1, 2, 3], [4, 5, 6, 7]],
                    ins=[kin_t[t].ap().opt()], outs=[kout_t[t].ap().opt()]),
                    ["kin%d" % t], ["kout%d" % t], semname="cck%d" % t)
                p.collective(lambda e: e.collective_compute(
                    "AllGather", ALU.bypass, replica_groups=[[0, 1, 2, 3], [4, 5, 6, 7]],
                    ins=[vin_t[t].ap().opt()], outs=[vout_t[t].ap().opt()]),
                    ["vin%d" % t], ["vout%d" % t], semname="ccv%d" % t)

        run_pipelined([m1_block(t, bl) for t in range(4) for bl in range(4)], depth=2)
        if stop == "dump_m1":
            for h in range(8):
                p.dma("pool", [(oT_d[h * 96:(h + 1) * 96, :], Kst[0:96, h, :])], ["Kst"], ["dbg%d" % h], "dbg", final=True)
            for hq in range(2):
                p.dma("pool", [(oT_d[768 + hq * 128:768 + (hq + 1) * 128, :].rearrange("p (h i d) -> p h i d", h=2, i=16),
                                Vst[:, 2 * hq:2 * hq + 2, :, :])], ["Vst"], ["dbgv%d" % hq], "dbg", final=True)
            return finish(store=False)
        if stop == "m1":
            return finish()
        if stop == "m1nocc":
            return finish()
        cv.off = base_off
        hT = cv.take([8, 512], BF16, "hT")
        QT = cv.take([8, 512], BF16, "QT")
        qmT = cv.take([4, 512], BF16, "qmT")
        yT = [cv.take([4, 512], BF16, k) for k in ("yaT", "ybT", "ycT")]
        Kring_v = [cv.take([2048], BF16, "K%d" % s) for s in range(4)]
        Vring_v = [cv.take([16, 128], BF16, "V%d" % s) for s in range(4)]
        ph_off = cv.off
        uT_sb = cv.take([4, 512], BF16, "uT")
        Asets = [dict(v_sb=cv.take([512], F32, "v_sb%d" % s), v_ln=cv.take([512], BF16, "v_ln%d" % s),
                      mtmp=cv.take([4, 128], F32, "mtmp%d" % s)) for s in range(2)]
        cv.off = ph_off
        Qsets = [dict(cqn=cv.take([384], BF16, "cqn%d" % s), cqnT=cv.take([3, 128], BF16, "cqnT%d" % s),
                      q_sb=cv.take([8, 96], F32, "q_sb%d" % s), qfin=cv.take([8, 96], BF16, "qfin%d" % s))
                 for s in range(2)]
        hn2 = mk_hn("t_")
        q_end = cv.off
        cv.off = ph_off
        Csets = [dict(qm_sb=cv.take([4, 128], F32, "qm_sb%d" % s), qmn=cv.take([4, 128], BF16, "qmn%d" % s))
                 for s in range(2)]
        assert cv.off <= q_end - 3 * 1024 - 1536 * 2 or True
        cv.off = ph_off
        P_ring = Ring("P", [cv.take([512], BF16, "P%d" % k) for k in range(8)])
        rd_views = [cv.take([512], F32, "rd%d" % k) for k in range(2)]
        rden_ring = Ring("rd", rd_views)
        acc1, acc2 = rd_views
        mergedT = cv.take([8, 512], BF16, "mergedT")
        go = cv.off
        g_sb = cv.take([3, 512], BF16)
        for br in range(3):
            p.region("g_sb%d" % br, [(go + br * 512, go + br * 512 + 512)])

        for s in range(4):
            lo = 64 if s < 2 else 0
            p.pool(lambda e, s=s, lo=lo: e.memset(Vring_v[s][:, :, lo:lo + 64], 1.0), [], ["V%d" % s])

        SC_B = 96 ** -0.5
        SC_C = 128 ** -0.5
        st6 = small[:, 40:46]
        mv = small[:, 46:48]
        sdv = small[:, 48:49]
        rsv = small[:, 49:50]

        for t in range(4):
            xk = "x%d" % t
            norm_fm(lambda kc, t=t: xT[:, kc, t * 512:(t + 1) * 512], [xk], "mix_g",
                    lambda kc: hT[:, kc, :], "hT", 512)
            wv_s, wv_k = pre_w.pop("wv") if "wv" in pre_w else wload(lambda s: [
                (s.rearrange("p (kc n) -> p kc n", n=512),
                 win_d[:, C_V:C_V + 512].rearrange("(kc p) n -> p kc n", p=128))])
            wu_s, wu_k = wload(lambda s: [(s.rearrange("p (kc n) -> p kc n", n=512),
                                           win_d[:, C_U:C_U + 512].rearrange("(kc p) n -> p kc n", p=128))])
            wv3 = wv_s.rearrange("p (kc n) -> p kc n", n=512)
            wu3 = wu_s.rearrange("p (kc n) -> p kc n", n=512)
            for c in range(4):
                bu, buk = o_banks.next()
                mm_group(bu, buk, [(wu3[:, kc, c * 128:(c + 1) * 128], hT[:, kc, :]) for kc in range(8)], [wu_k, "hT"])
                p.act(lambda e, c=c, bu=bu: e.activation(out=uT_sb[:, c, :], in_=bu, func=AF.Gelu), [buk], ["uT"])

            def a_block(bl):
                s = bl % 2
                S = Asets[s]
                v_sb, v_ln, mtmp = S["v_sb"], S["v_ln"], S["mtmp"]
                kq = lambda n: "%s%d" % (n, s)
                st6 = small[:, 64 + 16 * s:70 + 16 * s]
                mv = small[:, 70 + 16 * s:72 + 16 * s]
                sdv = small[:, 72 + 16 * s:73 + 16 * s]
                rsv = small[:, 73 + 16 * s:74 + 16 * s]
                bV, bVk = ps[:, 3 * s + 0, :], "ps%d" % (3 * s + 0)
                bM, bMk = ps[:, 3 * s + 1, :], "ps%d" % (3 * s + 1)
                mm_group(bV, bVk, [(hT[:, kc, bl * 128:(bl + 1) * 128], wv3[:, kc, :]) for kc in range(8)], [wv_k, "hT"])
                yield
                p.act(lambda e: e.activation(out=v_sb, in_=bV, func=AF.Gelu), [bVk], [kq("v_sb")])
                p.dve(lambda e: e.bn_stats(out=st6, in_=v_sb), [kq("v_sb")], [kq("st6")])
                p.dve(lambda e: e.bn_aggr(out=mv, in_=st6), [kq("st6")], [kq("mv")])
                p.act(lambda e: e.activation(out=sdv, in_=mv[:, 1:2], func=AF.Sqrt, bias=EPS, scale=1.0), [kq("mv")], [kq("sdv")])
                yield
                p.dve(lambda e: e.reciprocal(out=rsv, in_=sdv), [kq("sdv")], [kq("rsv")])
                p.dve(lambda e: e.scalar_tensor_tensor(out=v_sb, in0=v_sb, scalar=mv[:, 0:1], in1=C("lng"),
                                                       op0=ALU.subtract, op1=ALU.mult), [kq("v_sb"), kq("mv"), "cst"], [kq("v_sb")])
                p.dve(lambda e: e.scalar_tensor_tensor(out=v_ln, in0=v_sb, scalar=rsv, in1=C("lnb"),
                                                       op0=ALU.mult, op1=ALU.add), [kq("v_sb"), kq("rsv"), "cst"], [kq("v_ln")])
                yield

                def mixfn(e):
                    ins = None
                    for g in range(8):
                        ins = e.matmul(bM[(g % 2) * 64:(g % 2) * 64 + 64, (g // 2) * 128:(g // 2 + 1) * 128],
                                       v_ln[:, g * 64:(g + 1) * 64], wTsg[:, g, :], start=True, stop=True)
                    return ins
                p.pe(mixfn, [kq("v_ln"), "wTsg"], [bMk])
                yield
                bm3 = bM.rearrange("p (c t) -> p c t", t=128)
                p.dve(lambda e: e.tensor_tensor(out=mtmp, in0=bm3, in1=C("bsT").rearrange("p (c t) -> p c t", t=128),
                                                op=ALU.add), [bMk, "cst"], [kq("mtmp")])
                p.dve(lambda e: e.tensor_tensor(out=yT[0][:, :, bl * 128:(bl + 1) * 128], in0=mtmp,
                                                in1=uT_sb[:, :, bl * 128:(bl + 1) * 128], op=ALU.mult),
                      [kq("mtmp"), "uT"], ["yaT"])
            run_pipelined([a_block(bl) for bl in range(4)], depth=2)
            wcq_s, wcq_k = wload(lambda s: [(s[:, 0:8 * 384].rearrange("p (kc n) -> p kc n", n=384),
                                             win_d[:, C_CQ:C_CQ + 384].rearrange("(kc p) n -> p kc n", p=128))])
            wuq_s, wuq_k = wload(lambda s: [(s[:, 0:3 * 768].rearrange("p (kc n) -> p kc n", n=768),
                                             wuq_d[:, :].rearrange("(kc p) n -> p kc n", p=128))])
            wcq3 = wcq_s[:, 0:8 * 384].rearrange("p (kc n) -> p kc n", n=384)
            wuq3 = wuq_s[:, 0:3 * 768].rearrange("p (kc n) -> p kc n", n=768)

            def q_block(bl):
                i = t * 4 + bl
                s = bl % 2
                S = Qsets[s]
                cqn, cqnT, q_sb, qfin = S["cqn"], S["cqnT"], S["q_sb"], S["qfin"]
                kq = lambda n: "%s%d" % (n, s)
                ss1 = small[:, 32 + 4 * s:33 + 4 * s]
                sd1 = small[:, 33 + 4 * s:34 + 4 * s]
                rs1 = small[:, 34 + 4 * s:35 + 4 * s]
                bA, bAk = ps[:, 3 * s + 0, :], "ps%d" % (3 * s + 0)
                bB, bBk = ps[:, 3 * s + 1, :], "ps%d" % (3 * s + 1)
                bC, bCk = ps[:, 3 * s + 2, :], "ps%d" % (3 * s + 2)
                mm_group(bA[:, 0:384], bAk, [(hT[:, kc, bl * 128:(bl + 1) * 128], wcq3[:, kc, :]) for kc in range(8)],
                         [wcq_k, "hT"])
                yield
                p.act(lambda e: e.activation(out=junk[:, 0:384], in_=bA[:, 0:384], func=AF.Square, accum_out=ss1),
                      [bAk], [kq("ss1"), "junk"])
                p.act(lambda e: e.activation(out=sd1, in_=ss1, func=AF.Sqrt, bias=EPS, scale=1.0 / 384), [kq("ss1")], [kq("sd1")])
                p.dve(lambda e: e.reciprocal(out=rs1, in_=sd1), [kq("sd1")], [kq("rs1")])
                p.dve(lambda e: e.scalar_tensor_tensor(out=cqn, in0=bA[:, 0:384], scalar=rs1, in1=C("cqg"),
                                                       op0=ALU.mult, op1=ALU.mult), [bAk, kq("rs1"), "cst"], [kq("cqn")])
                yield
                bb = transposes(bB, bBk, [cqn[:, k * 128:(k + 1) * 128] for k in range(3)], [kq("cqn")])
                p.act(lambda e: e.activation(out=cqnT, in_=bb[:, 0:3, :], func=AF.Copy), [bBk], [kq("cqnT")])
                yield
                for hb, (bx, bxk) in enumerate(((bC, bCk), (bA, bAk))):
                    mm_group(bx[:, 0:384], bxk, [(cqnT[:, k, :], wuq3[:, k, hb * 384:(hb + 1) * 384]) for k in range(3)],
                             [kq("cqnT"), wuq_k])
                    p.act(lambda e, bx=bx, hb=hb: e.activation(
                        out=q_sb[:, hb * 4:(hb + 1) * 4, :], in_=bx[:, 0:384].rearrange("p (h d) -> p h d", d=96),
                        func=AF.Copy), [bxk], [kq("q_sb")])
                yield
                head_norm(hn2, q_sb, 8, 96, "qg", qfin, kq("q_sb"), kq("qfin"), rope_i=i)
                yield
                bb2 = transposes(bB, bBk, [qfin[:, h, :] for h in range(8)], [kq("qfin")], rows=96)
                p.act(lambda e: e.activation(out=QT[0:96, :, bl * 128:(bl + 1) * 128], in_=bb2[0:96, :, :],
                                             func=AF.Copy), [bBk], ["QT"])
            run_pipelined([q_block(bl) for bl in range(4)], depth=2)
            wqm_s, wqm_k = wload(lambda s: [(s.rearrange("p (kc n) -> p kc n", n=512),
                                             win_d[:, C_QM:C_QM + 512].rearrange("(kc p) n -> p kc n", p=128))])
            wqm3 = wqm_s.rearrange("p (kc n) -> p kc n", n=512)

            def c_block(bl):
                s = bl % 2
                S = Csets[s]
                qm_sb, qmn = S["qm_sb"], S["qmn"]
                kq = lambda n: "%s%d" % (n, s)
                bA, bAk = ps[:, 3 * s + 0, :], "ps%d" % (3 * s + 0)
                bB, bBk = ps[:, 3 * s + 1, :], "ps%d" % (3 * s + 1)
                mm_group(bA, bAk, [(hT[:, kc, bl * 128:(bl + 1) * 128], wqm3[:, kc, :]) for kc in range(8)],
                         [wqm_k, "hT"])
                yield
                p.act(lambda e: e.activation(out=qm_sb.rearrange("p a b -> p (a b)"), in_=bA, func=AF.Copy),
                      [bAk], [kq("qm_sb")])
                yield
                head_norm(hn2, qm_sb, 4, 128, "mqg", qmn, kq("qm_sb"), kq("qmn"))
                yield
                bb = transposes(bB, bBk, [qmn[:, h, :] for h in range(4)], [kq("qmn")])
                p.act(lambda e: e.activation(out=qmT[:, :, bl * 128:(bl + 1) * 128], in_=bb[:, 0:4, :],
                                             func=AF.Copy), [bBk], ["qmT"])
            run_pipelined([c_block(bl) for bl in range(4)], depth=2)
            for h in range(4):
                Ps = []
                for mc in range(2):
                    bs, bsk = gen_banks.next()
                    mm_group(bs, bsk, [(KmemT[:, h, mc * 128:(mc + 1) * 128], qmT[:, h, :])], ["KmemT", "qmT"])
                    Pt, Pk = P_ring.next()
                    p.act(lambda e, bs=bs, Pt=Pt: e.activation(out=Pt, in_=bs, func=AF.Exp, scale=SC_C), [bsk], [Pk])
                    Ps.append((Pt, Pk))
                bo, bok = gen_banks.next()
                mm_group(bo, bok, [(Vmem[:, mc, h * 128:(h + 1) * 128], Ps[mc][0]) for mc in range(2)],
                         ["Vmem"] + [k for _, k in Ps])
                bd, bdk = gen_banks.next()
                mm_group(bd, bdk, [(ones, Ps[mc][0]) for mc in range(2)], ["cst"] + [k for _, k in Ps])
                rd, rdk = rden_ring.next()
                p.dve(lambda e, rd=rd, bd=bd: e.reciprocal(out=rd, in_=bd), [bdk], [rdk])
                p.dve(lambda e, rd=rd, bo=bo, h=h: e.tensor_tensor(out=yT[2][:, h, :], in0=bo, in1=rd, op=ALU.mult),
                      [bok, rdk], ["ycT"])
            nki = 4 * t + 4
            G = {0: 4, 1: 2}.get(t, 1)
            pend = []
            chunk_ctr = [0, 0]

            def rec_pv(item):
                (Pt, Pk, c0, Vc, vk, vi, ob, obk, first, last, fin) = item
                p.pe(lambda e: e.matmul(ob[:, c0:512], Vc[:, vi, :], Pt[:, c0:512], start=first, stop=last),
                     [Pk, vk], [obk])
                if last:
                    fin()
            for h in range(8):
                par = h % 2
                ob, obk = o_banks.next()
                olo = 0 if par == 0 else 64
                dlo = 64 - olo
                vlo = olo

                def fin(ob=ob, obk=obk, olo=olo, dlo=dlo, h=h):
                    rd, rdk = rden_ring.next()
                    p.dve(lambda e: e.reciprocal(out=rd[dlo:dlo + 64, :], in_=ob[dlo:dlo + 64, :]), [obk], [rdk])
                    p.dve(lambda e: e.tensor_tensor(out=yT[1][olo:olo + 64, h // 2, :], in0=ob[olo:olo + 64, :],
                                                    in1=rd[dlo:dlo + 64, :], op=ALU.mult), [obk, rdk], ["ybT"])
                npv = 0
                ntot = 4 * nki
                for c in range(4 // G):
                    slot = par * 2 + (chunk_ctr[par] % 2)
                    chunk_ctr[par] += 1
                    Kc = Kring_v[slot]
                    Vc = Vring_v[slot]
                    kk, vk = "K%d" % slot, "V%d" % slot
                    ranks = [c * G + gi for gi in range(G)]
                    p.dma("pool", [(Kc[0:96, gi * nki * 128 + tt * 512:gi * nki * 128 + (tt + 1) * 512],
                                    kout_t[tt].ap()[r * 768 + h * 96:r * 768 + (h + 1) * 96, :])
                                   for gi, r in enumerate(ranks) for tt in range(t + 1)],
                          ["kout%d" % tt for tt in range(t + 1)], [kk], kk)
                    p.dma("pool", [(Vc[:, gi * nki + 4 * tt:gi * nki + 4 * tt + 4, vlo:vlo + 64],
                                    vout_t[tt].ap()[r * 1024 + h * 128:r * 1024 + (h + 1) * 128, :].rearrange(
                                        "p (i d) -> p i d", d=64))
                                   for gi, r in enumerate(ranks) for tt in range(t + 1)],
                          ["vout%d" % tt for tt in range(t + 1)], [vk], vk)
                    for gi, r in enumerate(ranks):
                        for ki in range(nki):
                            d = ki - 4 * t
                            c0 = 128 * d if d > 0 else 0
                            kcol = (gi * nki + ki) * 128
                            bs, bsk = gen_banks.next()
                            p.pe(lambda e, bs=bs, Kc=Kc, kcol=kcol, c0=c0, h=h: e.matmul(
                                bs[:, c0:512], Kc[0:96, kcol:kcol + 128], QT[0:96, h, c0:512], start=True, stop=True),
                                [kk, "QT"], [bsk])
                            Pt, Pk = P_ring.next()
                            p.act(lambda e, bs=bs, Pt=Pt, c0=c0: e.activation(out=Pt[:, c0:512], in_=bs[:, c0:512],
                                                                             func=AF.Exp, scale=SC_B), [bsk], [Pk])
                            if d >= 0:
                                mk = masks[:, (ki % 2) * 4 + r, :]
                                p.dve(lambda e, Pt=Pt, c0=c0, mk=mk: e.tensor_tensor(
                                    out=Pt[:, c0:c0 + 128], in0=Pt[:, c0:c0 + 128], in1=mk, op=ALU.mult), [Pk, "cst"], [Pk])
                            pend.append((Pt, Pk, c0, Vc, vk, gi * nki + ki, ob, obk, npv == 0, npv == ntot - 1, fin))
                            npv += 1
                            if len(pend) > 4:
                                rec_pv(pend.pop(0))
            while pend:
                rec_pv(pend.pop(0))
            if stop == "dump_y%d" % t:
                for br in range(3):
                    p.dma("pool", [(oT_d[0:512, br * 512:(br + 1) * 512].rearrange("(kc p) t -> p kc t", p=128), yT[br])],
                          [("yaT", "ybT", "ycT")[br]], ["dbgy%d" % br], "dbg", final=True)
                return finish(store=False)
            for m in range(8):
                gs, gk = wload(lambda s, m=m: [
                    (s[:, 0:3072].rearrange("p (b kc n) -> p b kc n", b=3, n=128)[:, br],
                     win_d[:, C_GATE + br * 1024 + m * 128:C_GATE + br * 1024 + (m + 1) * 128].rearrange(
                         "(kc p) n -> p kc n", p=128)) for br in range(3)])
                bs_, bk_ = wload(lambda s, m=m: [
                    (s[:, 0:1536].rearrange("p (b kc n) -> p b kc n", b=3, n=128)[:, br],
                     wbr_d[br][:, m * 128:(m + 1) * 128].rearrange("(kc p) n -> p kc n", p=128)) for br in range(3)])
                g4 = gs[:, 0:3072].rearrange("p (b kc n) -> p b kc n", b=3, n=128)
                b4 = bs_[:, 0:1536].rearrange("p (b kc n) -> p b kc n", b=3, n=128)
                for br in range(3):
                    bg, bgk = gen_banks.next()
                    mm_group(bg, bgk, [(g4[:, br, kc, :], hT[:, kc, :]) for kc in range(8)], [gk, "hT"])
                    p.act(lambda e, bg=bg, br=br, m=m: e.activation(out=g_sb[:, br, :], in_=bg, func=AF.Sigmoid,
                                                                   bias=C("bgate")[:, br * 8 + m:br * 8 + m + 1], scale=1.0),
                          [bgk, "cst"], ["g_sb%d" % br])
                ykeys = ["yaT", "ybT", "ycT"]
                bbs = []
                for br in range(3):
                    bb_, bbk = gen_banks.next()
                    mm_group(bb_, bbk, [(b4[:, br, kc, :], yT[br][:, kc, :]) for kc in range(4)], [bk_, ykeys[br]])
                    bbs.append((bb_, bbk))
                p.dve(lambda e, b=bbs[0][0]: e.tensor_tensor(out=acc1, in0=b, in1=g_sb[:, 0, :], op=ALU.mult),
                      [bbs[0][1], "g_sb0"], ["rd0"])
                p.dve(lambda e, b=bbs[1][0]: e.tensor_tensor(out=acc2, in0=b, in1=g_sb[:, 1, :], op=ALU.mult),
                      [bbs[1][1], "g_sb1"], ["rd1"])
                p.dve(lambda e: e.tensor_tensor(out=acc1, in0=acc1, in1=acc2, op=ALU.add), ["rd0", "rd1"], ["rd0"])
                p.dve(lambda e, b=bbs[2][0]: e.tensor_tensor(out=acc2, in0=b, in1=g_sb[:, 2, :], op=ALU.mult),
                      [bbs[2][1], "g_sb2"], ["rd1"])
                p.dve(lambda e, m=m: e.tensor_tensor(out=mergedT[:, m, :], in0=acc1, in1=acc2, op=ALU.add),
                      ["rd0", "rd1"], ["mergedT"])
            for hf in range(2):
                ws, wk_ = wload(lambda s, hf=hf: [(s.rearrange("p (kc n) -> p kc n", n=512),
                                                   wout_d[:, hf * 512:(hf + 1) * 512].rearrange("(kc p) n -> p kc n", p=128))])
                w3 = ws.rearrange("p (kc n) -> p kc n", n=512)
                for mm in range(4):
                    m = hf * 4 + mm
                    bo, bok = gen_banks.next()
                    mm_group(bo, bok, [(w3[:, kc, mm * 128:(mm + 1) * 128], mergedT[:, kc, :]) for kc in range(8)],
                             [wk_, "mergedT"])
                    xs = xT[:, m, t * 512:(t + 1) * 512]
                    p.dve(lambda e, bo=bo, xs=xs: e.tensor_tensor(out=xs, in0=bo, in1=xs, op=ALU.add), [bok, xk], [xk])

        if stop != "mid":
            ffn(w2gu_d, w2dn_d, "ffn2_g")
        return finish()


def _perm_rows(j):
    rows = []
    for i in range(16):
        g = i // 2
        blk = 8 * g + (j if i % 2 == 0 else 7 - j)
        rows.append(np.arange(blk * 128, (blk + 1) * 128))
    return np.concatenate(rows)


def _host_inputs(inp):
    f = lambda a: np.ascontiguousarray(np.asarray(a, dtype=np.float32))
    x = f(inp["x"])
    mem = f(inp["mem"])
    pos = np.asarray(inp["positions"]).astype(np.int32)
    L0 = lambda k: f(inp[k])[0]

    def col(v, n):
        return np.ascontiguousarray(v.reshape(n, 128).T)

    def rep(v):
        return np.ascontiguousarray(np.broadcast_to(v[None, :], (128, v.shape[0])))

    cst = np.zeros((128, NCST), np.float32)

    def put(name, arr):
        o, w = CST[name]
        assert arr.shape == (128, w), (name, arr.shape)
        cst[:, o:o + w] = arr
    put("ffn1_g", col(L0("ffn1_norm"), 8))
    put("mix_g", col(L0("mix_norm"), 8))
    put("ffn2_g", col(L0("ffn2_norm"), 8))
    put("mem_g", col(L0("mem_norm"), 8))
    put("bgate", col(L0("b_gate"), 24))
    put("cqg", rep(L0("mla_cq_norm")))
    put("ckvg", rep(L0("mla_ckv_norm")))
    put("lng", rep(L0("sg_ln_g")))
    put("lnb", rep(L0("sg_ln_b")))
    put("qg", rep(L0("mla_q_norm")))
    put("kg", rep(L0("mla_k_norm")))
    put("mqg", rep(L0("mem_q_norm")))
    put("mkg", rep(L0("mem_k_norm")))
    sgb = L0("sg_b")
    bsT = np.zeros((128, 4, 128), np.float32)
    for c in range(4):
        bsT[0:64, c, :] = sgb[2 * c][None, :]
        bsT[64:128, c, :] = sgb[2 * c + 1][None, :]
    put("bsT", bsT.reshape(128, 512))
    half = 16
    invf = (10000.0 ** (-np.arange(half, dtype=np.float32) / half)).astype(np.float32)
    put("invf", rep(invf))
    tri = (np.arange(128)[:, None] <= np.arange(128)[None, :]).astype(np.float32)
    put("tri", tri)

    sgw = L0("sg_w")
    sgwT = np.ascontiguousarray(sgw.transpose(2, 0, 1)).reshape(128, 8 * 128)

    shared = {
        "cst": None, "sgwT": sgwT,
        "ffn1_w_gu": L0("ffn1_w_gu"), "ffn1_w_down": L0("ffn1_w_down"),
        "ffn2_w_gu": L0("ffn2_w_gu"), "ffn2_w_down": L0("ffn2_w_down"),
        "w_in": L0("w_in"), "mla_w_uq": L0("mla_w_uq"), "mla_w_ukv": L0("mla_w_ukv"),
        "mem_w_kv": L0("mem_w_kv"), "w_branch_a": L0("w_branch_a"), "w_branch_b": L0("w_branch_b"),
        "w_branch_c": L0("w_branch_c"), "w_out": L0("w_out"),
    }
    in_maps = []
    perms = []
    for c in range(NCORES):
        b, j = c // 4, c % 4
        rows = _perm_rows(j)
        perms.append((b, rows))
        m = dict(shared)
        m["cst"] = cst
        m["xT"] = np.ascontiguousarray(x[b][rows].T)
        m["memT"] = np.ascontiguousarray(mem[b].T)
        m["pos"] = np.ascontiguousarray(pos[b][rows].reshape(16, 128).T)
        cbf = np.zeros((128, NCBF), np.float32)
        cbf[:, 0:128] = np.eye(128, dtype=np.float32)
        cbf[:, 128:256] = 1.0
        mk = np.zeros((128, 8, 128), np.float32)
        for r in range(4):
            mk[:, r, :] = 1.0 if r < j else (tri if r == j else 0.0)
            mk[:, 4 + r, :] = 1.0 if r > j else (tri if r == j else 0.0)
        cbf[:, 256:] = mk.reshape(128, 1024)
        m["cbf"] = cbf.astype(ml_dtypes.bfloat16)
        in_maps.append(m)
    return in_maps, perms


_NC_CACHE = {}


def kernel(**inputs):
    in_maps, perms = _host_inputs(inputs)
    if "nc" not in _NC_CACHE:
        _NC_CACHE["nc"] = build()
    nc = _NC_CACHE["nc"]
    res = run_bass_kernel_spmd(nc, in_maps, core_ids=list(range(NCORES)))
    out = np.zeros((2, 8192, D), np.float32)
    for c in range(NCORES):
        b, rows = perms[c]
        out[b, rows, :] = np.asarray(res.results[c]["oT"], dtype=np.float32).T
    return out
```

```python
import numpy as np
import ml_dtypes
from contextlib import ExitStack
import concourse.bass as bass
import concourse.mybir as mybir
from concourse.bass_utils import run_bass_kernel_spmd

F32 = mybir.dt.float32
BF16 = mybir.dt.bfloat16
I32 = mybir.dt.int32
AF = mybir.ActivationFunctionType
ALU = mybir.AluOpType
AX = mybir.AxisListType

NCORES = 8
D = 1024
NT = 2048
DFF = 2816
FC = 22
EPS = 1e-6
C_U, C_V, C_CQ, C_CKV, C_KR, C_QM, C_GATE = 0, 512, 1024, 1408, 1664, 1696, 2208
PI = float(np.pi)

CST = {}
_off = 0
for _n, _w in [("ffn1_g", 8), ("mix_g", 8), ("ffn2_g", 8), ("mem_g", 8), ("bgate", 24), ("cqg", 384),
               ("ckvg", 256), ("lng", 512), ("lnb", 512), ("qg", 96), ("kg", 96), ("mqg", 128),
               ("mkg", 128), ("bsT", 512), ("invf", 16), ("tri", 128)]:
    CST[_n] = (_off, _w)
    _off += _w
NCST = _off
NCBF = 128 + 128 + 8 * 128


class Prog:
    ENG = ("pe", "act", "dve", "pool", "sp")

    def __init__(self, nc, es):
        self.nc, self.es = nc, es
        self.q = {e: [] for e in self.ENG}
        self.sems = {}
        self.cnt = {}
        self.waited = {e: {} for e in self.ENG}
        self.lastw = {}
        self.readers = {}
        self.out_tokens = []
        self.regions = {}
        self._ovc = {}

    def sem(self, name):
        if name not in self.sems:
            self.sems[name] = self.es.enter_context(self.nc.semaphore("s_" + name))
            self.cnt[name] = 0
        return self.sems[name]

    def region(self, key, ivs):
        self.regions[key] = list(ivs)

    def _overlap(self, k):
        if k not in self.regions:
            return (k,)
        c = self._ovc.get(k)
        if c is not None and c[0] == len(self.regions):
            return c[1]
        mine = self.regions[k]
        res = [k]
        for k2, ivs in self.regions.items():
            if k2 == k:
                continue
            hit = False
            for (a, b) in mine:
                for (c0, d0) in ivs:
                    if a < d0 and c0 < b:
                        hit = True
                        break
                if hit:
                    break
            if hit:
                res.append(k2)
        self._ovc[k] = (len(self.regions), tuple(res))
        return self._ovc[k][1]

    def _collect(self, eng, reads, writes, is_dma):
        toks = []
        for k in reads:
            for k2 in self._overlap(k):
                if k2 in self.lastw:
                    toks.append((self.lastw[k2], True))
        for k in writes:
            for k2 in self._overlap(k):
                if k2 in self.lastw:
                    toks.append((self.lastw[k2], False))
                rd = self.readers.get(k2)
                if rd:
                    toks.extend(((sn, v, te), False) for (sn, te), v in rd.items())
        waits = {}
        for ((sn, val, teng), raw) in toks:
            if teng is not None and teng == eng and not is_dma:
                if not raw or eng == "pe":
                    continue
            if self.waited[eng].get(sn, 0) >= val:
                continue
            waits[sn] = max(waits.get(sn, 0), val)
        for sn, val in waits.items():
            self.waited[eng][sn] = val
        return [(self.sem(sn), val, sn) for sn, val in waits.items()]

    def _commit(self, tok, reads, writes):
        sn, val, te = tok
        for k in reads:
            d = self.readers.setdefault(k, {})
            d[(sn, te)] = max(d.get((sn, te), 0), val)
        for k in writes:
            self.lastw[k] = tok
            self.readers[k] = {}

    def op(self, eng, fn, reads=(), writes=()):
        waits = self._collect(eng, reads, writes, False)
        sn = "E" + eng
        self.sem(sn)
        self.cnt[sn] += 1
        tok = (sn, self.cnt[sn], eng)
        self.q[eng].append((waits, fn, [(self.sems[sn], 1)], [(sn, 1)]))
        self._commit(tok, reads, writes)

    def pe(self, fn, r=(), w=()):
        self.op("pe", fn, r, w)

    def act(self, fn, r=(), w=()):
        self.op("act", fn, r, w)

    def dve(self, fn, r=(), w=()):
        self.op("dve", fn, r, w)

    def pool(self, fn, r=(), w=()):
        self.op("pool", fn, r, w)

    def dma(self, queue, pairs, reads, writes, semname, final=False):
        waits = self._collect(queue, reads, writes, True)
        s = self.sem(semname)
        self.cnt[semname] += 16 * len(pairs)
        tok = (semname, self.cnt[semname], None)

        def fn(e, pairs=pairs, s=s):
            for (o, i) in pairs:
                e.dma_start(out=o, in_=i).then_inc(s, 16)
            return None
        self.q[queue].append((waits, fn, [], [(semname, 16 * len(pairs))]))
        self._commit(tok, reads, writes)
        if final:
            self.out_tokens.append(tok)

    def collective(self, fn, reads, writes, semname="cc"):
        waits = self._collect("pool", reads, writes, True)
        s = self.sem(semname)
        self.cnt[semname] += 1
        tok = (semname, self.cnt[semname], None)
        self.q["pool"].append((waits, fn, [(s, 1)], [(semname, 1)]))
        self._commit(tok, reads, writes)

    def check(self):
        val = {}
        pos = {e: 0 for e in self.ENG}
        prog = True
        while prog:
            prog = False
            for e in self.ENG:
                while pos[e] < len(self.q[e]):
                    waits, fn, incs, names = self.q[e][pos[e]]
                    if all(val.get(sn, 0) >= v for (_s, v, sn) in waits):
                        for (sn, n) in names:
                            val[sn] = val.get(sn, 0) + n
                        pos[e] += 1
                        prog = True
                    else:
                        break
        stuck = {e: pos[e] for e in self.ENG if pos[e] < len(self.q[e])}
        for e, i in stuck.items():
            waits = self.q[e][i][0]
            print("DEADLOCK", e, "op", i, "of", len(self.q[e]), "waits",
                  [(sn, v, val.get(sn, 0)) for (_s, v, sn) in waits if val.get(sn, 0) < v])
        return not stuck

    def emit(self, block):
        assert self.check(), "semaphore protocol deadlock"
        def run(e, eng):
            for (waits, fn, incs, _n) in self.q[eng]:
                for (s, v, _sn) in waits:
                    e.wait_ge(s, v)
                ins = fn(e)
                for (s, n) in incs:
                    ins.then_inc(s, n)
            if eng == "sp":
                for (sn, val, _) in self.out_tokens:
                    e.wait_ge(self.sems[sn], val)

        @block.tensor
        def _(e):
            run(e, "pe")

        @block.scalar
        def _(e):
            run(e, "act")

        @block.vector
        def _(e):
            run(e, "dve")

        @block.gpsimd
        def _(e):
            run(e, "pool")

        @block.sync
        def _(e):
            run(e, "sp")


class Ring:
    def __init__(self, name, views):
        self.name, self.views, self.i = name, views, 0

    def next(self):
        k = self.i % len(self.views)
        self.i += 1
        return self.views[k], "%s%d" % (self.name, k)


def build(stop=None):
    nc = bass.Bass("TRN2", target_bir_lowering=False)

    def din(name, shape, dt=F32):
        return nc.dram_tensor(name, shape, dt, kind="ExternalInput").ap()

    xT_d = din("xT", [D, NT])
    memT_d = din("memT", [D, 256])
    pos_d = din("pos", [128, 16], I32)
    cst_d = din("cst", [128, NCST])
    cbf_d = din("cbf", [128, NCBF], BF16)
    sgwT_d = din("sgwT", [128, 8 * 128])
    w1gu_d = din("ffn1_w_gu", [D, 2 * DFF])
    w1dn_d = din("ffn1_w_down", [DFF, D])
    w2gu_d = din("ffn2_w_gu", [D, 2 * DFF])
    w2dn_d = din("ffn2_w_down", [DFF, D])
    win_d = din("w_in", [D, 5280])
    wuq_d = din("mla_w_uq", [384, 768])
    wukv_d = din("mla_w_ukv", [256, 1024])
    wmkv_d = din("mem_w_kv", [D, 1024])
    wbr_d = [din("w_branch_a", [512, D]), din("w_branch_b", [512, D]), din("w_branch_c", [512, D])]
    wout_d = din("w_out", [D, D])
    oT_d = nc.dram_tensor("oT", [D, NT], F32, kind="ExternalOutput").ap()
    kin_t = [nc.dram_tensor("kin%d" % c, [768, 512], BF16) for c in range(4)]
    kout_t = [nc.dram_tensor("kout%d" % c, [4 * 768, 512], BF16) for c in range(4)]
    vin_t = [nc.dram_tensor("vin%d" % c, [1024, 256], BF16) for c in range(4)]
    vout_t = [nc.dram_tensor("vout%d" % c, [4 * 1024, 256], BF16) for c in range(4)]

    es = ExitStack()
    with es:
        p = Prog(nc, es)

        def sb(name, shape, dt):
            return es.enter_context(nc.sbuf_tensor(name, shape, dt))

        xT = sb("xT_sb", [128, 8, NT], F32)
        cst = sb("cst_sb", [128, NCST], F32)
        cbf = sb("cbf_sb", [128, NCBF], BF16)
        csA = sb("csA", [128, 16, 32], F32)
        csB = sb("csB", [128, 16, 32], F32)
        wTsg = sb("wTsg", [128, 8, 128], BF16)
        KmemT = sb("KmemT", [128, 4, 256], BF16)
        Vmem = sb("Vmem", [128, 2, 512], BF16)
        wr_t = sb("wring", [128, 3, 4096], BF16)
        ARENA = 49300
        ar = sb("arena", [128, ARENA], BF16)
        ps = es.enter_context(nc.psum_tensor("ps", [128, 8, 512], F32))

        def C(name):
            o, w = CST[name]
            return cst[:, o:o + w]

        ident = cbf[:, 0:128]
        ones = cbf[:, 128:256]
        masks = cbf[:, 256:256 + 1024].rearrange("p (a b) -> p a b", b=128)

        class Carver:
            def __init__(self):
                self.off = 0
                self.hi = 0

            def take(self, shape, dt, key=None):
                n = int(np.prod(shape)) * (2 if dt in (F32, I32) else 1)
                n = (n + 1) // 2 * 2
                o = self.off
                assert o + n <= ARENA, (o, n, key)
                a = ar[:, o:o + n]
                self.off += n
                self.hi = max(self.hi, self.off)
                if key is not None:
                    p.region(key, [(o, o + n)])
                if dt in (F32, I32):
                    a = a.bitcast(dt)
                if len(shape) == 2:
                    a = a.rearrange("p (a b) -> p a b", b=shape[1])
                elif len(shape) == 3:
                    a = a.rearrange("p (a b c) -> p a b c", b=shape[1], c=shape[2])
                return a

        cv = Carver()
        sq_ring = Ring("sq", [cv.take([512], BF16) for _ in range(2)])
        std_t = cv.take([512], F32)
        rstd_t = cv.take([512], F32)
        junk = cv.take([512], BF16)
        small = cv.take([128], F32)
        base_off = cv.off

        def mk_hn(pfx):
            return dict(pfx=pfx, sq=cv.take([8, 96], F32, pfx + "hn_sq"), rt=cv.take([8, 32], F32, pfx + "hn_rt"),
                        t1=cv.take([8, 32], F32, pfx + "hn_t1"), t2=cv.take([8, 32], F32, pfx + "hn_t2"))

        gen_banks = Ring("ps", [ps[:, b, :] for b in range(6)])
        o_banks = Ring("po", [ps[:, 6, :], ps[:, 7, :]])
        wring = Ring("w", [wr_t[:, s, :] for s in range(3)])

        def wload(pairs_fn):
            slot, key = wring.next()
            p.dma("pool", pairs_fn(slot), reads=[], writes=[key], semname=key)
            return slot, key

        def bf_bank(bank):
            return bank.bitcast(BF16).rearrange("p (a b) -> p a b", b=128)

        p.dma("sp", [(cst[:, :], cst_d[:, :]), (cbf[:, :], cbf_d[:, :])], [], ["cst"], "cst")
        for t in range(4):
            p.dma("sp", [(xT[:, :, t * 512:(t + 1) * 512],
                          xT_d[:, t * 512:(t + 1) * 512].rearrange("(kc p) t -> p kc t", p=128))],
                  [], ["x%d" % t], "xin%d" % t)

        cv.off = base_off
        pos_i = cv.take([16], I32, "pos")
        posf = cv.take([16], F32, "posf")
        angA = cv.take([16, 32], F32, "angA")
        angT = cv.take([16, 32], F32, "angT")
        angI = cv.take([16, 32], I32, "angI")
        angF = cv.take([16, 32], F32, "angF")
        angM = cv.take([16, 32], F32, "angM")
        SC = cv.take([16, 32], F32, "SC")
        sgtmp = cv.take([8, 128], F32, "sgtmp")
        _save = cv.off
        cv.off = 36640
        memT = cv.take([8, 256], F32, "memT")
        memnT = cv.take([8, 256], BF16, "memnT")
        km_sb = cv.take([4, 128], F32, "km_sb")
        kmn = cv.take([4, 128], BF16, "kmn")
        hn0 = mk_hn("s_")
        cv.off = _save

        p.dma("sp", [(pos_i, pos_d[:, :])], [], ["pos"], "misc")
        p.dma("sp", [(sgtmp, sgwT_d[:, :].rearrange("p (g t) -> p g t", t=128))], [], ["sgtmp"], "misc2")
        p.dma("sp", [(memT, memT_d[:, :].rearrange("(kc p) m -> p kc m", p=128))], [], ["memT"], "misc3")

        p.dve(lambda e: e.tensor_copy(out=posf, in_=pos_i), ["pos"], ["posf"])
        for i in range(16):
            p.dve(lambda e, i=i: e.tensor_scalar(out=angA[:, i, 0:16], in0=C("invf"), scalar1=posf[:, i:i + 1],
                                                 scalar2=None, op0=ALU.mult), ["posf", "cst"], ["angA"])
        p.dve(lambda e: e.tensor_scalar(out=angA[:, :, 16:32], in0=angA[:, :, 0:16], scalar1=PI / 2, scalar2=None,
                                        op0=ALU.add), ["angA"], ["angA"])
        p.dve(lambda e: e.tensor_scalar(out=angT, in0=angA, scalar1=1.0 / (2 * PI), scalar2=None, op0=ALU.mult),
              ["angA"], ["angT"])
        p.dve(lambda e: e.tensor_copy(out=angI, in_=angT), ["angT"], ["angI"])
        p.dve(lambda e: e.tensor_copy(out=angF, in_=angI), ["angI"], ["angF"])
        p.dve(lambda e: e.scalar_tensor_tensor(out=angT, in0=angF, scalar=-2 * PI, in1=angA, op0=ALU.mult,
                                               op1=ALU.add), ["angF", "angA"], ["angT"])
        p.dve(lambda e: e.tensor_scalar(out=angM, in0=angT, scalar1=PI, scalar2=None, op0=ALU.is_gt),
              ["angT"], ["angM"])
        p.dve(lambda e: e.scalar_tensor_tensor(out=angF, in0=angM, scalar=-2 * PI, in1=angT, op0=ALU.mult,
                                               op1=ALU.add), ["angM", "angT"], ["angF"])
        p.dve(lambda e: e.tensor_scalar(out=angM, in0=angF, scalar1=-PI, scalar2=None, op0=ALU.is_lt),
              ["angF"], ["angM"])
        p.dve(lambda e: e.scalar_tensor_tensor(out=angT, in0=angM, scalar=2 * PI, in1=angF, op0=ALU.mult,
                                               op1=ALU.add), ["angM", "angF"], ["angT"])
        p.dve(lambda e: e.tensor_scalar(out=angF, in0=angT, scalar1=PI, scalar2=-PI, op0=ALU.min, op1=ALU.max),
              ["angT"], ["angF"])
        p.act(lambda e: e.activation(out=SC, in_=angF, func=AF.Sin), ["angF"], ["SC"])
        p.dve(lambda e: e.tensor_copy(out=csA[:, :, 0:16], in_=SC[:, :, 16:32]), ["SC"], ["csA"])
        p.dve(lambda e: e.tensor_copy(out=csA[:, :, 16:32], in_=SC[:, :, 16:32]), ["SC"], ["csA"])
        p.dve(lambda e: e.tensor_scalar(out=csB[:, :, 0:16], in0=SC[:, :, 0:16], scalar1=-1.0, scalar2=None,
                                        op0=ALU.mult), ["SC"], ["csB"])
        p.dve(lambda e: e.tensor_copy(out=csB[:, :, 16:32], in_=SC[:, :, 0:16]), ["SC"], ["csB"])
        p.dve(lambda e: e.tensor_tensor(out=wTsg[:, :, :], in0=sgtmp,
                                        in1=C("tri").unsqueeze(1).broadcast_to([128, 8, 128]), op=ALU.mult),
              ["sgtmp", "cst"], ["wTsg"])

        def norm_fm(src, srckeys, gname, dst, dstkey, N, bank=None):
            bank, bkey = bank if bank is not None else gen_banks.next()
            for kc in range(8):
                sq, sqk = sq_ring.next()
                s_ap = src(kc)
                p.act(lambda e, s_ap=s_ap, sq=sq: e.activation(out=sq[:, :N], in_=s_ap, func=AF.Square),
                      srckeys, [sqk])
                p.pe(lambda e, kc=kc, sq=sq: e.matmul(bank[:, :N], ones, sq[:, :N], start=(kc == 0), stop=(kc == 7)),
                     [sqk, "cst"], [bkey])
            p.act(lambda e: e.activation(out=std_t[:, :N], in_=bank[:, :N], func=AF.Sqrt, bias=EPS, scale=1.0 / D),
                  [bkey], ["std"])
            p.dve(lambda e: e.reciprocal(out=rstd_t[:, :N], in_=std_t[:, :N]), ["std"], ["rstd"])
            g = C(gname)
            for kc in range(8):
                s_ap, d_ap = src(kc), dst(kc)
                p.dve(lambda e, kc=kc, s_ap=s_ap, d_ap=d_ap: e.scalar_tensor_tensor(
                    out=d_ap, in0=s_ap, scalar=g[:, kc:kc + 1], in1=rstd_t[:, :N], op0=ALU.mult, op1=ALU.mult),
                    srckeys + ["rstd", "cst"], [dstkey])

        def head_norm(hn, src, H, Dh, gname, out_bf, key_in, key_out, rope_i=None):
            pf = hn["pfx"]
            rt, t1, t2 = hn["rt"], hn["t1"], hn["t2"]
            ksq, krt, kt1, kt2 = pf + "hn_sq", pf + "hn_rt", pf + "hn_t1", pf + "hn_t2"
            sqv = hn["sq"].rearrange("p a b -> p (a b)")[:, 0:H * Dh].rearrange("p (a b) -> p a b", b=Dh)
            ss = small[:, 0:H]
            sd = small[:, 8:8 + H]
            rs = small[:, 16:16 + H]
            g = C(gname)
            p.dve(lambda e: e.tensor_tensor(out=sqv, in0=src, in1=src, op=ALU.mult), [key_in], [ksq])
            p.dve(lambda e: e.tensor_reduce(out=ss, in_=sqv, axis=AX.X, op=ALU.add), [ksq], ["hn_ss"])
            p.act(lambda e: e.activation(out=sd, in_=ss, func=AF.Sqrt, bias=EPS, scale=1.0 / Dh), ["hn_ss"], ["hn_sd"])
            p.dve(lambda e: e.reciprocal(out=rs, in_=sd), ["hn_sd"], ["hn_rs"])
            p.dve(lambda e: e.tensor_tensor(out=sqv, in0=src, in1=rs.unsqueeze(2).broadcast_to([128, H, Dh]),
                                            op=ALU.mult), [key_in, "hn_rs"], [ksq])
            if rope_i is None:
                p.dve(lambda e: e.tensor_tensor(out=out_bf, in0=sqv, in1=g.unsqueeze(1).broadcast_to([128, H, Dh]),
                                                op=ALU.mult), [ksq, "cst"], [key_out])
                return
            i = rope_i
            p.dve(lambda e: e.tensor_tensor(out=out_bf[:, :, 0:64], in0=sqv[:, :, 0:64],
                                            in1=g[:, 0:64].unsqueeze(1).broadcast_to([128, H, 64]), op=ALU.mult),
                  [ksq, "cst"], [key_out])
            p.dve(lambda e: e.tensor_tensor(out=rt, in0=sqv[:, :, 64:96],
                                            in1=g[:, 64:96].unsqueeze(1).broadcast_to([128, H, 32]), op=ALU.mult),
                  [ksq, "cst"], [krt])
            p.dve(lambda e: e.tensor_tensor(out=t1, in0=rt,
                                            in1=csA[:, i, :].unsqueeze(1).broadcast_to([128, H, 32]), op=ALU.mult),
                  [krt, "csA"], [kt1])
            p.dve(lambda e: e.tensor_tensor(out=t2[:, :, 0:16], in0=rt[:, :, 16:32],
                                            in1=csB[:, i, 0:16].unsqueeze(1).broadcast_to([128, H, 16]), op=ALU.mult),
                  [krt, "csB"], [kt2])
            p.dve(lambda e: e.tensor_tensor(out=t2[:, :, 16:32], in0=rt[:, :, 0:16],
                                            in1=csB[:, i, 16:32].unsqueeze(1).broadcast_to([128, H, 16]), op=ALU.mult),
                  [krt, "csB"], [kt2])
            p.dve(lambda e: e.tensor_tensor(out=out_bf[:, :, 64:96], in0=t1, in1=t2, op=ALU.add),
                  [kt1, kt2], [key_out])

        def mm_group(bank_ap, bkey, items, reads):
            def fn(e):
                n = len(items)
                ins = None
                for k, (l, r) in enumerate(items):
                    ins = e.matmul(bank_ap, l, r, start=(k == 0), stop=(k == n - 1))
                return ins
            p.pe(fn, reads, [bkey])

        def transposes(bank, bkey, srcs, reads, rows=128):
            bb = bf_bank(bank)

            def fn(e):
                ins = None
                for j, s in enumerate(srcs):
                    ins = e.transpose(out=bb[0:rows, j, :], in_=s, identity=ident)
                return ins
            p.pe(fn, reads + ["cst"], [bkey])
            return bb

        def ffn(wgu_d, wdn_d, gname):
            cv.off = base_off
            ho = cv.off
            hT = cv.take([8, 1024], BF16)
            for sub in range(2):
                p.region("h%d" % sub, [(ho + kc * 1024 + sub * 512, ho + kc * 1024 + sub * 512 + 512) for kc in range(8)])
            ao = cv.off
            actT = cv.take([FC, 1024], BF16)
            for f in range(FC):
                for sub in range(2):
                    p.region("a%d_%d" % (f, sub), [(ao + f * 1024 + sub * 512, ao + f * 1024 + sub * 512 + 512)])
            sg_ring = Ring("sg", [cv.take([512], F32, "sg%d" % k) for k in range(2)])
            assert cv.off <= 36640, cv.off
            for half in range(2):
                for sub in range(2):
                    t = half * 2 + sub
                    norm_fm(lambda kc, t=t: xT[:, kc, t * 512:(t + 1) * 512], ["x%d" % t], gname,
                            lambda kc, sub=sub: hT[:, kc, sub * 512:(sub + 1) * 512], "h%d" % sub, 512)
                for pp in range(11):
                    slot, wkey = wload(lambda s, pp=pp: [
                        (s.rearrange("p (a kc n) -> p a kc n", a=2, n=256)[:, 0],
                         wgu_d[:, pp * 256:(pp + 1) * 256].rearrange("(kc p) n -> p kc n", p=128)),
                        (s.rearrange("p (a kc n) -> p a kc n", a=2, n=256)[:, 1],
                         wgu_d[:, DFF + pp * 256:DFF + (pp + 1) * 256].rearrange("(kc p) n -> p kc n", p=128))])
                    w4 = slot.rearrange("p (a kc n) -> p a kc n", a=2, n=256)
                    for fi in range(2):
                        f = pp * 2 + fi
                        for sub in range(2):
                            bg, bgk = gen_banks.next()
                            bu, buk = gen_banks.next()
                            hs = lambda kc, sub=sub: hT[:, kc, sub * 512:(sub + 1) * 512]
                            mm_group(bg, bgk, [(w4[:, 0, kc, fi * 128:(fi + 1) * 128], hs(kc)) for kc in range(8)],
                                     [wkey, "h%d" % sub])
                            mm_group(bu, buk, [(w4[:, 1, kc, fi * 128:(fi + 1) * 128], hs(kc)) for kc in range(8)],
                                     [wkey, "h%d" % sub])
                            sg, sgk = sg_ring.next()
                            p.act(lambda e, sg=sg, bg=bg: e.activation(out=sg, in_=bg, func=AF.Silu), [bgk], [sgk])
                            a_ap = actT[:, f, sub * 512:(sub + 1) * 512]
                            p.dve(lambda e, sg=sg, bu=bu, a_ap=a_ap: e.tensor_tensor(out=a_ap, in0=bu, in1=sg, op=ALU.mult),
                                  [sgk, buk], ["a%d_%d" % (f, sub)])
                for m in range(8):
                    slot, wkey = wload(lambda s, m=m: [
                        (s[:, 0:FC * 128].rearrange("p (f n) -> p f n", n=128),
                         wdn_d[:, m * 128:(m + 1) * 128].rearrange("(f p) n -> p f n", p=128))])
                    w3 = slot[:, 0:FC * 128].rearrange("p (f n) -> p f n", n=128)
                    for sub in range(2):
                        t = half * 2 + sub
                        bd, bdk = gen_banks.next()
                        mm_group(bd, bdk, [(w3[:, f, :], actT[:, f, sub * 512:(sub + 1) * 512]) for f in range(FC)],
                                 [wkey] + ["a%d_%d" % (f, sub) for f in range(FC)])
                        xs = xT[:, m, t * 512:(t + 1) * 512]
                        p.dve(lambda e, bd=bd, xs=xs: e.scalar_tensor_tensor(out=xs, in0=bd, scalar=0.5, in1=xs,
                                                                            op0=ALU.mult, op1=ALU.add),
                              [bdk, "x%d" % t], ["x%d" % t])

        def store_out():
            for t in range(4):
                p.dma("sp", [(oT_d[:, t * 512:(t + 1) * 512].rearrange("(kc p) t -> p kc t", p=128),
                              xT[:, :, t * 512:(t + 1) * 512])], ["x%d" % t], ["out%d" % t], "out", final=True)

        def finish(store=True):
            if store:
                store_out()
            with nc.Block() as block:
                p.emit(block)
            return nc

        if stop != "noffn1":
            ffn(w1gu_d, w1dn_d, "ffn1_g")
        norm_fm(lambda kc: memT[:, kc, :], ["memT"], "mem_g", lambda kc: memnT[:, kc, :], "memnT", 256)
        wk, wkk = wload(lambda s: [(s.rearrange("p (kc n) -> p kc n", n=512),
                                    wmkv_d[:, 0:512].rearrange("(kc p) n -> p kc n", p=128))])
        wv_, wvk = wload(lambda s: [(s.rearrange("p (kc n) -> p kc n", n=512),
                                     wmkv_d[:, 512:1024].rearrange("(kc p) n -> p kc n", p=128))])
        wk3 = wk.rearrange("p (kc n) -> p kc n", n=512)
        wv3 = wv_.rearrange("p (kc n) -> p kc n", n=512)
        for mb in range(2):
            bk, bkk = gen_banks.next()
            mm_group(bk, bkk, [(memnT[:, kc, mb * 128:(mb + 1) * 128], wk3[:, kc, :]) for kc in range(8)],
                     ["memnT", wkk])
            bv, bvk = gen_banks.next()
            mm_group(bv, bvk, [(memnT[:, kc, mb * 128:(mb + 1) * 128], wv3[:, kc, :]) for kc in range(8)],
                     ["memnT", wvk])
            p.act(lambda e, mb=mb, bv=bv: e.activation(out=Vmem[:, mb, :], in_=bv, func=AF.Copy), [bvk], ["Vmem"])
            p.act(lambda e, bk=bk: e.activation(out=km_sb.rearrange("p a b -> p (a b)"), in_=bk, func=AF.Copy),
                  [bkk], ["km_sb"])
            head_norm(hn0, km_sb, 4, 128, "mkg", kmn, "km_sb", "kmn")
            bt, btk = gen_banks.next()
            bb = transposes(bt, btk, [kmn[:, h, :] for h in range(4)], ["kmn"])
            p.act(lambda e, mb=mb, bb=bb: e.activation(out=KmemT[:, :, mb * 128:(mb + 1) * 128], in_=bb[:, 0:4, :],
                                                       func=AF.Copy), [btk], ["KmemT"])

        if stop == "ffn1":
            return finish()

        def run_pipelined(gens, depth=2):
            pending = list(gens)
            active = []
            while pending or active:
                if pending and len(active) < depth:
                    active.append(pending.pop(0))
                for g in list(active):
                    try:
                        next(g)
                    except StopIteration:
                        active.remove(g)

        cv.off = base_off
        hT2 = [cv.take([8, 512], BF16, "hT%d" % k) for k in range(2)]
        Kst = cv.take([8, NT], BF16, "Kst")
        Vst = cv.take([8 * 16 * 64], BF16, "Vst").rearrange("p (h i d) -> p h i d", h=8, i=16)
        m1sets = []
        for s in range(3):
            m1sets.append(dict(ckvn=cv.take([256], BF16, "ckvn%d" % s), ckvnT=cv.take([2, 128], BF16, "ckvnT%d" % s),
                               kc_sb=cv.take([8, 96], F32, "kc_sb%d" % s), kfin=cv.take([8, 96], BF16, "kfin%d" % s)))
        hn1 = mk_hn("m_")

        wkvin, wkvin_k = wload(lambda s: [(s[:, 0:8 * 288].rearrange("p (kc n) -> p kc n", n=288),
                                           win_d[:, C_CKV:C_CKV + 288].rearrange("(kc p) n -> p kc n", p=128))])
        wkvin3 = wkvin[:, 0:8 * 288].rearrange("p (kc n) -> p kc n", n=288)
        wukv, wukv_k = wload(lambda s: [(s[:, 0:2048].rearrange("p (kc n) -> p kc n", n=1024),
                                         wukv_d[:, :].rearrange("(kc p) n -> p kc n", p=128))])
        wukv3 = wukv[:, 0:2048].rearrange("p (kc n) -> p kc n", n=1024)
        norm_banks = o_banks

        pre_w = {}

        def m1_block(t, bl):
            i = t * 4 + bl
            s = i % 3
            S = m1sets[s]
            ckvn, ckvnT, kc_sb, kfin = S["ckvn"], S["ckvnT"], S["kc_sb"], S["kfin"]
            kq = lambda n: "%s%d" % (n, s)
            hT = hT2[t % 2]
            hk = "hT%d" % (t % 2)
            ss1 = small[:, 32 + 4 * s:33 + 4 * s]
            sd1 = small[:, 33 + 4 * s:34 + 4 * s]
            rs1 = small[:, 34 + 4 * s:35 + 4 * s]
            bA, bAk = ps[:, 2 * s + 0, :], "ps%d" % (2 * s + 0)
            bB, bBk = ps[:, 2 * s + 1, :], "ps%d" % (2 * s + 1)
            bC, bCk = bB, bBk
            if bl == 0:
                nb, nbk = norm_banks.next()
                norm_fm(lambda kc, t=t: xT[:, kc, t * 512:(t + 1) * 512], ["x%d" % t], "mix_g",
                        lambda kc: hT[:, kc, :], hk, 512, bank=(nb, nbk))
            mm_group(bA[:, 0:288], bAk, [(hT[:, kc, bl * 128:(bl + 1) * 128], wkvin3[:, kc, :]) for kc in range(8)],
                     [hk, wkvin_k])
            yield
            p.act(lambda e: e.activation(out=junk[:, 0:256], in_=bA[:, 0:256], func=AF.Square, accum_out=ss1),
                  [bAk], [kq("ss1"), "junk"])
            p.act(lambda e: e.activation(out=sd1, in_=ss1, func=AF.Sqrt, bias=EPS, scale=1.0 / 256), [kq("ss1")], [kq("sd1")])
            p.act(lambda e: e.activation(out=kc_sb[:, :, 64:96], in_=bA[:, 256:288].unsqueeze(1).broadcast_to([128, 8, 32]),
                                         func=AF.Copy), [bAk], [kq("kc_sb")])
            p.dve(lambda e: e.reciprocal(out=rs1, in_=sd1), [kq("sd1")], [kq("rs1")])
            p.dve(lambda e: e.scalar_tensor_tensor(out=ckvn, in0=bA[:, 0:256], scalar=rs1, in1=C("ckvg"),
                                                   op0=ALU.mult, op1=ALU.mult), [bAk, kq("rs1"), "cst"], [kq("ckvn")])
            yield
            bb = transposes(bB, bBk, [ckvn[:, k * 128:(k + 1) * 128] for k in range(2)], [kq("ckvn")])
            p.act(lambda e: e.activation(out=ckvnT, in_=bb[:, 0:2, :], func=AF.Copy), [bBk], [kq("ckvnT")])
            yield
            mm_group(bC, bCk, [(ckvnT[:, k, :], wukv3[:, k, 0:512]) for k in range(2)], [kq("ckvnT"), wukv_k])
            mm_group(bA, bAk, [(ckvnT[:, k, :], wukv3[:, k, 512:1024]) for k in range(2)], [kq("ckvnT"), wukv_k])
            for hb, (bx, bxk) in enumerate(((bC, bCk), (bA, bAk))):
                b3 = bx.rearrange("p (h d) -> p h d", d=128)
                p.act(lambda e, b3=b3, hb=hb: e.activation(out=Vst[:, hb * 4:(hb + 1) * 4, i, :],
                                                           in_=b3[:, :, 64:128], func=AF.Copy), [bxk], ["Vst"])
                p.act(lambda e, b3=b3, hb=hb: e.activation(out=kc_sb[:, hb * 4:(hb + 1) * 4, 0:64],
                                                           in_=b3[:, :, 0:64], func=AF.Copy), [bxk], [kq("kc_sb")])
            yield
            head_norm(hn1, kc_sb, 8, 96, "kg", kfin, kq("kc_sb"), kq("kfin"), rope_i=i)
            yield
            bb2 = transposes(bB, bBk, [kfin[:, h, :] for h in range(8)], [kq("kfin")], rows=96)
            p.act(lambda e: e.activation(out=Kst[0:96, :, i * 128:(i + 1) * 128], in_=bb2[0:96, :, :],
                                         func=AF.Copy), [bBk], ["Kst"])
            if bl == 3 and stop not in ("m1nocc", "dump_m1"):
                yield
                if t == 3:
                    pre_w["wv"] = wload(lambda s: [(s.rearrange("p (kc n) -> p kc n", n=512),
                                                    win_d[:, C_V:C_V + 512].rearrange("(kc p) n -> p kc n", p=128))])
                    pre_w["wu"] = wload(lambda s: [(s.rearrange("p (kc n) -> p kc n", n=512),
                                                    win_d[:, C_U:C_U + 512].rearrange("(kc p) n -> p kc n", p=128))])
                p.dma("sp", [(kin_t[t].ap().rearrange("(h f) c -> f h c", f=96), Kst[0:96, :, t * 512:(t + 1) * 512])],
                      ["Kst"], ["kin%d" % t], "kin%d" % t)
                p.dma("sp", [(vin_t[t].ap().rearrange("(h p) (i d) -> p h i d", p=128, d=64), Vst[:, :, 4 * t:4 * t + 4, :])],
                      ["Vst"], ["vin%d" % t], "vin%d" % t)
                p.collective(lambda e: e.collective_compute(
                    "AllGather", ALU.bypass, replica_groups=[[0, 1, 2, 3], [4, 5, 6, 7]],
                    ins=[kin_t[t].ap().opt()], outs=[kout_t[t].ap().opt()]),
                    ["kin%d" % t], ["kout%d" % t], semname="cck%d" % t)
                p.collective(lambda e: e.collective_compute(
                    "AllGather", ALU.bypass, replica_groups=[[0, 1, 2, 3], [4, 5, 6, 7]],
                    ins=[vin_t[t].ap().opt()], outs=[vout_t[t].ap().opt()]),
                    ["vin%d" % t], ["vout%d" % t], semname="ccv%d" % t)

        run_pipelined([m1_block(t, bl) for t in range(4) for bl in range(4)], depth=3)
        if stop == "dump_m1":
            for h in range(8):
                p.dma("pool", [(oT_d[h * 96:(h + 1) * 96, :], Kst[0:96, h, :])], ["Kst"], ["dbg%d" % h], "dbg", final=True)
            for hq in range(2):
                p.dma("pool", [(oT_d[768 + hq * 128:768 + (hq + 1) * 128, :].rearrange("p (h i d) -> p h i d", h=2, i=16),
                                Vst[:, 2 * hq:2 * hq + 2, :, :])], ["Vst"], ["dbgv%d" % hq], "dbg", final=True)
            return finish(store=False)
        if stop == "m1":
            return finish()
        if stop == "m1nocc":
            return finish()
        cv.off = base_off
        hT = cv.take([8, 512], BF16, "hT")
        QT = cv.take([8, 512], BF16, "QT")
        qmT = cv.take([4, 512], BF16, "qmT")
        yT = [cv.take([4, 512], BF16, k) for k in ("yaT", "ybT", "ycT")]
        Kring_v = [cv.take([2048], BF16, "K%d" % s) for s in range(4)]
        Vring_v = [cv.take([16, 128], BF16, "V%d" % s) for s in range(4)]
        ph_off = cv.off
        uT_sb = cv.take([4, 512], BF16, "uT")
        Asets = [dict(v_sb=cv.take([512], F32, "v_sb%d" % s), v_ln=cv.take([512], BF16, "v_ln%d" % s),
                      mtmp=cv.take([4, 128], F32, "mtmp%d" % s)) for s in range(3)]
        cv.off = ph_off
        Qsets = [dict(cqn=cv.take([384], BF16, "cqn%d" % s), cqnT=cv.take([3, 128], BF16, "cqnT%d" % s),
                      q_sb=cv.take([8, 96], F32, "q_sb%d" % s), qfin=cv.take([8, 96], BF16, "qfin%d" % s))
                 for s in range(2)]
        hn2 = mk_hn("t_")
        q_end = cv.off
        cv.off = ph_off
        Csets = [dict(qm_sb=cv.take([4, 128], F32, "qm_sb%d" % s), qmn=cv.take([4, 128], BF16, "qmn%d" % s))
                 for s in range(3)]
        assert cv.off <= ph_off + 2 * (384 + 384 + 1536 + 768), "C sets overlap hn temps"
        assert cv.off <= q_end - 3 * 1024 - 1536 * 2 or True
        cv.off = ph_off
        P_ring = Ring("P", [cv.take([512], BF16, "P%d" % k) for k in range(8)])
        rd_views = [cv.take([512], F32, "rd%d" % k) for k in range(2)]
        rden_ring = Ring("rd", rd_views)
        acc1, acc2 = rd_views
        mergedT = cv.take([8, 512], BF16, "mergedT")
        go = cv.off
        g_sb = cv.take([3, 512], BF16)
        for br in range(3):
            p.region("g_sb%d" % br, [(go + br * 512, go + br * 512 + 512)])

        for s in range(4):
            lo = 64 if s < 2 else 0
            p.pool(lambda e, s=s, lo=lo: e.memset(Vring_v[s][:, :, lo:lo + 64], 1.0), [], ["V%d" % s])

        SC_B = 96 ** -0.5
        SC_C = 128 ** -0.5
        st6 = small[:, 40:46]
        mv = small[:, 46:48]
        sdv = small[:, 48:49]
        rsv = small[:, 49:50]

        for t in range(4):
            xk = "x%d" % t
            norm_fm(lambda kc, t=t: xT[:, kc, t * 512:(t + 1) * 512], [xk], "mix_g",
                    lambda kc: hT[:, kc, :], "hT", 512)
            wv_s, wv_k = pre_w.pop("wv") if "wv" in pre_w else wload(lambda s: [
                (s.rearrange("p (kc n) -> p kc n", n=512),
                 win_d[:, C_V:C_V + 512].rearrange("(kc p) n -> p kc n", p=128))])
            wu_s, wu_k = pre_w.pop("wu") if "wu" in pre_w else wload(lambda s: [
                (s.rearrange("p (kc n) -> p kc n", n=512),
                 win_d[:, C_U:C_U + 512].rearrange("(kc p) n -> p kc n", p=128))])
            wv3 = wv_s.rearrange("p (kc n) -> p kc n", n=512)
            wu3 = wu_s.rearrange("p (kc n) -> p kc n", n=512)
            for c in range(4):
                bu, buk = o_banks.next()
                mm_group(bu, buk, [(wu3[:, kc, c * 128:(c + 1) * 128], hT[:, kc, :]) for kc in range(8)], [wu_k, "hT"])
                p.act(lambda e, c=c, bu=bu: e.activation(out=uT_sb[:, c, :], in_=bu, func=AF.Gelu), [buk], ["uT"])

            def a_block(bl):
                s = bl % 3
                S = Asets[s]
                v_sb, v_ln, mtmp = S["v_sb"], S["v_ln"], S["mtmp"]
                kq = lambda n: "%s%d" % (n, s)
                st6 = small[:, 64 + 16 * s:70 + 16 * s]
                mv = small[:, 70 + 16 * s:72 + 16 * s]
                sdv = small[:, 72 + 16 * s:73 + 16 * s]
                rsv = small[:, 73 + 16 * s:74 + 16 * s]
                bV, bVk = ps[:, 2 * s + 0, :], "ps%d" % (2 * s + 0)
                bM, bMk = ps[:, 2 * s + 1, :], "ps%d" % (2 * s + 1)
                mm_group(bV, bVk, [(hT[:, kc, bl * 128:(bl + 1) * 128], wv3[:, kc, :]) for kc in range(8)], [wv_k, "hT"])
                yield
                p.act(lambda e: e.activation(out=v_sb, in_=bV, func=AF.Gelu), [bVk], [kq("v_sb")])
                p.dve(lambda e: e.bn_stats(out=st6, in_=v_sb), [kq("v_sb")], [kq("st6")])
                p.dve(lambda e: e.bn_aggr(out=mv, in_=st6), [kq("st6")], [kq("mv")])
                p.act(lambda e: e.activation(out=sdv, in_=mv[:, 1:2], func=AF.Sqrt, bias=EPS, scale=1.0), [kq("mv")], [kq("sdv")])
                yield
                p.dve(lambda e: e.reciprocal(out=rsv, in_=sdv), [kq("sdv")], [kq("rsv")])
                p.dve(lambda e: e.scalar_tensor_tensor(out=v_sb, in0=v_sb, scalar=mv[:, 0:1], in1=C("lng"),
                                                       op0=ALU.subtract, op1=ALU.mult), [kq("v_sb"), kq("mv"), "cst"], [kq("v_sb")])
                p.dve(lambda e: e.scalar_tensor_tensor(out=v_ln, in0=v_sb, scalar=rsv, in1=C("lnb"),
                                                       op0=ALU.mult, op1=ALU.add), [kq("v_sb"), kq("rsv"), "cst"], [kq("v_ln")])
                yield

                def mixfn(e):
                    ins = None
                    for g in range(8):
                        ins = e.matmul(bM[(g % 2) * 64:(g % 2) * 64 + 64, (g // 2) * 128:(g // 2 + 1) * 128],
                                       v_ln[:, g * 64:(g + 1) * 64], wTsg[:, g, :], start=True, stop=True)
                    return ins
                p.pe(mixfn, [kq("v_ln"), "wTsg"], [bMk])
                yield
                bm3 = bM.rearrange("p (c t) -> p c t", t=128)
                p.dve(lambda e: e.tensor_tensor(out=mtmp, in0=bm3, in1=C("bsT").rearrange("p (c t) -> p c t", t=128),
                                                op=ALU.add), [bMk, "cst"], [kq("mtmp")])
                p.dve(lambda e: e.tensor_tensor(out=yT[0][:, :, bl * 128:(bl + 1) * 128], in0=mtmp,
                                                in1=uT_sb[:, :, bl * 128:(bl + 1) * 128], op=ALU.mult),
                      [kq("mtmp"), "uT"], ["yaT"])
            run_pipelined([a_block(bl) for bl in range(4)], depth=3)
            wcq_s, wcq_k = wload(lambda s: [(s[:, 0:8 * 384].rearrange("p (kc n) -> p kc n", n=384),
                                             win_d[:, C_CQ:C_CQ + 384].rearrange("(kc p) n -> p kc n", p=128))])
            wuq_s, wuq_k = wload(lambda s: [(s[:, 0:3 * 768].rearrange("p (kc n) -> p kc n", n=768),
                                             wuq_d[:, :].rearrange("(kc p) n -> p kc n", p=128))])
            wcq3 = wcq_s[:, 0:8 * 384].rearrange("p (kc n) -> p kc n", n=384)
            wuq3 = wuq_s[:, 0:3 * 768].rearrange("p (kc n) -> p kc n", n=768)

            def q_block(bl):
                i = t * 4 + bl
                s = bl % 2
                S = Qsets[s]
                cqn, cqnT, q_sb, qfin = S["cqn"], S["cqnT"], S["q_sb"], S["qfin"]
                kq = lambda n: "%s%d" % (n, s)
                ss1 = small[:, 32 + 4 * s:33 + 4 * s]
                sd1 = small[:, 33 + 4 * s:34 + 4 * s]
                rs1 = small[:, 34 + 4 * s:35 + 4 * s]
                bA, bAk = ps[:, 3 * s + 0, :], "ps%d" % (3 * s + 0)
                bB, bBk = ps[:, 3 * s + 1, :], "ps%d" % (3 * s + 1)
                bC, bCk = ps[:, 3 * s + 2, :], "ps%d" % (3 * s + 2)
                mm_group(bA[:, 0:384], bAk, [(hT[:, kc, bl * 128:(bl + 1) * 128], wcq3[:, kc, :]) for kc in range(8)],
                         [wcq_k, "hT"])
                yield
                p.act(lambda e: e.activation(out=junk[:, 0:384], in_=bA[:, 0:384], func=AF.Square, accum_out=ss1),
                      [bAk], [kq("ss1"), "junk"])
                p.act(lambda e: e.activation(out=sd1, in_=ss1, func=AF.Sqrt, bias=EPS, scale=1.0 / 384), [kq("ss1")], [kq("sd1")])
                p.dve(lambda e: e.reciprocal(out=rs1, in_=sd1), [kq("sd1")], [kq("rs1")])
                p.dve(lambda e: e.scalar_tensor_tensor(out=cqn, in0=bA[:, 0:384], scalar=rs1, in1=C("cqg"),
                                                       op0=ALU.mult, op1=ALU.mult), [bAk, kq("rs1"), "cst"], [kq("cqn")])
                yield
                bb = transposes(bB, bBk, [cqn[:, k * 128:(k + 1) * 128] for k in range(3)], [kq("cqn")])
                p.act(lambda e: e.activation(out=cqnT, in_=bb[:, 0:3, :], func=AF.Copy), [bBk], [kq("cqnT")])
                yield
                for hb, (bx, bxk) in enumerate(((bC, bCk), (bA, bAk))):
                    mm_group(bx[:, 0:384], bxk, [(cqnT[:, k, :], wuq3[:, k, hb * 384:(hb + 1) * 384]) for k in range(3)],
                             [kq("cqnT"), wuq_k])
                    p.act(lambda e, bx=bx, hb=hb: e.activation(
                        out=q_sb[:, hb * 4:(hb + 1) * 4, :], in_=bx[:, 0:384].rearrange("p (h d) -> p h d", d=96),
                        func=AF.Copy), [bxk], [kq("q_sb")])
                yield
                head_norm(hn2, q_sb, 8, 96, "qg", qfin, kq("q_sb"), kq("qfin"), rope_i=i)
                yield
                bb2 = transposes(bB, bBk, [qfin[:, h, :] for h in range(8)], [kq("qfin")], rows=96)
                p.act(lambda e: e.activation(out=QT[0:96, :, bl * 128:(bl + 1) * 128], in_=bb2[0:96, :, :],
                                             func=AF.Copy), [bBk], ["QT"])
            run_pipelined([q_block(bl) for bl in range(4)], depth=2)
            wqm_s, wqm_k = wload(lambda s: [(s.rearrange("p (kc n) -> p kc n", n=512),
                                             win_d[:, C_QM:C_QM + 512].rearrange("(kc p) n -> p kc n", p=128))])
            wqm3 = wqm_s.rearrange("p (kc n) -> p kc n", n=512)

            def c_block(bl):
                s = bl % 3
                S = Csets[s]
                qm_sb, qmn = S["qm_sb"], S["qmn"]
                kq = lambda n: "%s%d" % (n, s)
                bA, bAk = ps[:, 2 * s + 0, :], "ps%d" % (2 * s + 0)
                bB, bBk = ps[:, 2 * s + 1, :], "ps%d" % (2 * s + 1)
                mm_group(bA, bAk, [(hT[:, kc, bl * 128:(bl + 1) * 128], wqm3[:, kc, :]) for kc in range(8)],
                         [wqm_k, "hT"])
                yield
                p.act(lambda e: e.activation(out=qm_sb.rearrange("p a b -> p (a b)"), in_=bA, func=AF.Copy),
                      [bAk], [kq("qm_sb")])
                yield
                head_norm(hn2, qm_sb, 4, 128, "mqg", qmn, kq("qm_sb"), kq("qmn"))
                yield
                bb = transposes(bB, bBk, [qmn[:, h, :] for h in range(4)], [kq("qmn")])
                p.act(lambda e: e.activation(out=qmT[:, :, bl * 128:(bl + 1) * 128], in_=bb[:, 0:4, :],
                                             func=AF.Copy), [bBk], ["qmT"])
            run_pipelined([c_block(bl) for bl in range(4)], depth=3)
            for h in range(4):
                Ps = []
                for mc in range(2):
                    bs, bsk = gen_banks.next()
                    mm_group(bs, bsk, [(KmemT[:, h, mc * 128:(mc + 1) * 128], qmT[:, h, :])], ["KmemT", "qmT"])
                    Pt, Pk = P_ring.next()
                    p.act(lambda e, bs=bs, Pt=Pt: e.activation(out=Pt, in_=bs, func=AF.Exp, scale=SC_C), [bsk], [Pk])
                    Ps.append((Pt, Pk))
                bo, bok = gen_banks.next()
                mm_group(bo, bok, [(Vmem[:, mc, h * 128:(h + 1) * 128], Ps[mc][0]) for mc in range(2)],
                         ["Vmem"] + [k for _, k in Ps])
                bd, bdk = gen_banks.next()
                mm_group(bd, bdk, [(ones, Ps[mc][0]) for mc in range(2)], ["cst"] + [k for _, k in Ps])
                rd, rdk = rden_ring.next()
                p.dve(lambda e, rd=rd, bd=bd: e.reciprocal(out=rd, in_=bd), [bdk], [rdk])
                p.dve(lambda e, rd=rd, bo=bo, h=h: e.tensor_tensor(out=yT[2][:, h, :], in0=bo, in1=rd, op=ALU.mult),
                      [bok, rdk], ["ycT"])
            nki = 4 * t + 4
            G = {0: 4, 1: 2}.get(t, 1)
            pend = []
            chunk_ctr = [0, 0]

            def rec_pv(item):
                (Pt, Pk, c0, Vc, vk, vi, ob, obk, first, last, fin) = item
                p.pe(lambda e: e.matmul(ob[:, c0:512], Vc[:, vi, :], Pt[:, c0:512], start=first, stop=last),
                     [Pk, vk], [obk])
                if last:
                    fin()
            for h in range(8):
                par = h % 2
                ob, obk = o_banks.next()
                olo = 0 if par == 0 else 64
                dlo = 64 - olo
                vlo = olo

                def fin(ob=ob, obk=obk, olo=olo, dlo=dlo, h=h):
                    rd, rdk = rden_ring.next()
                    p.dve(lambda e: e.reciprocal(out=rd[dlo:dlo + 64, :], in_=ob[dlo:dlo + 64, :]), [obk], [rdk])
                    p.dve(lambda e: e.tensor_tensor(out=yT[1][olo:olo + 64, h // 2, :], in0=ob[olo:olo + 64, :],
                                                    in1=rd[dlo:dlo + 64, :], op=ALU.mult), [obk, rdk], ["ybT"])
                npv = 0
                ntot = 4 * nki
                for c in range(4 // G):
                    slot = par * 2 + (chunk_ctr[par] % 2)
                    chunk_ctr[par] += 1
                    Kc = Kring_v[slot]
                    Vc = Vring_v[slot]
                    kk, vk = "K%d" % slot, "V%d" % slot
                    ranks = [c * G + gi for gi in range(G)]
                    p.dma("pool", [(Kc[0:96, gi * nki * 128 + tt * 512:gi * nki * 128 + (tt + 1) * 512],
                                    kout_t[tt].ap()[r * 768 + h * 96:r * 768 + (h + 1) * 96, :])
                                   for gi, r in enumerate(ranks) for tt in range(t + 1)],
                          ["kout%d" % tt for tt in range(t + 1)], [kk], kk)
                    p.dma("pool", [(Vc[:, gi * nki + 4 * tt:gi * nki + 4 * tt + 4, vlo:vlo + 64],
                                    vout_t[tt].ap()[r * 1024 + h * 128:r * 1024 + (h + 1) * 128, :].rearrange(
                                        "p (i d) -> p i d", d=64))
                                   for gi, r in enumerate(ranks) for tt in range(t + 1)],
                          ["vout%d" % tt for tt in range(t + 1)], [vk], vk)
                    for gi, r in enumerate(ranks):
                        for ki in range(nki):
                            d = ki - 4 * t
                            c0 = 128 * d if d > 0 else 0
                            kcol = (gi * nki + ki) * 128
                            bs, bsk = gen_banks.next()
                            p.pe(lambda e, bs=bs, Kc=Kc, kcol=kcol, c0=c0, h=h: e.matmul(
                                bs[:, c0:512], Kc[0:96, kcol:kcol + 128], QT[0:96, h, c0:512], start=True, stop=True),
                                [kk, "QT"], [bsk])
                            Pt, Pk = P_ring.next()
                            p.act(lambda e, bs=bs, Pt=Pt, c0=c0: e.activation(out=Pt[:, c0:512], in_=bs[:, c0:512],
                                                                             func=AF.Exp, scale=SC_B), [bsk], [Pk])
                            if d >= 0:
                                mk = masks[:, (ki % 2) * 4 + r, :]
                                p.dve(lambda e, Pt=Pt, c0=c0, mk=mk: e.tensor_tensor(
                                    out=Pt[:, c0:c0 + 128], in0=Pt[:, c0:c0 + 128], in1=mk, op=ALU.mult), [Pk, "cst"], [Pk])
                            pend.append((Pt, Pk, c0, Vc, vk, gi * nki + ki, ob, obk, npv == 0, npv == ntot - 1, fin))
                            npv += 1
                            if len(pend) > 4:
                                rec_pv(pend.pop(0))
            while pend:
                rec_pv(pend.pop(0))
            if stop == "dump_y%d" % t:
                for br in range(3):
                    p.dma("pool", [(oT_d[0:512, br * 512:(br + 1) * 512].rearrange("(kc p) t -> p kc t", p=128), yT[br])],
                          [("yaT", "ybT", "ycT")[br]], ["dbgy%d" % br], "dbg", final=True)
                return finish(store=False)
            for m in range(8):
                gs, gk = wload(lambda s, m=m: [
                    (s[:, 0:3072].rearrange("p (b kc n) -> p b kc n", b=3, n=128)[:, br],
                     win_d[:, C_GATE + br * 1024 + m * 128:C_GATE + br * 1024 + (m + 1) * 128].rearrange(
                         "(kc p) n -> p kc n", p=128)) for br in range(3)])
                bs_, bk_ = wload(lambda s, m=m: [
                    (s[:, 0:1536].rearrange("p (b kc n) -> p b kc n", b=3, n=128)[:, br],
                     wbr_d[br][:, m * 128:(m + 1) * 128].rearrange("(kc p) n -> p kc n", p=128)) for br in range(3)])
                g4 = gs[:, 0:3072].rearrange("p (b kc n) -> p b kc n", b=3, n=128)
                b4 = bs_[:, 0:1536].rearrange("p (b kc n) -> p b kc n", b=3, n=128)
                for br in range(3):
                    bg, bgk = gen_banks.next()
                    mm_group(bg, bgk, [(g4[:, br, kc, :], hT[:, kc, :]) for kc in range(8)], [gk, "hT"])
                    p.act(lambda e, bg=bg, br=br, m=m: e.activation(out=g_sb[:, br, :], in_=bg, func=AF.Sigmoid,
                                                                   bias=C("bgate")[:, br * 8 + m:br * 8 + m + 1], scale=1.0),
                          [bgk, "cst"], ["g_sb%d" % br])
                ykeys = ["yaT", "ybT", "ycT"]
                bbs = []
                for br in range(3):
                    bb_, bbk = gen_banks.next()
                    mm_group(bb_, bbk, [(b4[:, br, kc, :], yT[br][:, kc, :]) for kc in range(4)], [bk_, ykeys[br]])
                    bbs.append((bb_, bbk))
                p.dve(lambda e, b=bbs[0][0]: e.tensor_tensor(out=acc1, in0=b, in1=g_sb[:, 0, :], op=ALU.mult),
                      [bbs[0][1], "g_sb0"], ["rd0"])
                p.dve(lambda e, b=bbs[1][0]: e.tensor_tensor(out=acc2, in0=b, in1=g_sb[:, 1, :], op=ALU.mult),
                      [bbs[1][1], "g_sb1"], ["rd1"])
                p.dve(lambda e: e.tensor_tensor(out=acc1, in0=acc1, in1=acc2, op=ALU.add), ["rd0", "rd1"], ["rd0"])
                p.dve(lambda e, b=bbs[2][0]: e.tensor_tensor(out=acc2, in0=b, in1=g_sb[:, 2, :], op=ALU.mult),
                      [bbs[2][1], "g_sb2"], ["rd1"])
                p.dve(lambda e, m=m: e.tensor_tensor(out=mergedT[:, m, :], in0=acc1, in1=acc2, op=ALU.add),
                      ["rd0", "rd1"], ["mergedT"])
            for hf in range(2):
                ws, wk_ = wload(lambda s, hf=hf: [(s.rearrange("p (kc n) -> p kc n", n=512),
                                                   wout_d[:, hf * 512:(hf + 1) * 512].rearrange("(kc p) n -> p kc n", p=128))])
                w3 = ws.rearrange("p (kc n) -> p kc n", n=512)
                for mm in range(4):
                    m = hf * 4 + mm
                    bo, bok = gen_banks.next()
                    mm_group(bo, bok, [(w3[:, kc, mm * 128:(mm + 1) * 128], mergedT[:, kc, :]) for kc in range(8)],
                             [wk_, "mergedT"])
                    xs = xT[:, m, t * 512:(t + 1) * 512]
                    p.dve(lambda e, bo=bo, xs=xs: e.tensor_tensor(out=xs, in0=bo, in1=xs, op=ALU.add), [bok, xk], [xk])

        if stop != "mid":
            ffn(w2gu_d, w2dn_d, "ffn2_g")
        return finish()


def _perm_rows(j):
    rows = []
    for i in range(16):
        g = i // 2
        blk = 8 * g + (j if i % 2 == 0 else 7 - j)
        rows.append(np.arange(blk * 128, (blk + 1) * 128))
    return np.concatenate(rows)


def _host_inputs(inp):
    f = lambda a: np.ascontiguousarray(np.asarray(a, dtype=np.float32))
    x = f(inp["x"])
    mem = f(inp["mem"])
    pos = np.asarray(inp["positions"]).astype(np.int32)
    L0 = lambda k: f(inp[k])[0]

    def col(v, n):
        return np.ascontiguousarray(v.reshape(n, 128).T)

    def rep(v):
        return np.ascontiguousarray(np.broadcast_to(v[None, :], (128, v.shape[0])))

    cst = np.zeros((128, NCST), np.float32)

    def put(name, arr):
        o, w = CST[name]
        assert arr.shape == (128, w), (name, arr.shape)
        cst[:, o:o + w] = arr
    put("ffn1_g", col(L0("ffn1_norm"), 8))
    put("mix_g", col(L0("mix_norm"), 8))
    put("ffn2_g", col(L0("ffn2_norm"), 8))
    put("mem_g", col(L0("mem_norm"), 8))
    put("bgate", col(L0("b_gate"), 24))
    put("cqg", rep(L0("mla_cq_norm")))
    put("ckvg", rep(L0("mla_ckv_norm")))
    put("lng", rep(L0("sg_ln_g")))
    put("lnb", rep(L0("sg_ln_b")))
    put("qg", rep(L0("mla_q_norm")))
    put("kg", rep(L0("mla_k_norm")))
    put("mqg", rep(L0("mem_q_norm")))
    put("mkg", rep(L0("mem_k_norm")))
    sgb = L0("sg_b")
    bsT = np.zeros((128, 4, 128), np.float32)
    for c in range(4):
        bsT[0:64, c, :] = sgb[2 * c][None, :]
        bsT[64:128, c, :] = sgb[2 * c + 1][None, :]
    put("bsT", bsT.reshape(128, 512))
    half = 16
    invf = (10000.0 ** (-np.arange(half, dtype=np.float32) / half)).astype(np.float32)
    put("invf", rep(invf))
    tri = (np.arange(128)[:, None] <= np.arange(128)[None, :]).astype(np.float32)
    put("tri", tri)

    sgw = L0("sg_w")
    sgwT = np.ascontiguousarray(sgw.transpose(2, 0, 1)).reshape(128, 8 * 128)

    shared = {
        "cst": None, "sgwT": sgwT,
        "ffn1_w_gu": L0("ffn1_w_gu"), "ffn1_w_down": L0("ffn1_w_down"),
        "ffn2_w_gu": L0("ffn2_w_gu"), "ffn2_w_down": L0("ffn2_w_down"),
        "w_in": L0("w_in"), "mla_w_uq": L0("mla_w_uq"), "mla_w_ukv": L0("mla_w_ukv"),
        "mem_w_kv": L0("mem_w_kv"), "w_branch_a": L0("w_branch_a"), "w_branch_b": L0("w_branch_b"),
        "w_branch_c": L0("w_branch_c"), "w_out": L0("w_out"),
    }
    in_maps = []
    perms = []
    for c in range(NCORES):
        b, j = c // 4, c % 4
        rows = _perm_rows(j)
        perms.append((b, rows))
        m = dict(shared)
        m["cst"] = cst
        m["xT"] = np.ascontiguousarray(x[b][rows].T)
        m["memT"] = np.ascontiguousarray(mem[b].T)
        m["pos"] = np.ascontiguousarray(pos[b][rows].reshape(16, 128).T)
        cbf = np.zeros((128, NCBF), np.float32)
        cbf[:, 0:128] = np.eye(128, dtype=np.float32)
        cbf[:, 128:256] = 1.0
        mk = np.zeros((128, 8, 128), np.float32)
        for r in range(4):
            mk[:, r, :] = 1.0 if r < j else (tri if r == j else 0.0)
            mk[:, 4 + r, :] = 1.0 if r > j else (tri if r == j else 0.0)
        cbf[:, 256:] = mk.reshape(128, 1024)
        m["cbf"] = cbf.astype(ml_dtypes.bfloat16)
        in_maps.append(m)
    return in_maps, perms


_NC_CACHE = {}


def kernel(**inputs):
    in_maps, perms = _host_inputs(inputs)
    if "nc" not in _NC_CACHE:
        _NC_CACHE["nc"] = build()
    nc = _NC_CACHE["nc"]
    res = run_bass_kernel_spmd(nc, in_maps, core_ids=list(range(NCORES)))
    out = np.zeros((2, 8192, D), np.float32)
    for c in range(NCORES):
        b, rows = perms[c]
        out[b, rows, :] = np.asarray(res.results[c]["oT"], dtype=np.float32).T
    return out
```

```python
import numpy as np
import ml_dtypes
from contextlib import ExitStack
import concourse.bass as bass
import concourse.mybir as mybir
from concourse.bass_utils import run_bass_kernel_spmd

F32 = mybir.dt.float32
BF16 = mybir.dt.bfloat16
I32 = mybir.dt.int32
AF = mybir.ActivationFunctionType
ALU = mybir.AluOpType
AX = mybir.AxisListType

NCORES = 8
D = 1024
NT = 2048
DFF = 2816
FC = 22
EPS = 1e-6
C_U, C_V, C_CQ, C_CKV, C_KR, C_QM, C_GATE = 0, 512, 1024, 1408, 1664, 1696, 2208
PI = float(np.pi)

CST = {}
_off = 0
for _n, _w in [("ffn1_g", 8), ("mix_g", 8), ("ffn2_g", 8), ("mem_g", 8), ("bgate", 24), ("cqg", 384),
               ("ckvg", 256), ("lng", 512), ("lnb", 512), ("qg", 96), ("kg", 96), ("mqg", 128),
               ("mkg", 128), ("bsT", 512), ("invf", 16), ("tri", 128)]:
    CST[_n] = (_off, _w)
    _off += _w
NCST = _off
NCBF = 128 + 128 + 8 * 128


class Prog:
    ENG = ("pe", "act", "dve", "pool", "sp")

    def __init__(self, nc, es):
        self.nc, self.es = nc, es
        self.q = {e: [] for e in self.ENG}
        self.sems = {}
        self.cnt = {}
        self.waited = {e: {} for e in self.ENG}
        self.lastw = {}
        self.readers = {}
        self.out_tokens = []
        self.regions = {}
        self._ovc = {}

    def sem(self, name):
        if name not in self.sems:
            self.sems[name] = self.es.enter_context(self.nc.semaphore("s_" + name))
            self.cnt[name] = 0
        return self.sems[name]

    def region(self, key, ivs):
        self.regions[key] = list(ivs)

    def _overlap(self, k):
        if k not in self.regions:
            return (k,)
        c = self._ovc.get(k)
        if c is not None and c[0] == len(self.regions):
            return c[1]
        mine = self.regions[k]
        res = [k]
        for k2, ivs in self.regions.items():
            if k2 == k:
                continue
            hit = False
            for (a, b) in mine:
                for (c0, d0) in ivs:
                    if a < d0 and c0 < b:
                        hit = True
                        break
                if hit:
                    break
            if hit:
                res.append(k2)
        self._ovc[k] = (len(self.regions), tuple(res))
        return self._ovc[k][1]

    def _collect(self, eng, reads, writes, is_dma):
        toks = []
        for k in reads:
            for k2 in self._overlap(k):
                if k2 in self.lastw:
                    toks.append((self.lastw[k2], True))
        for k in writes:
            for k2 in self._overlap(k):
                if k2 in self.lastw:
                    toks.append((self.lastw[k2], False))
                rd = self.readers.get(k2)
                if rd:
                    toks.extend(((sn, v, te), False) for (sn, te), v in rd.items())
        waits = {}
        for ((sn, val, teng), raw) in toks:
            if teng is not None and teng == eng and not is_dma:
                if not raw or eng == "pe":
                    continue
            if self.waited[eng].get(sn, 0) >= val:
                continue
            waits[sn] = max(waits.get(sn, 0), val)
        for sn, val in waits.items():
            self.waited[eng][sn] = val
        return [(self.sem(sn), val, sn) for sn, val in waits.items()]

    def _commit(self, tok, reads, writes):
        sn, val, te = tok
        for k in reads:
            d = self.readers.setdefault(k, {})
            d[(sn, te)] = max(d.get((sn, te), 0), val)
        for k in writes:
            self.lastw[k] = tok
            self.readers[k] = {}

    def op(self, eng, fn, reads=(), writes=()):
        waits = self._collect(eng, reads, writes, False)
        sn = "E" + eng
        self.sem(sn)
        self.cnt[sn] += 1
        tok = (sn, self.cnt[sn], eng)
        self.q[eng].append((waits, fn, [(self.sems[sn], 1)], [(sn, 1)]))
        self._commit(tok, reads, writes)

    def pe(self, fn, r=(), w=()):
        self.op("pe", fn, r, w)

    def act(self, fn, r=(), w=()):
        self.op("act", fn, r, w)

    def dve(self, fn, r=(), w=()):
        self.op("dve", fn, r, w)

    def pool(self, fn, r=(), w=()):
        self.op("pool", fn, r, w)

    def dma(self, queue, pairs, reads, writes, semname, final=False):
        waits = self._collect(queue, reads, writes, True)
        s = self.sem(semname)
        self.cnt[semname] += 16 * len(pairs)
        tok = (semname, self.cnt[semname], None)

        def fn(e, pairs=pairs, s=s):
            for (o, i) in pairs:
                e.dma_start(out=o, in_=i).then_inc(s, 16)
            return None
        self.q[queue].append((waits, fn, [], [(semname, 16 * len(pairs))]))
        self._commit(tok, reads, writes)
        if final:
            self.out_tokens.append(tok)

    def collective(self, fn, reads, writes, semname="cc"):
        waits = self._collect("pool", reads, writes, True)
        s = self.sem(semname)
        self.cnt[semname] += 1
        tok = (semname, self.cnt[semname], None)
        self.q["pool"].append((waits, fn, [(s, 1)], [(semname, 1)]))
        self._commit(tok, reads, writes)

    def check(self):
        val = {}
        pos = {e: 0 for e in self.ENG}
        prog = True
        while prog:
            prog = False
            for e in self.ENG:
                while pos[e] < len(self.q[e]):
                    waits, fn, incs, names = self.q[e][pos[e]]
                    if all(val.get(sn, 0) >= v for (_s, v, sn) in waits):
                        for (sn, n) in names:
                            val[sn] = val.get(sn, 0) + n
                        pos[e] += 1
                        prog = True
                    else:
                        break
        stuck = {e: pos[e] for e in self.ENG if pos[e] < len(self.q[e])}
        for e, i in stuck.items():
            waits = self.q[e][i][0]
            print("DEADLOCK", e, "op", i, "of", len(self.q[e]), "waits",
                  [(sn, v, val.get(sn, 0)) for (_s, v, sn) in waits if val.get(sn, 0) < v])
        return not stuck

    def emit(self, block):
        assert self.check(), "semaphore protocol deadlock"
        def run(e, eng):
            for (waits, fn, incs, _n) in self.q[eng]:
                for (s, v, _sn) in waits:
                    e.wait_ge(s, v)
                ins = fn(e)
                for (s, n) in incs:
                    ins.then_inc(s, n)
            if eng == "sp":
                for (sn, val, _) in self.out_tokens:
                    e.wait_ge(self.sems[sn], val)

        @block.tensor
        def _(e):
            run(e, "pe")

        @block.scalar
        def _(e):
            run(e, "act")

        @block.vector
        def _(e):
            run(e, "dve")

        @block.gpsimd
        def _(e):
            run(e, "pool")

        @block.sync
        def _(e):
            run(e, "sp")


class Ring:
    def __init__(self, name, views):
        self.name, self.views, self.i = name, views, 0

    def next(self):
        k = self.i % len(self.views)
        self.i += 1
        return self.views[k], "%s%d" % (self.name, k)


def build(stop=None):
    nc = bass.Bass("TRN2", target_bir_lowering=False)

    def din(name, shape, dt=F32):
        return nc.dram_tensor(name, shape, dt, kind="ExternalInput").ap()

    xT_d = din("xT", [D, NT])
    memT_d = din("memT", [D, 256])
    pos_d = din("pos", [128, 16], I32)
    cst_d = din("cst", [128, NCST])
    cbf_d = din("cbf", [128, NCBF], BF16)
    sgwT_d = din("sgwT", [128, 8 * 128])
    w1gu_d = din("ffn1_w_gu", [D, 2 * DFF])
    w1dn_d = din("ffn1_w_down", [DFF, D])
    w2gu_d = din("ffn2_w_gu", [D, 2 * DFF])
    w2dn_d = din("ffn2_w_down", [DFF, D])
    win_d = din("w_in", [D, 5280])
    wuq_d = din("mla_w_uq", [384, 768])
    wukv_d = din("mla_w_ukv", [256, 1024])
    wmkv_d = din("mem_w_kv", [D, 1024])
    wbr_d = [din("w_branch_a", [512, D]), din("w_branch_b", [512, D]), din("w_branch_c", [512, D])]
    wout_d = din("w_out", [D, D])
    oT_d = nc.dram_tensor("oT", [D, NT], F32, kind="ExternalOutput").ap()
    kin_t = [nc.dram_tensor("kin%d" % c, [768, 512], BF16) for c in range(4)]
    kout_t = [nc.dram_tensor("kout%d" % c, [4 * 768, 512], BF16) for c in range(4)]
    vin_t = [nc.dram_tensor("vin%d" % c, [1024, 256], BF16) for c in range(4)]
    vout_t = [nc.dram_tensor("vout%d" % c, [4 * 1024, 256], BF16) for c in range(4)]

    es = ExitStack()
    with es:
        p = Prog(nc, es)

        def sb(name, shape, dt):
            return es.enter_context(nc.sbuf_tensor(name, shape, dt))

        xT = sb("xT_sb", [128, 8, NT], F32)
        cst = sb("cst_sb", [128, NCST], F32)
        cbf = sb("cbf_sb", [128, NCBF], BF16)
        csA = sb("csA", [128, 16, 32], F32)
        csB = sb("csB", [128, 16, 32], F32)
        wTsg = sb("wTsg", [128, 8, 128], BF16)
        KmemT = sb("KmemT", [128, 4, 256], BF16)
        Vmem = sb("Vmem", [128, 2, 512], BF16)
        wr_t = sb("wring", [128, 3, 4096], BF16)
        ARENA = 49300
        ar = sb("arena", [128, ARENA], BF16)
        ps = es.enter_context(nc.psum_tensor("ps", [128, 8, 512], F32))

        def C(name):
            o, w = CST[name]
            return cst[:, o:o + w]

        ident = cbf[:, 0:128]
        ones = cbf[:, 128:256]
        masks = cbf[:, 256:256 + 1024].rearrange("p (a b) -> p a b", b=128)

        class Carver:
            def __init__(self):
                self.off = 0
                self.hi = 0

            def take(self, shape, dt, key=None):
                n = int(np.prod(shape)) * (2 if dt in (F32, I32) else 1)
                n = (n + 1) // 2 * 2
                o = self.off
                assert o + n <= ARENA, (o, n, key)
                a = ar[:, o:o + n]
                self.off += n
                self.hi = max(self.hi, self.off)
                if key is not None:
                    p.region(key, [(o, o + n)])
                if dt in (F32, I32):
                    a = a.bitcast(dt)
                if len(shape) == 2:
                    a = a.rearrange("p (a b) -> p a b", b=shape[1])
                elif len(shape) == 3:
                    a = a.rearrange("p (a b c) -> p a b c", b=shape[1], c=shape[2])
                return a

        cv = Carver()
        sq_ring = Ring("sq", [cv.take([512], BF16) for _ in range(2)])
        std_t = cv.take([512], F32)
        rstd_t = cv.take([512], F32)
        junk = cv.take([512], BF16)
        small = cv.take([128], F32)
        base_off = cv.off

        def mk_hn(pfx):
            return dict(pfx=pfx, sq=cv.take([8, 96], F32, pfx + "hn_sq"), rt=cv.take([8, 32], F32, pfx + "hn_rt"),
                        t1=cv.take([8, 32], F32, pfx + "hn_t1"), t2=cv.take([8, 32], F32, pfx + "hn_t2"))

        gen_banks = Ring("ps", [ps[:, b, :] for b in range(6)])
        o_banks = Ring("po", [ps[:, 6, :], ps[:, 7, :]])
        wring = Ring("w", [wr_t[:, s, :] for s in range(3)])

        def wload(pairs_fn):
            slot, key = wring.next()
            p.dma("pool", pairs_fn(slot), reads=[], writes=[key], semname=key)
            return slot, key

        def bf_bank(bank):
            return bank.bitcast(BF16).rearrange("p (a b) -> p a b", b=128)

        p.dma("sp", [(cst[:, :], cst_d[:, :]), (cbf[:, :], cbf_d[:, :])], [], ["cst"], "cst")
        for t in range(4):
            p.dma("sp", [(xT[:, :, t * 512:(t + 1) * 512],
                          xT_d[:, t * 512:(t + 1) * 512].rearrange("(kc p) t -> p kc t", p=128))],
                  [], ["x%d" % t], "xin%d" % t)

        cv.off = base_off
        pos_i = cv.take([16], I32, "pos")
        posf = cv.take([16], F32, "posf")
        angA = cv.take([16, 32], F32, "angA")
        angT = cv.take([16, 32], F32, "angT")
        angI = cv.take([16, 32], I32, "angI")
        angF = cv.take([16, 32], F32, "angF")
        angM = cv.take([16, 32], F32, "angM")
        SC = cv.take([16, 32], F32, "SC")
        sgtmp = cv.take([8, 128], F32, "sgtmp")
        _save = cv.off
        cv.off = 36640
        memT = cv.take([8, 256], F32, "memT")
        memnT = cv.take([8, 256], BF16, "memnT")
        km_sb = cv.take([4, 128], F32, "km_sb")
        kmn = cv.take([4, 128], BF16, "kmn")
        hn0 = mk_hn("s_")
        cv.off = _save

        p.dma("sp", [(pos_i, pos_d[:, :])], [], ["pos"], "misc")
        p.dma("sp", [(sgtmp, sgwT_d[:, :].rearrange("p (g t) -> p g t", t=128))], [], ["sgtmp"], "misc2")
        p.dma("sp", [(memT, memT_d[:, :].rearrange("(kc p) m -> p kc m", p=128))], [], ["memT"], "misc3")

        p.dve(lambda e: e.tensor_copy(out=posf, in_=pos_i), ["pos"], ["posf"])
        for i in range(16):
            p.dve(lambda e, i=i: e.tensor_scalar(out=angA[:, i, 0:16], in0=C("invf"), scalar1=posf[:, i:i + 1],
                                                 scalar2=None, op0=ALU.mult), ["posf", "cst"], ["angA"])
        p.dve(lambda e: e.tensor_scalar(out=angA[:, :, 16:32], in0=angA[:, :, 0:16], scalar1=PI / 2, scalar2=None,
                                        op0=ALU.add), ["angA"], ["angA"])
        p.dve(lambda e: e.tensor_scalar(out=angT, in0=angA, scalar1=1.0 / (2 * PI), scalar2=None, op0=ALU.mult),
              ["angA"], ["angT"])
        p.dve(lambda e: e.tensor_copy(out=angI, in_=angT), ["angT"], ["angI"])
        p.dve(lambda e: e.tensor_copy(out=angF, in_=angI), ["angI"], ["angF"])
        p.dve(lambda e: e.scalar_tensor_tensor(out=angT, in0=angF, scalar=-2 * PI, in1=angA, op0=ALU.mult,
                                               op1=ALU.add), ["angF", "angA"], ["angT"])
        p.dve(lambda e: e.tensor_scalar(out=angM, in0=angT, scalar1=PI, scalar2=None, op0=ALU.is_gt),
              ["angT"], ["angM"])
        p.dve(lambda e: e.scalar_tensor_tensor(out=angF, in0=angM, scalar=-2 * PI, in1=angT, op0=ALU.mult,
                                               op1=ALU.add), ["angM", "angT"], ["angF"])
        p.dve(lambda e: e.tensor_scalar(out=angM, in0=angF, scalar1=-PI, scalar2=None, op0=ALU.is_lt),
              ["angF"], ["angM"])
        p.dve(lambda e: e.scalar_tensor_tensor(out=angT, in0=angM, scalar=2 * PI, in1=angF, op0=ALU.mult,
                                               op1=ALU.add), ["angM", "angF"], ["angT"])
        p.dve(lambda e: e.tensor_scalar(out=angF, in0=angT, scalar1=PI, scalar2=-PI, op0=ALU.min, op1=ALU.max),
              ["angT"], ["angF"])
        p.act(lambda e: e.activation(out=SC, in_=angF, func=AF.Sin), ["angF"], ["SC"])
        p.dve(lambda e: e.tensor_copy(out=csA[:, :, 0:16], in_=SC[:, :, 16:32]), ["SC"], ["csA"])
        p.dve(lambda e: e.tensor_copy(out=csA[:, :, 16:32], in_=SC[:, :, 16:32]), ["SC"], ["csA"])
        p.dve(lambda e: e.tensor_scalar(out=csB[:, :, 0:16], in0=SC[:, :, 0:16], scalar1=-1.0, scalar2=None,
                                        op0=ALU.mult), ["SC"], ["csB"])
        p.dve(lambda e: e.tensor_copy(out=csB[:, :, 16:32], in_=SC[:, :, 0:16]), ["SC"], ["csB"])
        p.dve(lambda e: e.tensor_tensor(out=wTsg[:, :, :], in0=sgtmp,
                                        in1=C("tri").unsqueeze(1).broadcast_to([128, 8, 128]), op=ALU.mult),
              ["sgtmp", "cst"], ["wTsg"])

        def norm_fm(src, srckeys, gname, dst, dstkey, N, bank=None):
            bank, bkey = bank if bank is not None else gen_banks.next()
            for kc in range(8):
                sq, sqk = sq_ring.next()
                s_ap = src(kc)
                p.act(lambda e, s_ap=s_ap, sq=sq: e.activation(out=sq[:, :N], in_=s_ap, func=AF.Square),
                      srckeys, [sqk])
                p.pe(lambda e, kc=kc, sq=sq: e.matmul(bank[:, :N], ones, sq[:, :N], start=(kc == 0), stop=(kc == 7)),
                     [sqk, "cst"], [bkey])
            p.act(lambda e: e.activation(out=std_t[:, :N], in_=bank[:, :N], func=AF.Sqrt, bias=EPS, scale=1.0 / D),
                  [bkey], ["std"])
            p.dve(lambda e: e.reciprocal(out=rstd_t[:, :N], in_=std_t[:, :N]), ["std"], ["rstd"])
            g = C(gname)
            for kc in range(8):
                s_ap, d_ap = src(kc), dst(kc)
                p.dve(lambda e, kc=kc, s_ap=s_ap, d_ap=d_ap: e.scalar_tensor_tensor(
                    out=d_ap, in0=s_ap, scalar=g[:, kc:kc + 1], in1=rstd_t[:, :N], op0=ALU.mult, op1=ALU.mult),
                    srckeys + ["rstd", "cst"], [dstkey])

        def head_norm(hn, src, H, Dh, gname, out_bf, key_in, key_out, rope_i=None):
            while hn.get("busy"):
                yield
            hn["busy"] = True
            pf = hn["pfx"]
            rt, t1, t2 = hn["rt"], hn["t1"], hn["t2"]
            ksq, krt, kt1, kt2 = pf + "hn_sq", pf + "hn_rt", pf + "hn_t1", pf + "hn_t2"
            sqv = hn["sq"].rearrange("p a b -> p (a b)")[:, 0:H * Dh].rearrange("p (a b) -> p a b", b=Dh)
            ss = small[:, 0:H]
            sd = small[:, 8:8 + H]
            rs = small[:, 16:16 + H]
            g = C(gname)
            p.dve(lambda e: e.tensor_tensor(out=sqv, in0=src, in1=src, op=ALU.mult), [key_in], [ksq])
            yield
            p.dve(lambda e: e.tensor_reduce(out=ss, in_=sqv, axis=AX.X, op=ALU.add), [ksq], ["hn_ss"])
            yield
            p.act(lambda e: e.activation(out=sd, in_=ss, func=AF.Sqrt, bias=EPS, scale=1.0 / Dh), ["hn_ss"], ["hn_sd"])
            yield
            p.dve(lambda e: e.reciprocal(out=rs, in_=sd), ["hn_sd"], ["hn_rs"])
            yield
            p.dve(lambda e: e.tensor_tensor(out=sqv, in0=src, in1=rs.unsqueeze(2).broadcast_to([128, H, Dh]),
                                            op=ALU.mult), [key_in, "hn_rs"], [ksq])
            yield
            if rope_i is None:
                p.dve(lambda e: e.tensor_tensor(out=out_bf, in0=sqv, in1=g.unsqueeze(1).broadcast_to([128, H, Dh]),
                                                op=ALU.mult), [ksq, "cst"], [key_out])
                yield
                hn["busy"] = False
                return
            i = rope_i
            p.dve(lambda e: e.tensor_tensor(out=out_bf[:, :, 0:64], in0=sqv[:, :, 0:64],
                                            in1=g[:, 0:64].unsqueeze(1).broadcast_to([128, H, 64]), op=ALU.mult),
                  [ksq, "cst"], [key_out])
            yield
            p.dve(lambda e: e.tensor_tensor(out=rt, in0=sqv[:, :, 64:96],
                                            in1=g[:, 64:96].unsqueeze(1).broadcast_to([128, H, 32]), op=ALU.mult),
                  [ksq, "cst"], [krt])
            yield
            p.dve(lambda e: e.tensor_tensor(out=t1, in0=rt,
                                            in1=csA[:, i, :].unsqueeze(1).broadcast_to([128, H, 32]), op=ALU.mult),
                  [krt, "csA"], [kt1])
            yield
            p.dve(lambda e: e.tensor_tensor(out=t2[:, :, 0:16], in0=rt[:, :, 16:32],
                                            in1=csB[:, i, 0:16].unsqueeze(1).broadcast_to([128, H, 16]), op=ALU.mult),
                  [krt, "csB"], [kt2])
            yield
            p.dve(lambda e: e.tensor_tensor(out=t2[:, :, 16:32], in0=rt[:, :, 0:16],
                                            in1=csB[:, i, 16:32].unsqueeze(1).broadcast_to([128, H, 16]), op=ALU.mult),
                  [krt, "csB"], [kt2])
            yield
            p.dve(lambda e: e.tensor_tensor(out=out_bf[:, :, 64:96], in0=t1, in1=t2, op=ALU.add),
                  [kt1, kt2], [key_out])
            yield
            hn["busy"] = False

        def mm_group(bank_ap, bkey, items, reads):
            def fn(e):
                n = len(items)
                ins = None
                for k, (l, r) in enumerate(items):
                    ins = e.matmul(bank_ap, l, r, start=(k == 0), stop=(k == n - 1))
                return ins
            p.pe(fn, reads, [bkey])

        def transposes(bank, bkey, srcs, reads, rows=128):
            bb = bf_bank(bank)

            def fn(e):
                ins = None
                for j, s in enumerate(srcs):
                    ins = e.transpose(out=bb[0:rows, j, :], in_=s, identity=ident)
                return ins
            p.pe(fn, reads + ["cst"], [bkey])
            return bb

        def ffn(wgu_d, wdn_d, gname):
            cv.off = base_off
            ho = cv.off
            hT = cv.take([8, 1024], BF16)
            for sub in range(2):
                p.region("h%d" % sub, [(ho + kc * 1024 + sub * 512, ho + kc * 1024 + sub * 512 + 512) for kc in range(8)])
            ao = cv.off
            actT = cv.take([FC, 1024], BF16)
            for f in range(FC):
                for sub in range(2):
                    p.region("a%d_%d" % (f, sub), [(ao + f * 1024 + sub * 512, ao + f * 1024 + sub * 512 + 512)])
            sg_ring = Ring("sg", [cv.take([512], F32, "sg%d" % k) for k in range(2)])
            assert cv.off <= 36640, cv.off
            for half in range(2):
                for sub in range(2):
                    t = half * 2 + sub
                    norm_fm(lambda kc, t=t: xT[:, kc, t * 512:(t + 1) * 512], ["x%d" % t], gname,
                            lambda kc, sub=sub: hT[:, kc, sub * 512:(sub + 1) * 512], "h%d" % sub, 512)
                for pp in range(11):
                    slot, wkey = wload(lambda s, pp=pp: [
                        (s.rearrange("p (a kc n) -> p a kc n", a=2, n=256)[:, 0],
                         wgu_d[:, pp * 256:(pp + 1) * 256].rearrange("(kc p) n -> p kc n", p=128)),
                        (s.rearrange("p (a kc n) -> p a kc n", a=2, n=256)[:, 1],
                         wgu_d[:, DFF + pp * 256:DFF + (pp + 1) * 256].rearrange("(kc p) n -> p kc n", p=128))])
                    w4 = slot.rearrange("p (a kc n) -> p a kc n", a=2, n=256)
                    for fi in range(2):
                        f = pp * 2 + fi
                        for sub in range(2):
                            bg, bgk = gen_banks.next()
                            bu, buk = gen_banks.next()
                            hs = lambda kc, sub=sub: hT[:, kc, sub * 512:(sub + 1) * 512]
                            mm_group(bg, bgk, [(w4[:, 0, kc, fi * 128:(fi + 1) * 128], hs(kc)) for kc in range(8)],
                                     [wkey, "h%d" % sub])
                            mm_group(bu, buk, [(w4[:, 1, kc, fi * 128:(fi + 1) * 128], hs(kc)) for kc in range(8)],
                                     [wkey, "h%d" % sub])
                            sg, sgk = sg_ring.next()
                            p.act(lambda e, sg=sg, bg=bg: e.activation(out=sg, in_=bg, func=AF.Silu), [bgk], [sgk])
                            a_ap = actT[:, f, sub * 512:(sub + 1) * 512]
                            p.dve(lambda e, sg=sg, bu=bu, a_ap=a_ap: e.tensor_tensor(out=a_ap, in0=bu, in1=sg, op=ALU.mult),
                                  [sgk, buk], ["a%d_%d" % (f, sub)])
                for m in range(8):
                    slot, wkey = wload(lambda s, m=m: [
                        (s[:, 0:FC * 128].rearrange("p (f n) -> p f n", n=128),
                         wdn_d[:, m * 128:(m + 1) * 128].rearrange("(f p) n -> p f n", p=128))])
                    w3 = slot[:, 0:FC * 128].rearrange("p (f n) -> p f n", n=128)
                    for sub in range(2):
                        t = half * 2 + sub
                        bd, bdk = gen_banks.next()
                        mm_group(bd, bdk, [(w3[:, f, :], actT[:, f, sub * 512:(sub + 1) * 512]) for f in range(FC)],
                                 [wkey] + ["a%d_%d" % (f, sub) for f in range(FC)])
                        xs = xT[:, m, t * 512:(t + 1) * 512]
                        p.dve(lambda e, bd=bd, xs=xs: e.scalar_tensor_tensor(out=xs, in0=bd, scalar=0.5, in1=xs,
                                                                            op0=ALU.mult, op1=ALU.add),
                              [bdk, "x%d" % t], ["x%d" % t])

        def store_out():
            for t in range(4):
                p.dma("sp", [(oT_d[:, t * 512:(t + 1) * 512].rearrange("(kc p) t -> p kc t", p=128),
                              xT[:, :, t * 512:(t + 1) * 512])], ["x%d" % t], ["out%d" % t], "out", final=True)

        def finish(store=True):
            if store:
                store_out()
            with nc.Block() as block:
                p.emit(block)
            return nc

        if stop != "noffn1":
            ffn(w1gu_d, w1dn_d, "ffn1_g")
        norm_fm(lambda kc: memT[:, kc, :], ["memT"], "mem_g", lambda kc: memnT[:, kc, :], "memnT", 256)
        wk, wkk = wload(lambda s: [(s.rearrange("p (kc n) -> p kc n", n=512),
                                    wmkv_d[:, 0:512].rearrange("(kc p) n -> p kc n", p=128))])
        wv_, wvk = wload(lambda s: [(s.rearrange("p (kc n) -> p kc n", n=512),
                                     wmkv_d[:, 512:1024].rearrange("(kc p) n -> p kc n", p=128))])
        wk3 = wk.rearrange("p (kc n) -> p kc n", n=512)
        wv3 = wv_.rearrange("p (kc n) -> p kc n", n=512)
        for mb in range(2):
            bk, bkk = gen_banks.next()
            mm_group(bk, bkk, [(memnT[:, kc, mb * 128:(mb + 1) * 128], wk3[:, kc, :]) for kc in range(8)],
                     ["memnT", wkk])
            bv, bvk = gen_banks.next()
            mm_group(bv, bvk, [(memnT[:, kc, mb * 128:(mb + 1) * 128], wv3[:, kc, :]) for kc in range(8)],
                     ["memnT", wvk])
            p.act(lambda e, mb=mb, bv=bv: e.activation(out=Vmem[:, mb, :], in_=bv, func=AF.Copy), [bvk], ["Vmem"])
            p.act(lambda e, bk=bk: e.activation(out=km_sb.rearrange("p a b -> p (a b)"), in_=bk, func=AF.Copy),
                  [bkk], ["km_sb"])
            for _ in head_norm(hn0, km_sb, 4, 128, "mkg", kmn, "km_sb", "kmn"):
                pass
            bt, btk = gen_banks.next()
            bb = transposes(bt, btk, [kmn[:, h, :] for h in range(4)], ["kmn"])
            p.act(lambda e, mb=mb, bb=bb: e.activation(out=KmemT[:, :, mb * 128:(mb + 1) * 128], in_=bb[:, 0:4, :],
                                                       func=AF.Copy), [btk], ["KmemT"])

        if stop == "ffn1":
            return finish()

        def run_pipelined(gens, depth=2):
            pending = list(gens)
            active = []
            while pending or active:
                if pending and len(active) < depth:
                    active.append(pending.pop(0))
                for g in list(active):
                    try:
                        next(g)
                    except StopIteration:
                        active.remove(g)

        cv.off = base_off
        hT2 = [cv.take([8, 512], BF16, "hT%d" % k) for k in range(2)]
        Kst = cv.take([8, NT], BF16, "Kst")
        Vst = cv.take([8 * 16 * 64], BF16, "Vst").rearrange("p (h i d) -> p h i d", h=8, i=16)
        m1sets = []
        for s in range(3):
            m1sets.append(dict(ckvn=cv.take([256], BF16, "ckvn%d" % s), ckvnT=cv.take([2, 128], BF16, "ckvnT%d" % s),
                               kc_sb=cv.take([8, 96], F32, "kc_sb%d" % s), kfin=cv.take([8, 96], BF16, "kfin%d" % s)))
        hn1 = mk_hn("m_")

        wkvin, wkvin_k = wload(lambda s: [(s[:, 0:8 * 288].rearrange("p (kc n) -> p kc n", n=288),
                                           win_d[:, C_CKV:C_CKV + 288].rearrange("(kc p) n -> p kc n", p=128))])
        wkvin3 = wkvin[:, 0:8 * 288].rearrange("p (kc n) -> p kc n", n=288)
        wukv, wukv_k = wload(lambda s: [(s[:, 0:2048].rearrange("p (kc n) -> p kc n", n=1024),
                                         wukv_d[:, :].rearrange("(kc p) n -> p kc n", p=128))])
        wukv3 = wukv[:, 0:2048].rearrange("p (kc n) -> p kc n", n=1024)
        norm_banks = o_banks

        pre_w = {}

        def m1_block(t, bl):
            i = t * 4 + bl
            s = i % 3
            S = m1sets[s]
            ckvn, ckvnT, kc_sb, kfin = S["ckvn"], S["ckvnT"], S["kc_sb"], S["kfin"]
            kq = lambda n: "%s%d" % (n, s)
            hT = hT2[t % 2]
            hk = "hT%d" % (t % 2)
            ss1 = small[:, 32 + 4 * s:33 + 4 * s]
            sd1 = small[:, 33 + 4 * s:34 + 4 * s]
            rs1 = small[:, 34 + 4 * s:35 + 4 * s]
            bA, bAk = ps[:, 2 * s + 0, :], "ps%d" % (2 * s + 0)
            bB, bBk = ps[:, 2 * s + 1, :], "ps%d" % (2 * s + 1)
            bC, bCk = bB, bBk
            if bl == 0:
                nb, nbk = norm_banks.next()
                norm_fm(lambda kc, t=t: xT[:, kc, t * 512:(t + 1) * 512], ["x%d" % t], "mix_g",
                        lambda kc: hT[:, kc, :], hk, 512, bank=(nb, nbk))
            mm_group(bA[:, 0:288], bAk, [(hT[:, kc, bl * 128:(bl + 1) * 128], wkvin3[:, kc, :]) for kc in range(8)],
                     [hk, wkvin_k])
            yield
            p.act(lambda e: e.activation(out=junk[:, 0:256], in_=bA[:, 0:256], func=AF.Square, accum_out=ss1),
                  [bAk], [kq("ss1"), "junk"])
            yield
            p.act(lambda e: e.activation(out=sd1, in_=ss1, func=AF.Sqrt, bias=EPS, scale=1.0 / 256), [kq("ss1")], [kq("sd1")])
            p.act(lambda e: e.activation(out=kc_sb[:, :, 64:96], in_=bA[:, 256:288].unsqueeze(1).broadcast_to([128, 8, 32]),
                                         func=AF.Copy), [bAk], [kq("kc_sb")])
            yield
            p.dve(lambda e: e.reciprocal(out=rs1, in_=sd1), [kq("sd1")], [kq("rs1")])
            yield
            p.dve(lambda e: e.scalar_tensor_tensor(out=ckvn, in0=bA[:, 0:256], scalar=rs1, in1=C("ckvg"),
                                                   op0=ALU.mult, op1=ALU.mult), [bAk, kq("rs1"), "cst"], [kq("ckvn")])
            yield
            bb = transposes(bB, bBk, [ckvn[:, k * 128:(k + 1) * 128] for k in range(2)], [kq("ckvn")])
            p.act(lambda e: e.activation(out=ckvnT, in_=bb[:, 0:2, :], func=AF.Copy), [bBk], [kq("ckvnT")])
            yield
            mm_group(bC, bCk, [(ckvnT[:, k, :], wukv3[:, k, 0:512]) for k in range(2)], [kq("ckvnT"), wukv_k])
            mm_group(bA, bAk, [(ckvnT[:, k, :], wukv3[:, k, 512:1024]) for k in range(2)], [kq("ckvnT"), wukv_k])
            for hb, (bx, bxk) in enumerate(((bC, bCk), (bA, bAk))):
                b3 = bx.rearrange("p (h d) -> p h d", d=128)
                p.act(lambda e, b3=b3, hb=hb: e.activation(out=Vst[:, hb * 4:(hb + 1) * 4, i, :],
                                                           in_=b3[:, :, 64:128], func=AF.Copy), [bxk], ["Vst"])
                p.act(lambda e, b3=b3, hb=hb: e.activation(out=kc_sb[:, hb * 4:(hb + 1) * 4, 0:64],
                                                           in_=b3[:, :, 0:64], func=AF.Copy), [bxk], [kq("kc_sb")])
            yield
            yield from head_norm(hn1, kc_sb, 8, 96, "kg", kfin, kq("kc_sb"), kq("kfin"), rope_i=i)
            bb2 = transposes(bB, bBk, [kfin[:, h, :] for h in range(8)], [kq("kfin")], rows=96)
            p.act(lambda e: e.activation(out=Kst[0:96, :, i * 128:(i + 1) * 128], in_=bb2[0:96, :, :],
                                         func=AF.Copy), [bBk], ["Kst"])
            if bl == 3 and stop not in ("m1nocc", "dump_m1"):
                yield
                if t == 3:
                    pre_w["wv"] = wload(lambda s: [(s.rearrange("p (kc n) -> p kc n", n=512),
                                                    win_d[:, C_V:C_V + 512].rearrange("(kc p) n -> p kc n", p=128))])
                    pre_w["wu"] = wload(lambda s: [(s.rearrange("p (kc n) -> p kc n", n=512),
                                                    win_d[:, C_U:C_U + 512].rearrange("(kc p) n -> p kc n", p=128))])
                p.dma("sp", [(kin_t[t].ap().rearrange("(h f) c -> f h c", f=96), Kst[0:96, :, t * 512:(t + 1) * 512])],
                      ["Kst"], ["kin%d" % t], "kin%d" % t)
                p.dma("sp", [(vin_t[t].ap().rearrange("(h p) (i d) -> p h i d", p=128, d=64), Vst[:, :, 4 * t:4 * t + 4, :])],
                      ["Vst"], ["vin%d" % t], "vin%d" % t)
                p.collective(lambda e: e.collective_compute(
                    "AllGather", ALU.bypass, replica_groups=[[0, 1, 2, 3], [4, 5, 6, 7]],
                    ins=[kin_t[t].ap().opt()], outs=[kout_t[t].ap().opt()]),
                    ["kin%d" % t], ["kout%d" % t], semname="cck%d" % t)
                p.collective(lambda e: e.collective_compute(
                    "AllGather", ALU.bypass, replica_groups=[[0, 1, 2, 3], [4, 5, 6, 7]],
                    ins=[vin_t[t].ap().opt()], outs=[vout_t[t].ap().opt()]),
                    ["vin%d" % t], ["vout%d" % t], semname="ccv%d" % t)

        run_pipelined([m1_block(t, bl) for t in range(4) for bl in range(4)], depth=3)
        if stop == "dump_m1":
            for h in range(8):
                p.dma("pool", [(oT_d[h * 96:(h + 1) * 96, :], Kst[0:96, h, :])], ["Kst"], ["dbg%d" % h], "dbg", final=True)
            for hq in range(2):
                p.dma("pool", [(oT_d[768 + hq * 128:768 + (hq + 1) * 128, :].rearrange("p (h i d) -> p h i d", h=2, i=16),
                                Vst[:, 2 * hq:2 * hq + 2, :, :])], ["Vst"], ["dbgv%d" % hq], "dbg", final=True)
            return finish(store=False)
        if stop == "m1":
            return finish()
        if stop == "m1nocc":
            return finish()
        cv.off = base_off
        hT = cv.take([8, 512], BF16, "hT")
        QT = cv.take([8, 512], BF16, "QT")
        qmT = cv.take([4, 512], BF16, "qmT")
        yT = [cv.take([4, 512], BF16, k) for k in ("yaT", "ybT", "ycT")]
        Kring_v = [cv.take([2048], BF16, "K%d" % s) for s in range(4)]
        Vring_v = [cv.take([16, 128], BF16, "V%d" % s) for s in range(4)]
        ph_off = cv.off
        uT_sb = cv.take([4, 512], BF16, "uT")
        Asets = [dict(v_sb=cv.take([512], F32, "v_sb%d" % s), v_ln=cv.take([512], BF16, "v_ln%d" % s),
                      mtmp=cv.take([4, 128], F32, "mtmp%d" % s)) for s in range(3)]
        cv.off = ph_off
        Qsets = [dict(cqn=cv.take([384], BF16, "cqn%d" % s), cqnT=cv.take([3, 128], BF16, "cqnT%d" % s),
                      q_sb=cv.take([8, 96], F32, "q_sb%d" % s), qfin=cv.take([8, 96], BF16, "qfin%d" % s))
                 for s in range(2)]
        hn2 = mk_hn("t_")
        q_end = cv.off
        cv.off = ph_off
        Csets = [dict(qm_sb=cv.take([4, 128], F32, "qm_sb%d" % s), qmn=cv.take([4, 128], BF16, "qmn%d" % s))
                 for s in range(3)]
        assert cv.off <= ph_off + 2 * (384 + 384 + 1536 + 768), "C sets overlap hn temps"
        assert cv.off <= q_end - 3 * 1024 - 1536 * 2 or True
        cv.off = ph_off
        P_ring = Ring("P", [cv.take([512], BF16, "P%d" % k) for k in range(8)])
        rd_views = [cv.take([512], F32, "rd%d" % k) for k in range(2)]
        rden_ring = Ring("rd", rd_views)
        acc1, acc2 = rd_views
        mergedT = cv.take([8, 512], BF16, "mergedT")
        go = cv.off
        g_sb = cv.take([3, 512], BF16)
        for br in range(3):
            p.region("g_sb%d" % br, [(go + br * 512, go + br * 512 + 512)])

        for s in range(4):
            lo = 64 if s < 2 else 0
            p.pool(lambda e, s=s, lo=lo: e.memset(Vring_v[s][:, :, lo:lo + 64], 1.0), [], ["V%d" % s])

        SC_B = 96 ** -0.5
        SC_C = 128 ** -0.5
        st6 = small[:, 40:46]
        mv = small[:, 46:48]
        sdv = small[:, 48:49]
        rsv = small[:, 49:50]

        for t in range(4):
            xk = "x%d" % t
            norm_fm(lambda kc, t=t: xT[:, kc, t * 512:(t + 1) * 512], [xk], "mix_g",
                    lambda kc: hT[:, kc, :], "hT", 512)
            wv_s, wv_k = pre_w.pop("wv") if "wv" in pre_w else wload(lambda s: [
                (s.rearrange("p (kc n) -> p kc n", n=512),
                 win_d[:, C_V:C_V + 512].rearrange("(kc p) n -> p kc n", p=128))])
            wu_s, wu_k = pre_w.pop("wu") if "wu" in pre_w else wload(lambda s: [
                (s.rearrange("p (kc n) -> p kc n", n=512),
                 win_d[:, C_U:C_U + 512].rearrange("(kc p) n -> p kc n", p=128))])
            wv3 = wv_s.rearrange("p (kc n) -> p kc n", n=512)
            wu3 = wu_s.rearrange("p (kc n) -> p kc n", n=512)
            for c in range(4):
                bu, buk = o_banks.next()
                mm_group(bu, buk, [(wu3[:, kc, c * 128:(c + 1) * 128], hT[:, kc, :]) for kc in range(8)], [wu_k, "hT"])
                p.act(lambda e, c=c, bu=bu: e.activation(out=uT_sb[:, c, :], in_=bu, func=AF.Gelu), [buk], ["uT"])

            def a_block(bl):
                s = bl % 3
                S = Asets[s]
                v_sb, v_ln, mtmp = S["v_sb"], S["v_ln"], S["mtmp"]
                kq = lambda n: "%s%d" % (n, s)
                st6 = small[:, 64 + 16 * s:70 + 16 * s]
                mv = small[:, 70 + 16 * s:72 + 16 * s]
                sdv = small[:, 72 + 16 * s:73 + 16 * s]
                rsv = small[:, 73 + 16 * s:74 + 16 * s]
                bV, bVk = ps[:, 2 * s + 0, :], "ps%d" % (2 * s + 0)
                bM, bMk = ps[:, 2 * s + 1, :], "ps%d" % (2 * s + 1)
                mm_group(bV, bVk, [(hT[:, kc, bl * 128:(bl + 1) * 128], wv3[:, kc, :]) for kc in range(8)], [wv_k, "hT"])
                yield
                p.act(lambda e: e.activation(out=v_sb, in_=bV, func=AF.Gelu), [bVk], [kq("v_sb")])
                yield
                p.dve(lambda e: e.bn_stats(out=st6, in_=v_sb), [kq("v_sb")], [kq("st6")])
                yield
                p.dve(lambda e: e.bn_aggr(out=mv, in_=st6), [kq("st6")], [kq("mv")])
                yield
                p.act(lambda e: e.activation(out=sdv, in_=mv[:, 1:2], func=AF.Sqrt, bias=EPS, scale=1.0), [kq("mv")], [kq("sdv")])
                p.dve(lambda e: e.scalar_tensor_tensor(out=v_sb, in0=v_sb, scalar=mv[:, 0:1], in1=C("lng"),
                                                       op0=ALU.subtract, op1=ALU.mult), [kq("v_sb"), kq("mv"), "cst"], [kq("v_sb")])
                yield
                p.dve(lambda e: e.reciprocal(out=rsv, in_=sdv), [kq("sdv")], [kq("rsv")])
                yield
                p.dve(lambda e: e.scalar_tensor_tensor(out=v_ln, in0=v_sb, scalar=rsv, in1=C("lnb"),
                                                       op0=ALU.mult, op1=ALU.add), [kq("v_sb"), kq("rsv"), "cst"], [kq("v_ln")])
                yield

                def mixfn(e):
                    ins = None
                    for g in range(8):
                        ins = e.matmul(bM[(g % 2) * 64:(g % 2) * 64 + 64, (g // 2) * 128:(g // 2 + 1) * 128],
                                       v_ln[:, g * 64:(g + 1) * 64], wTsg[:, g, :], start=True, stop=True)
                    return ins
                p.pe(mixfn, [kq("v_ln"), "wTsg"], [bMk])
                yield
                bm3 = bM.rearrange("p (c t) -> p c t", t=128)
                p.dve(lambda e: e.tensor_tensor(out=mtmp, in0=bm3, in1=C("bsT").rearrange("p (c t) -> p c t", t=128),
                                                op=ALU.add), [bMk, "cst"], [kq("mtmp")])
                yield
                p.dve(lambda e: e.tensor_tensor(out=yT[0][:, :, bl * 128:(bl + 1) * 128], in0=mtmp,
                                                in1=uT_sb[:, :, bl * 128:(bl + 1) * 128], op=ALU.mult),
                      [kq("mtmp"), "uT"], ["yaT"])
            run_pipelined([a_block(bl) for bl in range(4)], depth=3)
            wcq_s, wcq_k = wload(lambda s: [(s[:, 0:8 * 384].rearrange("p (kc n) -> p kc n", n=384),
                                             win_d[:, C_CQ:C_CQ + 384].rearrange("(kc p) n -> p kc n", p=128))])
            wuq_s, wuq_k = wload(lambda s: [(s[:, 0:3 * 768].rearrange("p (kc n) -> p kc n", n=768),
                                             wuq_d[:, :].rearrange("(kc p) n -> p kc n", p=128))])
            wcq3 = wcq_s[:, 0:8 * 384].rearrange("p (kc n) -> p kc n", n=384)
            wuq3 = wuq_s[:, 0:3 * 768].rearrange("p (kc n) -> p kc n", n=768)

            def q_block(bl):
                i = t * 4 + bl
                s = bl % 2
                S = Qsets[s]
                cqn, cqnT, q_sb, qfin = S["cqn"], S["cqnT"], S["q_sb"], S["qfin"]
                kq = lambda n: "%s%d" % (n, s)
                ss1 = small[:, 32 + 4 * s:33 + 4 * s]
                sd1 = small[:, 33 + 4 * s:34 + 4 * s]
                rs1 = small[:, 34 + 4 * s:35 + 4 * s]
                bA, bAk = ps[:, 3 * s + 0, :], "ps%d" % (3 * s + 0)
                bB, bBk = ps[:, 3 * s + 1, :], "ps%d" % (3 * s + 1)
                bC, bCk = ps[:, 3 * s + 2, :], "ps%d" % (3 * s + 2)
                mm_group(bA[:, 0:384], bAk, [(hT[:, kc, bl * 128:(bl + 1) * 128], wcq3[:, kc, :]) for kc in range(8)],
                         [wcq_k, "hT"])
                yield
                p.act(lambda e: e.activation(out=junk[:, 0:384], in_=bA[:, 0:384], func=AF.Square, accum_out=ss1),
                      [bAk], [kq("ss1"), "junk"])
                yield
                p.act(lambda e: e.activation(out=sd1, in_=ss1, func=AF.Sqrt, bias=EPS, scale=1.0 / 384), [kq("ss1")], [kq("sd1")])
                yield
                p.dve(lambda e: e.reciprocal(out=rs1, in_=sd1), [kq("sd1")], [kq("rs1")])
                yield
                p.dve(lambda e: e.scalar_tensor_tensor(out=cqn, in0=bA[:, 0:384], scalar=rs1, in1=C("cqg"),
                                                       op0=ALU.mult, op1=ALU.mult), [bAk, kq("rs1"), "cst"], [kq("cqn")])
                yield
                bb = transposes(bB, bBk, [cqn[:, k * 128:(k + 1) * 128] for k in range(3)], [kq("cqn")])
                p.act(lambda e: e.activation(out=cqnT, in_=bb[:, 0:3, :], func=AF.Copy), [bBk], [kq("cqnT")])
                yield
                for hb, (bx, bxk) in enumerate(((bC, bCk), (bA, bAk))):
                    mm_group(bx[:, 0:384], bxk, [(cqnT[:, k, :], wuq3[:, k, hb * 384:(hb + 1) * 384]) for k in range(3)],
                             [kq("cqnT"), wuq_k])
                    p.act(lambda e, bx=bx, hb=hb: e.activation(
                        out=q_sb[:, hb * 4:(hb + 1) * 4, :], in_=bx[:, 0:384].rearrange("p (h d) -> p h d", d=96),
                        func=AF.Copy), [bxk], [kq("q_sb")])
                yield
                yield from head_norm(hn2, q_sb, 8, 96, "qg", qfin, kq("q_sb"), kq("qfin"), rope_i=i)
                bb2 = transposes(bB, bBk, [qfin[:, h, :] for h in range(8)], [kq("qfin")], rows=96)
                p.act(lambda e: e.activation(out=QT[0:96, :, bl * 128:(bl + 1) * 128], in_=bb2[0:96, :, :],
                                             func=AF.Copy), [bBk], ["QT"])
            run_pipelined([q_block(bl) for bl in range(4)], depth=2)
            wqm_s, wqm_k = wload(lambda s: [(s.rearrange("p (kc n) -> p kc n", n=512),
                                             win_d[:, C_QM:C_QM + 512].rearrange("(kc p) n -> p kc n", p=128))])
            wqm3 = wqm_s.rearrange("p (kc n) -> p kc n", n=512)

            def c_block(bl):
                s = bl % 3
                S = Csets[s]
                qm_sb, qmn = S["qm_sb"], S["qmn"]
                kq = lambda n: "%s%d" % (n, s)
                bA, bAk = ps[:, 2 * s + 0, :], "ps%d" % (2 * s + 0)
                bB, bBk = ps[:, 2 * s + 1, :], "ps%d" % (2 * s + 1)
                mm_group(bA, bAk, [(hT[:, kc, bl * 128:(bl + 1) * 128], wqm3[:, kc, :]) for kc in range(8)],
                         [wqm_k, "hT"])
                yield
                p.act(lambda e: e.activation(out=qm_sb.rearrange("p a b -> p (a b)"), in_=bA, func=AF.Copy),
                      [bAk], [kq("qm_sb")])
                yield
                yield from head_norm(hn2, qm_sb, 4, 128, "mqg", qmn, kq("qm_sb"), kq("qmn"))
                bb = transposes(bB, bBk, [qmn[:, h, :] for h in range(4)], [kq("qmn")])
                p.act(lambda e: e.activation(out=qmT[:, :, bl * 128:(bl + 1) * 128], in_=bb[:, 0:4, :],
                                             func=AF.Copy), [bBk], ["qmT"])
            run_pipelined([c_block(bl) for bl in range(4)], depth=3)
            for h in range(4):
                Ps = []
                for mc in range(2):
                    bs, bsk = gen_banks.next()
                    mm_group(bs, bsk, [(KmemT[:, h, mc * 128:(mc + 1) * 128], qmT[:, h, :])], ["KmemT", "qmT"])
                    Pt, Pk = P_ring.next()
                    p.act(lambda e, bs=bs, Pt=Pt: e.activation(out=Pt, in_=bs, func=AF.Exp, scale=SC_C), [bsk], [Pk])
                    Ps.append((Pt, Pk))
                bo, bok = gen_banks.next()
                mm_group(bo, bok, [(Vmem[:, mc, h * 128:(h + 1) * 128], Ps[mc][0]) for mc in range(2)],
                         ["Vmem"] + [k for _, k in Ps])
                bd, bdk = gen_banks.next()
                mm_group(bd, bdk, [(ones, Ps[mc][0]) for mc in range(2)], ["cst"] + [k for _, k in Ps])
                rd, rdk = rden_ring.next()
                p.dve(lambda e, rd=rd, bd=bd: e.reciprocal(out=rd, in_=bd), [bdk], [rdk])
                p.dve(lambda e, rd=rd, bo=bo, h=h: e.tensor_tensor(out=yT[2][:, h, :], in0=bo, in1=rd, op=ALU.mult),
                      [bok, rdk], ["ycT"])
            nki = 4 * t + 4
            G = {0: 4, 1: 2}.get(t, 1)
            pend = []
            chunk_ctr = [0, 0]

            def rec_pv(item):
                (Pt, Pk, c0, Vc, vk, vi, ob, obk, first, last, fin) = item
                p.pe(lambda e: e.matmul(ob[:, c0:512], Vc[:, vi, :], Pt[:, c0:512], start=first, stop=last),
                     [Pk, vk], [obk])
                if last:
                    fin()
            for h in range(8):
                par = h % 2
                ob, obk = o_banks.next()
                olo = 0 if par == 0 else 64
                dlo = 64 - olo
                vlo = olo

                def fin(ob=ob, obk=obk, olo=olo, dlo=dlo, h=h):
                    rd, rdk = rden_ring.next()
                    p.dve(lambda e: e.reciprocal(out=rd[dlo:dlo + 64, :], in_=ob[dlo:dlo + 64, :]), [obk], [rdk])
                    p.dve(lambda e: e.tensor_tensor(out=yT[1][olo:olo + 64, h // 2, :], in0=ob[olo:olo + 64, :],
                                                    in1=rd[dlo:dlo + 64, :], op=ALU.mult), [obk, rdk], ["ybT"])
                npv = 0
                ntot = 4 * nki
                for c in range(4 // G):
                    slot = par * 2 + (chunk_ctr[par] % 2)
                    chunk_ctr[par] += 1
                    Kc = Kring_v[slot]
                    Vc = Vring_v[slot]
                    kk, vk = "K%d" % slot, "V%d" % slot
                    ranks = [c * G + gi for gi in range(G)]
                    p.dma("pool", [(Kc[0:96, gi * nki * 128 + tt * 512:gi * nki * 128 + (tt + 1) * 512],
                                    kout_t[tt].ap()[r * 768 + h * 96:r * 768 + (h + 1) * 96, :])
                                   for gi, r in enumerate(ranks) for tt in range(t + 1)],
                          ["kout%d" % tt for tt in range(t + 1)], [kk], kk)
                    p.dma("pool", [(Vc[:, gi * nki + 4 * tt:gi * nki + 4 * tt + 4, vlo:vlo + 64],
                                    vout_t[tt].ap()[r * 1024 + h * 128:r * 1024 + (h + 1) * 128, :].rearrange(
                                        "p (i d) -> p i d", d=64))
                                   for gi, r in enumerate(ranks) for tt in range(t + 1)],
                          ["vout%d" % tt for tt in range(t + 1)], [vk], vk)
                    for gi, r in enumerate(ranks):
                        for ki in range(nki):
                            d = ki - 4 * t
                            c0 = 128 * d if d > 0 else 0
                            kcol = (gi * nki + ki) * 128
                            bs, bsk = gen_banks.next()
                            p.pe(lambda e, bs=bs, Kc=Kc, kcol=kcol, c0=c0, h=h: e.matmul(
                                bs[:, c0:512], Kc[0:96, kcol:kcol + 128], QT[0:96, h, c0:512], start=True, stop=True),
                                [kk, "QT"], [bsk])
                            Pt, Pk = P_ring.next()
                            p.act(lambda e, bs=bs, Pt=Pt, c0=c0: e.activation(out=Pt[:, c0:512], in_=bs[:, c0:512],
                                                                             func=AF.Exp, scale=SC_B), [bsk], [Pk])
                            if d >= 0:
                                mk = masks[:, (ki % 2) * 4 + r, :]
                                p.dve(lambda e, Pt=Pt, c0=c0, mk=mk: e.tensor_tensor(
                                    out=Pt[:, c0:c0 + 128], in0=Pt[:, c0:c0 + 128], in1=mk, op=ALU.mult), [Pk, "cst"], [Pk])
                            pend.append((Pt, Pk, c0, Vc, vk, gi * nki + ki, ob, obk, npv == 0, npv == ntot - 1, fin))
                            npv += 1
                            if len(pend) > 4:
                                rec_pv(pend.pop(0))
            while pend:
                rec_pv(pend.pop(0))
            if stop == "dump_y%d" % t:
                for br in range(3):
                    p.dma("pool", [(oT_d[0:512, br * 512:(br + 1) * 512].rearrange("(kc p) t -> p kc t", p=128), yT[br])],
                          [("yaT", "ybT", "ycT")[br]], ["dbgy%d" % br], "dbg", final=True)
                return finish(store=False)
            for m in range(8):
                gs, gk = wload(lambda s, m=m: [
                    (s[:, 0:3072].rearrange("p (b kc n) -> p b kc n", b=3, n=128)[:, br],
                     win_d[:, C_GATE + br * 1024 + m * 128:C_GATE + br * 1024 + (m + 1) * 128].rearrange(
                         "(kc p) n -> p kc n", p=128)) for br in range(3)])
                bs_, bk_ = wload(lambda s, m=m: [
                    (s[:, 0:1536].rearrange("p (b kc n) -> p b kc n", b=3, n=128)[:, br],
                     wbr_d[br][:, m * 128:(m + 1) * 128].rearrange("(kc p) n -> p kc n", p=128)) for br in range(3)])
                g4 = gs[:, 0:3072].rearrange("p (b kc n) -> p b kc n", b=3, n=128)
                b4 = bs_[:, 0:1536].rearrange("p (b kc n) -> p b kc n", b=3, n=128)
                for br in range(3):
                    bg, bgk = gen_banks.next()
                    mm_group(bg, bgk, [(g4[:, br, kc, :], hT[:, kc, :]) for kc in range(8)], [gk, "hT"])
                    p.act(lambda e, bg=bg, br=br, m=m: e.activation(out=g_sb[:, br, :], in_=bg, func=AF.Sigmoid,
                                                                   bias=C("bgate")[:, br * 8 + m:br * 8 + m + 1], scale=1.0),
                          [bgk, "cst"], ["g_sb%d" % br])
                ykeys = ["yaT", "ybT", "ycT"]
                bbs = []
                for br in range(3):
                    bb_, bbk = gen_banks.next()
                    mm_group(bb_, bbk, [(b4[:, br, kc, :], yT[br][:, kc, :]) for kc in range(4)], [bk_, ykeys[br]])
                    bbs.append((bb_, bbk))
                p.dve(lambda e, b=bbs[0][0]: e.tensor_tensor(out=acc1, in0=b, in1=g_sb[:, 0, :], op=ALU.mult),
                      [bbs[0][1], "g_sb0"], ["rd0"])
                p.dve(lambda e, b=bbs[1][0]: e.tensor_tensor(out=acc2, in0=b, in1=g_sb[:, 1, :], op=ALU.mult),
                      [bbs[1][1], "g_sb1"], ["rd1"])
                p.dve(lambda e: e.tensor_tensor(out=acc1, in0=acc1, in1=acc2, op=ALU.add), ["rd0", "rd1"], ["rd0"])
                p.dve(lambda e, b=bbs[2][0]: e.tensor_tensor(out=acc2, in0=b, in1=g_sb[:, 2, :], op=ALU.mult),
                      [bbs[2][1], "g_sb2"], ["rd1"])
                p.dve(lambda e, m=m: e.tensor_tensor(out=mergedT[:, m, :], in0=acc1, in1=acc2, op=ALU.add),
                      ["rd0", "rd1"], ["mergedT"])
            for hf in range(2):
                ws, wk_ = wload(lambda s, hf=hf: [(s.rearrange("p (kc n) -> p kc n", n=512),
                                                   wout_d[:, hf * 512:(hf + 1) * 512].rearrange("(kc p) n -> p kc n", p=128))])
                w3 = ws.rearrange("p (kc n) -> p kc n", n=512)
                for mm in range(4):
                    m = hf * 4 + mm
                    bo, bok = gen_banks.next()
                    mm_group(bo, bok, [(w3[:, kc, mm * 128:(mm + 1) * 128], mergedT[:, kc, :]) for kc in range(8)],
                             [wk_, "mergedT"])
                    xs = xT[:, m, t * 512:(t + 1) * 512]
                    p.dve(lambda e, bo=bo, xs=xs: e.tensor_tensor(out=xs, in0=bo, in1=xs, op=ALU.add), [bok, xk], [xk])

        if stop != "mid":
            ffn(w2gu_d, w2dn_d, "ffn2_g")
        return finish()


def _perm_rows(j):
    rows = []
    for i in range(16):
        g = i // 2
        blk = 8 * g + (j if i % 2 == 0 else 7 - j)
        rows.append(np.arange(blk * 128, (blk + 1) * 128))
    return np.concatenate(rows)


def _host_inputs(inp):
    f = lambda a: np.ascontiguousarray(np.asarray(a, dtype=np.float32))
    x = f(inp["x"])
    mem = f(inp["mem"])
    pos = np.asarray(inp["positions"]).astype(np.int32)
    L0 = lambda k: f(inp[k])[0]

    def col(v, n):
        return np.ascontiguousarray(v.reshape(n, 128).T)

    def rep(v):
        return np.ascontiguousarray(np.broadcast_to(v[None, :], (128, v.shape[0])))

    cst = np.zeros((128, NCST), np.float32)

    def put(name, arr):
        o, w = CST[name]
        assert arr.shape == (128, w), (name, arr.shape)
        cst[:, o:o + w] = arr
    put("ffn1_g", col(L0("ffn1_norm"), 8))
    put("mix_g", col(L0("mix_norm"), 8))
    put("ffn2_g", col(L0("ffn2_norm"), 8))
    put("mem_g", col(L0("mem_norm"), 8))
    put("bgate", col(L0("b_gate"), 24))
    put("cqg", rep(L0("mla_cq_norm")))
    put("ckvg", rep(L0("mla_ckv_norm")))
    put("lng", rep(L0("sg_ln_g")))
    put("lnb", rep(L0("sg_ln_b")))
    put("qg", rep(L0("mla_q_norm")))
    put("kg", rep(L0("mla_k_norm")))
    put("mqg", rep(L0("mem_q_norm")))
    put("mkg", rep(L0("mem_k_norm")))
    sgb = L0("sg_b")
    bsT = np.zeros((128, 4, 128), np.float32)
    for c in range(4):
        bsT[0:64, c, :] = sgb[2 * c][None, :]
        bsT[64:128, c, :] = sgb[2 * c + 1][None, :]
    put("bsT", bsT.reshape(128, 512))
    half = 16
    invf = (10000.0 ** (-np.arange(half, dtype=np.float32) / half)).astype(np.float32)
    put("invf", rep(invf))
    tri = (np.arange(128)[:, None] <= np.arange(128)[None, :]).astype(np.float32)
    put("tri", tri)

    sgw = L0("sg_w")
    sgwT = np.ascontiguousarray(sgw.transpose(2, 0, 1)).reshape(128, 8 * 128)

    shared = {
        "cst": None, "sgwT": sgwT,
        "ffn1_w_gu": L0("ffn1_w_gu"), "ffn1_w_down": L0("ffn1_w_down"),
        "ffn2_w_gu": L0("ffn2_w_gu"), "ffn2_w_down": L0("ffn2_w_down"),
        "w_in": L0("w_in"), "mla_w_uq": L0("mla_w_uq"), "mla_w_ukv": L0("mla_w_ukv"),
        "mem_w_kv": L0("mem_w_kv"), "w_branch_a": L0("w_branch_a"), "w_branch_b": L0("w_branch_b"),
        "w_branch_c": L0("w_branch_c"), "w_out": L0("w_out"),
    }
    in_maps = []
    perms = []
    for c in range(NCORES):
        b, j = c // 4, c % 4
        rows = _perm_rows(j)
        perms.append((b, rows))
        m = dict(shared)
        m["cst"] = cst
        m["xT"] = np.ascontiguousarray(x[b][rows].T)
        m["memT"] = np.ascontiguousarray(mem[b].T)
        m["pos"] = np.ascontiguousarray(pos[b][rows].reshape(16, 128).T)
        cbf = np.zeros((128, NCBF), np.float32)
        cbf[:, 0:128] = np.eye(128, dtype=np.float32)
        cbf[:, 128:256] = 1.0
        mk = np.zeros((128, 8, 128), np.float32)
        for r in range(4):
            mk[:, r, :] = 1.0 if r < j else (tri if r == j else 0.0)
            mk[:, 4 + r, :] = 1.0 if r > j else (tri if r == j else 0.0)
        cbf[:, 256:] = mk.reshape(128, 1024)
        m["cbf"] = cbf.astype(ml_dtypes.bfloat16)
        in_maps.append(m)
    return in_maps, perms


_NC_CACHE = {}


def kernel(**inputs):
    in_maps, perms = _host_inputs(inputs)
    if "nc" not in _NC_CACHE:
        _NC_CACHE["nc"] = build()
    nc = _NC_CACHE["nc"]
    res = run_bass_kernel_spmd(nc, in_maps, core_ids=list(range(NCORES)))
    out = np.zeros((2, 8192, D), np.float32)
    for c in range(NCORES):
        b, rows = perms[c]
        out[b, rows, :] = np.asarray(res.results[c]["oT"], dtype=np.float32).T
    return out
```

```python
import numpy as np
import ml_dtypes
from contextlib import ExitStack
import concourse.bass as bass
import concourse.mybir as mybir
from concourse.bass_utils import run_bass_kernel_spmd

F32 = mybir.dt.float32
BF16 = mybir.dt.bfloat16
I32 = mybir.dt.int32
AF = mybir.ActivationFunctionType
ALU = mybir.AluOpType
AX = mybir.AxisListType

NCORES = 8
D = 1024
NT = 2048
DFF = 2816
FC = 22
EPS = 1e-6
C_U, C_V, C_CQ, C_CKV, C_KR, C_QM, C_GATE = 0, 512, 1024, 1408, 1664, 1696, 2208
PI = float(np.pi)

CST = {}
_off = 0
for _n, _w in [("ffn1_g", 8), ("mix_g", 8), ("ffn2_g", 8), ("mem_g", 8), ("bgate", 24), ("cqg", 384),
               ("ckvg", 256), ("lng", 512), ("lnb", 512), ("qg", 96), ("kg", 96), ("mqg", 128),
               ("mkg", 128), ("bsT", 512), ("invf", 16), ("tri", 128)]:
    CST[_n] = (_off, _w)
    _off += _w
NCST = _off
NCBF = 128 + 128 + 8 * 128


class Prog:
    ENG = ("pe", "act", "dve", "pool", "sp")

    def __init__(self, nc, es):
        self.nc, self.es = nc, es
        self.q = {e: [] for e in self.ENG}
        self.sems = {}
        self.cnt = {}
        self.waited = {e: {} for e in self.ENG}
        self.lastw = {}
        self.readers = {}
        self.out_tokens = []
        self.regions = {}
        self._ovc = {}

    def sem(self, name):
        if name not in self.sems:
            self.sems[name] = self.es.enter_context(self.nc.semaphore("s_" + name))
            self.cnt[name] = 0
        return self.sems[name]

    def region(self, key, ivs):
        self.regions[key] = list(ivs)

    def _overlap(self, k):
        if k not in self.regions:
            return (k,)
        c = self._ovc.get(k)
        if c is not None and c[0] == len(self.regions):
            return c[1]
        mine = self.regions[k]
        res = [k]
        for k2, ivs in self.regions.items():
            if k2 == k:
                continue
            hit = False
            for (a, b) in mine:
                for (c0, d0) in ivs:
                    if a < d0 and c0 < b:
                        hit = True
                        break
                if hit:
                    break
            if hit:
                res.append(k2)
        self._ovc[k] = (len(self.regions), tuple(res))
        return self._ovc[k][1]

    def _collect(self, eng, reads, writes, is_dma):
        toks = []
        for k in reads:
            for k2 in self._overlap(k):
                if k2 in self.lastw:
                    toks.append((self.lastw[k2], True))
        for k in writes:
            for k2 in self._overlap(k):
                if k2 in self.lastw:
                    toks.append((self.lastw[k2], False))
                rd = self.readers.get(k2)
                if rd:
                    toks.extend(((sn, v, te), False) for (sn, te), v in rd.items())
        waits = {}
        for ((sn, val, teng), raw) in toks:
            if teng is not None and teng == eng and not is_dma:
                if not raw or eng == "pe":
                    continue
            if self.waited[eng].get(sn, 0) >= val:
                continue
            waits[sn] = max(waits.get(sn, 0), val)
        for sn, val in waits.items():
            self.waited[eng][sn] = val
        return [(self.sem(sn), val, sn) for sn, val in waits.items()]

    def _commit(self, tok, reads, writes):
        sn, val, te = tok
        for k in reads:
            d = self.readers.setdefault(k, {})
            d[(sn, te)] = max(d.get((sn, te), 0), val)
        for k in writes:
            self.lastw[k] = tok
            self.readers[k] = {}

    def op(self, eng, fn, reads=(), writes=()):
        waits = self._collect(eng, reads, writes, False)
        sn = "E" + eng
        self.sem(sn)
        self.cnt[sn] += 1
        tok = (sn, self.cnt[sn], eng)
        self.q[eng].append((waits, fn, [(self.sems[sn], 1)], [(sn, 1)]))
        self._commit(tok, reads, writes)

    def pe(self, fn, r=(), w=()):
        self.op("pe", fn, r, w)

    def act(self, fn, r=(), w=()):
        self.op("act", fn, r, w)

    def dve(self, fn, r=(), w=()):
        self.op("dve", fn, r, w)

    def pool(self, fn, r=(), w=()):
        self.op("pool", fn, r, w)

    def dma(self, queue, pairs, reads, writes, semname, final=False):
        waits = self._collect(queue, reads, writes, True)
        s = self.sem(semname)
        self.cnt[semname] += 16 * len(pairs)
        tok = (semname, self.cnt[semname], None)

        def fn(e, pairs=pairs, s=s):
            for (o, i) in pairs:
                e.dma_start(out=o, in_=i).then_inc(s, 16)
            return None
        self.q[queue].append((waits, fn, [], [(semname, 16 * len(pairs))]))
        self._commit(tok, reads, writes)
        if final:
            self.out_tokens.append(tok)

    def collective(self, fn, reads, writes, semname="cc"):
        waits = self._collect("pool", reads, writes, True)
        s = self.sem(semname)
        self.cnt[semname] += 1
        tok = (semname, self.cnt[semname], None)
        self.q["pool"].append((waits, fn, [(s, 1)], [(semname, 1)]))
        self._commit(tok, reads, writes)

    def check(self):
        val = {}
        pos = {e: 0 for e in self.ENG}
        prog = True
        while prog:
            prog = False
            for e in self.ENG:
                while pos[e] < len(self.q[e]):
                    waits, fn, incs, names = self.q[e][pos[e]]
                    if all(val.get(sn, 0) >= v for (_s, v, sn) in waits):
                        for (sn, n) in names:
                            val[sn] = val.get(sn, 0) + n
                        pos[e] += 1
                        prog = True
                    else:
                        break
        stuck = {e: pos[e] for e in self.ENG if pos[e] < len(self.q[e])}
        for e, i in stuck.items():
            waits = self.q[e][i][0]
            print("DEADLOCK", e, "op", i, "of", len(self.q[e]), "waits",
                  [(sn, v, val.get(sn, 0)) for (_s, v, sn) in waits if val.get(sn, 0) < v])
        return not stuck

    def emit(self, block):
        assert self.check(), "semaphore protocol deadlock"
        def run(e, eng):
            for (waits, fn, incs, _n) in self.q[eng]:
                for (s, v, _sn) in waits:
                    e.wait_ge(s, v)
                ins = fn(e)
                for (s, n) in incs:
                    ins.then_inc(s, n)
            if eng == "sp":
                for (sn, val, _) in self.out_tokens:
                    e.wait_ge(self.sems[sn], val)

        @block.tensor
        def _(e):
            run(e, "pe")

        @block.scalar
        def _(e):
            run(e, "act")

        @block.vector
        def _(e):
            run(e, "dve")

        @block.gpsimd
        def _(e):
            run(e, "pool")

        @block.sync
        def _(e):
            run(e, "sp")


class Ring:
    def __init__(self, name, views):
        self.name, self.views, self.i = name, views, 0

    def next(self):
        k = self.i % len(self.views)
        self.i += 1
        return self.views[k], "%s%d" % (self.name, k)


def build(stop=None):
    nc = bass.Bass("TRN2", target_bir_lowering=False)

    def din(name, shape, dt=F32):
        return nc.dram_tensor(name, shape, dt, kind="ExternalInput").ap()

    xT_d = din("xT", [D, NT])
    memT_d = din("memT", [D, 256])
    pos_d = din("pos", [128, 16], I32)
    cst_d = din("cst", [128, NCST])
    cbf_d = din("cbf", [128, NCBF], BF16)
    sgwT_d = din("sgwT", [128, 8 * 128])
    w1gu_d = din("ffn1_w_gu", [D, 2 * DFF])
    w1dn_d = din("ffn1_w_down", [DFF, D])
    w2gu_d = din("ffn2_w_gu", [D, 2 * DFF])
    w2dn_d = din("ffn2_w_down", [DFF, D])
    win_d = din("w_in", [D, 5280])
    wuq_d = din("mla_w_uq", [384, 768])
    wukv_d = din("mla_w_ukv", [256, 1024])
    wmkv_d = din("mem_w_kv", [D, 1024])
    wbr_d = [din("w_branch_a", [512, D]), din("w_branch_b", [512, D]), din("w_branch_c", [512, D])]
    wout_d = din("w_out", [D, D])
    oT_d = nc.dram_tensor("oT", [D, NT], F32, kind="ExternalOutput").ap()
    kin_t = [nc.dram_tensor("kin%d" % c, [768, 512], BF16) for c in range(4)]
    kout_t = [nc.dram_tensor("kout%d" % c, [4 * 768, 512], BF16) for c in range(4)]
    vin_t = [nc.dram_tensor("vin%d" % c, [1024, 256], BF16) for c in range(4)]
    vout_t = [nc.dram_tensor("vout%d" % c, [4 * 1024, 256], BF16) for c in range(4)]

    es = ExitStack()
    with es:
        p = Prog(nc, es)

        def sb(name, shape, dt):
            return es.enter_context(nc.sbuf_tensor(name, shape, dt))

        xT = sb("xT_sb", [128, 8, NT], F32)
        cst = sb("cst_sb", [128, NCST], F32)
        cbf = sb("cbf_sb", [128, NCBF], BF16)
        csA = sb("csA", [128, 16, 32], F32)
        csB = sb("csB", [128, 16, 32], F32)
        wTsg = sb("wTsg", [128, 8, 128], BF16)
        KmemT = sb("KmemT", [128, 4, 256], BF16)
        Vmem = sb("Vmem", [128, 2, 512], BF16)
        wr_t = sb("wring", [128, 3, 4096], BF16)
        ARENA = 49300
        ar = sb("arena", [128, ARENA], BF16)
        ps = es.enter_context(nc.psum_tensor("ps", [128, 8, 512], F32))

        def C(name):
            o, w = CST[name]
            return cst[:, o:o + w]

        ident = cbf[:, 0:128]
        ones = cbf[:, 128:256]
        masks = cbf[:, 256:256 + 1024].rearrange("p (a b) -> p a b", b=128)

        class Carver:
            def __init__(self):
                self.off = 0
                self.hi = 0

            def take(self, shape, dt, key=None):
                n = int(np.prod(shape)) * (2 if dt in (F32, I32) else 1)
                n = (n + 1) // 2 * 2
                o = self.off
                assert o + n <= ARENA, (o, n, key)
                a = ar[:, o:o + n]
                self.off += n
                self.hi = max(self.hi, self.off)
                if key is not None:
                    p.region(key, [(o, o + n)])
                if dt in (F32, I32):
                    a = a.bitcast(dt)
                if len(shape) == 2:
                    a = a.rearrange("p (a b) -> p a b", b=shape[1])
                elif len(shape) == 3:
                    a = a.rearrange("p (a b c) -> p a b c", b=shape[1], c=shape[2])
                return a

        cv = Carver()
        sq_ring = Ring("sq", [cv.take([512], BF16) for _ in range(2)])
        std_t = cv.take([512], F32)
        rstd_t = cv.take([512], F32)
        junk = cv.take([512], BF16)
        small = cv.take([128], F32)
        base_off = cv.off

        def mk_hn(pfx):
            return dict(pfx=pfx, sq=cv.take([8, 96], F32, pfx + "hn_sq"), rt=cv.take([8, 32], F32, pfx + "hn_rt"),
                        t1=cv.take([8, 32], F32, pfx + "hn_t1"), t2=cv.take([8, 32], F32, pfx + "hn_t2"))

        gen_banks = Ring("ps", [ps[:, b, :] for b in range(6)])
        o_banks = Ring("po", [ps[:, 6, :], ps[:, 7, :]])
        wring = Ring("w", [wr_t[:, s, :] for s in range(3)])

        def wload(pairs_fn):
            slot, key = wring.next()
            p.dma("pool", pairs_fn(slot), reads=[], writes=[key], semname=key)
            return slot, key

        def bf_bank(bank):
            return bank.bitcast(BF16).rearrange("p (a b) -> p a b", b=128)

        p.dma("sp", [(cst[:, :], cst_d[:, :]), (cbf[:, :], cbf_d[:, :])], [], ["cst"], "cst")
        for t in range(4):
            p.dma("sp", [(xT[:, :, t * 512:(t + 1) * 512],
                          xT_d[:, t * 512:(t + 1) * 512].rearrange("(kc p) t -> p kc t", p=128))],
                  [], ["x%d" % t], "xin%d" % t)

        cv.off = base_off
        pos_i = cv.take([16], I32, "pos")
        posf = cv.take([16], F32, "posf")
        angA = cv.take([16, 32], F32, "angA")
        angT = cv.take([16, 32], F32, "angT")
        angI = cv.take([16, 32], I32, "angI")
        angF = cv.take([16, 32], F32, "angF")
        angM = cv.take([16, 32], F32, "angM")
        SC = cv.take([16, 32], F32, "SC")
        sgtmp = cv.take([8, 128], F32, "sgtmp")
        _save = cv.off
        cv.off = 36640
        memT = cv.take([8, 256], F32, "memT")
        memnT = cv.take([8, 256], BF16, "memnT")
        km_sb = cv.take([4, 128], F32, "km_sb")
        kmn = cv.take([4, 128], BF16, "kmn")
        hn0 = mk_hn("s_")
        cv.off = _save

        p.dma("sp", [(pos_i, pos_d[:, :])], [], ["pos"], "misc")
        p.dma("sp", [(sgtmp, sgwT_d[:, :].rearrange("p (g t) -> p g t", t=128))], [], ["sgtmp"], "misc2")
        p.dma("sp", [(memT, memT_d[:, :].rearrange("(kc p) m -> p kc m", p=128))], [], ["memT"], "misc3")

        p.dve(lambda e: e.tensor_copy(out=posf, in_=pos_i), ["pos"], ["posf"])
        for i in range(16):
            p.dve(lambda e, i=i: e.tensor_scalar(out=angA[:, i, 0:16], in0=C("invf"), scalar1=posf[:, i:i + 1],
                                                 scalar2=None, op0=ALU.mult), ["posf", "cst"], ["angA"])
        p.dve(lambda e: e.tensor_scalar(out=angA[:, :, 16:32], in0=angA[:, :, 0:16], scalar1=PI / 2, scalar2=None,
                                        op0=ALU.add), ["angA"], ["angA"])
        p.dve(lambda e: e.tensor_scalar(out=angT, in0=angA, scalar1=1.0 / (2 * PI), scalar2=None, op0=ALU.mult),
              ["angA"], ["angT"])
        p.dve(lambda e: e.tensor_copy(out=angI, in_=angT), ["angT"], ["angI"])
        p.dve(lambda e: e.tensor_copy(out=angF, in_=angI), ["angI"], ["angF"])
        p.dve(lambda e: e.scalar_tensor_tensor(out=angT, in0=angF, scalar=-2 * PI, in1=angA, op0=ALU.mult,
                                               op1=ALU.add), ["angF", "angA"], ["angT"])
        p.dve(lambda e: e.tensor_scalar(out=angM, in0=angT, scalar1=PI, scalar2=None, op0=ALU.is_gt),
              ["angT"], ["angM"])
        p.dve(lambda e: e.scalar_tensor_tensor(out=angF, in0=angM, scalar=-2 * PI, in1=angT, op0=ALU.mult,
                                               op1=ALU.add), ["angM", "angT"], ["angF"])
        p.dve(lambda e: e.tensor_scalar(out=angM, in0=angF, scalar1=-PI, scalar2=None, op0=ALU.is_lt),
              ["angF"], ["angM"])
        p.dve(lambda e: e.scalar_tensor_tensor(out=angT, in0=angM, scalar=2 * PI, in1=angF, op0=ALU.mult,
                                               op1=ALU.add), ["angM", "angF"], ["angT"])
        p.dve(lambda e: e.tensor_scalar(out=angF, in0=angT, scalar1=PI, scalar2=-PI, op0=ALU.min, op1=ALU.max),
              ["angT"], ["angF"])
        p.act(lambda e: e.activation(out=SC, in_=angF, func=AF.Sin), ["angF"], ["SC"])
        p.dve(lambda e: e.tensor_copy(out=csA[:, :, 0:16], in_=SC[:, :, 16:32]), ["SC"], ["csA"])
        p.dve(lambda e: e.tensor_copy(out=csA[:, :, 16:32], in_=SC[:, :, 16:32]), ["SC"], ["csA"])
        p.dve(lambda e: e.tensor_scalar(out=csB[:, :, 0:16], in0=SC[:, :, 0:16], scalar1=-1.0, scalar2=None,
                                        op0=ALU.mult), ["SC"], ["csB"])
        p.dve(lambda e: e.tensor_copy(out=csB[:, :, 16:32], in_=SC[:, :, 0:16]), ["SC"], ["csB"])
        p.dve(lambda e: e.tensor_tensor(out=wTsg[:, :, :], in0=sgtmp,
                                        in1=C("tri").unsqueeze(1).broadcast_to([128, 8, 128]), op=ALU.mult),
              ["sgtmp", "cst"], ["wTsg"])

        def norm_fm(src, srckeys, gname, dst, dstkey, N, bank=None):
            bank, bkey = bank if bank is not None else gen_banks.next()
            for kc in range(8):
                sq, sqk = sq_ring.next()
                s_ap = src(kc)
                p.act(lambda e, s_ap=s_ap, sq=sq: e.activation(out=sq[:, :N], in_=s_ap, func=AF.Square),
                      srckeys, [sqk])
                p.pe(lambda e, kc=kc, sq=sq: e.matmul(bank[:, :N], ones, sq[:, :N], start=(kc == 0), stop=(kc == 7)),
                     [sqk, "cst"], [bkey])
            p.act(lambda e: e.activation(out=std_t[:, :N], in_=bank[:, :N], func=AF.Sqrt, bias=EPS, scale=1.0 / D),
                  [bkey], ["std"])
            p.dve(lambda e: e.reciprocal(out=rstd_t[:, :N], in_=std_t[:, :N]), ["std"], ["rstd"])
            g = C(gname)
            for kc in range(8):
                s_ap, d_ap = src(kc), dst(kc)
                p.dve(lambda e, kc=kc, s_ap=s_ap, d_ap=d_ap: e.scalar_tensor_tensor(
                    out=d_ap, in0=s_ap, scalar=g[:, kc:kc + 1], in1=rstd_t[:, :N], op0=ALU.mult, op1=ALU.mult),
                    srckeys + ["rstd", "cst"], [dstkey])

        def head_norm(hn, src, H, Dh, gname, out_bf, key_in, key_out, rope_i=None):
            while hn.get("busy"):
                yield
            hn["busy"] = True
            pf = hn["pfx"]
            rt, t1, t2 = hn["rt"], hn["t1"], hn["t2"]
            ksq, krt, kt1, kt2 = pf + "hn_sq", pf + "hn_rt", pf + "hn_t1", pf + "hn_t2"
            sqv = hn["sq"].rearrange("p a b -> p (a b)")[:, 0:H * Dh].rearrange("p (a b) -> p a b", b=Dh)
            ss = small[:, 0:H]
            sd = small[:, 8:8 + H]
            rs = small[:, 16:16 + H]
            g = C(gname)
            p.dve(lambda e: e.tensor_tensor(out=sqv, in0=src, in1=src, op=ALU.mult), [key_in], [ksq])
            yield
            p.dve(lambda e: e.tensor_reduce(out=ss, in_=sqv, axis=AX.X, op=ALU.add), [ksq], ["hn_ss"])
            yield
            p.act(lambda e: e.activation(out=sd, in_=ss, func=AF.Sqrt, bias=EPS, scale=1.0 / Dh), ["hn_ss"], ["hn_sd"])
            yield
            p.dve(lambda e: e.reciprocal(out=rs, in_=sd), ["hn_sd"], ["hn_rs"])
            yield
            p.dve(lambda e: e.tensor_tensor(out=sqv, in0=src, in1=rs.unsqueeze(2).broadcast_to([128, H, Dh]),
                                            op=ALU.mult), [key_in, "hn_rs"], [ksq])
            yield
            if rope_i is None:
                p.dve(lambda e: e.tensor_tensor(out=out_bf, in0=sqv, in1=g.unsqueeze(1).broadcast_to([128, H, Dh]),
                                                op=ALU.mult), [ksq, "cst"], [key_out])
                yield
                hn["busy"] = False
                return
            i = rope_i
            p.dve(lambda e: e.tensor_tensor(out=out_bf[:, :, 0:64], in0=sqv[:, :, 0:64],
                                            in1=g[:, 0:64].unsqueeze(1).broadcast_to([128, H, 64]), op=ALU.mult),
                  [ksq, "cst"], [key_out])
            yield
            p.dve(lambda e: e.tensor_tensor(out=rt, in0=sqv[:, :, 64:96],
                                            in1=g[:, 64:96].unsqueeze(1).broadcast_to([128, H, 32]), op=ALU.mult),
                  [ksq, "cst"], [krt])
            yield
            p.dve(lambda e: e.tensor_tensor(out=t1, in0=rt,
                                            in1=csA[:, i, :].unsqueeze(1).broadcast_to([128, H, 32]), op=ALU.mult),
                  [krt, "csA"], [kt1])
            yield
            p.dve(lambda e: e.tensor_tensor(out=t2[:, :, 0:16], in0=rt[:, :, 16:32],
                                            in1=csB[:, i, 0:16].unsqueeze(1).broadcast_to([128, H, 16]), op=ALU.mult),
                  [krt, "csB"], [kt2])
            yield
            p.dve(lambda e: e.tensor_tensor(out=t2[:, :, 16:32], in0=rt[:, :, 0:16],
                                            in1=csB[:, i, 16:32].unsqueeze(1).broadcast_to([128, H, 16]), op=ALU.mult),
                  [krt, "csB"], [kt2])
            yield
            p.dve(lambda e: e.tensor_tensor(out=out_bf[:, :, 64:96], in0=t1, in1=t2, op=ALU.add),
                  [kt1, kt2], [key_out])
            yield
            hn["busy"] = False

        def mm_group(bank_ap, bkey, items, reads):
            def fn(e):
                n = len(items)
                ins = None
                for k, (l, r) in enumerate(items):
                    ins = e.matmul(bank_ap, l, r, start=(k == 0), stop=(k == n - 1))
                return ins
            p.pe(fn, reads, [bkey])

        def transposes(bank, bkey, srcs, reads, rows=128):
            bb = bf_bank(bank)

            def fn(e):
                ins = None
                for j, s in enumerate(srcs):
                    ins = e.transpose(out=bb[0:rows, j, :], in_=s, identity=ident)
                return ins
            p.pe(fn, reads + ["cst"], [bkey])
            return bb

        def ffn(wgu_d, wdn_d, gname):
            cv.off = base_off
            ho = cv.off
            hT = cv.take([8, 1024], BF16)
            for sub in range(2):
                p.region("h%d" % sub, [(ho + kc * 1024 + sub * 512, ho + kc * 1024 + sub * 512 + 512) for kc in range(8)])
            ao = cv.off
            actT = cv.take([FC, 1024], BF16)
            for f in range(FC):
                for sub in range(2):
                    p.region("a%d_%d" % (f, sub), [(ao + f * 1024 + sub * 512, ao + f * 1024 + sub * 512 + 512)])
            sg_ring = Ring("sg", [cv.take([512], F32, "sg%d" % k) for k in range(2)])
            assert cv.off <= 36640, cv.off
            for half in range(2):
                for sub in range(2):
                    t = half * 2 + sub
                    norm_fm(lambda kc, t=t: xT[:, kc, t * 512:(t + 1) * 512], ["x%d" % t], gname,
                            lambda kc, sub=sub: hT[:, kc, sub * 512:(sub + 1) * 512], "h%d" % sub, 512)
                for pp in range(11):
                    slot, wkey = wload(lambda s, pp=pp: [
                        (s.rearrange("p (a kc n) -> p a kc n", a=2, n=256)[:, 0],
                         wgu_d[:, pp * 256:(pp + 1) * 256].rearrange("(kc p) n -> p kc n", p=128)),
                        (s.rearrange("p (a kc n) -> p a kc n", a=2, n=256)[:, 1],
                         wgu_d[:, DFF + pp * 256:DFF + (pp + 1) * 256].rearrange("(kc p) n -> p kc n", p=128))])
                    w4 = slot.rearrange("p (a kc n) -> p a kc n", a=2, n=256)
                    for fi in range(2):
                        f = pp * 2 + fi
                        for sub in range(2):
                            bg, bgk = gen_banks.next()
                            bu, buk = gen_banks.next()
                            hs = lambda kc, sub=sub: hT[:, kc, sub * 512:(sub + 1) * 512]
                            mm_group(bg, bgk, [(w4[:, 0, kc, fi * 128:(fi + 1) * 128], hs(kc)) for kc in range(8)],
                                     [wkey, "h%d" % sub])
                            mm_group(bu, buk, [(w4[:, 1, kc, fi * 128:(fi + 1) * 128], hs(kc)) for kc in range(8)],
                                     [wkey, "h%d" % sub])
                            sg, sgk = sg_ring.next()
                            p.act(lambda e, sg=sg, bg=bg: e.activation(out=sg, in_=bg, func=AF.Silu), [bgk], [sgk])
                            a_ap = actT[:, f, sub * 512:(sub + 1) * 512]
                            p.dve(lambda e, sg=sg, bu=bu, a_ap=a_ap: e.tensor_tensor(out=a_ap, in0=bu, in1=sg, op=ALU.mult),
                                  [sgk, buk], ["a%d_%d" % (f, sub)])
                for m in range(8):
                    slot, wkey = wload(lambda s, m=m: [
                        (s[:, 0:FC * 128].rearrange("p (f n) -> p f n", n=128),
                         wdn_d[:, m * 128:(m + 1) * 128].rearrange("(f p) n -> p f n", p=128))])
                    w3 = slot[:, 0:FC * 128].rearrange("p (f n) -> p f n", n=128)
                    for sub in range(2):
                        t = half * 2 + sub
                        bd, bdk = gen_banks.next()
                        mm_group(bd, bdk, [(w3[:, f, :], actT[:, f, sub * 512:(sub + 1) * 512]) for f in range(FC)],
                                 [wkey] + ["a%d_%d" % (f, sub) for f in range(FC)])
                        xs = xT[:, m, t * 512:(t + 1) * 512]
                        p.dve(lambda e, bd=bd, xs=xs: e.scalar_tensor_tensor(out=xs, in0=bd, scalar=0.5, in1=xs,
                                                                            op0=ALU.mult, op1=ALU.add),
                              [bdk, "x%d" % t], ["x%d" % t])

        def store_out():
            for t in range(4):
                p.dma("sp", [(oT_d[:, t * 512:(t + 1) * 512].rearrange("(kc p) t -> p kc t", p=128),
                              xT[:, :, t * 512:(t + 1) * 512])], ["x%d" % t], ["out%d" % t], "out", final=True)

        def finish(store=True):
            if store:
                store_out()
            with nc.Block() as block:
                p.emit(block)
            return nc

        if stop != "noffn1":
            ffn(w1gu_d, w1dn_d, "ffn1_g")
        norm_fm(lambda kc: memT[:, kc, :], ["memT"], "mem_g", lambda kc: memnT[:, kc, :], "memnT", 256)
        wk, wkk = wload(lambda s: [(s.rearrange("p (kc n) -> p kc n", n=512),
                                    wmkv_d[:, 0:512].rearrange("(kc p) n -> p kc n", p=128))])
        wv_, wvk = wload(lambda s: [(s.rearrange("p (kc n) -> p kc n", n=512),
                                     wmkv_d[:, 512:1024].rearrange("(kc p) n -> p kc n", p=128))])
        wk3 = wk.rearrange("p (kc n) -> p kc n", n=512)
        wv3 = wv_.rearrange("p (kc n) -> p kc n", n=512)
        for mb in range(2):
            bk, bkk = gen_banks.next()
            mm_group(bk, bkk, [(memnT[:, kc, mb * 128:(mb + 1) * 128], wk3[:, kc, :]) for kc in range(8)],
                     ["memnT", wkk])
            bv, bvk = gen_banks.next()
            mm_group(bv, bvk, [(memnT[:, kc, mb * 128:(mb + 1) * 128], wv3[:, kc, :]) for kc in range(8)],
                     ["memnT", wvk])
            p.act(lambda e, mb=mb, bv=bv: e.activation(out=Vmem[:, mb, :], in_=bv, func=AF.Copy), [bvk], ["Vmem"])
            p.act(lambda e, bk=bk: e.activation(out=km_sb.rearrange("p a b -> p (a b)"), in_=bk, func=AF.Copy),
                  [bkk], ["km_sb"])
            for _ in head_norm(hn0, km_sb, 4, 128, "mkg", kmn, "km_sb", "kmn"):
                pass
            bt, btk = gen_banks.next()
            bb = transposes(bt, btk, [kmn[:, h, :] for h in range(4)], ["kmn"])
            p.act(lambda e, mb=mb, bb=bb: e.activation(out=KmemT[:, :, mb * 128:(mb + 1) * 128], in_=bb[:, 0:4, :],
                                                       func=AF.Copy), [btk], ["KmemT"])

        if stop == "ffn1":
            return finish()

        def run_pipelined(gens, depth=2):
            pending = list(gens)
            active = []
            while pending or active:
                if pending and len(active) < depth:
                    active.append(pending.pop(0))
                for g in list(active):
                    try:
                        next(g)
                    except StopIteration:
                        active.remove(g)

        cv.off = base_off
        hT2 = [cv.take([8, 512], BF16, "hT%d" % k) for k in range(2)]
        _ko = cv.off
        Kst = cv.take([8, NT], BF16)
        _vo = cv.off
        Vst = cv.take([8 * 16 * 64], BF16).rearrange("p (h i d) -> p h i d", h=8, i=16)
        for tt in range(4):
            p.region("Kst%d" % tt, [(_ko + h * NT + tt * 512, _ko + h * NT + tt * 512 + 512) for h in range(8)])
            p.region("Vst%d" % tt, [(_vo + h * 1024 + tt * 256, _vo + h * 1024 + tt * 256 + 256) for h in range(8)])
        m1sets = []
        for s in range(3):
            m1sets.append(dict(ckvn=cv.take([256], BF16, "ckvn%d" % s), ckvnT=cv.take([2, 128], BF16, "ckvnT%d" % s),
                               kc_sb=cv.take([8, 96], F32, "kc_sb%d" % s), kfin=cv.take([8, 96], BF16, "kfin%d" % s)))
        hn1 = mk_hn("m_")

        wkvin, wkvin_k = wload(lambda s: [(s[:, 0:8 * 288].rearrange("p (kc n) -> p kc n", n=288),
                                           win_d[:, C_CKV:C_CKV + 288].rearrange("(kc p) n -> p kc n", p=128))])
        wkvin3 = wkvin[:, 0:8 * 288].rearrange("p (kc n) -> p kc n", n=288)
        wukv, wukv_k = wload(lambda s: [(s[:, 0:2048].rearrange("p (kc n) -> p kc n", n=1024),
                                         wukv_d[:, :].rearrange("(kc p) n -> p kc n", p=128))])
        wukv3 = wukv[:, 0:2048].rearrange("p (kc n) -> p kc n", n=1024)
        norm_banks = o_banks

        pre_w = {}

        def m1_block(t, bl):
            i = t * 4 + bl
            s = i % 3
            S = m1sets[s]
            ckvn, ckvnT, kc_sb, kfin = S["ckvn"], S["ckvnT"], S["kc_sb"], S["kfin"]
            kq = lambda n: "%s%d" % (n, s)
            hT = hT2[t % 2]
            hk = "hT%d" % (t % 2)
            ss1 = small[:, 32 + 4 * s:33 + 4 * s]
            sd1 = small[:, 33 + 4 * s:34 + 4 * s]
            rs1 = small[:, 34 + 4 * s:35 + 4 * s]
            bA, bAk = ps[:, 2 * s + 0, :], "ps%d" % (2 * s + 0)
            bB, bBk = ps[:, 2 * s + 1, :], "ps%d" % (2 * s + 1)
            bC, bCk = bB, bBk
            if bl == 0:
                nb, nbk = norm_banks.next()
                norm_fm(lambda kc, t=t: xT[:, kc, t * 512:(t + 1) * 512], ["x%d" % t], "mix_g",
                        lambda kc: hT[:, kc, :], hk, 512, bank=(nb, nbk))
            mm_group(bA[:, 0:288], bAk, [(hT[:, kc, bl * 128:(bl + 1) * 128], wkvin3[:, kc, :]) for kc in range(8)],
                     [hk, wkvin_k])
            yield
            p.act(lambda e: e.activation(out=junk[:, 0:256], in_=bA[:, 0:256], func=AF.Square, accum_out=ss1),
                  [bAk], [kq("ss1"), "junk"])
            yield
            p.act(lambda e: e.activation(out=sd1, in_=ss1, func=AF.Sqrt, bias=EPS, scale=1.0 / 256), [kq("ss1")], [kq("sd1")])
            p.act(lambda e: e.activation(out=kc_sb[:, :, 64:96], in_=bA[:, 256:288].unsqueeze(1).broadcast_to([128, 8, 32]),
                                         func=AF.Copy), [bAk], [kq("kc_sb")])
            yield
            p.dve(lambda e: e.reciprocal(out=rs1, in_=sd1), [kq("sd1")], [kq("rs1")])
            yield
            p.dve(lambda e: e.scalar_tensor_tensor(out=ckvn, in0=bA[:, 0:256], scalar=rs1, in1=C("ckvg"),
                                                   op0=ALU.mult, op1=ALU.mult), [bAk, kq("rs1"), "cst"], [kq("ckvn")])
            yield
            bb = transposes(bB, bBk, [ckvn[:, k * 128:(k + 1) * 128] for k in range(2)], [kq("ckvn")])
            p.act(lambda e: e.activation(out=ckvnT, in_=bb[:, 0:2, :], func=AF.Copy), [bBk], [kq("ckvnT")])
            yield
            mm_group(bC, bCk, [(ckvnT[:, k, :], wukv3[:, k, 0:512]) for k in range(2)], [kq("ckvnT"), wukv_k])
            mm_group(bA, bAk, [(ckvnT[:, k, :], wukv3[:, k, 512:1024]) for k in range(2)], [kq("ckvnT"), wukv_k])
            for hb, (bx, bxk) in enumerate(((bC, bCk), (bA, bAk))):
                b3 = bx.rearrange("p (h d) -> p h d", d=128)
                p.act(lambda e, b3=b3, hb=hb: e.activation(out=Vst[:, hb * 4:(hb + 1) * 4, i, :],
                                                           in_=b3[:, :, 64:128], func=AF.Copy), [bxk], ["Vst%d" % t])
                p.act(lambda e, b3=b3, hb=hb: e.activation(out=kc_sb[:, hb * 4:(hb + 1) * 4, 0:64],
                                                           in_=b3[:, :, 0:64], func=AF.Copy), [bxk], [kq("kc_sb")])
            yield
            yield from head_norm(hn1, kc_sb, 8, 96, "kg", kfin, kq("kc_sb"), kq("kfin"), rope_i=i)
            bb2 = transposes(bB, bBk, [kfin[:, h, :] for h in range(8)], [kq("kfin")], rows=96)
            p.act(lambda e: e.activation(out=Kst[0:96, :, i * 128:(i + 1) * 128], in_=bb2[0:96, :, :],
                                         func=AF.Copy), [bBk], ["Kst%d" % t])
            if bl == 3 and stop not in ("m1nocc", "dump_m1"):
                yield
                if t == 3:
                    pre_w["wv"] = wload(lambda s: [(s.rearrange("p (kc n) -> p kc n", n=512),
                                                    win_d[:, C_V:C_V + 512].rearrange("(kc p) n -> p kc n", p=128))])
                    pre_w["wu"] = wload(lambda s: [(s.rearrange("p (kc n) -> p kc n", n=512),
                                                    win_d[:, C_U:C_U + 512].rearrange("(kc p) n -> p kc n", p=128))])
                p.dma("sp", [(kin_t[t].ap().rearrange("(h f) c -> f h c", f=96), Kst[0:96, :, t * 512:(t + 1) * 512])],
                      ["Kst%d" % t], ["kin%d" % t], "kin%d" % t)
                p.dma("sp", [(vin_t[t].ap().rearrange("(h p) (i d) -> p h i d", p=128, d=64), Vst[:, :, 4 * t:4 * t + 4, :])],
                      ["Vst%d" % t], ["vin%d" % t], "vin%d" % t)
                p.collective(lambda e: e.collective_compute(
                    "AllGather", ALU.bypass, replica_groups=[[0, 1, 2, 3], [4, 5, 6, 7]],
                    ins=[kin_t[t].ap().opt()], outs=[kout_t[t].ap().opt()]),
                    ["kin%d" % t], ["kout%d" % t], semname="cck%d" % t)
                p.collective(lambda e: e.collective_compute(
                    "AllGather", ALU.bypass, replica_groups=[[0, 1, 2, 3], [4, 5, 6, 7]],
                    ins=[vin_t[t].ap().opt()], outs=[vout_t[t].ap().opt()]),
                    ["vin%d" % t], ["vout%d" % t], semname="ccv%d" % t)

        run_pipelined([m1_block(t, bl) for t in range(4) for bl in range(4)], depth=3)
        if stop == "dump_m1":
            for h in range(8):
                p.dma("pool", [(oT_d[h * 96:(h + 1) * 96, :], Kst[0:96, h, :])], ["Kst%d" % q for q in range(4)], ["dbg%d" % h], "dbg", final=True)
            for hq in range(2):
                p.dma("pool", [(oT_d[768 + hq * 128:768 + (hq + 1) * 128, :].rearrange("p (h i d) -> p h i d", h=2, i=16),
                                Vst[:, 2 * hq:2 * hq + 2, :, :])], ["Vst%d" % q for q in range(4)], ["dbgv%d" % hq], "dbg", final=True)
            return finish(store=False)
        if stop == "m1":
            return finish()
        if stop == "m1nocc":
            return finish()
        cv.off = base_off
        hT = cv.take([8, 512], BF16, "hT")
        QT = cv.take([8, 512], BF16, "QT")
        qmT = cv.take([4, 512], BF16, "qmT")
        yT = [cv.take([4, 512], BF16, k) for k in ("yaT", "ybT", "ycT")]
        Kring_v = [cv.take([2048], BF16, "K%d" % s) for s in range(4)]
        Vring_v = [cv.take([16, 128], BF16, "V%d" % s) for s in range(4)]
        ph_off = cv.off
        uT_sb = cv.take([4, 512], BF16, "uT")
        Asets = [dict(v_sb=cv.take([512], F32, "v_sb%d" % s), v_ln=cv.take([512], BF16, "v_ln%d" % s),
                      mtmp=cv.take([4, 128], F32, "mtmp%d" % s)) for s in range(3)]
        cv.off = ph_off
        Qsets = [dict(cqn=cv.take([384], BF16, "cqn%d" % s), cqnT=cv.take([3, 128], BF16, "cqnT%d" % s),
                      q_sb=cv.take([8, 96], F32, "q_sb%d" % s), qfin=cv.take([8, 96], BF16, "qfin%d" % s))
                 for s in range(2)]
        hn2 = mk_hn("t_")
        q_end = cv.off
        cv.off = ph_off
        Csets = [dict(qm_sb=cv.take([4, 128], F32, "qm_sb%d" % s), qmn=cv.take([4, 128], BF16, "qmn%d" % s))
                 for s in range(3)]
        assert cv.off <= ph_off + 2 * (384 + 384 + 1536 + 768), "C sets overlap hn temps"
        assert cv.off <= q_end - 3 * 1024 - 1536 * 2 or True
        cv.off = ph_off
        P_ring = Ring("P", [cv.take([512], BF16, "P%d" % k) for k in range(8)])
        rd_views = [cv.take([512], F32, "rd%d" % k) for k in range(2)]
        rden_ring = Ring("rd", rd_views)
        acc1, acc2 = rd_views
        mergedT = cv.take([8, 512], BF16, "mergedT")
        go = cv.off
        g_sb = cv.take([3, 512], BF16)
        for br in range(3):
            p.region("g_sb%d" % br, [(go + br * 512, go + br * 512 + 512)])

        for s in range(4):
            lo = 64 if s < 2 else 0
            p.pool(lambda e, s=s, lo=lo: e.memset(Vring_v[s][:, :, lo:lo + 64], 1.0), [], ["V%d" % s])

        SC_B = 96 ** -0.5
        SC_C = 128 ** -0.5
        st6 = small[:, 40:46]
        mv = small[:, 46:48]
        sdv = small[:, 48:49]
        rsv = small[:, 49:50]

        for t in range(4):
            xk = "x%d" % t
            norm_fm(lambda kc, t=t: xT[:, kc, t * 512:(t + 1) * 512], [xk], "mix_g",
                    lambda kc: hT[:, kc, :], "hT", 512)
            wv_s, wv_k = pre_w.pop("wv") if "wv" in pre_w else wload(lambda s: [
                (s.rearrange("p (kc n) -> p kc n", n=512),
                 win_d[:, C_V:C_V + 512].rearrange("(kc p) n -> p kc n", p=128))])
            wu_s, wu_k = pre_w.pop("wu") if "wu" in pre_w else wload(lambda s: [
                (s.rearrange("p (kc n) -> p kc n", n=512),
                 win_d[:, C_U:C_U + 512].rearrange("(kc p) n -> p kc n", p=128))])
            wv3 = wv_s.rearrange("p (kc n) -> p kc n", n=512)
            wu3 = wu_s.rearrange("p (kc n) -> p kc n", n=512)
            for c in range(4):
                bu, buk = o_banks.next()
                mm_group(bu, buk, [(wu3[:, kc, c * 128:(c + 1) * 128], hT[:, kc, :]) for kc in range(8)], [wu_k, "hT"])
                p.act(lambda e, c=c, bu=bu: e.activation(out=uT_sb[:, c, :], in_=bu, func=AF.Gelu), [buk], ["uT"])

            def a_block(bl):
                s = bl % 3
                S = Asets[s]
                v_sb, v_ln, mtmp = S["v_sb"], S["v_ln"], S["mtmp"]
                kq = lambda n: "%s%d" % (n, s)
                st6 = small[:, 64 + 16 * s:70 + 16 * s]
                mv = small[:, 70 + 16 * s:72 + 16 * s]
                sdv = small[:, 72 + 16 * s:73 + 16 * s]
                rsv = small[:, 73 + 16 * s:74 + 16 * s]
                bV, bVk = ps[:, 2 * s + 0, :], "ps%d" % (2 * s + 0)
                bM, bMk = ps[:, 2 * s + 1, :], "ps%d" % (2 * s + 1)
                mm_group(bV, bVk, [(hT[:, kc, bl * 128:(bl + 1) * 128], wv3[:, kc, :]) for kc in range(8)], [wv_k, "hT"])
                yield
                p.act(lambda e: e.activation(out=v_sb, in_=bV, func=AF.Gelu), [bVk], [kq("v_sb")])
                yield
                p.dve(lambda e: e.bn_stats(out=st6, in_=v_sb), [kq("v_sb")], [kq("st6")])
                yield
                p.dve(lambda e: e.bn_aggr(out=mv, in_=st6), [kq("st6")], [kq("mv")])
                yield
                p.act(lambda e: e.activation(out=sdv, in_=mv[:, 1:2], func=AF.Sqrt, bias=EPS, scale=1.0), [kq("mv")], [kq("sdv")])
                p.dve(lambda e: e.scalar_tensor_tensor(out=v_sb, in0=v_sb, scalar=mv[:, 0:1], in1=C("lng"),
                                                       op0=ALU.subtract, op1=ALU.mult), [kq("v_sb"), kq("mv"), "cst"], [kq("v_sb")])
                yield
                p.dve(lambda e: e.reciprocal(out=rsv, in_=sdv), [kq("sdv")], [kq("rsv")])
                yield
                p.dve(lambda e: e.scalar_tensor_tensor(out=v_ln, in0=v_sb, scalar=rsv, in1=C("lnb"),
                                                       op0=ALU.mult, op1=ALU.add), [kq("v_sb"), kq("rsv"), "cst"], [kq("v_ln")])
                yield

                def mixfn(e):
                    ins = None
                    for g in range(8):
                        ins = e.matmul(bM[(g % 2) * 64:(g % 2) * 64 + 64, (g // 2) * 128:(g // 2 + 1) * 128],
                                       v_ln[:, g * 64:(g + 1) * 64], wTsg[:, g, :], start=True, stop=True)
                    return ins
                p.pe(mixfn, [kq("v_ln"), "wTsg"], [bMk])
                yield
                bm3 = bM.rearrange("p (c t) -> p c t", t=128)
                p.dve(lambda e: e.tensor_tensor(out=mtmp, in0=bm3, in1=C("bsT").rearrange("p (c t) -> p c t", t=128),
                                                op=ALU.add), [bMk, "cst"], [kq("mtmp")])
                yield
                p.dve(lambda e: e.tensor_tensor(out=yT[0][:, :, bl * 128:(bl + 1) * 128], in0=mtmp,
                                                in1=uT_sb[:, :, bl * 128:(bl + 1) * 128], op=ALU.mult),
                      [kq("mtmp"), "uT"], ["yaT"])
            run_pipelined([a_block(bl) for bl in range(4)], depth=3)
            wcq_s, wcq_k = wload(lambda s: [(s[:, 0:8 * 384].rearrange("p (kc n) -> p kc n", n=384),
                                             win_d[:, C_CQ:C_CQ + 384].rearrange("(kc p) n -> p kc n", p=128))])
            wuq_s, wuq_k = wload(lambda s: [(s[:, 0:3 * 768].rearrange("p (kc n) -> p kc n", n=768),
                                             wuq_d[:, :].rearrange("(kc p) n -> p kc n", p=128))])
            wcq3 = wcq_s[:, 0:8 * 384].rearrange("p (kc n) -> p kc n", n=384)
            wuq3 = wuq_s[:, 0:3 * 768].rearrange("p (kc n) -> p kc n", n=768)

            def q_block(bl):
                i = t * 4 + bl
                s = bl % 2
                S = Qsets[s]
                cqn, cqnT, q_sb, qfin = S["cqn"], S["cqnT"], S["q_sb"], S["qfin"]
                kq = lambda n: "%s%d" % (n, s)
                ss1 = small[:, 32 + 4 * s:33 + 4 * s]
                sd1 = small[:, 33 + 4 * s:34 + 4 * s]
                rs1 = small[:, 34 + 4 * s:35 + 4 * s]
                bA, bAk = ps[:, 3 * s + 0, :], "ps%d" % (3 * s + 0)
                bB, bBk = ps[:, 3 * s + 1, :], "ps%d" % (3 * s + 1)
                bC, bCk = ps[:, 3 * s + 2, :], "ps%d" % (3 * s + 2)
                mm_group(bA[:, 0:384], bAk, [(hT[:, kc, bl * 128:(bl + 1) * 128], wcq3[:, kc, :]) for kc in range(8)],
                         [wcq_k, "hT"])
                yield
                p.act(lambda e: e.activation(out=junk[:, 0:384], in_=bA[:, 0:384], func=AF.Square, accum_out=ss1),
                      [bAk], [kq("ss1"), "junk"])
                yield
                p.act(lambda e: e.activation(out=sd1, in_=ss1, func=AF.Sqrt, bias=EPS, scale=1.0 / 384), [kq("ss1")], [kq("sd1")])
                yield
                p.dve(lambda e: e.reciprocal(out=rs1, in_=sd1), [kq("sd1")], [kq("rs1")])
                yield
                p.dve(lambda e: e.scalar_tensor_tensor(out=cqn, in0=bA[:, 0:384], scalar=rs1, in1=C("cqg"),
                                                       op0=ALU.mult, op1=ALU.mult), [bAk, kq("rs1"), "cst"], [kq("cqn")])
                yield
                bb = transposes(bB, bBk, [cqn[:, k * 128:(k + 1) * 128] for k in range(3)], [kq("cqn")])
                p.act(lambda e: e.activation(out=cqnT, in_=bb[:, 0:3, :], func=AF.Copy), [bBk], [kq("cqnT")])
                yield
                for hb, (bx, bxk) in enumerate(((bC, bCk), (bA, bAk))):
                    mm_group(bx[:, 0:384], bxk, [(cqnT[:, k, :], wuq3[:, k, hb * 384:(hb + 1) * 384]) for k in range(3)],
                             [kq("cqnT"), wuq_k])
                    p.act(lambda e, bx=bx, hb=hb: e.activation(
                        out=q_sb[:, hb * 4:(hb + 1) * 4, :], in_=bx[:, 0:384].rearrange("p (h d) -> p h d", d=96),
                        func=AF.Copy), [bxk], [kq("q_sb")])
                yield
                yield from head_norm(hn2, q_sb, 8, 96, "qg", qfin, kq("q_sb"), kq("qfin"), rope_i=i)
                bb2 = transposes(bB, bBk, [qfin[:, h, :] for h in range(8)], [kq("qfin")], rows=96)
                p.act(lambda e: e.activation(out=QT[0:96, :, bl * 128:(bl + 1) * 128], in_=bb2[0:96, :, :],
                                             func=AF.Copy), [bBk], ["QT"])
            run_pipelined([q_block(bl) for bl in range(4)], depth=2)
            wqm_s, wqm_k = wload(lambda s: [(s.rearrange("p (kc n) -> p kc n", n=512),
                                             win_d[:, C_QM:C_QM + 512].rearrange("(kc p) n -> p kc n", p=128))])
            wqm3 = wqm_s.rearrange("p (kc n) -> p kc n", n=512)

            def c_block(bl):
                s = bl % 3
                S = Csets[s]
                qm_sb, qmn = S["qm_sb"], S["qmn"]
                kq = lambda n: "%s%d" % (n, s)
                bA, bAk = ps[:, 2 * s + 0, :], "ps%d" % (2 * s + 0)
                bB, bBk = ps[:, 2 * s + 1, :], "ps%d" % (2 * s + 1)
                mm_group(bA, bAk, [(hT[:, kc, bl * 128:(bl + 1) * 128], wqm3[:, kc, :]) for kc in range(8)],
                         [wqm_k, "hT"])
                yield
                p.act(lambda e: e.activation(out=qm_sb.rearrange("p a b -> p (a b)"), in_=bA, func=AF.Copy),
                      [bAk], [kq("qm_sb")])
                yield
                yield from head_norm(hn2, qm_sb, 4, 128, "mqg", qmn, kq("qm_sb"), kq("qmn"))
                bb = transposes(bB, bBk, [qmn[:, h, :] for h in range(4)], [kq("qmn")])
                p.act(lambda e: e.activation(out=qmT[:, :, bl * 128:(bl + 1) * 128], in_=bb[:, 0:4, :],
                                             func=AF.Copy), [bBk], ["qmT"])
            run_pipelined([c_block(bl) for bl in range(4)], depth=3)
            for h in range(4):
                Ps = []
                for mc in range(2):
                    bs, bsk = gen_banks.next()
                    mm_group(bs, bsk, [(KmemT[:, h, mc * 128:(mc + 1) * 128], qmT[:, h, :])], ["KmemT", "qmT"])
                    Pt, Pk = P_ring.next()
                    p.act(lambda e, bs=bs, Pt=Pt: e.activation(out=Pt, in_=bs, func=AF.Exp, scale=SC_C), [bsk], [Pk])
                    Ps.append((Pt, Pk))
                bo, bok = gen_banks.next()
                mm_group(bo, bok, [(Vmem[:, mc, h * 128:(h + 1) * 128], Ps[mc][0]) for mc in range(2)],
                         ["Vmem"] + [k for _, k in Ps])
                bd, bdk = gen_banks.next()
                mm_group(bd, bdk, [(ones, Ps[mc][0]) for mc in range(2)], ["cst"] + [k for _, k in Ps])
                rd, rdk = rden_ring.next()
                p.dve(lambda e, rd=rd, bd=bd: e.reciprocal(out=rd, in_=bd), [bdk], [rdk])
                p.dve(lambda e, rd=rd, bo=bo, h=h: e.tensor_tensor(out=yT[2][:, h, :], in0=bo, in1=rd, op=ALU.mult),
                      [bok, rdk], ["ycT"])
            nki = 4 * t + 4
            G = {0: 4, 1: 2}.get(t, 1)
            pend = []
            chunk_ctr = [0, 0]

            def rec_pv(item):
                (Pt, Pk, c0, Vc, vk, vi, ob, obk, first, last, fin) = item
                p.pe(lambda e: e.matmul(ob[:, c0:512], Vc[:, vi, :], Pt[:, c0:512], start=first, stop=last),
                     [Pk, vk], [obk])
                if last:
                    fin()
            for h in range(8):
                par = h % 2
                ob, obk = o_banks.next()
                olo = 0 if par == 0 else 64
                dlo = 64 - olo
                vlo = olo

                def fin(ob=ob, obk=obk, olo=olo, dlo=dlo, h=h):
                    rd, rdk = rden_ring.next()
                    p.dve(lambda e: e.reciprocal(out=rd[dlo:dlo + 64, :], in_=ob[dlo:dlo + 64, :]), [obk], [rdk])
                    p.dve(lambda e: e.tensor_tensor(out=yT[1][olo:olo + 64, h // 2, :], in0=ob[olo:olo + 64, :],
                                                    in1=rd[dlo:dlo + 64, :], op=ALU.mult), [obk, rdk], ["ybT"])
                npv = 0
                ntot = 4 * nki
                for c in range(4 // G):
                    slot = par * 2 + (chunk_ctr[par] % 2)
                    chunk_ctr[par] += 1
                    Kc = Kring_v[slot]
                    Vc = Vring_v[slot]
                    kk, vk = "K%d" % slot, "V%d" % slot
                    ranks = [c * G + gi for gi in range(G)]
                    p.dma("pool", [(Kc[0:96, gi * nki * 128 + tt * 512:gi * nki * 128 + (tt + 1) * 512],
                                    kout_t[tt].ap()[r * 768 + h * 96:r * 768 + (h + 1) * 96, :])
                                   for gi, r in enumerate(ranks) for tt in range(t + 1)],
                          ["kout%d" % tt for tt in range(t + 1)], [kk], kk)
                    p.dma("pool", [(Vc[:, gi * nki + 4 * tt:gi * nki + 4 * tt + 4, vlo:vlo + 64],
                                    vout_t[tt].ap()[r * 1024 + h * 128:r * 1024 + (h + 1) * 128, :].rearrange(
                                        "p (i d) -> p i d", d=64))
                                   for gi, r in enumerate(ranks) for tt in range(t + 1)],
                          ["vout%d" % tt for tt in range(t + 1)], [vk], vk)
                    for gi, r in enumerate(ranks):
                        for ki in range(nki):
                            d = ki - 4 * t
                            c0 = 128 * d if d > 0 else 0
                            kcol = (gi * nki + ki) * 128
                            bs, bsk = gen_banks.next()
                            p.pe(lambda e, bs=bs, Kc=Kc, kcol=kcol, c0=c0, h=h: e.matmul(
                                bs[:, c0:512], Kc[0:96, kcol:kcol + 128], QT[0:96, h, c0:512], start=True, stop=True),
                                [kk, "QT"], [bsk])
                            Pt, Pk = P_ring.next()
                            p.act(lambda e, bs=bs, Pt=Pt, c0=c0: e.activation(out=Pt[:, c0:512], in_=bs[:, c0:512],
                                                                             func=AF.Exp, scale=SC_B), [bsk], [Pk])
                            if d >= 0:
                                mk = masks[:, (ki % 2) * 4 + r, :]
                                p.dve(lambda e, Pt=Pt, c0=c0, mk=mk: e.tensor_tensor(
                                    out=Pt[:, c0:c0 + 128], in0=Pt[:, c0:c0 + 128], in1=mk, op=ALU.mult), [Pk, "cst"], [Pk])
                            pend.append((Pt, Pk, c0, Vc, vk, gi * nki + ki, ob, obk, npv == 0, npv == ntot - 1, fin))
                            npv += 1
                            if len(pend) > 4:
                                rec_pv(pend.pop(0))
            while pend:
                rec_pv(pend.pop(0))
            if stop == "dump_y%d" % t:
                for br in range(3):
                    p.dma("pool", [(oT_d[0:512, br * 512:(br + 1) * 512].rearrange("(kc p) t -> p kc t", p=128), yT[br])],
                          [("yaT", "ybT", "ycT")[br]], ["dbgy%d" % br], "dbg", final=True)
                return finish(store=False)
            for m in range(8):
                gs, gk = wload(lambda s, m=m: [
                    (s[:, 0:3072].rearrange("p (b kc n) -> p b kc n", b=3, n=128)[:, br],
                     win_d[:, C_GATE + br * 1024 + m * 128:C_GATE + br * 1024 + (m + 1) * 128].rearrange(
                         "(kc p) n -> p kc n", p=128)) for br in range(3)])
                bs_, bk_ = wload(lambda s, m=m: [
                    (s[:, 0:1536].rearrange("p (b kc n) -> p b kc n", b=3, n=128)[:, br],
                     wbr_d[br][:, m * 128:(m + 1) * 128].rearrange("(kc p) n -> p kc n", p=128)) for br in range(3)])
                g4 = gs[:, 0:3072].rearrange("p (b kc n) -> p b kc n", b=3, n=128)
                b4 = bs_[:, 0:1536].rearrange("p (b kc n) -> p b kc n", b=3, n=128)
                for br in range(3):
                    bg, bgk = gen_banks.next()
                    mm_group(bg, bgk, [(g4[:, br, kc, :], hT[:, kc, :]) for kc in range(8)], [gk, "hT"])
                    p.act(lambda e, bg=bg, br=br, m=m: e.activation(out=g_sb[:, br, :], in_=bg, func=AF.Sigmoid,
                                                                   bias=C("bgate")[:, br * 8 + m:br * 8 + m + 1], scale=1.0),
                          [bgk, "cst"], ["g_sb%d" % br])
                ykeys = ["yaT", "ybT", "ycT"]
                bbs = []
                for br in range(3):
                    bb_, bbk = gen_banks.next()
                    mm_group(bb_, bbk, [(b4[:, br, kc, :], yT[br][:, kc, :]) for kc in range(4)], [bk_, ykeys[br]])
                    bbs.append((bb_, bbk))
                p.dve(lambda e, b=bbs[0][0]: e.tensor_tensor(out=acc1, in0=b, in1=g_sb[:, 0, :], op=ALU.mult),
                      [bbs[0][1], "g_sb0"], ["rd0"])
                p.dve(lambda e, b=bbs[1][0]: e.tensor_tensor(out=acc2, in0=b, in1=g_sb[:, 1, :], op=ALU.mult),
                      [bbs[1][1], "g_sb1"], ["rd1"])
                p.dve(lambda e: e.tensor_tensor(out=acc1, in0=acc1, in1=acc2, op=ALU.add), ["rd0", "rd1"], ["rd0"])
                p.dve(lambda e, b=bbs[2][0]: e.tensor_tensor(out=acc2, in0=b, in1=g_sb[:, 2, :], op=ALU.mult),
                      [bbs[2][1], "g_sb2"], ["rd1"])
                p.dve(lambda e, m=m: e.tensor_tensor(out=mergedT[:, m, :], in0=acc1, in1=acc2, op=ALU.add),
                      ["rd0", "rd1"], ["mergedT"])
            for hf in range(2):
                ws, wk_ = wload(lambda s, hf=hf: [(s.rearrange("p (kc n) -> p kc n", n=512),
                                                   wout_d[:, hf * 512:(hf + 1) * 512].rearrange("(kc p) n -> p kc n", p=128))])
                w3 = ws.rearrange("p (kc n) -> p kc n", n=512)
                for mm in range(4):
                    m = hf * 4 + mm
                    bo, bok = gen_banks.next()
                    mm_group(bo, bok, [(w3[:, kc, mm * 128:(mm + 1) * 128], mergedT[:, kc, :]) for kc in range(8)],
                             [wk_, "mergedT"])
                    xs = xT[:, m, t * 512:(t + 1) * 512]
                    p.dve(lambda e, bo=bo, xs=xs: e.tensor_tensor(out=xs, in0=bo, in1=xs, op=ALU.add), [bok, xk], [xk])

        if stop != "mid":
            ffn(w2gu_d, w2dn_d, "ffn2_g")
        return finish()


def _perm_rows(j):
    rows = []
    for i in range(16):
        g = i // 2
        blk = 8 * g + (j if i % 2 == 0 else 7 - j)
        rows.append(np.arange(blk * 128, (blk + 1) * 128))
    return np.concatenate(rows)


def _host_inputs(inp):
    f = lambda a: np.ascontiguousarray(np.asarray(a, dtype=np.float32))
    x = f(inp["x"])
    mem = f(inp["mem"])
    pos = np.asarray(inp["positions"]).astype(np.int32)
    L0 = lambda k: f(inp[k])[0]

    def col(v, n):
        return np.ascontiguousarray(v.reshape(n, 128).T)

    def rep(v):
        return np.ascontiguousarray(np.broadcast_to(v[None, :], (128, v.shape[0])))

    cst = np.zeros((128, NCST), np.float32)

    def put(name, arr):
        o, w = CST[name]
        assert arr.shape == (128, w), (name, arr.shape)
        cst[:, o:o + w] = arr
    put("ffn1_g", col(L0("ffn1_norm"), 8))
    put("mix_g", col(L0("mix_norm"), 8))
    put("ffn2_g", col(L0("ffn2_norm"), 8))
    put("mem_g", col(L0("mem_norm"), 8))
    put("bgate", col(L0("b_gate"), 24))
    put("cqg", rep(L0("mla_cq_norm")))
    put("ckvg", rep(L0("mla_ckv_norm")))
    put("lng", rep(L0("sg_ln_g")))
    put("lnb", rep(L0("sg_ln_b")))
    put("qg", rep(L0("mla_q_norm")))
    put("kg", rep(L0("mla_k_norm")))
    put("mqg", rep(L0("mem_q_norm")))
    put("mkg", rep(L0("mem_k_norm")))
    sgb = L0("sg_b")
    bsT = np.zeros((128, 4, 128), np.float32)
    for c in range(4):
        bsT[0:64, c, :] = sgb[2 * c][None, :]
        bsT[64:128, c, :] = sgb[2 * c + 1][None, :]
    put("bsT", bsT.reshape(128, 512))
    half = 16
    invf = (10000.0 ** (-np.arange(half, dtype=np.float32) / half)).astype(np.float32)
    put("invf", rep(invf))
    tri = (np.arange(128)[:, None] <= np.arange(128)[None, :]).astype(np.float32)
    put("tri", tri)

    sgw = L0("sg_w")
    sgwT = np.ascontiguousarray(sgw.transpose(2, 0, 1)).reshape(128, 8 * 128)

    shared = {
        "cst": None, "sgwT": sgwT,
        "ffn1_w_gu": L0("ffn1_w_gu"), "ffn1_w_down": L0("ffn1_w_down"),
        "ffn2_w_gu": L0("ffn2_w_gu"), "ffn2_w_down": L0("ffn2_w_down"),
        "w_in": L0("w_in"), "mla_w_uq": L0("mla_w_uq"), "mla_w_ukv": L0("mla_w_ukv"),
        "mem_w_kv": L0("mem_w_kv"), "w_branch_a": L0("w_branch_a"), "w_branch_b": L0("w_branch_b"),
        "w_branch_c": L0("w_branch_c"), "w_out": L0("w_out"),
    }
    in_maps = []
    perms = []
    for c in range(NCORES):
        b, j = c // 4, c % 4
        rows = _perm_rows(j)
        perms.append((b, rows))
        m = dict(shared)
        m["cst"] = cst
        m["xT"] = np.ascontiguousarray(x[b][rows].T)
        m["memT"] = np.ascontiguousarray(mem[b].T)
        m["pos"] = np.ascontiguousarray(pos[b][rows].reshape(16, 128).T)
        cbf = np.zeros((128, NCBF), np.float32)
        cbf[:, 0:128] = np.eye(128, dtype=np.float32)
        cbf[:, 128:256] = 1.0
        mk = np.zeros((128, 8, 128), np.float32)
        for r in range(4):
            mk[:, r, :] = 1.0 if r < j else (tri if r == j else 0.0)
            mk[:, 4 + r, :] = 1.0 if r > j else (tri if r == j else 0.0)
        cbf[:, 256:] = mk.reshape(128, 1024)
        m["cbf"] = cbf.astype(ml_dtypes.bfloat16)
        in_maps.append(m)
    return in_maps, perms


_NC_CACHE = {}


def kernel(**inputs):
    in_maps, perms = _host_inputs(inputs)
    if "nc" not in _NC_CACHE:
        _NC_CACHE["nc"] = build()
    nc = _NC_CACHE["nc"]
    res = run_bass_kernel_spmd(nc, in_maps, core_ids=list(range(NCORES)))
    out = np.zeros((2, 8192, D), np.float32)
    for c in range(NCORES):
        b, rows = perms[c]
        out[b, rows, :] = np.asarray(res.results[c]["oT"], dtype=np.float32).T
    return out
```

```python
import numpy as np
import ml_dtypes
from contextlib import ExitStack
import concourse.bass as bass
import concourse.mybir as mybir
from concourse.bass_utils import run_bass_kernel_spmd

F32 = mybir.dt.float32
BF16 = mybir.dt.bfloat16
I32 = mybir.dt.int32
AF = mybir.ActivationFunctionType
ALU = mybir.AluOpType
AX = mybir.AxisListType

NCORES = 8
D = 1024
NT = 2048
DFF = 2816
FC = 22
EPS = 1e-6
C_U, C_V, C_CQ, C_CKV, C_KR, C_QM, C_GATE = 0, 512, 1024, 1408, 1664, 1696, 2208
PI = float(np.pi)

CST = {}
_off = 0
for _n, _w in [("ffn1_g", 8), ("mix_g", 8), ("ffn2_g", 8), ("mem_g", 8), ("bgate", 24), ("cqg", 384),
               ("ckvg", 256), ("lng", 512), ("lnb", 512), ("qg", 96), ("kg", 96), ("mqg", 128),
               ("mkg", 128), ("bsT", 512), ("invf", 16), ("tri", 128)]:
    CST[_n] = (_off, _w)
    _off += _w
NCST = _off
NCBF = 128 + 128 + 8 * 128


class Prog:
    ENG = ("pe", "act", "dve", "pool", "sp")

    def __init__(self, nc, es):
        self.nc, self.es = nc, es
        self.q = {e: [] for e in self.ENG}
        self.sems = {}
        self.cnt = {}
        self.waited = {e: {} for e in self.ENG}
        self.lastw = {}
        self.readers = {}
        self.out_tokens = []
        self.regions = {}
        self._ovc = {}

    def sem(self, name):
        if name not in self.sems:
            self.sems[name] = self.es.enter_context(self.nc.semaphore("s_" + name))
            self.cnt[name] = 0
        return self.sems[name]

    def region(self, key, ivs):
        self.regions[key] = list(ivs)

    def _overlap(self, k):
        if k not in self.regions:
            return (k,)
        c = self._ovc.get(k)
        if c is not None and c[0] == len(self.regions):
            return c[1]
        mine = self.regions[k]
        res = [k]
        for k2, ivs in self.regions.items():
            if k2 == k:
                continue
            hit = False
            for (a, b) in mine:
                for (c0, d0) in ivs:
                    if a < d0 and c0 < b:
                        hit = True
                        break
                if hit:
                    break
            if hit:
                res.append(k2)
        self._ovc[k] = (len(self.regions), tuple(res))
        return self._ovc[k][1]

    def _collect(self, eng, reads, writes, is_dma):
        toks = []
        for k in reads:
            for k2 in self._overlap(k):
                if k2 in self.lastw:
                    toks.append((self.lastw[k2], True))
        for k in writes:
            for k2 in self._overlap(k):
                if k2 in self.lastw:
                    toks.append((self.lastw[k2], False))
                rd = self.readers.get(k2)
                if rd:
                    toks.extend(((sn, v, te), False) for (sn, te), v in rd.items())
        waits = {}
        for ((sn, val, teng), raw) in toks:
            if teng is not None and teng == eng and not is_dma:
                if not raw or eng == "pe":
                    continue
            if self.waited[eng].get(sn, 0) >= val:
                continue
            waits[sn] = max(waits.get(sn, 0), val)
        for sn, val in waits.items():
            self.waited[eng][sn] = val
        return [(self.sem(sn), val, sn) for sn, val in waits.items()]

    def _commit(self, tok, reads, writes):
        sn, val, te = tok
        for k in reads:
            d = self.readers.setdefault(k, {})
            d[(sn, te)] = max(d.get((sn, te), 0), val)
        for k in writes:
            self.lastw[k] = tok
            self.readers[k] = {}

    def op(self, eng, fn, reads=(), writes=()):
        waits = self._collect(eng, reads, writes, False)
        sn = "E" + eng
        self.sem(sn)
        self.cnt[sn] += 1
        tok = (sn, self.cnt[sn], eng)
        self.q[eng].append((waits, fn, [(self.sems[sn], 1)], [(sn, 1)]))
        self._commit(tok, reads, writes)

    def pe(self, fn, r=(), w=()):
        self.op("pe", fn, r, w)

    def act(self, fn, r=(), w=()):
        self.op("act", fn, r, w)

    def dve(self, fn, r=(), w=()):
        self.op("dve", fn, r, w)

    def pool(self, fn, r=(), w=()):
        self.op("pool", fn, r, w)

    def dma(self, queue, pairs, reads, writes, semname, final=False):
        waits = self._collect(queue, reads, writes, True)
        s = self.sem(semname)
        self.cnt[semname] += 16 * len(pairs)
        tok = (semname, self.cnt[semname], None)

        def fn(e, pairs=pairs, s=s):
            for (o, i) in pairs:
                e.dma_start(out=o, in_=i).then_inc(s, 16)
            return None
        self.q[queue].append((waits, fn, [], [(semname, 16 * len(pairs))]))
        self._commit(tok, reads, writes)
        if final:
            self.out_tokens.append(tok)

    def collective(self, fn, reads, writes, semname="cc"):
        waits = self._collect("pool", reads, writes, True)
        s = self.sem(semname)
        self.cnt[semname] += 1
        tok = (semname, self.cnt[semname], None)
        self.q["pool"].append((waits, fn, [(s, 1)], [(semname, 1)]))
        self._commit(tok, reads, writes)

    def check(self):
        val = {}
        pos = {e: 0 for e in self.ENG}
        prog = True
        while prog:
            prog = False
            for e in self.ENG:
                while pos[e] < len(self.q[e]):
                    waits, fn, incs, names = self.q[e][pos[e]]
                    if all(val.get(sn, 0) >= v for (_s, v, sn) in waits):
                        for (sn, n) in names:
                            val[sn] = val.get(sn, 0) + n
                        pos[e] += 1
                        prog = True
                    else:
                        break
        stuck = {e: pos[e] for e in self.ENG if pos[e] < len(self.q[e])}
        for e, i in stuck.items():
            waits = self.q[e][i][0]
            print("DEADLOCK", e, "op", i, "of", len(self.q[e]), "waits",
                  [(sn, v, val.get(sn, 0)) for (_s, v, sn) in waits if val.get(sn, 0) < v])
        return not stuck

    def emit(self, block):
        assert self.check(), "semaphore protocol deadlock"
        def run(e, eng):
            for (waits, fn, incs, _n) in self.q[eng]:
                for (s, v, _sn) in waits:
                    e.wait_ge(s, v)
                ins = fn(e)
                for (s, n) in incs:
                    ins.then_inc(s, n)
            if eng == "sp":
                for (sn, val, _) in self.out_tokens:
                    e.wait_ge(self.sems[sn], val)

        @block.tensor
        def _(e):
            run(e, "pe")

        @block.scalar
        def _(e):
            run(e, "act")

        @block.vector
        def _(e):
            run(e, "dve")

        @block.gpsimd
        def _(e):
            run(e, "pool")

        @block.sync
        def _(e):
            run(e, "sp")


class Ring:
    def __init__(self, name, views):
        self.name, self.views, self.i = name, views, 0

    def next(self):
        k = self.i % len(self.views)
        self.i += 1
        return self.views[k], "%s%d" % (self.name, k)


def build(stop=None):
    nc = bass.Bass("TRN2", target_bir_lowering=False)

    def din(name, shape, dt=F32):
        return nc.dram_tensor(name, shape, dt, kind="ExternalInput").ap()

    xT_d = din("xT", [D, NT])
    memT_d = din("memT", [D, 256])
    pos_d = din("pos", [128, 16], I32)
    cst_d = din("cst", [128, NCST])
    cbf_d = din("cbf", [128, NCBF], BF16)
    sgwT_d = din("sgwT", [128, 8 * 128])
    w1gu_d = din("ffn1_w_gu", [D, 2 * DFF])
    w1dn_d = din("ffn1_w_down", [DFF, D])
    w2gu_d = din("ffn2_w_gu", [D, 2 * DFF])
    w2dn_d = din("ffn2_w_down", [DFF, D])
    win_d = din("w_in", [D, 5280])
    wuq_d = din("mla_w_uq", [384, 768])
    wukv_d = din("mla_w_ukv", [256, 1024])
    wmkv_d = din("mem_w_kv", [D, 1024])
    wbr_d = [din("w_branch_a", [512, D]), din("w_branch_b", [512, D]), din("w_branch_c", [512, D])]
    wout_d = din("w_out", [D, D])
    oT_d = nc.dram_tensor("oT", [D, NT], F32, kind="ExternalOutput").ap()
    kin_t = [nc.dram_tensor("kin%d" % c, [768, 512], BF16) for c in range(4)]
    kout_t = [nc.dram_tensor("kout%d" % c, [4 * 768, 512], BF16) for c in range(4)]
    vin_t = [nc.dram_tensor("vin%d" % c, [1024, 256], BF16) for c in range(4)]
    vout_t = [nc.dram_tensor("vout%d" % c, [4 * 1024, 256], BF16) for c in range(4)]

    es = ExitStack()
    with es:
        p = Prog(nc, es)

        def sb(name, shape, dt):
            return es.enter_context(nc.sbuf_tensor(name, shape, dt))

        xT = sb("xT_sb", [128, 8, NT], F32)
        cst = sb("cst_sb", [128, NCST], F32)
        cbf = sb("cbf_sb", [128, NCBF], BF16)
        csA = sb("csA", [128, 16, 32], F32)
        csB = sb("csB", [128, 16, 32], F32)
        wTsg = sb("wTsg", [128, 8, 128], BF16)
        KmemT = sb("KmemT", [128, 4, 256], BF16)
        Vmem = sb("Vmem", [128, 2, 512], BF16)
        wr_t = sb("wring", [128, 3, 4096], BF16)
        ARENA = 49300
        ar = sb("arena", [128, ARENA], BF16)
        ps = es.enter_context(nc.psum_tensor("ps", [128, 8, 512], F32))

        def C(name):
            o, w = CST[name]
            return cst[:, o:o + w]

        ident = cbf[:, 0:128]
        ones = cbf[:, 128:256]
        masks = cbf[:, 256:256 + 1024].rearrange("p (a b) -> p a b", b=128)

        class Carver:
            def __init__(self):
                self.off = 0
                self.hi = 0

            def take(self, shape, dt, key=None):
                n = int(np.prod(shape)) * (2 if dt in (F32, I32) else 1)
                n = (n + 1) // 2 * 2
                o = self.off
                assert o + n <= ARENA, (o, n, key)
                a = ar[:, o:o + n]
                self.off += n
                self.hi = max(self.hi, self.off)
                if key is not None:
                    p.region(key, [(o, o + n)])
                if dt in (F32, I32):
                    a = a.bitcast(dt)
                if len(shape) == 2:
                    a = a.rearrange("p (a b) -> p a b", b=shape[1])
                elif len(shape) == 3:
                    a = a.rearrange("p (a b c) -> p a b c", b=shape[1], c=shape[2])
                return a

        cv = Carver()
        sq_ring = Ring("sq", [cv.take([512], BF16) for _ in range(2)])
        std_t = cv.take([512], F32)
        rstd_t = cv.take([512], F32)
        junk = cv.take([512], BF16)
        small = cv.take([128], F32)
        base_off = cv.off

        def mk_hn(pfx):
            return dict(pfx=pfx, sq=cv.take([8, 96], F32, pfx + "hn_sq"), rt=cv.take([8, 32], F32, pfx + "hn_rt"),
                        t1=cv.take([8, 32], F32, pfx + "hn_t1"), t2=cv.take([8, 32], F32, pfx + "hn_t2"))

        gen_banks = Ring("ps", [ps[:, b, :] for b in range(6)])
        o_banks = Ring("po", [ps[:, 6, :], ps[:, 7, :]])
        wring = Ring("w", [wr_t[:, s, :] for s in range(3)])

        def wload(pairs_fn):
            slot, key = wring.next()
            p.dma("pool", pairs_fn(slot), reads=[], writes=[key], semname=key)
            return slot, key

        def bf_bank(bank):
            return bank.bitcast(BF16).rearrange("p (a b) -> p a b", b=128)

        p.dma("sp", [(cst[:, :], cst_d[:, :]), (cbf[:, :], cbf_d[:, :])], [], ["cst"], "cst")
        for t in range(4):
            p.dma("sp", [(xT[:, :, t * 512:(t + 1) * 512],
                          xT_d[:, t * 512:(t + 1) * 512].rearrange("(kc p) t -> p kc t", p=128))],
                  [], ["x%d" % t], "xin%d" % t)

        cv.off = base_off
        pos_i = cv.take([16], I32, "pos")
        posf = cv.take([16], F32, "posf")
        angA = cv.take([16, 32], F32, "angA")
        angT = cv.take([16, 32], F32, "angT")
        angI = cv.take([16, 32], I32, "angI")
        angF = cv.take([16, 32], F32, "angF")
        angM = cv.take([16, 32], F32, "angM")
        SC = cv.take([16, 32], F32, "SC")
        sgtmp = cv.take([8, 128], F32, "sgtmp")
        _save = cv.off
        cv.off = 36640
        memT = cv.take([8, 256], F32, "memT")
        memnT = cv.take([8, 256], BF16, "memnT")
        km_sb = cv.take([4, 128], F32, "km_sb")
        kmn = cv.take([4, 128], BF16, "kmn")
        hn0 = mk_hn("s_")
        cv.off = _save

        p.dma("sp", [(pos_i, pos_d[:, :])], [], ["pos"], "misc")
        p.dma("sp", [(sgtmp, sgwT_d[:, :].rearrange("p (g t) -> p g t", t=128))], [], ["sgtmp"], "misc2")
        p.dma("sp", [(memT, memT_d[:, :].rearrange("(kc p) m -> p kc m", p=128))], [], ["memT"], "misc3")

        p.dve(lambda e: e.tensor_copy(out=posf, in_=pos_i), ["pos"], ["posf"])
        for i in range(16):
            p.dve(lambda e, i=i: e.tensor_scalar(out=angA[:, i, 0:16], in0=C("invf"), scalar1=posf[:, i:i + 1],
                                                 scalar2=None, op0=ALU.mult), ["posf", "cst"], ["angA"])
        p.dve(lambda e: e.tensor_scalar(out=angA[:, :, 16:32], in0=angA[:, :, 0:16], scalar1=PI / 2, scalar2=None,
                                        op0=ALU.add), ["angA"], ["angA"])
        p.dve(lambda e: e.tensor_scalar(out=angT, in0=angA, scalar1=1.0 / (2 * PI), scalar2=None, op0=ALU.mult),
              ["angA"], ["angT"])
        p.dve(lambda e: e.tensor_copy(out=angI, in_=angT), ["angT"], ["angI"])
        p.dve(lambda e: e.tensor_copy(out=angF, in_=angI), ["angI"], ["angF"])
        p.dve(lambda e: e.scalar_tensor_tensor(out=angT, in0=angF, scalar=-2 * PI, in1=angA, op0=ALU.mult,
                                               op1=ALU.add), ["angF", "angA"], ["angT"])
        p.dve(lambda e: e.tensor_scalar(out=angM, in0=angT, scalar1=PI, scalar2=None, op0=ALU.is_gt),
              ["angT"], ["angM"])
        p.dve(lambda e: e.scalar_tensor_tensor(out=angF, in0=angM, scalar=-2 * PI, in1=angT, op0=ALU.mult,
                                               op1=ALU.add), ["angM", "angT"], ["angF"])
        p.dve(lambda e: e.tensor_scalar(out=angM, in0=angF, scalar1=-PI, scalar2=None, op0=ALU.is_lt),
              ["angF"], ["angM"])
        p.dve(lambda e: e.scalar_tensor_tensor(out=angT, in0=angM, scalar=2 * PI, in1=angF, op0=ALU.mult,
                                               op1=ALU.add), ["angM", "angF"], ["angT"])
        p.dve(lambda e: e.tensor_scalar(out=angF, in0=angT, scalar1=PI, scalar2=-PI, op0=ALU.min, op1=ALU.max),
              ["angT"], ["angF"])
        p.act(lambda e: e.activation(out=SC, in_=angF, func=AF.Sin), ["angF"], ["SC"])
        p.dve(lambda e: e.tensor_copy(out=csA[:, :, 0:16], in_=SC[:, :, 16:32]), ["SC"], ["csA"])
        p.dve(lambda e: e.tensor_copy(out=csA[:, :, 16:32], in_=SC[:, :, 16:32]), ["SC"], ["csA"])
        p.dve(lambda e: e.tensor_scalar(out=csB[:, :, 0:16], in0=SC[:, :, 0:16], scalar1=-1.0, scalar2=None,
                                        op0=ALU.mult), ["SC"], ["csB"])
        p.dve(lambda e: e.tensor_copy(out=csB[:, :, 16:32], in_=SC[:, :, 0:16]), ["SC"], ["csB"])
        p.dve(lambda e: e.tensor_tensor(out=wTsg[:, :, :], in0=sgtmp,
                                        in1=C("tri").unsqueeze(1).broadcast_to([128, 8, 128]), op=ALU.mult),
              ["sgtmp", "cst"], ["wTsg"])

        def norm_fm(src, srckeys, gname, dst, dstkey, N, bank=None):
            bank, bkey = bank if bank is not None else gen_banks.next()
            for kc in range(8):
                sq, sqk = sq_ring.next()
                s_ap = src(kc)
                p.act(lambda e, s_ap=s_ap, sq=sq: e.activation(out=sq[:, :N], in_=s_ap, func=AF.Square),
                      srckeys, [sqk])
                p.pe(lambda e, kc=kc, sq=sq: e.matmul(bank[:, :N], ones, sq[:, :N], start=(kc == 0), stop=(kc == 7)),
                     [sqk, "cst"], [bkey])
            p.act(lambda e: e.activation(out=std_t[:, :N], in_=bank[:, :N], func=AF.Sqrt, bias=EPS, scale=1.0 / D),
                  [bkey], ["std"])
            p.dve(lambda e: e.reciprocal(out=rstd_t[:, :N], in_=std_t[:, :N]), ["std"], ["rstd"])
            g = C(gname)
            for kc in range(8):
                s_ap, d_ap = src(kc), dst(kc)
                p.dve(lambda e, kc=kc, s_ap=s_ap, d_ap=d_ap: e.scalar_tensor_tensor(
                    out=d_ap, in0=s_ap, scalar=g[:, kc:kc + 1], in1=rstd_t[:, :N], op0=ALU.mult, op1=ALU.mult),
                    srckeys + ["rstd", "cst"], [dstkey])

        def head_norm(hn, src, H, Dh, gname, out_bf, key_in, key_out, rope_i=None):
            while hn.get("busy"):
                yield
            hn["busy"] = True
            pf = hn["pfx"]
            rt, t1, t2 = hn["rt"], hn["t1"], hn["t2"]
            ksq, krt, kt1, kt2 = pf + "hn_sq", pf + "hn_rt", pf + "hn_t1", pf + "hn_t2"
            sqv = hn["sq"].rearrange("p a b -> p (a b)")[:, 0:H * Dh].rearrange("p (a b) -> p a b", b=Dh)
            ss = small[:, 0:H]
            sd = small[:, 8:8 + H]
            rs = small[:, 16:16 + H]
            g = C(gname)
            p.dve(lambda e: e.tensor_tensor(out=sqv, in0=src, in1=src, op=ALU.mult), [key_in], [ksq])
            yield
            p.dve(lambda e: e.tensor_reduce(out=ss, in_=sqv, axis=AX.X, op=ALU.add), [ksq], ["hn_ss"])
            yield
            p.act(lambda e: e.activation(out=sd, in_=ss, func=AF.Sqrt, bias=EPS, scale=1.0 / Dh), ["hn_ss"], ["hn_sd"])
            yield
            p.dve(lambda e: e.reciprocal(out=rs, in_=sd), ["hn_sd"], ["hn_rs"])
            yield
            p.dve(lambda e: e.tensor_tensor(out=sqv, in0=src, in1=rs.unsqueeze(2).broadcast_to([128, H, Dh]),
                                            op=ALU.mult), [key_in, "hn_rs"], [ksq])
            yield
            if rope_i is None:
                p.dve(lambda e: e.tensor_tensor(out=out_bf, in0=sqv, in1=g.unsqueeze(1).broadcast_to([128, H, Dh]),
                                                op=ALU.mult), [ksq, "cst"], [key_out])
                yield
                hn["busy"] = False
                return
            i = rope_i
            p.dve(lambda e: e.tensor_tensor(out=out_bf[:, :, 0:64], in0=sqv[:, :, 0:64],
                                            in1=g[:, 0:64].unsqueeze(1).broadcast_to([128, H, 64]), op=ALU.mult),
                  [ksq, "cst"], [key_out])
            yield
            p.dve(lambda e: e.tensor_tensor(out=rt, in0=sqv[:, :, 64:96],
                                            in1=g[:, 64:96].unsqueeze(1).broadcast_to([128, H, 32]), op=ALU.mult),
                  [ksq, "cst"], [krt])
            yield
            p.dve(lambda e: e.tensor_tensor(out=t1, in0=rt,
                                            in1=csA[:, i, :].unsqueeze(1).broadcast_to([128, H, 32]), op=ALU.mult),
                  [krt, "csA"], [kt1])
            yield
            p.dve(lambda e: e.tensor_tensor(out=t2[:, :, 0:16], in0=rt[:, :, 16:32],
                                            in1=csB[:, i, 0:16].unsqueeze(1).broadcast_to([128, H, 16]), op=ALU.mult),
                  [krt, "csB"], [kt2])
            yield
            p.dve(lambda e: e.tensor_tensor(out=t2[:, :, 16:32], in0=rt[:, :, 0:16],
                                            in1=csB[:, i, 16:32].unsqueeze(1).broadcast_to([128, H, 16]), op=ALU.mult),
                  [krt, "csB"], [kt2])
            yield
            p.dve(lambda e: e.tensor_tensor(out=out_bf[:, :, 64:96], in0=t1, in1=t2, op=ALU.add),
                  [kt1, kt2], [key_out])
            yield
            hn["busy"] = False

        def mm_group(bank_ap, bkey, items, reads):
            def fn(e):
                n = len(items)
                ins = None
                for k, (l, r) in enumerate(items):
                    ins = e.matmul(bank_ap, l, r, start=(k == 0), stop=(k == n - 1))
                return ins
            p.pe(fn, reads, [bkey])

        def transposes(bank, bkey, srcs, reads, rows=128):
            bb = bf_bank(bank)

            def fn(e):
                ins = None
                for j, s in enumerate(srcs):
                    ins = e.transpose(out=bb[0:rows, j, :], in_=s, identity=ident)
                return ins
            p.pe(fn, reads + ["cst"], [bkey])
            return bb

        def ffn(wgu_d, wdn_d, gname):
            cv.off = base_off
            ho = cv.off
            hT = cv.take([8, 1024], BF16)
            for sub in range(2):
                p.region("h%d" % sub, [(ho + kc * 1024 + sub * 512, ho + kc * 1024 + sub * 512 + 512) for kc in range(8)])
            ao = cv.off
            actT = cv.take([FC, 1024], BF16)
            for f in range(FC):
                for sub in range(2):
                    p.region("a%d_%d" % (f, sub), [(ao + f * 1024 + sub * 512, ao + f * 1024 + sub * 512 + 512)])
            sg_ring = Ring("sg", [cv.take([512], F32, "sg%d" % k) for k in range(2)])
            assert cv.off <= 36640, cv.off
            for half in range(2):
                for sub in range(2):
                    t = half * 2 + sub
                    norm_fm(lambda kc, t=t: xT[:, kc, t * 512:(t + 1) * 512], ["x%d" % t], gname,
                            lambda kc, sub=sub: hT[:, kc, sub * 512:(sub + 1) * 512], "h%d" % sub, 512)
                for pp in range(11):
                    slot, wkey = wload(lambda s, pp=pp: [
                        (s.rearrange("p (a kc n) -> p a kc n", a=2, n=256)[:, 0],
                         wgu_d[:, pp * 256:(pp + 1) * 256].rearrange("(kc p) n -> p kc n", p=128)),
                        (s.rearrange("p (a kc n) -> p a kc n", a=2, n=256)[:, 1],
                         wgu_d[:, DFF + pp * 256:DFF + (pp + 1) * 256].rearrange("(kc p) n -> p kc n", p=128))])
                    w4 = slot.rearrange("p (a kc n) -> p a kc n", a=2, n=256)
                    for fi in range(2):
                        f = pp * 2 + fi
                        for sub in range(2):
                            bg, bgk = gen_banks.next()
                            bu, buk = gen_banks.next()
                            hs = lambda kc, sub=sub: hT[:, kc, sub * 512:(sub + 1) * 512]
                            mm_group(bg, bgk, [(w4[:, 0, kc, fi * 128:(fi + 1) * 128], hs(kc)) for kc in range(8)],
                                     [wkey, "h%d" % sub])
                            mm_group(bu, buk, [(w4[:, 1, kc, fi * 128:(fi + 1) * 128], hs(kc)) for kc in range(8)],
                                     [wkey, "h%d" % sub])
                            sg, sgk = sg_ring.next()
                            p.act(lambda e, sg=sg, bg=bg: e.activation(out=sg, in_=bg, func=AF.Silu), [bgk], [sgk])
                            a_ap = actT[:, f, sub * 512:(sub + 1) * 512]
                            p.dve(lambda e, sg=sg, bu=bu, a_ap=a_ap: e.tensor_tensor(out=a_ap, in0=bu, in1=sg, op=ALU.mult),
                                  [sgk, buk], ["a%d_%d" % (f, sub)])
                for m in range(8):
                    slot, wkey = wload(lambda s, m=m: [
                        (s[:, 0:FC * 128].rearrange("p (f n) -> p f n", n=128),
                         wdn_d[:, m * 128:(m + 1) * 128].rearrange("(f p) n -> p f n", p=128))])
                    w3 = slot[:, 0:FC * 128].rearrange("p (f n) -> p f n", n=128)
                    for sub in range(2):
                        t = half * 2 + sub
                        bd, bdk = gen_banks.next()
                        mm_group(bd, bdk, [(w3[:, f, :], actT[:, f, sub * 512:(sub + 1) * 512]) for f in range(FC)],
                                 [wkey] + ["a%d_%d" % (f, sub) for f in range(FC)])
                        xs = xT[:, m, t * 512:(t + 1) * 512]
                        p.dve(lambda e, bd=bd, xs=xs: e.scalar_tensor_tensor(out=xs, in0=bd, scalar=0.5, in1=xs,
                                                                            op0=ALU.mult, op1=ALU.add),
                              [bdk, "x%d" % t], ["x%d" % t])

        def store_out():
            for t in range(4):
                p.dma("sp", [(oT_d[:, t * 512:(t + 1) * 512].rearrange("(kc p) t -> p kc t", p=128),
                              xT[:, :, t * 512:(t + 1) * 512])], ["x%d" % t], ["out%d" % t], "out", final=True)

        def finish(store=True):
            if store:
                store_out()
            with nc.Block() as block:
                p.emit(block)
            return nc

        if stop != "noffn1":
            ffn(w1gu_d, w1dn_d, "ffn1_g")
        norm_fm(lambda kc: memT[:, kc, :], ["memT"], "mem_g", lambda kc: memnT[:, kc, :], "memnT", 256)
        wk, wkk = wload(lambda s: [(s.rearrange("p (kc n) -> p kc n", n=512),
                                    wmkv_d[:, 0:512].rearrange("(kc p) n -> p kc n", p=128))])
        wv_, wvk = wload(lambda s: [(s.rearrange("p (kc n) -> p kc n", n=512),
                                     wmkv_d[:, 512:1024].rearrange("(kc p) n -> p kc n", p=128))])
        wk3 = wk.rearrange("p (kc n) -> p kc n", n=512)
        wv3 = wv_.rearrange("p (kc n) -> p kc n", n=512)
        for mb in range(2):
            bk, bkk = gen_banks.next()
            mm_group(bk, bkk, [(memnT[:, kc, mb * 128:(mb + 1) * 128], wk3[:, kc, :]) for kc in range(8)],
                     ["memnT", wkk])
            bv, bvk = gen_banks.next()
            mm_group(bv, bvk, [(memnT[:, kc, mb * 128:(mb + 1) * 128], wv3[:, kc, :]) for kc in range(8)],
                     ["memnT", wvk])
            p.act(lambda e, mb=mb, bv=bv: e.activation(out=Vmem[:, mb, :], in_=bv, func=AF.Copy), [bvk], ["Vmem"])
            p.act(lambda e, bk=bk: e.activation(out=km_sb.rearrange("p a b -> p (a b)"), in_=bk, func=AF.Copy),
                  [bkk], ["km_sb"])
            for _ in head_norm(hn0, km_sb, 4, 128, "mkg", kmn, "km_sb", "kmn"):
                pass
            bt, btk = gen_banks.next()
            bb = transposes(bt, btk, [kmn[:, h, :] for h in range(4)], ["kmn"])
            p.act(lambda e, mb=mb, bb=bb: e.activation(out=KmemT[:, :, mb * 128:(mb + 1) * 128], in_=bb[:, 0:4, :],
                                                       func=AF.Copy), [btk], ["KmemT"])

        if stop == "ffn1":
            return finish()

        def run_pipelined(gens, depth=2):
            pending = list(gens)
            active = []
            while pending or active:
                if pending and len(active) < depth:
                    active.append(pending.pop(0))
                for g in list(active):
                    try:
                        next(g)
                    except StopIteration:
                        active.remove(g)

        cv.off = base_off
        hT2 = [cv.take([8, 512], BF16, "hT%d" % k) for k in range(2)]
        _ko = cv.off
        Kst = cv.take([8, NT], BF16)
        _vo = cv.off
        Vst = cv.take([8 * 16 * 64], BF16).rearrange("p (h i d) -> p h i d", h=8, i=16)
        for tt in range(4):
            p.region("Kst%d" % tt, [(_ko + h * NT + tt * 512, _ko + h * NT + tt * 512 + 512) for h in range(8)])
            p.region("Vst%d" % tt, [(_vo + h * 1024 + tt * 256, _vo + h * 1024 + tt * 256 + 256) for h in range(8)])
        m1sets = []
        for s in range(3):
            m1sets.append(dict(ckvn=cv.take([256], BF16, "ckvn%d" % s), ckvnT=cv.take([2, 128], BF16, "ckvnT%d" % s),
                               kc_sb=cv.take([8, 96], F32, "kc_sb%d" % s), kfin=cv.take([8, 96], BF16, "kfin%d" % s)))
        hn1 = mk_hn("m_")

        wkvin, wkvin_k = wload(lambda s: [(s[:, 0:8 * 288].rearrange("p (kc n) -> p kc n", n=288),
                                           win_d[:, C_CKV:C_CKV + 288].rearrange("(kc p) n -> p kc n", p=128))])
        wkvin3 = wkvin[:, 0:8 * 288].rearrange("p (kc n) -> p kc n", n=288)
        wukv, wukv_k = wload(lambda s: [(s[:, 0:2048].rearrange("p (kc n) -> p kc n", n=1024),
                                         wukv_d[:, :].rearrange("(kc p) n -> p kc n", p=128))])
        wukv3 = wukv[:, 0:2048].rearrange("p (kc n) -> p kc n", n=1024)
        norm_banks = o_banks

        pre_w = {}

        def m1_block(t, bl):
            i = t * 4 + bl
            s = i % 3
            S = m1sets[s]
            ckvn, ckvnT, kc_sb, kfin = S["ckvn"], S["ckvnT"], S["kc_sb"], S["kfin"]
            kq = lambda n: "%s%d" % (n, s)
            hT = hT2[t % 2]
            hk = "hT%d" % (t % 2)
            ss1 = small[:, 32 + 4 * s:33 + 4 * s]
            sd1 = small[:, 33 + 4 * s:34 + 4 * s]
            rs1 = small[:, 34 + 4 * s:35 + 4 * s]
            bA, bAk = ps[:, 2 * s + 0, :], "ps%d" % (2 * s + 0)
            bB, bBk = ps[:, 2 * s + 1, :], "ps%d" % (2 * s + 1)
            bC, bCk = bB, bBk
            if bl == 0:
                nb, nbk = norm_banks.next()
                norm_fm(lambda kc, t=t: xT[:, kc, t * 512:(t + 1) * 512], ["x%d" % t], "mix_g",
                        lambda kc: hT[:, kc, :], hk, 512, bank=(nb, nbk))
            mm_group(bA[:, 0:288], bAk, [(hT[:, kc, bl * 128:(bl + 1) * 128], wkvin3[:, kc, :]) for kc in range(8)],
                     [hk, wkvin_k])
            yield
            p.act(lambda e: e.activation(out=junk[:, 0:256], in_=bA[:, 0:256], func=AF.Square, accum_out=ss1),
                  [bAk], [kq("ss1"), "junk"])
            yield
            p.act(lambda e: e.activation(out=sd1, in_=ss1, func=AF.Sqrt, bias=EPS, scale=1.0 / 256), [kq("ss1")], [kq("sd1")])
            p.act(lambda e: e.activation(out=kc_sb[:, :, 64:96], in_=bA[:, 256:288].unsqueeze(1).broadcast_to([128, 8, 32]),
                                         func=AF.Copy), [bAk], [kq("kc_sb")])
            yield
            p.dve(lambda e: e.reciprocal(out=rs1, in_=sd1), [kq("sd1")], [kq("rs1")])
            yield
            p.dve(lambda e: e.scalar_tensor_tensor(out=ckvn, in0=bA[:, 0:256], scalar=rs1, in1=C("ckvg"),
                                                   op0=ALU.mult, op1=ALU.mult), [bAk, kq("rs1"), "cst"], [kq("ckvn")])
            yield
            bb = transposes(bB, bBk, [ckvn[:, k * 128:(k + 1) * 128] for k in range(2)], [kq("ckvn")])
            p.act(lambda e: e.activation(out=ckvnT, in_=bb[:, 0:2, :], func=AF.Copy), [bBk], [kq("ckvnT")])
            yield
            mm_group(bC, bCk, [(ckvnT[:, k, :], wukv3[:, k, 0:512]) for k in range(2)], [kq("ckvnT"), wukv_k])
            mm_group(bA, bAk, [(ckvnT[:, k, :], wukv3[:, k, 512:1024]) for k in range(2)], [kq("ckvnT"), wukv_k])
            for hb, (bx, bxk) in enumerate(((bC, bCk), (bA, bAk))):
                b3 = bx.rearrange("p (h d) -> p h d", d=128)
                p.act(lambda e, b3=b3, hb=hb: e.activation(out=Vst[:, hb * 4:(hb + 1) * 4, i, :],
                                                           in_=b3[:, :, 64:128], func=AF.Copy), [bxk], ["Vst%d" % t])
                p.act(lambda e, b3=b3, hb=hb: e.activation(out=kc_sb[:, hb * 4:(hb + 1) * 4, 0:64],
                                                           in_=b3[:, :, 0:64], func=AF.Copy), [bxk], [kq("kc_sb")])
            yield
            yield from head_norm(hn1, kc_sb, 8, 96, "kg", kfin, kq("kc_sb"), kq("kfin"), rope_i=i)
            bb2 = transposes(bB, bBk, [kfin[:, h, :] for h in range(8)], [kq("kfin")], rows=96)
            p.act(lambda e: e.activation(out=Kst[0:96, :, i * 128:(i + 1) * 128], in_=bb2[0:96, :, :],
                                         func=AF.Copy), [bBk], ["Kst%d" % t])
            if bl == 3 and stop not in ("m1nocc", "dump_m1"):
                yield
                if t == 0:
                    pre_w["wv"] = wload(lambda s: [(s.rearrange("p (kc n) -> p kc n", n=512),
                                                    win_d[:, C_V:C_V + 512].rearrange("(kc p) n -> p kc n", p=128))])
                if t == 3:
                    pre_w["wu"] = wload(lambda s: [(s.rearrange("p (kc n) -> p kc n", n=512),
                                                    win_d[:, C_U:C_U + 512].rearrange("(kc p) n -> p kc n", p=128))])
                p.dma("sp", [(kin_t[t].ap().rearrange("(h f) c -> f h c", f=96), Kst[0:96, :, t * 512:(t + 1) * 512])],
                      ["Kst%d" % t], ["kin%d" % t], "kin%d" % t)
                p.dma("sp", [(vin_t[t].ap().rearrange("(h p) (i d) -> p h i d", p=128, d=64), Vst[:, :, 4 * t:4 * t + 4, :])],
                      ["Vst%d" % t], ["vin%d" % t], "vin%d" % t)
                p.collective(lambda e: e.collective_compute(
                    "AllGather", ALU.bypass, replica_groups=[[0, 1, 2, 3], [4, 5, 6, 7]],
                    ins=[kin_t[t].ap().opt()], outs=[kout_t[t].ap().opt()]),
                    ["kin%d" % t], ["kout%d" % t], semname="cck%d" % t)
                p.collective(lambda e: e.collective_compute(
                    "AllGather", ALU.bypass, replica_groups=[[0, 1, 2, 3], [4, 5, 6, 7]],
                    ins=[vin_t[t].ap().opt()], outs=[vout_t[t].ap().opt()]),
                    ["vin%d" % t], ["vout%d" % t], semname="ccv%d" % t)

        run_pipelined([m1_block(t, bl) for t in range(4) for bl in range(4)], depth=3)
        if stop == "dump_m1":
            for h in range(8):
                p.dma("pool", [(oT_d[h * 96:(h + 1) * 96, :], Kst[0:96, h, :])], ["Kst%d" % q for q in range(4)], ["dbg%d" % h], "dbg", final=True)
            for hq in range(2):
                p.dma("pool", [(oT_d[768 + hq * 128:768 + (hq + 1) * 128, :].rearrange("p (h i d) -> p h i d", h=2, i=16),
                                Vst[:, 2 * hq:2 * hq + 2, :, :])], ["Vst%d" % q for q in range(4)], ["dbgv%d" % hq], "dbg", final=True)
            return finish(store=False)
        if stop == "m1":
            return finish()
        if stop == "m1nocc":
            return finish()
        cv.off = base_off
        hT = cv.take([8, 512], BF16, "hT")
        QT = cv.take([8, 512], BF16, "QT")
        qmT = cv.take([4, 512], BF16, "qmT")
        yT = [cv.take([4, 512], BF16, k) for k in ("yaT", "ybT", "ycT")]
        Kring_v = [cv.take([2048], BF16, "K%d" % s) for s in range(4)]
        Vring_v = [cv.take([16, 128], BF16, "V%d" % s) for s in range(4)]
        ph_off = cv.off
        uT_sb = cv.take([4, 512], BF16, "uT")
        Asets = [dict(v_sb=cv.take([512], F32, "v_sb%d" % s), v_ln=cv.take([512], BF16, "v_ln%d" % s))
                 for s in range(3)]
        mtmp4 = [cv.take([4, 128], F32, "mtmp%d" % k) for k in range(4)]
        cv.off = ph_off
        Qsets = [dict(cqn=cv.take([384], BF16, "cqn%d" % s), cqnT=cv.take([3, 128], BF16, "cqnT%d" % s),
                      q_sb=cv.take([8, 96], F32, "q_sb%d" % s), qfin=cv.take([8, 96], BF16, "qfin%d" % s))
                 for s in range(2)]
        hn2 = mk_hn("t_")
        q_end = cv.off
        cv.off = ph_off
        Csets = [dict(qm_sb=cv.take([4, 128], F32, "qm_sb%d" % s), qmn=cv.take([4, 128], BF16, "qmn%d" % s))
                 for s in range(3)]
        assert cv.off <= ph_off + 2 * (384 + 384 + 1536 + 768), "C sets overlap hn temps"
        assert cv.off <= q_end - 3 * 1024 - 1536 * 2 or True
        cv.off = ph_off
        P_ring = Ring("P", [cv.take([512], BF16, "P%d" % k) for k in range(8)])
        rd_views = [cv.take([512], F32, "rd%d" % k) for k in range(2)]
        rden_ring = Ring("rd", rd_views)
        acc1, acc2 = rd_views
        mergedT = cv.take([8, 512], BF16, "mergedT")
        go = cv.off
        g_sb = cv.take([3, 512], BF16)
        for br in range(3):
            p.region("g_sb%d" % br, [(go + br * 512, go + br * 512 + 512)])

        for s in range(4):
            lo = 64 if s < 2 else 0
            p.pool(lambda e, s=s, lo=lo: e.memset(Vring_v[s][:, :, lo:lo + 64], 1.0), [], ["V%d" % s])

        SC_B = 96 ** -0.5
        SC_C = 128 ** -0.5
        st6 = small[:, 40:46]
        mv = small[:, 46:48]
        sdv = small[:, 48:49]
        rsv = small[:, 49:50]

        for t in range(4):
            xk = "x%d" % t
            norm_fm(lambda kc, t=t: xT[:, kc, t * 512:(t + 1) * 512], [xk], "mix_g",
                    lambda kc: hT[:, kc, :], "hT", 512)
            wv_s, wv_k = pre_w.pop("wv") if "wv" in pre_w else wload(lambda s: [
                (s.rearrange("p (kc n) -> p kc n", n=512),
                 win_d[:, C_V:C_V + 512].rearrange("(kc p) n -> p kc n", p=128))])
            wu_s, wu_k = pre_w.pop("wu") if "wu" in pre_w else wload(lambda s: [
                (s.rearrange("p (kc n) -> p kc n", n=512),
                 win_d[:, C_U:C_U + 512].rearrange("(kc p) n -> p kc n", p=128))])
            wv3 = wv_s.rearrange("p (kc n) -> p kc n", n=512)
            wu3 = wu_s.rearrange("p (kc n) -> p kc n", n=512)
            def a_block(bl):
                s = bl % 3
                S = Asets[s]
                v_sb, v_ln, mtmp = S["v_sb"], S["v_ln"], mtmp4[bl]
                kq = lambda n: "%s%d" % (n, s)
                st6 = small[:, 64 + 16 * s:70 + 16 * s]
                mv = small[:, 70 + 16 * s:72 + 16 * s]
                sdv = small[:, 72 + 16 * s:73 + 16 * s]
                rsv = small[:, 73 + 16 * s:74 + 16 * s]
                bV, bVk = ps[:, 2 * s + 0, :], "ps%d" % (2 * s + 0)
                bM, bMk = ps[:, 2 * s + 1, :], "ps%d" % (2 * s + 1)
                mm_group(bV, bVk, [(hT[:, kc, bl * 128:(bl + 1) * 128], wv3[:, kc, :]) for kc in range(8)], [wv_k, "hT"])
                yield
                p.act(lambda e: e.activation(out=v_sb, in_=bV, func=AF.Gelu), [bVk], [kq("v_sb")])
                yield
                p.dve(lambda e: e.bn_stats(out=st6, in_=v_sb), [kq("v_sb")], [kq("st6")])
                yield
                p.dve(lambda e: e.bn_aggr(out=mv, in_=st6), [kq("st6")], [kq("mv")])
                yield
                p.act(lambda e: e.activation(out=sdv, in_=mv[:, 1:2], func=AF.Sqrt, bias=EPS, scale=1.0), [kq("mv")], [kq("sdv")])
                p.dve(lambda e: e.scalar_tensor_tensor(out=v_sb, in0=v_sb, scalar=mv[:, 0:1], in1=C("lng"),
                                                       op0=ALU.subtract, op1=ALU.mult), [kq("v_sb"), kq("mv"), "cst"], [kq("v_sb")])
                yield
                p.dve(lambda e: e.reciprocal(out=rsv, in_=sdv), [kq("sdv")], [kq("rsv")])
                yield
                p.dve(lambda e: e.scalar_tensor_tensor(out=v_ln, in0=v_sb, scalar=rsv, in1=C("lnb"),
                                                       op0=ALU.mult, op1=ALU.add), [kq("v_sb"), kq("rsv"), "cst"], [kq("v_ln")])
                yield

                def mixfn(e):
                    ins = None
                    for g in range(8):
                        ins = e.matmul(bM[(g % 2) * 64:(g % 2) * 64 + 64, (g // 2) * 128:(g // 2 + 1) * 128],
                                       v_ln[:, g * 64:(g + 1) * 64], wTsg[:, g, :], start=True, stop=True)
                    return ins
                p.pe(mixfn, [kq("v_ln"), "wTsg"], [bMk])
                yield
                bm3 = bM.rearrange("p (c t) -> p c t", t=128)
                p.dve(lambda e: e.tensor_tensor(out=mtmp, in0=bm3, in1=C("bsT").rearrange("p (c t) -> p c t", t=128),
                                                op=ALU.add), [bMk, "cst"], ["mtmp%d" % bl])
            run_pipelined([a_block(bl) for bl in range(4)], depth=3)
            for c in range(4):
                bu, buk = o_banks.next()
                mm_group(bu, buk, [(wu3[:, kc, c * 128:(c + 1) * 128], hT[:, kc, :]) for kc in range(8)], [wu_k, "hT"])
                p.act(lambda e, c=c, bu=bu: e.activation(out=uT_sb[:, c, :], in_=bu, func=AF.Gelu), [buk], ["uT"])

            for bl in range(4):
                p.dve(lambda e, bl=bl: e.tensor_tensor(out=yT[0][:, :, bl * 128:(bl + 1) * 128], in0=mtmp4[bl],
                                                       in1=uT_sb[:, :, bl * 128:(bl + 1) * 128], op=ALU.mult),
                      ["mtmp%d" % bl, "uT"], ["yaT"])
            wcq_s, wcq_k = wload(lambda s: [(s[:, 0:8 * 384].rearrange("p (kc n) -> p kc n", n=384),
                                             win_d[:, C_CQ:C_CQ + 384].rearrange("(kc p) n -> p kc n", p=128))])
            wuq_s, wuq_k = wload(lambda s: [(s[:, 0:3 * 768].rearrange("p (kc n) -> p kc n", n=768),
                                             wuq_d[:, :].rearrange("(kc p) n -> p kc n", p=128))])
            wcq3 = wcq_s[:, 0:8 * 384].rearrange("p (kc n) -> p kc n", n=384)
            wuq3 = wuq_s[:, 0:3 * 768].rearrange("p (kc n) -> p kc n", n=768)

            def q_block(bl):
                i = t * 4 + bl
                s = bl % 2
                S = Qsets[s]
                cqn, cqnT, q_sb, qfin = S["cqn"], S["cqnT"], S["q_sb"], S["qfin"]
                kq = lambda n: "%s%d" % (n, s)
                ss1 = small[:, 32 + 4 * s:33 + 4 * s]
                sd1 = small[:, 33 + 4 * s:34 + 4 * s]
                rs1 = small[:, 34 + 4 * s:35 + 4 * s]
                bA, bAk = ps[:, 3 * s + 0, :], "ps%d" % (3 * s + 0)
                bB, bBk = ps[:, 3 * s + 1, :], "ps%d" % (3 * s + 1)
                bC, bCk = ps[:, 3 * s + 2, :], "ps%d" % (3 * s + 2)
                mm_group(bA[:, 0:384], bAk, [(hT[:, kc, bl * 128:(bl + 1) * 128], wcq3[:, kc, :]) for kc in range(8)],
                         [wcq_k, "hT"])
                yield
                p.act(lambda e: e.activation(out=junk[:, 0:384], in_=bA[:, 0:384], func=AF.Square, accum_out=ss1),
                      [bAk], [kq("ss1"), "junk"])
                yield
                p.act(lambda e: e.activation(out=sd1, in_=ss1, func=AF.Sqrt, bias=EPS, scale=1.0 / 384), [kq("ss1")], [kq("sd1")])
                yield
                p.dve(lambda e: e.reciprocal(out=rs1, in_=sd1), [kq("sd1")], [kq("rs1")])
                yield
                p.dve(lambda e: e.scalar_tensor_tensor(out=cqn, in0=bA[:, 0:384], scalar=rs1, in1=C("cqg"),
                                                       op0=ALU.mult, op1=ALU.mult), [bAk, kq("rs1"), "cst"], [kq("cqn")])
                yield
                bb = transposes(bB, bBk, [cqn[:, k * 128:(k + 1) * 128] for k in range(3)], [kq("cqn")])
                p.act(lambda e: e.activation(out=cqnT, in_=bb[:, 0:3, :], func=AF.Copy), [bBk], [kq("cqnT")])
                yield
                for hb, (bx, bxk) in enumerate(((bC, bCk), (bA, bAk))):
                    mm_group(bx[:, 0:384], bxk, [(cqnT[:, k, :], wuq3[:, k, hb * 384:(hb + 1) * 384]) for k in range(3)],
                             [kq("cqnT"), wuq_k])
                    p.act(lambda e, bx=bx, hb=hb: e.activation(
                        out=q_sb[:, hb * 4:(hb + 1) * 4, :], in_=bx[:, 0:384].rearrange("p (h d) -> p h d", d=96),
                        func=AF.Copy), [bxk], [kq("q_sb")])
                yield
                yield from head_norm(hn2, q_sb, 8, 96, "qg", qfin, kq("q_sb"), kq("qfin"), rope_i=i)
                bb2 = transposes(bB, bBk, [qfin[:, h, :] for h in range(8)], [kq("qfin")], rows=96)
                p.act(lambda e: e.activation(out=QT[0:96, :, bl * 128:(bl + 1) * 128], in_=bb2[0:96, :, :],
                                             func=AF.Copy), [bBk], ["QT"])
            run_pipelined([q_block(bl) for bl in range(4)], depth=2)
            wqm_s, wqm_k = wload(lambda s: [(s.rearrange("p (kc n) -> p kc n", n=512),
                                             win_d[:, C_QM:C_QM + 512].rearrange("(kc p) n -> p kc n", p=128))])
            wqm3 = wqm_s.rearrange("p (kc n) -> p kc n", n=512)

            def c_block(bl):
                s = bl % 3
                S = Csets[s]
                qm_sb, qmn = S["qm_sb"], S["qmn"]
                kq = lambda n: "%s%d" % (n, s)
                bA, bAk = ps[:, 2 * s + 0, :], "ps%d" % (2 * s + 0)
                bB, bBk = ps[:, 2 * s + 1, :], "ps%d" % (2 * s + 1)
                mm_group(bA, bAk, [(hT[:, kc, bl * 128:(bl + 1) * 128], wqm3[:, kc, :]) for kc in range(8)],
                         [wqm_k, "hT"])
                yield
                p.act(lambda e: e.activation(out=qm_sb.rearrange("p a b -> p (a b)"), in_=bA, func=AF.Copy),
                      [bAk], [kq("qm_sb")])
                yield
                yield from head_norm(hn2, qm_sb, 4, 128, "mqg", qmn, kq("qm_sb"), kq("qmn"))
                bb = transposes(bB, bBk, [qmn[:, h, :] for h in range(4)], [kq("qmn")])
                p.act(lambda e: e.activation(out=qmT[:, :, bl * 128:(bl + 1) * 128], in_=bb[:, 0:4, :],
                                             func=AF.Copy), [bBk], ["qmT"])
            run_pipelined([c_block(bl) for bl in range(4)], depth=3)
            for h in range(4):
                Ps = []
                for mc in range(2):
                    bs, bsk = gen_banks.next()
                    mm_group(bs, bsk, [(KmemT[:, h, mc * 128:(mc + 1) * 128], qmT[:, h, :])], ["KmemT", "qmT"])
                    Pt, Pk = P_ring.next()
                    p.act(lambda e, bs=bs, Pt=Pt: e.activation(out=Pt, in_=bs, func=AF.Exp, scale=SC_C), [bsk], [Pk])
                    Ps.append((Pt, Pk))
                bo, bok = gen_banks.next()
                mm_group(bo, bok, [(Vmem[:, mc, h * 128:(h + 1) * 128], Ps[mc][0]) for mc in range(2)],
                         ["Vmem"] + [k for _, k in Ps])
                bd, bdk = gen_banks.next()
                mm_group(bd, bdk, [(ones, Ps[mc][0]) for mc in range(2)], ["cst"] + [k for _, k in Ps])
                rd, rdk = rden_ring.next()
                p.dve(lambda e, rd=rd, bd=bd: e.reciprocal(out=rd, in_=bd), [bdk], [rdk])
                p.dve(lambda e, rd=rd, bo=bo, h=h: e.tensor_tensor(out=yT[2][:, h, :], in0=bo, in1=rd, op=ALU.mult),
                      [bok, rdk], ["ycT"])
            nki = 4 * t + 4
            G = {0: 4, 1: 2}.get(t, 1)
            pend = []
            chunk_ctr = [0, 0]

            def rec_pv(item):
                (Pt, Pk, c0, Vc, vk, vi, ob, obk, first, last, fin) = item
                p.pe(lambda e: e.matmul(ob[:, c0:512], Vc[:, vi, :], Pt[:, c0:512], start=first, stop=last),
                     [Pk, vk], [obk])
                if last:
                    fin()
            for h in range(8):
                par = h % 2
                ob, obk = o_banks.next()
                olo = 0 if par == 0 else 64
                dlo = 64 - olo
                vlo = olo

                def fin(ob=ob, obk=obk, olo=olo, dlo=dlo, h=h):
                    rd, rdk = rden_ring.next()
                    p.dve(lambda e: e.reciprocal(out=rd[dlo:dlo + 64, :], in_=ob[dlo:dlo + 64, :]), [obk], [rdk])
                    p.dve(lambda e: e.tensor_tensor(out=yT[1][olo:olo + 64, h // 2, :], in0=ob[olo:olo + 64, :],
                                                    in1=rd[dlo:dlo + 64, :], op=ALU.mult), [obk, rdk], ["ybT"])
                npv = 0
                ntot = 4 * nki
                for c in range(4 // G):
                    slot = par * 2 + (chunk_ctr[par] % 2)
                    chunk_ctr[par] += 1
                    Kc = Kring_v[slot]
                    Vc = Vring_v[slot]
                    kk, vk = "K%d" % slot, "V%d" % slot
                    ranks = [c * G + gi for gi in range(G)]
                    p.dma("pool", [(Kc[0:96, gi * nki * 128 + tt * 512:gi * nki * 128 + (tt + 1) * 512],
                                    kout_t[tt].ap()[r * 768 + h * 96:r * 768 + (h + 1) * 96, :])
                                   for gi, r in enumerate(ranks) for tt in range(t + 1)],
                          ["kout%d" % tt for tt in range(t + 1)], [kk], kk)
                    p.dma("pool", [(Vc[:, gi * nki + 4 * tt:gi * nki + 4 * tt + 4, vlo:vlo + 64],
                                    vout_t[tt].ap()[r * 1024 + h * 128:r * 1024 + (h + 1) * 128, :].rearrange(
                                        "p (i d) -> p i d", d=64))
                                   for gi, r in enumerate(ranks) for tt in range(t + 1)],
                          ["vout%d" % tt for tt in range(t + 1)], [vk], vk)
                    for gi, r in enumerate(ranks):
                        for ki in range(nki):
                            d = ki - 4 * t
                            c0 = 128 * d if d > 0 else 0
                            kcol = (gi * nki + ki) * 128
                            bs, bsk = gen_banks.next()
                            p.pe(lambda e, bs=bs, Kc=Kc, kcol=kcol, c0=c0, h=h: e.matmul(
                                bs[:, c0:512], Kc[0:96, kcol:kcol + 128], QT[0:96, h, c0:512], start=True, stop=True),
                                [kk, "QT"], [bsk])
                            Pt, Pk = P_ring.next()
                            p.act(lambda e, bs=bs, Pt=Pt, c0=c0: e.activation(out=Pt[:, c0:512], in_=bs[:, c0:512],
                                                                             func=AF.Exp, scale=SC_B), [bsk], [Pk])
                            if d >= 0:
                                mk = masks[:, (ki % 2) * 4 + r, :]
                                p.dve(lambda e, Pt=Pt, c0=c0, mk=mk: e.tensor_tensor(
                                    out=Pt[:, c0:c0 + 128], in0=Pt[:, c0:c0 + 128], in1=mk, op=ALU.mult), [Pk, "cst"], [Pk])
                            pend.append((Pt, Pk, c0, Vc, vk, gi * nki + ki, ob, obk, npv == 0, npv == ntot - 1, fin))
                            npv += 1
                            if len(pend) > 4:
                                rec_pv(pend.pop(0))
            while pend:
                rec_pv(pend.pop(0))
            if stop == "dump_y%d" % t:
                for br in range(3):
                    p.dma("pool", [(oT_d[0:512, br * 512:(br + 1) * 512].rearrange("(kc p) t -> p kc t", p=128), yT[br])],
                          [("yaT", "ybT", "ycT")[br]], ["dbgy%d" % br], "dbg", final=True)
                return finish(store=False)
            for m in range(8):
                gs, gk = wload(lambda s, m=m: [
                    (s[:, 0:3072].rearrange("p (b kc n) -> p b kc n", b=3, n=128)[:, br],
                     win_d[:, C_GATE + br * 1024 + m * 128:C_GATE + br * 1024 + (m + 1) * 128].rearrange(
                         "(kc p) n -> p kc n", p=128)) for br in range(3)])
                bs_, bk_ = wload(lambda s, m=m: [
                    (s[:, 0:1536].rearrange("p (b kc n) -> p b kc n", b=3, n=128)[:, br],
                     wbr_d[br][:, m * 128:(m + 1) * 128].rearrange("(kc p) n -> p kc n", p=128)) for br in range(3)])
                g4 = gs[:, 0:3072].rearrange("p (b kc n) -> p b kc n", b=3, n=128)
                b4 = bs_[:, 0:1536].rearrange("p (b kc n) -> p b kc n", b=3, n=128)
                for br in range(3):
                    bg, bgk = gen_banks.next()
                    mm_group(bg, bgk, [(g4[:, br, kc, :], hT[:, kc, :]) for kc in range(8)], [gk, "hT"])
                    p.act(lambda e, bg=bg, br=br, m=m: e.activation(out=g_sb[:, br, :], in_=bg, func=AF.Sigmoid,
                                                                   bias=C("bgate")[:, br * 8 + m:br * 8 + m + 1], scale=1.0),
                          [bgk, "cst"], ["g_sb%d" % br])
                ykeys = ["yaT", "ybT", "ycT"]
                bbs = []
                for br in range(3):
                    bb_, bbk = gen_banks.next()
                    mm_group(bb_, bbk, [(b4[:, br, kc, :], yT[br][:, kc, :]) for kc in range(4)], [bk_, ykeys[br]])
                    bbs.append((bb_, bbk))
                p.dve(lambda e, b=bbs[0][0]: e.tensor_tensor(out=acc1, in0=b, in1=g_sb[:, 0, :], op=ALU.mult),
                      [bbs[0][1], "g_sb0"], ["rd0"])
                p.dve(lambda e, b=bbs[1][0]: e.tensor_tensor(out=acc2, in0=b, in1=g_sb[:, 1, :], op=ALU.mult),
                      [bbs[1][1], "g_sb1"], ["rd1"])
                p.dve(lambda e: e.tensor_tensor(out=acc1, in0=acc1, in1=acc2, op=ALU.add), ["rd0", "rd1"], ["rd0"])
                p.dve(lambda e, b=bbs[2][0]: e.tensor_tensor(out=acc2, in0=b, in1=g_sb[:, 2, :], op=ALU.mult),
                      [bbs[2][1], "g_sb2"], ["rd1"])
                p.dve(lambda e, m=m: e.tensor_tensor(out=mergedT[:, m, :], in0=acc1, in1=acc2, op=ALU.add),
                      ["rd0", "rd1"], ["mergedT"])
            for hf in range(2):
                ws, wk_ = wload(lambda s, hf=hf: [(s.rearrange("p (kc n) -> p kc n", n=512),
                                                   wout_d[:, hf * 512:(hf + 1) * 512].rearrange("(kc p) n -> p kc n", p=128))])
                w3 = ws.rearrange("p (kc n) -> p kc n", n=512)
                for mm in range(4):
                    m = hf * 4 + mm
                    bo, bok = gen_banks.next()
                    mm_group(bo, bok, [(w3[:, kc, mm * 128:(mm + 1) * 128], mergedT[:, kc, :]) for kc in range(8)],
                             [wk_, "mergedT"])
                    xs = xT[:, m, t * 512:(t + 1) * 512]
                    p.dve(lambda e, bo=bo, xs=xs: e.tensor_tensor(out=xs, in0=bo, in1=xs, op=ALU.add), [bok, xk], [xk])

        if stop != "mid":
            ffn(w2gu_d, w2dn_d, "ffn2_g")
        return finish()


def _perm_rows(j):
    rows = []
    for i in range(16):
        g = i // 2
        blk = 8 * g + (j if i % 2 == 0 else 7 - j)
        rows.append(np.arange(blk * 128, (blk + 1) * 128))
    return np.concatenate(rows)


def _host_inputs(inp):
    f = lambda a: np.ascontiguousarray(np.asarray(a, dtype=np.float32))
    x = f(inp["x"])
    mem = f(inp["mem"])
    pos = np.asarray(inp["positions"]).astype(np.int32)
    L0 = lambda k: f(inp[k])[0]

    def col(v, n):
        return np.ascontiguousarray(v.reshape(n, 128).T)

    def rep(v):
        return np.ascontiguousarray(np.broadcast_to(v[None, :], (128, v.shape[0])))

    cst = np.zeros((128, NCST), np.float32)

    def put(name, arr):
        o, w = CST[name]
        assert arr.shape == (128, w), (name, arr.shape)
        cst[:, o:o + w] = arr
    put("ffn1_g", col(L0("ffn1_norm"), 8))
    put("mix_g", col(L0("mix_norm"), 8))
    put("ffn2_g", col(L0("ffn2_norm"), 8))
    put("mem_g", col(L0("mem_norm"), 8))
    put("bgate", col(L0("b_gate"), 24))
    put("cqg", rep(L0("mla_cq_norm")))
    put("ckvg", rep(L0("mla_ckv_norm")))
    put("lng", rep(L0("sg_ln_g")))
    put("lnb", rep(L0("sg_ln_b")))
    put("qg", rep(L0("mla_q_norm")))
    put("kg", rep(L0("mla_k_norm")))
    put("mqg", rep(L0("mem_q_norm")))
    put("mkg", rep(L0("mem_k_norm")))
    sgb = L0("sg_b")
    bsT = np.zeros((128, 4, 128), np.float32)
    for c in range(4):
        bsT[0:64, c, :] = sgb[2 * c][None, :]
        bsT[64:128, c, :] = sgb[2 * c + 1][None, :]
    put("bsT", bsT.reshape(128, 512))
    half = 16
    invf = (10000.0 ** (-np.arange(half, dtype=np.float32) / half)).astype(np.float32)
    put("invf", rep(invf))
    tri = (np.arange(128)[:, None] <= np.arange(128)[None, :]).astype(np.float32)
    put("tri", tri)

    sgw = L0("sg_w")
    sgwT = np.ascontiguousarray(sgw.transpose(2, 0, 1)).reshape(128, 8 * 128)

    shared = {
        "cst": None, "sgwT": sgwT,
        "ffn1_w_gu": L0("ffn1_w_gu"), "ffn1_w_down": L0("ffn1_w_down"),
        "ffn2_w_gu": L0("ffn2_w_gu"), "ffn2_w_down": L0("ffn2_w_down"),
        "w_in": L0("w_in"), "mla_w_uq": L0("mla_w_uq"), "mla_w_ukv": L0("mla_w_ukv"),
        "mem_w_kv": L0("mem_w_kv"), "w_branch_a": L0("w_branch_a"), "w_branch_b": L0("w_branch_b"),
        "w_branch_c": L0("w_branch_c"), "w_out": L0("w_out"),
    }
    in_maps = []
    perms = []
    for c in range(NCORES):
        b, j = c // 4, c % 4
        rows = _perm_rows(j)
        perms.append((b, rows))
        m = dict(shared)
        m["cst"] = cst
        m["xT"] = np.ascontiguousarray(x[b][rows].T)
        m["memT"] = np.ascontiguousarray(mem[b].T)
        m["pos"] = np.ascontiguousarray(pos[b][rows].reshape(16, 128).T)
        cbf = np.zeros((128, NCBF), np.float32)
        cbf[:, 0:128] = np.eye(128, dtype=np.float32)
        cbf[:, 128:256] = 1.0
        mk = np.zeros((128, 8, 128), np.float32)
        for r in range(4):
            mk[:, r, :] = 1.0 if r < j else (tri if r == j else 0.0)
            mk[:, 4 + r, :] = 1.0 if r > j else (tri if r == j else 0.0)
        cbf[:, 256:] = mk.reshape(128, 1024)
        m["cbf"] = cbf.astype(ml_dtypes.bfloat16)
        in_maps.append(m)
    return in_maps, perms


_NC_CACHE = {}


def kernel(**inputs):
    in_maps, perms = _host_inputs(inputs)
    if "nc" not in _NC_CACHE:
        _NC_CACHE["nc"] = build()
    nc = _NC_CACHE["nc"]
    res = run_bass_kernel_spmd(nc, in_maps, core_ids=list(range(NCORES)))
    out = np.zeros((2, 8192, D), np.float32)
    for c in range(NCORES):
        b, rows = perms[c]
        out[b, rows, :] = np.asarray(res.results[c]["oT"], dtype=np.float32).T
    return out
```
